# Optimizing a Trainium2 kernel written in Bass

```python
import math
import jax, jax.numpy as jnp
from jax import lax
import numpy as np

D_MODEL = 1024
BATCH = 4
SEQ = 4096
DEPTH = 2
DEC_BATCH = 32
DEC_SEQ = 32
PAST_LEN = 1024

CHUNK = 64
Q_BLOCK = 128
HEAD_DIM = 64
N_GROUPS = 4
GROUP_WIDTH = D_MODEL // N_GROUPS
N_HEADS_G = GROUP_WIDTH // HEAD_DIM
DIFF_HALF = HEAD_DIM // 2
BAND_CHUNKS = 8
BAND_ROWS = BAND_CHUNKS * CHUNK
REL_CLIP = 128
CONV_W = 4
D_FF = ((8 * D_MODEL + 3 * 256 - 1) // (3 * 256)) * 256
FOX_FORGET_BIAS = 4.0
EPS = 1e-6
IN_SPLIT_SIZES = (GROUP_WIDTH, GROUP_WIDTH, GROUP_WIDTH, N_HEADS_G,
                  GROUP_WIDTH, GROUP_WIDTH, GROUP_WIDTH,
                  2 * GROUP_WIDTH, GROUP_WIDTH, N_HEADS_G, N_HEADS_G, GROUP_WIDTH,
                  GROUP_WIDTH, GROUP_WIDTH, GROUP_WIDTH)
N_IN = 13 * GROUP_WIDTH + 3 * N_HEADS_G

kernel_name = 'hybrid_streaming_encoder_step'


def _rms(x, g):
    x32 = x.astype(jnp.float32)
    y = x32 * lax.rsqrt(jnp.mean(x32 * x32, axis=-1, keepdims=True) + EPS)
    return (y * g.astype(jnp.float32)).astype(x.dtype)


def _heads(t):
    return t.reshape(t.shape[:-1] + (N_HEADS_G, HEAD_DIM))


def _split_cols(z):
    cuts = np.cumsum(IN_SPLIT_SIZES)[:-1].tolist()
    return jnp.split(z, cuts, axis=-1)


def _alibi_slopes(n):
    return 2.0 ** (-8.0 * (jnp.arange(n, dtype=jnp.float32) + 1.0) / n)


def _to_blocks(a, size):
    b, t = a.shape[:2]
    return jnp.moveaxis(a.reshape((b, t // size, size) + a.shape[2:]), 1, 0)


def _from_blocks(o):
    return jnp.moveaxis(o, 0, 1).reshape((o.shape[1], o.shape[0] * o.shape[2]) + o.shape[3:])


def _fox_attend(q, k, v, cum_q, cum_k, q_pos, k_pos):
    s = jnp.einsum('bqhd,bkhd->bhqk', q, k).astype(jnp.float32) * HEAD_DIM ** -0.5
    s = s + jnp.swapaxes(cum_q, 1, 2)[..., :, None] - jnp.swapaxes(cum_k, 1, 2)[..., None, :]
    s = jnp.where(k_pos[None, :] <= q_pos[:, None], s, -jnp.inf)
    p = jax.nn.softmax(s, axis=-1).astype(v.dtype)
    return jnp.einsum('bhqk,bkhd->bqhd', p, v)


def _diff_attend(q, k, v, lam, q_pos, k_pos):
    s = jnp.einsum('bqhjd,bkhjd->bhjqk', q, k).astype(jnp.float32) * DIFF_HALF ** -0.5
    dist = jnp.abs(q_pos[:, None] - k_pos[None, :]).astype(jnp.float32)
    s = s - _alibi_slopes(N_HEADS_G)[:, None, None, None] * dist
    seen = (k_pos[None, :] // CHUNK) <= (q_pos[:, None] // CHUNK)
    s = jnp.where(seen, s, -jnp.inf)
    p = jax.nn.softmax(s, axis=-1)
    a = (p[:, :, 0] - lam * p[:, :, 1]).astype(v.dtype)
    return jnp.einsum('bhqk,bkhd->bqhd', a, v)


def _band_attend(q, k, v, rel, valid, table):
    s = jnp.einsum('bcqhd,bckhd->bchqk', q, k).astype(jnp.float32) * HEAD_DIM ** -0.5
    s = s + table.astype(jnp.float32)[:, jnp.clip(rel, -REL_CLIP, REL_CLIP) + REL_CLIP]
    s = jnp.where(valid[None, :, None, None, :], s, -jnp.inf)
    p = jax.nn.softmax(s, axis=-1).astype(v.dtype)
    return jnp.einsum('bchqk,bckhd->bcqhd', p, v)


def _mlstm_chunk(carry, inp):
    C, n, m = carry
    q, k, v, ig, lf = inp
    L = q.shape[1]
    F = jnp.swapaxes(jnp.cumsum(lf, axis=1), 1, 2)
    igh = jnp.swapaxes(ig, 1, 2)
    causal = jnp.tril(jnp.ones((L, L), bool))
    logw = jnp.where(causal, F[..., :, None] - F[..., None, :] + igh[..., None, :], -jnp.inf)
    logb = F + m[..., None]
    m_t = jnp.maximum(logb, logw.max(-1))
    q32, k32, v32 = q.astype(jnp.float32), k.astype(jnp.float32), v.astype(jnp.float32)
    w = jnp.exp(logw - m_t[..., None]) * jnp.einsum('bthd,bshd->bhts', q32, k32)
    a = jnp.exp(logb - m_t)
    num = jnp.einsum('bhts,bshd->bhtd', w, v32) + a[..., None] * jnp.einsum('bthk,bhkv->bhtv', q32, C)
    den = w.sum(-1) + a * jnp.einsum('bthk,bhk->bht', q32, n)
    h = num / jnp.maximum(jnp.abs(den), jnp.exp(-m_t))[..., None]
    F_end = F[..., -1]
    log_end = F_end[..., None] - F + igh
    m_new = jnp.maximum(F_end + m, log_end.max(-1))
    w_end = jnp.exp(log_end - m_new[..., None])
    decay = jnp.exp(F_end + m - m_new)
    C_new = decay[..., None, None] * C + jnp.einsum('bhs,bshk,bshv->bhkv', w_end, k32, v32)
    n_new = decay[..., None] * n + jnp.einsum('bhs,bshk->bhk', w_end, k32)
    return (C_new, n_new, m_new), jnp.swapaxes(h, 1, 2)


def _causal_conv(u, buf, w, b):
    t = u.shape[1]
    up = jnp.concatenate([buf.astype(u.dtype), u], axis=1)
    y = b + up[:, 0:t] * w[0]
    for j in range(1, CONV_W):
        y = y + up[:, j:j + t] * w[j]
    return jax.nn.silu(y), up[:, up.shape[1] - (CONV_W - 1):]


def _layer(x, c, lp, li, st):
    (g1, g2, w_mod, b_mod, w_in, b_in, gq_fox, gq_diff, gq_band, conv_w, conv_b,
     lam_p, g_subln, g_mh, rel_table, w_out, w_gate, w_up, w_down) = lp
    B, T, _ = x.shape
    sh1, sc1, gt1, sh2, sc2, gt2 = jnp.split((jax.nn.silu(c) @ w_mod + b_mod)[:, None, :], 6, axis=-1)
    h = _rms(x, g1) * (1.0 + sc1) + sh1
    (fq, fk, fv, ff, dq, dk, dv, mqk, mv, mi, mf, mo, bq, bk, bv) = _split_cols(h @ w_in + b_in)
    fq = _rms(_heads(fq), gq_fox[0])
    fk = _rms(_heads(fk), gq_fox[1])
    fv = _heads(fv)
    f_logf = jax.nn.log_sigmoid(ff.astype(jnp.float32))
    dq = _rms(dq.reshape(B, T, N_HEADS_G, 2, DIFF_HALF), gq_diff[0])
    dk = _rms(dk.reshape(B, T, N_HEADS_G, 2, DIFF_HALF), gq_diff[1])
    dv = _heads(dv)
    lam_init = 0.8 - 0.6 * math.exp(-0.3 * li)
    l32 = lam_p.astype(jnp.float32)
    lam = jnp.exp(jnp.sum(l32[0] * l32[1])) - jnp.exp(jnp.sum(l32[2] * l32[3])) + lam_init
    conv_buf = jnp.zeros((B, CONV_W - 1, 2 * GROUP_WIDTH), x.dtype) if st is None else st[10]
    mqk, new_conv = _causal_conv(mqk, conv_buf, conv_w, conv_b)
    mq, mk = jnp.split(mqk, 2, axis=-1)
    mq = _heads(mq)
    mk = _heads(mk) * HEAD_DIM ** -0.5
    mv = _heads(mv)
    m_ig = mi.astype(jnp.float32)
    m_lf = jax.nn.log_sigmoid(mf.astype(jnp.float32))
    bq = _rms(_heads(bq), gq_band[0])
    bk = _rms(_heads(bk), gq_band[1])
    bv = _heads(bv)

    if st is None:
        pos = jnp.arange(T)
        cum = jnp.cumsum(f_logf, axis=1)
        o_fox = _from_blocks(lax.map(
            lambda a: _fox_attend(a[0], fk, fv, a[1], cum, a[2], pos),
            (_to_blocks(fq, Q_BLOCK), _to_blocks(cum, Q_BLOCK), pos.reshape(-1, Q_BLOCK))))
        o_diff = _from_blocks(lax.map(
            lambda a: _diff_attend(a[0], dk, dv, lam, a[1], pos),
            (_to_blocks(dq, Q_BLOCK), pos.reshape(-1, Q_BLOCK))))
        n_c = T // CHUNK
        pad = ((0, 0), (BAND_ROWS, 0), (0, 0), (0, 0))
        idx = (jnp.arange(n_c) * CHUNK)[:, None] + jnp.arange(BAND_ROWS + CHUNK)[None, :]
        rel = jnp.arange(CHUNK)[:, None] + BAND_ROWS - jnp.arange(BAND_ROWS + CHUNK)[None, :]
        o_band = _band_attend(bq.reshape(B, n_c, CHUNK, N_HEADS_G, HEAD_DIM),
                              jnp.pad(bk, pad)[:, idx], jnp.pad(bv, pad)[:, idx],
                              rel, idx >= BAND_ROWS, rel_table).reshape(B, T, N_HEADS_G, HEAD_DIM)
        carry0 = (jnp.zeros((B, N_HEADS_G, HEAD_DIM, HEAD_DIM), jnp.float32),
                  jnp.zeros((B, N_HEADS_G, HEAD_DIM), jnp.float32),
                  jnp.zeros((B, N_HEADS_G), jnp.float32))
        (mC, mn, mm), hs = lax.scan(_mlstm_chunk, carry0,
                                    tuple(_to_blocks(a, CHUNK) for a in (mq, mk, mv, m_ig, m_lf)))
        h_ml = _from_blocks(hs)
        keep = min(BAND_ROWS, T)
        band_k_new, band_v_new = bk[:, T - keep:], bv[:, T - keep:]
    else:
        (c_fk, c_fv, c_flf, c_dk, c_dv, c_bk, c_bv, s_c, s_n, s_m, _) = st
        P = c_fk.shape[1]
        cum = jnp.cumsum(jnp.concatenate([c_flf.astype(jnp.float32), f_logf], axis=1), axis=1)
        o_fox = _fox_attend(fq, jnp.concatenate([c_fk, fk], axis=1), jnp.concatenate([c_fv, fv], axis=1),
                            cum[:, P:], cum, P + jnp.arange(T), jnp.arange(P + T))
        Pd = c_dk.shape[1]
        o_diff = _diff_attend(dq, jnp.concatenate([c_dk, dk], axis=1), jnp.concatenate([c_dv, dv], axis=1),
                              lam, Pd + jnp.arange(T), jnp.arange(Pd + T))
        Lb = c_bk.shape[1]
        rel = (Lb + jnp.arange(T))[:, None] - jnp.arange(Lb + T)[None, :]
        o_band = _band_attend(bq[:, None], jnp.concatenate([c_bk, bk], axis=1)[:, None],
                              jnp.concatenate([c_bv, bv], axis=1)[:, None],
                              rel, jnp.ones((1, Lb + T), bool), rel_table)[:, 0]
        (mC, mn, mm), h_ml = _mlstm_chunk(
            (s_c.astype(jnp.float32), s_n.astype(jnp.float32), s_m.astype(jnp.float32)),
            (mq, mk, mv, m_ig, m_lf))
        band_k_new, band_v_new = bk, bv

    o_diff = _rms(o_diff, g_subln) * (1.0 - lam_init)
    o_ml = _rms(jax.nn.sigmoid(mo).reshape(B, T, N_HEADS_G, HEAD_DIM) * h_ml.astype(x.dtype), g_mh)
    mix = jnp.concatenate([o_fox, o_diff, o_ml, o_band], axis=2).reshape(B, T, D_MODEL)
    x = x + gt1 * (mix @ w_out)
    h2 = _rms(x, g2) * (1.0 + sc2) + sh2
    x = x + gt2 * ((jax.nn.silu(h2 @ w_gate) * (h2 @ w_up)) @ w_down)
    return x, (fk, fv, f_logf, dk, dv, band_k_new, band_v_new, mC, mn, mm, new_conv)


def setup_inputs(seed: int = 0) -> dict:
    key = jax.random.key(seed)
    ks = iter(jax.random.split(key, 48))

    def nrm(shape, scale):
        return scale * jax.random.normal(next(ks), shape, jnp.float32)

    D, H, d = D_MODEL, N_HEADS_G, HEAD_DIM
    band_len = min(BAND_ROWS, PAST_LEN)
    starts = np.concatenate([[0], np.cumsum(IN_SPLIT_SIZES)]).tolist()
    b_in = nrm((DEPTH, N_IN), 0.02)
    b_in = b_in.at[:, int(starts[3]):int(starts[4])].add(FOX_FORGET_BIAS)
    b_in = b_in.at[:, int(starts[10]):int(starts[11])].add(jnp.linspace(3.0, 6.0, H))
    return {
        'x_prompt': nrm((BATCH, SEQ, D), 1.0),
        'x_sample': nrm((DEC_BATCH, DEC_SEQ, D), 1.0),
        'c_prompt': nrm((BATCH, D), 1.0),
        'c_sample': nrm((DEC_BATCH, D), 1.0),
        'cache_fox_k': nrm((DEPTH, DEC_BATCH, PAST_LEN, H, d), 1.0),
        'cache_fox_v': nrm((DEPTH, DEC_BATCH, PAST_LEN, H, d), 1.0),
        'cache_fox_logf': jax.nn.log_sigmoid(FOX_FORGET_BIAS + nrm((DEPTH, DEC_BATCH, PAST_LEN, H), 1.0)),
        'cache_diff_k': nrm((DEPTH, DEC_BATCH, PAST_LEN, H, 2, DIFF_HALF), 1.0),
        'cache_diff_v': nrm((DEPTH, DEC_BATCH, PAST_LEN, H, d), 1.0),
        'cache_band_k': nrm((DEPTH, DEC_BATCH, band_len, H, d), 1.0),
        'cache_band_v': nrm((DEPTH, DEC_BATCH, band_len, H, d), 1.0),
        'state_mlstm_c': nrm((DEPTH, DEC_BATCH, H, d, d), 0.1),
        'state_mlstm_n': nrm((DEPTH, DEC_BATCH, H, d), 0.1),
        'state_mlstm_m': nrm((DEPTH, DEC_BATCH, H), 1.0),
        'state_conv': nrm((DEPTH, DEC_BATCH, CONV_W - 1, 2 * GROUP_WIDTH), 1.0),
        'norm1_g': 1.0 + nrm((DEPTH, D), 0.02),
        'norm2_g': 1.0 + nrm((DEPTH, D), 0.02),
        'w_mod': nrm((DEPTH, D, 6 * D), 0.5 * D ** -0.5),
        'b_mod': nrm((DEPTH, 6 * D), 0.02),
        'w_in': nrm((DEPTH, D, N_IN), D ** -0.5),
        'b_in': b_in,
        'qk_g_fox': 1.0 + nrm((DEPTH, 2, d), 0.02),
        'qk_g_diff': 1.0 + nrm((DEPTH, 2, DIFF_HALF), 0.02),
        'qk_g_band': 1.0 + nrm((DEPTH, 2, d), 0.02),
        'conv_w': nrm((DEPTH, CONV_W, 2 * GROUP_WIDTH), CONV_W ** -0.5),
        'conv_b': nrm((DEPTH, 2 * GROUP_WIDTH), 0.02),
        'diff_lambda': nrm((DEPTH, 4, DIFF_HALF), 0.1),
        'diff_subln_g': 1.0 + nrm((DEPTH, d), 0.02),
        'mlstm_norm_g': 1.0 + nrm((DEPTH, d), 0.02),
        'band_rel_bias': nrm((DEPTH, H, 2 * REL_CLIP + 1), 0.1),
        'w_out': nrm((DEPTH, D, D), D ** -0.5),
        'w_ffn_gate': nrm((DEPTH, D, D_FF), D ** -0.5),
        'w_ffn_up': nrm((DEPTH, D, D_FF), D ** -0.5),
        'w_ffn_down': nrm((DEPTH, D_FF, D), D_FF ** -0.5),
    }


def reference(x_prompt, x_sample, c_prompt, c_sample, cache_fox_k, cache_fox_v, cache_fox_logf,
              cache_diff_k, cache_diff_v, cache_band_k, cache_band_v, state_mlstm_c, state_mlstm_n,
              state_mlstm_m, state_conv, norm1_g, norm2_g, w_mod, b_mod, w_in, b_in, qk_g_fox,
              qk_g_diff, qk_g_band, conv_w, conv_b, diff_lambda, diff_subln_g, mlstm_norm_g,
              band_rel_bias, w_out, w_ffn_gate, w_ffn_up, w_ffn_down):
    weights = (norm1_g, norm2_g, w_mod, b_mod, w_in, b_in, qk_g_fox, qk_g_diff, qk_g_band,
               conv_w, conv_b, diff_lambda, diff_subln_g, mlstm_norm_g, band_rel_bias,
               w_out, w_ffn_gate, w_ffn_up, w_ffn_down)
    caches = (cache_fox_k, cache_fox_v, cache_fox_logf, cache_diff_k, cache_diff_v,
              cache_band_k, cache_band_v, state_mlstm_c, state_mlstm_n, state_mlstm_m, state_conv)
    x = x_prompt
    p_out = []
    for l in range(DEPTH):
        x, s = _layer(x, c_prompt, tuple(w[l] for w in weights), l, None)
        p_out.append(s)
    y_prompt = x
    x = x_sample
    s_out = []
    for l in range(DEPTH):
        x, s = _layer(x, c_sample, tuple(w[l] for w in weights), l, tuple(cc[l] for cc in caches))
        s_out.append(s)
    y_sample = x
    (p_fox_k, p_fox_v, p_fox_logf, p_diff_k, p_diff_v, p_band_k, p_band_v,
     p_mlstm_c, p_mlstm_n, p_mlstm_m, p_conv) = [jnp.stack(z) for z in zip(*p_out)]
    (s_fox_k, s_fox_v, s_fox_logf, s_diff_k, s_diff_v, s_band_k, s_band_v,
     s_mlstm_c, s_mlstm_n, s_mlstm_m, s_conv) = [jnp.stack(z) for z in zip(*s_out)]
    return (y_prompt, y_sample,
            p_fox_k, p_fox_v, p_fox_logf, p_diff_k, p_diff_v, p_band_k, p_band_v,
            p_mlstm_c, p_mlstm_n, p_mlstm_m, p_conv,
            s_fox_k, s_fox_v, s_fox_logf, s_diff_k, s_diff_v, s_band_k, s_band_v,
            s_mlstm_c, s_mlstm_n, s_mlstm_m, s_conv)
```

```python
import math
import numpy as np
import concourse.bass as bass
import concourse.mybir as mybir
from concourse.bass_utils import run_bass_kernel_spmd
from contextlib import ExitStack

F32 = mybir.dt.float32
BF16 = mybir.dt.bfloat16
AF = mybir.ActivationFunctionType
ALU = mybir.AluOpType
AX = mybir.AxisListType

D = 1024
DFF = 2816
NIN = 3340
EPS = 1e-6
DEPTH = 2


class Res:
    __slots__ = ("name", "w", "r")

    def __init__(self, name=""):
        self.name = name
        self.w = {}
        self.r = {}


class Sched:
    NDS = 48
    NHW = 32

    def __init__(self, nc, stack):
        self.nc = nc
        self.names = ["pe", "act", "dve", "pool", "sp"]
        self.streams = {e: [] for e in self.names}
        self.sems = {e: stack.enter_context(nc.semaphore("s_" + e)) for e in ["pe", "act", "dve", "pool"]}
        self.cnt = {e: 0 for e in self.sems}
        self.dsems = [stack.enter_context(nc.semaphore("d%d" % i)) for i in range(self.NDS)]
        self.dval = [0] * self.NDS
        self.drr = 0
        self.drr_sw = 0
        self.seen = {e: {} for e in self.names}
        self.nops = 0

    def _waits(self, eng, reads, writes, extra=(), par=False):
        waits = {}

        def add(m):
            if m is None:
                return
            k, v = m
            if waits.get(k, 0) < v:
                waits[k] = v
        for r in reads:
            for k, v in r.w.items():
                add((k, v))
        for r in writes:
            if not par:
                for k, v in r.w.items():
                    add((k, v))
            for k, v in r.r.items():
                add((k, v))
        for m in extra:
            add(m)
        need = []
        for k, v in waits.items():
            if eng == "pe" and k == ("e", "pe"):
                continue
            if self.seen[eng].get(k, 0) >= v:
                continue
            self.seen[eng][k] = v
            need.append((k, v))
        return need

    def _commit(self, mark, reads, writes, par=False):
        k, v = mark
        for r in reads:
            if r.r.get(k, 0) < v:
                r.r[k] = v
        for r in writes:
            if par and not r.r:
                if r.w.get(k, 0) < v:
                    r.w[k] = v
            else:
                r.w = {k: v}
            r.r = {}

    def op(self, eng, fn, reads=(), writes=(), par=False):
        need = self._waits(eng, reads, writes, par=par)
        self.cnt[eng] += 1
        mark = (("e", eng), self.cnt[eng])
        self.streams[eng].append((need, fn, mark))
        self._commit(mark, reads, writes, par=par)
        self.nops += 1
        return mark

    def dma(self, q, out, in_, reads=(), writes=(), par=False, **kw):
        if q == "pool":
            j = self.NHW + self.drr_sw
            self.drr_sw = (self.drr_sw + 1) % (self.NDS - self.NHW)
        else:
            j = self.drr
            self.drr = (j + 1) % self.NHW
        extra = []
        if self.dval[j] > 0:
            extra.append((("d", j), self.dval[j]))
        need = self._waits(q, reads, writes, extra, par=par)
        self.dval[j] += 16
        mark = (("d", j), self.dval[j])
        self.streams[q].append((need, (lambda e: e.dma_start(out=out, in_=in_, **kw)), mark))
        self._commit(mark, reads, writes, par=par)
        self.nops += 1
        return mark

    def barrier(self):
        for eng in self.names:
            need = []
            for e2 in self.sems:
                if e2 == eng:
                    continue
                k = ("e", e2)
                v = self.cnt[e2]
                if v > 0 and self.seen[eng].get(k, 0) < v:
                    self.seen[eng][k] = v
                    need.append((k, v))
            for j in range(self.NDS):
                k = ("d", j)
                v = self.dval[j]
                if v > 0 and self.seen[eng].get(k, 0) < v:
                    self.seen[eng][k] = v
                    need.append((k, v))
            if need:
                self.streams[eng].append((need, None, None))

    def _h(self, k):
        return self.sems[k[1]] if k[0] == "e" else self.dsems[k[1]]

    def replay(self, name, e):
        for need, fn, mark in self.streams[name]:
            for k, v in need:
                e.wait_ge(self._h(k), v)
            if fn is None:
                continue
            ins = fn(e)
            if mark is not None:
                ins.then_inc(self._h(mark[0]), 16 if mark[0][0] == "d" else 1)

    def emit(self):
        nc = self.nc
        with nc.Block() as block:
            @block.tensor
            def _(e):
                self.replay("pe", e)

            @block.scalar
            def _(e):
                self.replay("act", e)

            @block.vector
            def _(e):
                self.replay("dve", e)

            @block.gpsimd
            def _(e):
                self.replay("pool", e)

            @block.sync
            def _(e):
                self.replay("sp", e)


class Arena:
    def __init__(self, ap, nwords):
        self.ap = ap
        self.n = nwords
        self.off = 0
        self.marks = []

    def take(self, shape, dtype):
        size = 2 if dtype == BF16 else 4
        n = 1
        for s in shape:
            n *= s
        words = (n * size + 3) // 4
        words = (words + 7) // 8 * 8
        assert self.off + words <= self.n, ("arena overflow", self.off, words, self.n)
        a = self.ap[:, self.off:self.off + words]
        self.off += words
        if dtype != F32:
            a = a.bitcast(dtype)
        a = a[:, 0:n]
        if len(shape) == 2:
            a = a.rearrange("p (a b) -> p a b", a=shape[0], b=shape[1])
        elif len(shape) == 3:
            a = a.rearrange("p (a b c) -> p a b c", a=shape[0], b=shape[1], c=shape[2])
        return a

    def push(self):
        self.marks.append(self.off)

    def pop(self):
        self.off = self.marks.pop()


def alibi_slopes():
    return [2.0 ** (-8.0 * (i + 1.0) / 4) for i in range(4)]


C_FQ, C_FK, C_FV, C_FF = 0, 256, 512, 768
C_DQ, C_DK, C_DV = 772, 1028, 1284
C_MQK, C_MV, C_MI, C_MF, C_MO = 1540, 2052, 2308, 2312, 2316
C_BQ, C_BK, C_BV = 2572, 2828, 3084
FM_COLS = [C_FQ, C_FQ + 128, C_FK, C_FK + 128, C_DQ, C_DQ + 128, C_DK, C_DK + 128,
           C_BQ, C_BQ + 128, C_BK, C_BK + 128, C_MQK, C_MQK + 128, C_MQK + 256, C_MQK + 384]
TM_SRC = [(C_FV, 256), (C_DV, 256), (C_BV, 256), (C_MV, 256), (C_MO, 256), (C_FF, 4)]
NTM = 1284


def build_program(T, P, LB, debug=False):
    TT = T + 128
    NPB = T // 512
    nc = bass.Bass("TRN2", target_bir_lowering=False)

    def din(name, shape, dt=F32):
        return nc.dram_tensor(name, list(shape), dt, kind="ExternalInput").ap()

    def dout(name, shape, dt=F32):
        return nc.dram_tensor(name, list(shape), dt, kind="ExternalOutput").ap()

    def dscr(name, shape, dt=F32):
        return nc.dram_tensor(name, list(shape), dt, kind="Internal").ap()

    I = {}
    I["xp"] = din("xp", [T, D])
    I["xs"] = din("xs", [128, D])
    I["cT"] = din("cT", [128, 8, 5])
    I["w_mod"] = din("w_mod", [DEPTH, D, 6 * D])
    I["b_modT"] = din("b_modT", [DEPTH, 128, 48])
    I["w_in"] = din("w_in", [DEPTH, D, NIN])
    I["w_out"] = din("w_out", [DEPTH, D, D])
    I["w_gate"] = din("w_gate", [DEPTH, D, DFF])
    I["w_up"] = din("w_up", [DEPTH, D, DFF])
    I["w_down"] = din("w_down", [DEPTH, DFF, D])
    I["g1T"] = din("g1T", [DEPTH, 128, 8])
    I["g2T"] = din("g2T", [DEPTH, 128, 8])
    I["binT"] = din("binT", [DEPTH, 128, 16])
    I["bgate"] = din("bgate", [DEPTH, 4, 2])
    I["btm"] = din("btm", [DEPTH, 128, NTM])
    I["gqk"] = din("gqk", [DEPTH, 128, 12])
    I["convw"] = din("convw", [DEPTH, 128, 4, 4])
    I["convb"] = din("convb", [DEPTH, 128, 4])
    I["convst"] = din("convst", [DEPTH, 128, 4, 4, 3])
    I["lamb"] = din("lamb", [DEPTH, 128, 4, 32])
    I["gsub"] = din("gsub", [DEPTH, 128, 64])
    I["gmh"] = din("gmh", [DEPTH, 128, 64])
    I["bandT"] = din("bandT", [DEPTH, 4, 3, 128, 128])
    I["c_fk"] = din("c_fk", [DEPTH, 4, P, 256])
    I["c_fv"] = din("c_fv", [DEPTH, 4, P, 256])
    I["c_flf"] = din("c_flf", [DEPTH, 4, P, 4])
    I["c_dk"] = din("c_dk", [DEPTH, 4, P, 256])
    I["c_dv"] = din("c_dv", [DEPTH, 4, P, 256])
    I["c_bk"] = din("c_bk", [DEPTH, 4, LB, 256])
    I["c_bv"] = din("c_bv", [DEPTH, 4, LB, 256])
    I["s_c"] = din("s_c", [DEPTH, 4, 4, 64, 64])
    I["s_n"] = din("s_n", [DEPTH, 4, 4, 64])
    I["s_m"] = din("s_m", [DEPTH, 4, 4])
    I["ident"] = din("ident", [128, 128])
    I["blk64"] = din("blk64", [128, 128])
    I["blk32"] = din("blk32", [128, 128])
    I["tri"] = din("tri", [128, 128])
    I["cmask"] = din("cmask", [128, 128])
    I["m4"] = din("m4", [128, 128])
    I["alcol"] = din("alcol", [128, 4, 33])
    I["emd_arg"] = din("emd_arg", [4, 128, 128])
    I["sel0"] = din("sel0", [128, 128])
    I["alw"] = din("alw", [128, 4, 36])
    I["cdiff"] = din("cdiff", [4, 128, 128])
    I["sel127"] = din("sel127", [128, 128])
    I["sel31"] = din("sel31", [128, 128])
    I["s_mcol"] = din("s_mcol", [DEPTH, 4, 4, 1])
    I["s_mb"] = din("s_mb", [DEPTH, 4, 128, 4])

    O = {}
    O["YT"] = dout("YT", [D, TT])
    O["o_fk"] = dout("o_fk", [DEPTH, 256, TT])
    O["o_fv"] = dout("o_fv", [DEPTH, TT, 256])
    O["o_flf"] = dout("o_flf", [DEPTH, TT, 4])
    O["o_dk"] = dout("o_dk", [DEPTH, 256, TT])
    O["o_dv"] = dout("o_dv", [DEPTH, TT, 256])
    O["o_bk"] = dout("o_bk", [DEPTH, 256, TT])
    O["o_bv"] = dout("o_bv", [DEPTH, TT, 256])
    O["o_mc"] = dout("o_mc", [DEPTH, 5, 4, 64, 65])
    O["o_mm"] = dout("o_mm", [DEPTH, 5, 4])
    O["o_conv"] = dout("o_conv", [DEPTH, 4, 128, 5, 3])
    if debug:
        O["dbg"] = dout("dbg", [128, 4096])

    XT = [dscr("XT0", [D, TT]), dscr("XT1", [D, TT])]
    XM = dscr("XM", [D, TT])
    QS = [dscr("QS%d" % g, [256, TT], BF16) for g in range(4)]
    KS = [dscr("KS%d" % g, [256, TT], BF16) for g in range(4)]
    VS = [dscr("VS%d" % g, [TT, 256], BF16) for g in range(4)]
    SIGO = dscr("SIGO", [TT, 256])
    GI = dscr("GI", [4, TT])
    GF = dscr("GF", [4, TT])
    LOGF = dscr("LOGF", [TT, 4])
    MIX = dscr("MIX", [TT, D], BF16)

    st = ExitStack()
    with st:
        S = Sched(nc, st)
        NW = 52000
        arena_t = st.enter_context(nc.sbuf_tensor("arena", [128, NW], F32))
        A = Arena(arena_t[:], NW)
        PS = [st.enter_context(nc.psum_tensor("ps%d" % i, [128, 512], F32)) for i in range(8)]
        RPS = [Res("ps%d" % i) for i in range(8)]

        ident_f = A.take([128], F32)
        ident_b = A.take([128], BF16)
        ones_b = A.take([128], BF16)
        blk64 = A.take([128], BF16)
        blk32 = A.take([128], BF16)
        tri_f = A.take([128], F32)
        cmask_f = A.take([128], F32)
        m4_f = A.take([128], F32)
        alcol = A.take([4, 33], F32)
        modT = A.take([48, 5], F32)
        A1 = A.take([8, 5], F32)
        A2 = A.take([8, 5], F32)
        R_const = Res("const")
        R_mod = Res("mod")
        tmpc = A.take([128], F32)
        S.dma("sp", ident_f, I["ident"], writes=[R_const])
        S.dma("sp", tri_f, I["tri"], writes=[R_const])
        S.dma("sp", cmask_f, I["cmask"], writes=[R_const])
        S.dma("sp", m4_f, I["m4"], writes=[R_const])
        S.dma("sp", alcol, I["alcol"], writes=[R_const])
        S.dma("pool", blk64, I["blk64"], writes=[R_const])
        S.dma("pool", blk32, I["blk32"], writes=[R_const])
        S.dma("pool", ident_b, I["ident"], writes=[R_const])
        S.op("dve", lambda e: e.memset(ones_b, 1.0), writes=[R_const])
        S.barrier()

        blocks = [(i * 512, 512, [(0, 512, 0)]) for i in range(NPB)]
        blocks.append((T, 128, [(32 * s, 32, 1 + s) for s in range(4)]))

        def act(fn, reads, writes, par=False):
            return S.op("act", fn, reads, writes, par=par)

        def dve(fn, reads, writes, par=False):
            return S.op("dve", fn, reads, writes, par=par)

        def pe(fn, reads, writes, par=False):
            return S.op("pe", fn, reads, writes, par=par)

        modTn = A.take([48, 5], F32)
        A1n = A.take([8, 5], F32)
        A2n = A.take([8, 5], F32)
        R_modn = Res("modn")

        def gen_M(l, modT_d, A1_d, A2_d, R_d):
            cT = A.take([8, 5], F32)
            scT = A.take([8, 5], BF16)
            sgm = A.take([8, 5], F32)
            bmod = A.take([48], F32)
            g1 = A.take([8], F32)
            g2 = A.take([8], F32)
            wm = [A.take([8, 1536], BF16) for _ in range(2)]
            R_wm = [Res(), Res()]
            R_c = Res()
            R_sc = Res()
            S.dma("sp", cT, I["cT"], writes=[R_c])
            S.dma("sp", bmod, I["b_modT"][l], writes=[R_c])
            S.dma("sp", g1, I["g1T"][l], writes=[R_c])
            S.dma("sp", g2, I["g2T"][l], writes=[R_c])
            S.op("act", lambda e: e.activation(out=sgm, in_=cT, func=AF.Sigmoid), [R_c], [R_sc])
            S.op("dve", lambda e: e.tensor_tensor(out=scT, in0=sgm, in1=cT, op=ALU.mult), [R_sc, R_c], [R_sc])
            for sl in range(4):
                b = sl % 2
                S.dma("pool", wm[b], I["w_mod"][l][:, sl * 1536:(sl + 1) * 1536].rearrange("(c p) n -> p c n", p=128),
                      writes=[R_wm[b]])
                for jj in range(12):
                    j = sl * 12 + jj

                    def mm(e, b=b, jj=jj):
                        for c in range(8):
                            ins = e.matmul(out=PS[0][:, 0:5], lhsT=wm[b][:, c, jj * 128:(jj + 1) * 128], rhs=scT[:, c, :],
                                           start=(c == 0), stop=(c == 7))
                        return ins
                    S.op("pe", mm, [R_wm[b], R_sc], [RPS[0]])
                    S.op("dve", lambda e, j=j: e.tensor_scalar(out=modT_d[:, j, :], in0=PS[0][:, 0:5], scalar1=bmod[:, j:j + 1],
                                                               scalar2=None, op0=ALU.add), [RPS[0], R_c], [R_d])
                    if jj % 2 == 1:
                        yield
            for s5 in range(5):
                S.op("dve", lambda e, s5=s5: e.scalar_tensor_tensor(out=A1_d[:, :, s5], in0=modT_d[:, 8:16, s5], scalar=1.0, in1=g1,
                                                                    op0=ALU.add, op1=ALU.mult), [R_d, R_c], [R_d])
                S.op("dve", lambda e, s5=s5: e.scalar_tensor_tensor(out=A2_d[:, :, s5], in0=modT_d[:, 32:40, s5], scalar=1.0, in1=g2,
                                                                    op0=ALU.add, op1=ALU.mult), [R_d, R_c], [R_d])
            yield

        R_MIX = Res("MIX")
        R_XT = Res("XT")

        ones_f = A.take([128], F32)
        sel0 = A.take([128], F32)
        emd = A.take([4, 128], F32)
        alw = A.take([4, 36], F32)
        cdiff = A.take([4, 128], F32)
        S.dma("sp", alw, I["alw"], writes=[R_const])
        S.dma("sp", cdiff, I["cdiff"].rearrange("h k q -> k h q"), writes=[R_const])
        S.dma("sp", sel0, I["sel0"], writes=[R_const])
        S.dma("sp", emd, I["emd_arg"].rearrange("h k q -> k h q"), writes=[R_const])
        S.op("dve", lambda e: e.memset(ones_f, 1.0), writes=[R_const])
        S.op("act", lambda e: e.activation(out=emd, in_=emd, func=AF.Exp), reads=[R_const], writes=[R_const])
        for h_ in range(4):
            S.op("dve", lambda e, h_=h_: e.tensor_tensor(out=emd[:, h_, :], in0=emd[:, h_, :], in1=cmask_f, op=ALU.mult),
                 reads=[R_const], writes=[R_const])
        S.barrier()
        NKB_P = T // 128
        NCB = P // 128
        NLB = LB // 128

        sel127 = A.take([128], F32)
        sel31 = A.take([128], F32)
        S.dma("sp", sel127, I["sel127"], writes=[R_const])
        S.dma("sp", sel31, I["sel31"], writes=[R_const])

        def phase_ML(l):
            A.push()
            NTM_ = max(T, 128)
            gi_r = A.take([NTM_], F32)
            gf_r = A.take([NTM_], F32)
            G_r = A.take([NTM_], F32)
            MM_r = A.take([NTM_], F32)
            mt_r = A.take([NTM_], F32)
            ones_r = A.take([NTM_], F32)
            dve(lambda e: e.memset(ones_r[0:4, :], 1.0), [], [R_const])
            NCM = max(T // 128, 1)
            cols = A.take([NCM, 12], F32)
            MMe = A.take([NCM, 4], F32)
            MMp = A.take([NCM, 4], F32)
            egs = A.take([NCM, 4], F32)
            eM = A.take([NCM, 4], F32)
            acol = A.take([NCM, 4], F32)
            dec = A.take([NCM, 4], F32)
            emt = A.take([NCM, 4], F32)
            m0c = A.take([1], F32)
            gmh = A.take([64], F32)
            Cn = A.take([2, 65], F32)
            Cnb = A.take([2, 65], BF16)
            Cnb2 = [Cnb, A.take([2, 65], BF16)]
            R_Cnb2 = [Res(), Res()]
            dec2 = A.take([NCM, 2], F32)
            R_osq = Res()
            R_oms = Res()
            qTm = [A.take([2, 128], BF16) for _ in range(2)]
            kTm = [A.take([2, 128], BF16) for _ in range(2)]
            vst = [A.take([256], BF16) for _ in range(2)]
            VAm = [A.take([4, 65], BF16) for _ in range(2)]
            sig = [A.take([256], F32) for _ in range(2)]
            R_in = [Res(), Res()]
            R_VAm = [Res(), Res()]
            kw = A.take([256], BF16)
            wT = [A.take([128], BF16) for _ in range(2)]
            R_wT = [Res(), Res()]
            tmpA = A.take([4, 65], F32)
            resm = A.take([4, 65], F32)
            den = A.take([4], F32)
            o4 = A.take([4, 64], F32)
            osq = A.take([4, 64], F32)
            oms = A.take([4], F32)
            o4b = [A.take([256], BF16) for _ in range(2)]
            R_o4b = [Res(), Res()]
            R_rows, R_cols, R_ex, R_Cn, R_Cnb, R_kw, R_tmp, R_res, R_o4, R_g = [Res() for _ in range(10)]
            S.dma("sp", gmh, I["gmh"][l], writes=[R_g])
            for vm in VAm:
                dve(lambda e, vm=vm: e.memset(vm[:, :, 64:65], 1.0), [], [R_VAm[0], R_VAm[1]])
            LN8 = math.log(0.125)
            mgen = gen_M(l + 1, modTn, A1n, A2n, R_modn) if l + 1 < DEPTH else iter(())
            SB_ = [2, 6]
            PB_ = [4, 7]

            def run_ml(seq, tok0, NTOK, L):
                NC_ = NTOK // L
                sel = sel127 if L == 128 else sel31
                S.dma("sp", gi_r[0:4, 0:NTOK], GI[:, tok0:tok0 + NTOK], writes=[R_rows], par=True)
                S.dma("sp", gf_r[0:4, 0:NTOK], GF[:, tok0:tok0 + NTOK], writes=[R_rows], par=True)
                if seq == 0:
                    init = 0.0
                    dve(lambda e: e.memset(MMp[:, 0, :], 0.0), [], [R_ex])
                    dve(lambda e: e.memset(Cn, 0.0), [], [R_Cn])
                    rdi = []
                else:
                    S.dma("sp", m0c[0:4, :], I["s_mcol"][l, seq - 1], writes=[R_rows])
                    S.dma("sp", MMp[:, 0, :], I["s_mb"][l, seq - 1], writes=[R_ex])
                    init = m0c[0:4, 0:1]
                    for h in range(4):
                        r0 = (h % 2) * 64
                        S.dma("sp", Cn[r0:r0 + 64, h // 2, 0:64], I["s_c"][l, seq - 1, h], writes=[R_Cn], par=True)
                        S.dma("sp", Cn[r0:r0 + 64, h // 2, 64:65], I["s_n"][l, seq - 1, h].rearrange("(k o) -> k o", o=1), writes=[R_Cn], par=True)
                dve(lambda e: e.tensor_copy(out=Cnb2[0], in_=Cn), [R_Cn], [R_Cnb2[0]])
                dve(lambda e: e.tensor_tensor_scan(out=mt_r[0:4, 0:NTOK], data0=ones_r[0:4, 0:NTOK], data1=gf_r[0:4, 0:NTOK],
                                                   initial=0.0, op0=ALU.mult, op1=ALU.add), [R_rows, R_const], [R_rows])
                dve(lambda e: e.tensor_tensor(out=G_r[0:4, 0:NTOK], in0=gi_r[0:4, 0:NTOK], in1=mt_r[0:4, 0:NTOK], op=ALU.subtract), [R_rows], [R_rows])
                dve(lambda e: e.tensor_tensor_scan(out=MM_r[0:4, 0:NTOK], data0=ones_r[0:4, 0:NTOK], data1=G_r[0:4, 0:NTOK],
                                                   initial=init, op0=ALU.mult, op1=ALU.max), [R_rows, R_const], [R_rows])
                dve(lambda e: e.tensor_tensor(out=mt_r[0:4, 0:NTOK], in0=mt_r[0:4, 0:NTOK], in1=MM_r[0:4, 0:NTOK], op=ALU.add), [R_rows], [R_rows])
                if debug == "ml0":
                    return
                S.dma("pool", O["o_mm"][l, seq].rearrange("(h o) -> h o", o=1), mt_r[0:4, NTOK - 1:NTOK], reads=[R_rows])
                if debug == "ml1":
                    return
                if L < 128:
                    dve(lambda e: e.memset(cols[:, 0:NC_, :], 0.0), [], [R_cols])
                for c in range(NC_):
                    def trc(e, c=c):
                        for i_, rr_ in enumerate([G_r, MM_r, mt_r]):
                            ins = e.matmul(out=PS[0][0:L, i_ * 4:(i_ + 1) * 4], lhsT=rr_[0:4, c * L:(c + 1) * L], rhs=ident_f[0:4, 0:4],
                                           start=True, stop=True, skip_group_check=True)
                        return ins
                    pe(trc, [R_rows, R_const], [RPS[0]])
                    act(lambda e, c=c: e.activation(out=cols[0:L, c, :], in_=PS[0][0:L, 0:12], func=AF.Copy), [RPS[0]], [R_cols])
                if debug == "ml2":
                    return
                n4 = NC_ * 4
                pe(lambda e: e.matmul(out=PS[1][:, 0:n4], lhsT=sel, rhs=cols[:, 0:NC_, 4:8], start=True, stop=True), [R_cols, R_const], [RPS[1]])
                dve(lambda e: e.tensor_copy(out=MMe[:, 0:NC_, :], in_=PS[1][:, 0:n4].rearrange("p (c h) -> p c h", h=4)), [RPS[1]], [R_ex])
                if NC_ > 1:
                    dve(lambda e: e.tensor_copy(out=MMp[:, 1:NC_, :], in_=MMe[:, 0:NC_ - 1, :]), [R_ex], [R_ex])
                Gc = cols[:, 0:NC_, 0:4]
                MMc = cols[:, 0:NC_, 4:8]
                mtc = cols[:, 0:NC_, 8:12]
                for (dst_, a_, b_, bias_) in [(egs, Gc, MMe, LN8), (eM, MMe, MMc, 0.0), (acol, MMp, MMc, 0.0), (dec, MMp, MMe, 0.0)]:
                    dve(lambda e, dst_=dst_, a_=a_, b_=b_: e.tensor_tensor(out=dst_[:, 0:NC_, :], in0=a_ if a_ is Gc else a_[:, 0:NC_, :],
                                                                           in1=b_ if b_ is MMc else b_[:, 0:NC_, :], op=ALU.subtract),
                        [R_cols, R_ex], [R_ex])
                    if bias_ != 0.0:
                        dve(lambda e, dst_=dst_, bias_=bias_: e.tensor_scalar(out=dst_[:, 0:NC_, :], in0=dst_[:, 0:NC_, :], scalar1=bias_, scalar2=None, op0=ALU.add),
                            [R_ex], [R_ex])
                    act(lambda e, dst_=dst_: e.activation(out=dst_[:, 0:NC_, :], in_=dst_[:, 0:NC_, :], func=AF.Exp), [R_ex], [R_ex])
                act(lambda e: e.activation(out=emt[:, 0:NC_, :], in_=mtc, func=AF.Exp, scale=-1.0), [R_cols], [R_ex])
                dve(lambda e: e.tensor_copy(out=dec2[0:64, 0:NC_, :], in_=dec[0:64, 0:NC_, 0:4:2]), [R_ex], [R_ex])
                dve(lambda e: e.tensor_copy(out=dec2[64:128, 0:NC_, :], in_=dec[64:128, 0:NC_, 1:4:2]), [R_ex], [R_ex])
                if debug == "ml3":
                    return
                for c in range(NC_):
                    ib = c % 2
                    t0 = tok0 + c * L
                    for hp in range(2):
                        S.dma("sp", qTm[ib][:, hp, 0:L], QS[3][hp * 128:(hp + 1) * 128, t0:t0 + L], writes=[R_in[ib]], par=True)
                        S.dma("sp", kTm[ib][:, hp, 0:L], KS[3][hp * 128:(hp + 1) * 128, t0:t0 + L], writes=[R_in[ib]], par=True)
                    S.dma("sp", vst[ib][0:L, :], VS[3][t0:t0 + L, :], writes=[R_in[ib]], par=True)
                    S.dma("sp", sig[ib][0:L, :], SIGO[t0:t0 + L, :], writes=[R_in[ib]], par=True)
                    act(lambda e, ib=ib: e.activation(out=VAm[ib][0:L, :, 0:64], in_=vst[ib][0:L, :].rearrange("p (h d) -> p h d", h=4), func=AF.Copy),
                        [R_in[ib]], [R_VAm[ib]])
                    if debug == "ml4":
                        continue
                    psb = PS[1][:, :].bitcast(BF16)

                    def trk2(e, ib=ib, psb=psb):
                        for hp in range(2):
                            ins = e.transpose(out=psb[0:L, hp * 128:(hp + 1) * 128], in_=kTm[ib][:, hp, 0:L], identity=ident_b)
                        return ins
                    pe(trk2, [R_in[ib], R_const], [RPS[1]])
                    dve(lambda e, c=c, psb=psb: e.tensor_tensor(out=kw[0:L, :].rearrange("p (h d) -> p h d", h=4),
                                                                in0=psb[0:L, 0:256].rearrange("p (h d) -> p h d", h=4),
                                                                in1=egs[0:L, c, :].unsqueeze(2).to_broadcast([L, 4, 64]), op=ALU.mult),
                        [RPS[1], R_ex], [R_kw])

                    if debug == "ml5":
                        continue

                    def qk(e, ib=ib):
                        for h in range(4):
                            r0 = (h % 2) * 64
                            ins = e.matmul(out=PS[SB_[h % 2]][0:L, h * L:(h + 1) * L], lhsT=kTm[ib][r0:r0 + 64, h // 2, 0:L], rhs=qTm[ib][r0:r0 + 64, h // 2, 0:L],
                                           start=True, stop=True, skip_group_check=True)
                        return ins
                    pe(qk, [R_in[ib]], [RPS[2], RPS[6]])

                    def p2(e, ib=ib, c=c):
                        for h in range(4):
                            r0 = (h % 2) * 64
                            ins = e.matmul(out=PS[PB_[h % 2]][0:L, h * 65:(h + 1) * 65], lhsT=qTm[ib][r0:r0 + 64, h // 2, 0:L], rhs=Cnb2[c % 2][r0:r0 + 64, h // 2, :],
                                           start=True, stop=True, skip_group_check=True)
                        return ins
                    pe(p2, [R_in[ib], R_Cnb2[c % 2]], [RPS[4], RPS[7]])
                    if debug == "ml6":
                        continue
                    for h in range(4):
                        wb_ = h % 2
                        dve(lambda e, h=h, c=c, wb_=wb_: e.scalar_tensor_tensor(out=wT[wb_][0:L, 0:L], in0=PS[SB_[h % 2]][0:L, h * L:(h + 1) * L],
                                                                                scalar=egs[0:L, c, h:h + 1], in1=tri_f[0:L, 0:L],
                                                                                op0=ALU.mult, op1=ALU.mult), [RPS[SB_[h % 2]], R_ex, R_const], [R_wT[wb_]])
                        pe(lambda e, h=h, wb_=wb_, ib=ib: e.matmul(out=PS[3][0:L, h * 65:(h + 1) * 65], lhsT=wT[wb_][0:L, 0:L], rhs=VAm[ib][0:L, h, :],
                                                                   start=True, stop=True, skip_group_check=True), [R_wT[wb_], R_VAm[ib]], [RPS[3]])
                        pe(lambda e, h=h, ib=ib: e.matmul(out=PS[5][(h % 2) * 64:(h % 2) * 64 + 64, (h // 2) * 65:(h // 2) * 65 + 65], lhsT=kw[0:L, h * 64:(h + 1) * 64],
                                                          rhs=VAm[ib][0:L, h, :], start=True, stop=True, skip_group_check=True,
                                                          tile_position=(0, (h % 2) * 64)), [R_kw, R_VAm[ib]], [RPS[5]])
                    cb_n = (c + 1) % 2
                    for hp_ in range(2):
                        dve(lambda e, hp_=hp_, c=c: e.scalar_tensor_tensor(out=Cn[:, hp_, :], in0=Cn[:, hp_, :], scalar=dec2[:, c, hp_:hp_ + 1],
                                                                         in1=PS[5][:, hp_ * 65:(hp_ + 1) * 65], op0=ALU.mult, op1=ALU.add),
                            [R_Cn, R_ex, RPS[5]], [R_Cn])
                    dve(lambda e, cb_n=cb_n: e.tensor_copy(out=Cnb2[cb_n], in_=Cn), [R_Cn], [R_Cnb2[cb_n]])
                    if seq == 0:
                        next(mgen, None)
                    if debug == "ml7":
                        continue
                    for h in range(4):
                        act(lambda e, h=h, c=c: e.activation(out=tmpA[0:L, h, :], in_=PS[PB_[h % 2]][0:L, h * 65:(h + 1) * 65], func=AF.Identity, scale=acol[0:L, c, h:h + 1]),
                            [RPS[PB_[h % 2]], R_ex], [R_tmp])
                    dve(lambda e, c=c: e.tensor_tensor(out=resm[0:L], in0=PS[3][0:L, 0:260].rearrange("p (h d) -> p h d", d=65),
                                                       in1=eM[0:L, c, :].unsqueeze(2).to_broadcast([L, 4, 65]), op=ALU.mult), [RPS[3], R_ex], [R_res])
                    dve(lambda e: e.tensor_tensor(out=resm[0:L], in0=resm[0:L], in1=tmpA[0:L], op=ALU.add), [R_res, R_tmp], [R_res])
                    act(lambda e: e.activation(out=den[0:L, :], in_=resm[0:L, :, 64], func=AF.Abs), [R_res], [R_res])
                    dve(lambda e, c=c: e.tensor_tensor(out=den[0:L, :], in0=den[0:L, :], in1=emt[0:L, c, :], op=ALU.max), [R_res, R_ex], [R_res])
                    dve(lambda e: e.reciprocal(out=den[0:L, :], in_=den[0:L, :]), [R_res], [R_res])
                    dve(lambda e: e.tensor_tensor(out=o4[0:L], in0=resm[0:L, :, 0:64], in1=den[0:L, :].unsqueeze(2).to_broadcast([L, 4, 64]), op=ALU.mult),
                        [R_res], [R_o4])
                    S.op("pool", lambda e, ib=ib: e.tensor_tensor(out=o4[0:L], in0=o4[0:L], in1=sig[ib][0:L, :].rearrange("p (h d) -> p h d", h=4), op=ALU.mult),
                         [R_o4, R_in[ib]], [R_o4])
                    S.op("pool", lambda e: e.tensor_tensor(out=osq[0:L], in0=o4[0:L], in1=o4[0:L], op=ALU.mult), [R_o4], [R_osq])
                    dve(lambda e: e.tensor_reduce(out=oms[0:L, :], in_=osq[0:L], axis=AX.X, op=ALU.add), [R_osq], [R_oms])
                    act(lambda e: e.activation(out=oms[0:L, :], in_=oms[0:L, :], func=AF.Ln, scale=1.0 / 64, bias=EPS), [R_oms], [R_oms])
                    act(lambda e: e.activation(out=oms[0:L, :], in_=oms[0:L, :], func=AF.Exp, scale=-0.5), [R_oms], [R_oms])
                    S.op("pool", lambda e: e.tensor_tensor(out=o4[0:L], in0=o4[0:L], in1=oms[0:L, :].unsqueeze(2).to_broadcast([L, 4, 64]), op=ALU.mult),
                         [R_o4, R_oms, R_osq], [R_o4])
                    ob = c % 2
                    S.op("pool", lambda e, ob=ob: e.tensor_tensor(out=o4b[ob][0:L, :].rearrange("p (h d) -> p h d", h=4), in0=o4[0:L],
                                                                 in1=gmh[0:L, :].unsqueeze(1).to_broadcast([L, 4, 64]), op=ALU.mult), [R_o4, R_g], [R_o4b[ob]])
                    S.dma("pool", MIX[t0:t0 + L, 512:768], o4b[ob][0:L, :], reads=[R_o4b[ob]], writes=[R_MIX], par=True)
                for h in range(4):
                    r0 = (h % 2) * 64
                    S.dma("pool", O["o_mc"][l, seq, h], Cn[r0:r0 + 64, h // 2, :], reads=[R_Cn])

            run_ml(0, 0, T, 128)
            for _ in mgen:
                pass
            for s_ in range(4):
                if debug == "mlp":
                    break
                run_ml(1 + s_, T + 32 * s_, 32, 32)
            S.barrier()
            A.pop()

        def phase_B(l, lam_init):
            A.push()
            emb = A.take([4, 5, 128], F32)
            R_emb = Res()
            for h_ in range(4):
                for t_, dst_ in [(0, 4), (1, 1), (2, 2)]:
                    S.dma("sp", emb[:, h_, dst_, :], I["bandT"][l, h_, t_], writes=[R_emb])
            act(lambda e: e.activation(out=emb[:, :, 1:3, :], in_=emb[:, :, 1:3, :], func=AF.Exp), [R_emb], [R_emb])
            act(lambda e: e.activation(out=emb[:, :, 4, :], in_=emb[:, :, 4, :], func=AF.Exp), [R_emb], [R_emb])
            for h_ in range(4):
                dve(lambda e, h_=h_: e.tensor_tensor(out=emb[:, h_, 0, :], in0=emb[:, h_, 4, :], in1=cmask_f, op=ALU.mult), [R_emb, R_const], [R_emb])
                dve(lambda e, h_=h_: e.tensor_tensor(out=emb[:, h_, 3, :], in0=emb[:, h_, 2, :], in1=m4_f, op=ALU.mult), [R_emb, R_const], [R_emb])
            lamt = A.take([4, 32], F32)
            lamp = A.take([2, 32], F32)
            lam2 = A.take([2], F32)
            nlam = A.take([1], F32)
            gsub = A.take([64], F32)
            R_lam = Res()
            S.dma("sp", lamt, I["lamb"][l], writes=[R_lam])
            S.dma("sp", gsub, I["gsub"][l], writes=[R_lam])
            dve(lambda e: e.tensor_tensor(out=lamp[:, 0, :], in0=lamt[:, 0, :], in1=lamt[:, 1, :], op=ALU.mult), [R_lam], [R_lam])
            dve(lambda e: e.tensor_tensor(out=lamp[:, 1, :], in0=lamt[:, 2, :], in1=lamt[:, 3, :], op=ALU.mult), [R_lam], [R_lam])
            dve(lambda e: e.tensor_reduce(out=lam2, in_=lamp, axis=AX.X, op=ALU.add), [R_lam], [R_lam])
            act(lambda e: e.activation(out=lam2, in_=lam2, func=AF.Exp), [R_lam], [R_lam])
            dve(lambda e: e.tensor_tensor(out=nlam, in0=lam2[:, 1:2], in1=lam2[:, 0:1], op=ALU.subtract), [R_lam], [R_lam])
            dve(lambda e: e.tensor_scalar(out=nlam, in0=nlam, scalar1=-lam_init, scalar2=None, op0=ALU.add), [R_lam], [R_lam])
            dve(lambda e: e.tensor_scalar(out=gsub, in0=gsub, scalar1=(1.0 - lam_init), scalar2=None, op0=ALU.mult), [R_lam], [R_lam])

            KT = A.take([2, T], BF16) if T >= P + 32 else A.take([2, P + 32], BF16)
            VST = A.take([max(NKB_P, NCB + 1), 256], BF16)
            VSTF = A.take([256], F32)
            VA = A.take([max(NKB_P, NCB + 1), 4, 65], BF16)
            QT = A.take([2, T], BF16)
            FB = A.take([max(NKB_P, 1), max(NKB_P, NCB + 1), 4], F32)
            lfc = A.take([max(NKB_P, NCB + 1), 4], F32)
            cum = A.take([max(NKB_P, NCB + 1), 4], F32)
            totb = A.take([4, max(NKB_P, NCB + 1)], F32)
            crefbc = A.take([max(NKB_P, NCB + 1), 4], F32)
            ktmA = A.take([max(NCB, 1), 256], F32)
            vstA = A.take([max(NCB, 1), 256], F32)
            R_ktmA = Res()
            R_vstA = Res()
            R_KT, R_VST, R_VSTF, R_VA, R_QT, R_FB, R_lfc, R_cum = [Res() for _ in range(8)]
            pt = [A.take([512], BF16) for _ in range(4)]
            R_pt = [Res() for _ in range(4)]
            pf = [A.take([128], F32) for _ in range(2)]
            R_pf = [Res() for _ in range(2)]
            og = [A.take([4, 256], BF16) for _ in range(2)]
            R_og = [Res(), Res()]
            rec = [A.take([4, 2], F32) for _ in range(2)]
            R_rec = [Res(), Res()]
            d0 = A.take([4, 64], F32)
            d1 = A.take([4, 64], F32)
            dsq = A.take([4, 64], F32)
            dms = A.take([4], F32)
            R_d = Res()
            dve(lambda e: e.memset(VA[:, :, :, 64:65], 1.0), [], [R_VA])
            cS = [0]
            cP = [0]
            cA = [0]
            cO = [0]
            ABANKS = [5, 6, 7, 1]
            TB = [0, 4]
            SBANKS = [2, 3, 7, 1]
            cR = [0]
            oT = [A.take([512], F32) for _ in range(2)]
            R_oT = [Res(), Res()]

            def run_seq(g, tok0, NQ, cache):
                sc = (1.0 / math.sqrt(32.0)) if g == 1 else 0.125
                nmap = 2 if g == 1 else 1
                if cache is None:
                    kbl = [(i * 128, 128) for i in range(NKB_P)]
                    nck = 0
                    qsubs = [(i * 128, 128) for i in range(NKB_P)]
                    qgroups = [list(range(4 * J, 4 * J + 4)) for J in range(NKB_P // 4)]
                else:
                    nck = (NLB if g == 2 else NCB)
                    kbl = [(i * 128, 128) for i in range(nck)] + [(nck * 128, 32)]
                    qsubs = [(0, 32)]
                    qgroups = [[0]]
                nkb = len(kbl)
                NK = kbl[-1][0] + kbl[-1][1]
                for hp in range(2):
                    S.dma("sp", QT[:, hp, 0:NQ], QS[g][hp * 128:(hp + 1) * 128, tok0:tok0 + NQ], writes=[R_QT], par=True)
                    S.dma("sp", KT[:, hp, nck * 128:nck * 128 + NQ], KS[g][hp * 128:(hp + 1) * 128, tok0:tok0 + NQ], writes=[R_KT], par=True)
                if cache is None:
                    S.dma("sp", VST[:, 0:nkb, :], VS[g][0:T, :].rearrange("(k p) c -> p k c", p=128), writes=[R_VST])
                    act(lambda e: e.activation(out=VA[:, 0:nkb, :, 0:64], in_=VST[:, 0:nkb, :].rearrange("p k (h d) -> p k h d", h=4),
                                               func=AF.Copy), [R_VST], [R_VA])
                else:
                    ck = [I["c_fk"], I["c_dk"], I["c_bk"]][g][l, cache]
                    cv = [I["c_fv"], I["c_dv"], I["c_bv"]][g][l, cache]
                    S.dma("sp", ktmA[:, 0:nck, :], ck.rearrange("(k p) c -> p k c", p=128), writes=[R_ktmA])
                    S.dma("sp", vstA[:, 0:nck, :], cv.rearrange("(k p) c -> p k c", p=128), writes=[R_vstA])
                    for kb in range(nck):
                        tb = kb % 2

                        def trk(e, tb=tb, kb=kb):
                            for hp in range(2):
                                ins = e.transpose(out=PS[tb][:, hp * 128:(hp + 1) * 128], in_=ktmA[:, kb, hp * 128:(hp + 1) * 128], identity=ident_f)
                            return ins
                        pe(trk, [R_ktmA, R_const], [RPS[tb]])
                        act(lambda e, kb=kb, tb=tb: e.activation(out=KT[:, :, kb * 128:(kb + 1) * 128],
                                                                 in_=PS[tb][:, 0:256].rearrange("p (a b) -> p a b", a=2), func=AF.Copy),
                            [RPS[tb]], [R_KT], par=True)
                    dve(lambda e: e.tensor_copy(out=VA[:, 0:nck, :, 0:64], in_=vstA[:, 0:nck, :].rearrange("p k (h d) -> p k h d", h=4)),
                        [R_vstA], [R_VA])
                    S.dma("sp", VST[0:32, 0, :], VS[g][tok0:tok0 + 32, :], writes=[R_VST])
                    act(lambda e: e.activation(out=VA[0:32, nck, :, 0:64], in_=VST[0:32, 0, :].rearrange("p (h d) -> p h d", h=4), func=AF.Copy),
                        [R_VST], [R_VA])
                if g == 0:
                    dve(lambda e: e.memset(lfc[:, 0:nkb, :], 0.0), [], [R_lfc])
                    if cache is None:
                        S.dma("sp", lfc[:, 0:nkb, :], LOGF[0:T, :].rearrange("(k p) h -> p k h", p=128), writes=[R_lfc])
                    else:
                        S.dma("sp", lfc[:, 0:nck, :], I["c_flf"][l, cache].rearrange("(k p) h -> p k h", p=128), writes=[R_lfc])
                        S.dma("sp", lfc[0:32, nck, :], LOGF[tok0:tok0 + 32, :], writes=[R_lfc])
                    n4 = nkb * 4
                    lf2 = lfc[:, 0:nkb, :].rearrange("p k h -> p (k h)")
                    pe(lambda e: e.matmul(out=PS[0][:, 0:n4], lhsT=ones_f, rhs=lf2, start=True, stop=True), [R_lfc, R_const], [RPS[0]])
                    for h_ in range(4):
                        dve(lambda e, h_=h_: e.tensor_tensor_scan(out=totb[:, h_, 0:nkb], data0=ones_f[:, 0:nkb],
                                                                  data1=PS[0][:, 0:n4].rearrange("p (k h) -> p h k", h=4)[:, h_, :],
                                                                  initial=0.0, op0=ALU.mult, op1=ALU.add), [RPS[0], R_const], [R_cum])
                    pe(lambda e: e.matmul(out=PS[1][:, 0:n4], lhsT=tri_f, rhs=lf2, start=True, stop=True), [R_lfc, R_const], [RPS[1]])
                    dve(lambda e: e.tensor_tensor(out=cum[:, 0:nkb, :], in0=PS[1][:, 0:n4].rearrange("p (k h) -> p k h", h=4),
                                                  in1=totb[:, :, 0:nkb].rearrange("p h k -> p k h"), op=ALU.add), [RPS[1], R_cum], [R_cum])
                    dve(lambda e: e.tensor_tensor(out=cum[:, 0:nkb, :], in0=cum[:, 0:nkb, :],
                                                  in1=PS[0][:, 0:n4].rearrange("p (k h) -> p k h", h=4), op=ALU.subtract), [RPS[0], R_cum], [R_cum])
                    cum2 = cum[:, 0:nkb, :].rearrange("p k h -> p (k h)")
                    pe(lambda e: e.matmul(out=PS[0][:, 0:n4], lhsT=sel0, rhs=cum2, start=True, stop=True), [R_cum, R_const], [RPS[0]])
                    dve(lambda e: e.tensor_copy(out=crefbc[:, 0:nkb, :], in_=PS[0][:, 0:n4].rearrange("p (k h) -> p k h", h=4)), [RPS[0]], [R_cum])
                    for qi_, (q0, nq) in enumerate(qsubs):
                        kq = (q0 // 128) if cache is None else nck
                        dve(lambda e, qi_=qi_, kq=kq: e.tensor_tensor(out=FB[:, qi_, 0:nkb, :],
                                                                      in0=crefbc[:, kq:kq + 1, :].to_broadcast([128, nkb, 4]),
                                                                      in1=cum[:, 0:nkb, :], op=ALU.subtract), [R_cum], [R_FB])

                def mode(kb, qs):
                    k0, nk = kbl[kb]
                    if cache is None:
                        off = qs - kb
                        if g == 0:
                            if off < 0:
                                return None
                            return ((lambda h: FB[0:nk, qs, kb, h:h + 1]), ((lambda h: tri_f) if off == 0 else None))
                        if g == 1:
                            if off < 0:
                                return None
                            if off == 0:
                                return (None, (lambda h: emd[:, h, :]))
                            return ((lambda h: alcol[0:nk, h, off:off + 1]), None)
                        if off < 0 or off > 4:
                            return None
                        ti = [0, 1, 2, 2, 3][off]
                        return (None, (lambda h: emb[:, h, ti, :]))
                    else:
                        new = (kb == nck)
                        if g == 0:
                            return ((lambda h: FB[0:nk, 0, kb, h:h + 1]), ((lambda h: tri_f[0:32, 0:32]) if new else None))
                        if g == 1:
                            if new:
                                return (None, (lambda h: emd[0:32, h, 0:32]))
                            off = nck - kb
                            return ((lambda h: alcol[0:nk, h, off:off + 1]), None)
                        if new:
                            return (None, (lambda h: emb[0:32, h, 4, 0:32]))
                        off = nck - kb
                        ti = 1 if off == 1 else 2
                        return (None, (lambda h: emb[:, h, ti, 0:32]))

                for qg in qgroups:
                    ob = cO[0] % 2
                    cO[0] += 1
                    nsub = len(qg)
                    nq = qsubs[qg[0]][1]
                    if g == 1:
                        lane_groups = [[(h_, 0), (h_, 1)] for h_ in range(4)]
                    else:
                        lane_groups = [[(0, 0), (1, 0)], [(2, 0), (3, 0)]]
                    for lanes in lane_groups:
                        lacc = [5, 6]
                        first = [True, True]
                        kneed = [kb for kb in range(nkb) if any(mode(kb, qs) is not None for qs in qg)]
                        units = []
                        for kb in kneed:
                            k0, nk = kbl[kb]
                            subs = [si for si, qs in enumerate(qg) if mode(kb, qs) is not None]
                            slo, shi = subs[0], subs[-1] + 1
                            qa = qsubs[qg[slo]][0]
                            qb_ = qsubs[qg[shi - 1]][0] + nq
                            for li, (h_, m_) in enumerate(lanes):
                                units.append(dict(kb=kb, k0=k0, nk=nk, slo=slo, shi=shi, qa=qa, qb_=qb_, m=m_, h=h_, li=li))

                        def emit_qk(u):
                            m = u["m"]
                            h = u["h"]
                            hp = h // 2
                            if g == 1:
                                r0 = (h % 2) * 64 + m * 32
                                r1 = r0 + 32
                            else:
                                r0 = (h % 2) * 64
                                r1 = r0 + 64
                            sb = SBANKS[cS[0] % 4]
                            cS[0] += 1
                            u["sb"] = sb
                            kw = {"tile_position": (r0, 0)} if r0 == 96 else {}
                            k0, nk, qa, qb_ = u["k0"], u["nk"], u["qa"], u["qb_"]
                            pe(lambda e, sb=sb, r0=r0, r1=r1, hp=hp, k0=k0, nk=nk, qa=qa, qb_=qb_, kw=kw: e.matmul(
                                out=PS[sb][0:nk, 0:qb_ - qa], lhsT=KT[r0:r1, hp, k0:k0 + nk], rhs=QT[r0:r1, hp, qa:qb_],
                                start=True, stop=True, **kw), [R_KT, R_QT], [RPS[sb]])

                        def emit_rest(u):
                            kb, k0, nk, slo, shi, m, sb = u["kb"], u["k0"], u["nk"], u["slo"], u["shi"], u["m"], u["sb"]
                            h = u["h"]
                            li = u["li"]
                            wide = (cache is None) and (g == 0 or g == 1)
                            if wide:
                                pi = cP[0] % 4
                                cP[0] += 1
                                width = (shi - slo) * nq
                                if g == 1 and h == 0:
                                    for half in range(2):
                                        lo_s = max(slo, 2 * half)
                                        hi_s = min(shi, 2 * half + 2)
                                        if hi_s <= lo_s:
                                            continue
                                        oo = (qg[0] + 2 * half - kb) + 3
                                        bcol = alw[0:nk, h, oo:oo + 1]
                                        c_lo = (lo_s - slo) * nq
                                        c_hi = (hi_s - slo) * nq
                                        act(lambda e, sb=sb, nk=nk, c_lo=c_lo, c_hi=c_hi, pi=pi, bcol=bcol: e.activation(
                                            out=pt[pi][0:nk, c_lo:c_hi], in_=PS[sb][0:nk, c_lo:c_hi], func=AF.Exp, scale=sc, bias=bcol),
                                            [RPS[sb], R_const], [R_pt[pi]], par=True)
                                else:
                                    if g == 0:
                                        bcol = FB[0:nk, qg[0], kb, h:h + 1]
                                        rdw = [RPS[sb], R_FB]
                                    else:
                                        bcol = alw[0:nk, h, (qg[0] - kb) + 3:(qg[0] - kb) + 4]
                                        rdw = [RPS[sb], R_const]
                                    act(lambda e, sb=sb, nk=nk, width=width, pi=pi, bcol=bcol: e.activation(
                                        out=pt[pi][0:nk, 0:width], in_=PS[sb][0:nk, 0:width], func=AF.Exp, scale=sc, bias=bcol),
                                        rdw, [R_pt[pi]])
                                if kb >= qg[0]:
                                    corr = tri_f if g == 0 else cdiff[:, h, :]
                                    dve(lambda e, pi=pi, nq=nq, corr=corr: e.tensor_tensor(
                                        out=pt[pi][:, 0:nq], in0=pt[pi][:, 0:nq], in1=corr, op=ALU.mult), [R_pt[pi], R_const], [R_pt[pi]])
                                ab = lacc[li]
                                st_ = first[li]
                                first[li] = False
                                pe(lambda e, ab=ab, slo=slo, nq=nq, nk=nk, pi=pi, kb=kb, h=h, st_=st_, width=width: e.matmul(
                                    out=PS[ab][0:65, slo * nq:slo * nq + width], lhsT=VA[0:nk, kb, h, :], rhs=pt[pi][0:nk, 0:width],
                                    start=st_, stop=(kb == kneed[-1])), [R_pt[pi], R_VA], [RPS[ab]])
                                return
                            for si in range(slo, shi):
                                qs = qg[si]
                                md = mode(kb, qs)
                                if md is None:
                                    continue
                                bfn, efn = md
                                c0 = (si - slo) * nq
                                pi = cP[0] % 4
                                cP[0] += 1
                                bias_kw = {"bias": bfn(h)} if bfn is not None else {}
                                rd = [RPS[sb]] + ([R_FB] if (bfn is not None and g == 0) else []) + [R_const]
                                if efn is None:
                                    act(lambda e, sb=sb, nk=nk, c0=c0, nq=nq, pi=pi, bias_kw=bias_kw: e.activation(
                                        out=pt[pi][0:nk, 0:nq], in_=PS[sb][0:nk, c0:c0 + nq], func=AF.Exp, scale=sc, **bias_kw),
                                        rd, [R_pt[pi]])
                                else:
                                    fi = pi % 2
                                    act(lambda e, sb=sb, nk=nk, c0=c0, nq=nq, fi=fi, bias_kw=bias_kw: e.activation(
                                        out=pf[fi][0:nk, 0:nq], in_=PS[sb][0:nk, c0:c0 + nq], func=AF.Exp, scale=sc, **bias_kw),
                                        rd, [R_pf[fi]])
                                    em = efn(h)
                                    dve(lambda e, nk=nk, nq=nq, fi=fi, pi=pi, em=em: e.tensor_tensor(
                                        out=pt[pi][0:nk, 0:nq], in0=pf[fi][0:nk, 0:nq], in1=em[0:nk, 0:nq] if nk < 128 else em, op=ALU.mult),
                                        [R_pf[fi], R_const, R_emb], [R_pt[pi]])
                                ab = lacc[li]
                                st_ = first[li]
                                first[li] = False
                                pe(lambda e, ab=ab, si=si, nq=nq, nk=nk, pi=pi, kb=kb, h=h, st_=st_: e.matmul(
                                    out=PS[ab][0:nq, si * 65:(si + 1) * 65], lhsT=pt[pi][0:nk, 0:nq], rhs=VA[0:nk, kb, h, :],
                                    start=st_, stop=True, skip_group_check=True), [R_pt[pi], R_VA], [RPS[ab]])

                        nl = len(lanes)
                        for j in range(0, len(units), nl):
                            if j == 0:
                                for u in units[0:nl]:
                                    emit_qk(u)
                            for u in units[j + nl:j + 2 * nl]:
                                emit_qk(u)
                            for u in units[j:j + nl]:
                                emit_rest(u)
                        if g == 1:
                            fin_list = [(lanes[0][0], [lacc[0], lacc[1]], [0, 1])]
                        else:
                            fin_list = [(lanes[li_][0], [lacc[li_]], [li_]) for li_ in range(len(lanes))]
                        for (h, accs, tix) in fin_list:
                            cR[0] += 1
                            rb = cR[0] % 2
                            wide_h = (cache is None) and (g == 0 or g == 1)
                            if wide_h:
                                for m in range(len(accs)):
                                    ti = tix[m]
                                    act(lambda e, ti=ti, ab=accs[m]: e.activation(out=oT[ti][0:65, :], in_=PS[ab][0:65, 0:512], func=AF.Copy), [RPS[accs[m]]], [R_oT[ti]])

                                    def trO(e, ti=ti):
                                        for si in range(4):
                                            ins = e.transpose(out=PS[TB[ti]][:, si * 65:(si + 1) * 65], in_=oT[ti][0:65, si * 128:(si + 1) * 128], identity=ident_f[0:65, 0:65])
                                        return ins
                                    pe(trO, [R_oT[ti], R_const], [RPS[TB[ti]]])
                                accs = [TB[tix[m]] for m in range(len(accs))]
                            a0 = PS[accs[0]][0:nq, 0:nsub * 65].rearrange("p (s d) -> p s d", d=65)
                            dve(lambda e, rb=rb, a0=a0, nq=nq, nsub=nsub: e.reciprocal(out=rec[rb][0:nq, 0:nsub, 0:1], in_=a0[:, :, 64:65]),
                                [RPS[accs[0]]], [R_rec[rb]])
                            if g != 1:
                                dve(lambda e, rb=rb, a0=a0, nq=nq, nsub=nsub, ob=ob, h=h: e.tensor_tensor(
                                    out=og[ob][0:nq, 0:nsub, h * 64:(h + 1) * 64], in0=a0[:, :, 0:64],
                                    in1=rec[rb][0:nq, 0:nsub, 0:1].to_broadcast([nq, nsub, 64]), op=ALU.mult),
                                    [RPS[accs[0]], R_rec[rb]], [R_og[ob]])
                            else:
                                a1 = PS[accs[1]][0:nq, 0:nsub * 65].rearrange("p (s d) -> p s d", d=65)
                                dve(lambda e, rb=rb, a1=a1, nq=nq, nsub=nsub: e.reciprocal(out=rec[rb][0:nq, 0:nsub, 1:2], in_=a1[:, :, 64:65]),
                                    [RPS[accs[1]]], [R_rec[rb]])
                                dve(lambda e, rb=rb, nq=nq, nsub=nsub: e.tensor_scalar(out=rec[rb][0:nq, 0:nsub, 1:2], in0=rec[rb][0:nq, 0:nsub, 1:2],
                                                                                     scalar1=nlam[0:nq, 0:1], scalar2=None, op0=ALU.mult),
                                    [R_rec[rb], R_lam], [R_rec[rb]])
                                dve(lambda e, rb=rb, a0=a0, nq=nq, nsub=nsub: e.tensor_tensor(
                                    out=d0[0:nq, 0:nsub, :], in0=a0[:, :, 0:64], in1=rec[rb][0:nq, 0:nsub, 0:1].to_broadcast([nq, nsub, 64]), op=ALU.mult),
                                    [RPS[accs[0]], R_rec[rb]], [R_d])
                                dve(lambda e, rb=rb, a1=a1, nq=nq, nsub=nsub: e.tensor_tensor(
                                    out=d1[0:nq, 0:nsub, :], in0=a1[:, :, 0:64], in1=rec[rb][0:nq, 0:nsub, 1:2].to_broadcast([nq, nsub, 64]), op=ALU.mult),
                                    [RPS[accs[1]], R_rec[rb], R_d], [R_d])
                                dve(lambda e, nq=nq, nsub=nsub: e.tensor_tensor(out=d0[0:nq, 0:nsub, :], in0=d0[0:nq, 0:nsub, :], in1=d1[0:nq, 0:nsub, :], op=ALU.add),
                                    [R_d], [R_d])
                                dve(lambda e, nq=nq, nsub=nsub: e.tensor_tensor(out=dsq[0:nq, 0:nsub, :], in0=d0[0:nq, 0:nsub, :], in1=d0[0:nq, 0:nsub, :], op=ALU.mult),
                                    [R_d], [R_d])
                                dve(lambda e, nq=nq, nsub=nsub: e.tensor_reduce(out=dms[0:nq, 0:nsub], in_=dsq[0:nq, 0:nsub, :], axis=AX.X, op=ALU.add), [R_d], [R_d])
                                act(lambda e, nq=nq, nsub=nsub: e.activation(out=dms[0:nq, 0:nsub], in_=dms[0:nq, 0:nsub], func=AF.Ln, scale=1.0 / 64, bias=EPS), [R_d], [R_d])
                                act(lambda e, nq=nq, nsub=nsub: e.activation(out=dms[0:nq, 0:nsub], in_=dms[0:nq, 0:nsub], func=AF.Exp, scale=-0.5), [R_d], [R_d])
                                dve(lambda e, nq=nq, nsub=nsub: e.tensor_tensor(out=d0[0:nq, 0:nsub, :], in0=d0[0:nq, 0:nsub, :],
                                                                              in1=dms[0:nq, 0:nsub].unsqueeze(2).to_broadcast([nq, nsub, 64]), op=ALU.mult), [R_d], [R_d])
                                dve(lambda e, nq=nq, nsub=nsub, ob=ob, h=h: e.tensor_tensor(
                                    out=og[ob][0:nq, 0:nsub, h * 64:(h + 1) * 64], in0=d0[0:nq, 0:nsub, :],
                                    in1=gsub[0:nq, :].unsqueeze(1).to_broadcast([nq, nsub, 64]), op=ALU.mult), [R_d, R_lam], [R_og[ob]])
                    qa = tok0 + qsubs[qg[0]][0]
                    mc0 = [0, 256, 768][g]
                    if cache is None:
                        S.dma("pool", MIX[qa:qa + nsub * 128, mc0:mc0 + 256].rearrange("(s p) c -> p s c", p=128), og[ob][:, 0:nsub, :],
                              reads=[R_og[ob]], writes=[R_MIX], par=True)
                    else:
                        S.dma("pool", MIX[qa:qa + 32, mc0:mc0 + 256], og[ob][0:32, 0, :], reads=[R_og[ob]], writes=[R_MIX], par=True)

            for g in range(3):
                run_seq(g, 0, T, None)
                for s_ in range(4):
                    run_seq(g, T + 32 * s_, 32, s_)
            S.barrier()
            A.pop()
            phase_ML(l)

        for l in range(DEPTH):
            lam_init = 0.8 - 0.6 * math.exp(-0.3 * l)
            if l == 0:
                A.push()
                for _ in gen_M(0, modT, A1, A2, R_mod):
                    pass
                S.barrier()
                A.pop()
            else:
                dve(lambda e: e.tensor_copy(out=modT, in_=modTn), [R_modn], [R_mod])
                dve(lambda e: e.tensor_copy(out=A1, in_=A1n), [R_modn], [R_mod])
                dve(lambda e: e.tensor_copy(out=A2, in_=A2n), [R_modn], [R_mod])
                S.barrier()

            A.push()
            wfm = A.take([8, 2048], BF16)
            wg = A.take([8, 8], BF16)
            wtm = A.take([8, NTM], BF16)
            R_w = Res("w_in")
            w_in_l = I["w_in"][l].rearrange("(c p) n -> p c n", p=128)
            for j in range(16):
                S.dma("pool", wfm[:, :, j * 128:(j + 1) * 128], w_in_l[:, :, FM_COLS[j]:FM_COLS[j] + 128], writes=[R_w])
            S.dma("pool", wg[:, :, 0:4], w_in_l[:, :, C_MI:C_MI + 4], writes=[R_w])
            S.dma("pool", wg[:, :, 4:8], w_in_l[:, :, C_MF:C_MF + 4], writes=[R_w])
            o = 0
            for (c0, n) in TM_SRC:
                S.dma("pool", wtm[:, :, o:o + n], w_in_l[:, :, c0:c0 + n], writes=[R_w])
                o += n
            binT = A.take([16], F32)
            bgate = A.take([2], F32)
            nbgate = A.take([2], F32)
            btm = A.take([NTM], F32)
            gqk = A.take([12], F32)
            convw = A.take([4, 4], F32)
            convb = A.take([4], F32)
            R_p = Res("params")
            S.dma("sp", binT, I["binT"][l], writes=[R_p])
            S.dma("sp", bgate[0:4, :], I["bgate"][l], writes=[R_p])
            S.dma("sp", btm, I["btm"][l], writes=[R_p])
            S.dma("sp", gqk, I["gqk"][l], writes=[R_p])
            S.dma("sp", convw, I["convw"][l], writes=[R_p])
            S.dma("sp", convb, I["convb"][l], writes=[R_p])
            dve(lambda e: e.tensor_scalar(out=nbgate[0:4, :], in0=bgate[0:4, :], scalar1=-1.0, scalar2=None, op0=ALU.mult),
                [R_p], [R_p])
            xT = [A.take([8, 512], F32) for _ in range(2)]
            R_xT = [Res(), Res()]
            xtm = [A.take([1024], F32) for _ in range(2)]
            R_xtm = [Res(), Res()]
            sqb = A.take([8, 512], BF16)
            R_sq = Res()
            rstd = A.take([512], F32)
            R_rstd = Res()
            t1 = A.take([8, 512], F32)
            R_t1 = Res()
            hT2 = [A.take([8, 512], BF16) for _ in range(2)]
            R_hT2 = [Res(), Res()]
            U = [A.take([4, 35], F32) for _ in range(4)]
            UP = [A.take([515], F32) for _ in range(4)]
            R_U = [Res() for _ in range(4)]
            u32 = [A.take([512], F32) for _ in range(2)]
            R_u32 = [Res(), Res()]
            sqk = [A.take([512], BF16) for _ in range(2)]
            R_sqk = [Res(), Res()]
            rr = [A.take([512], F32) for _ in range(2)]
            R_rr = [Res(), Res()]
            kn32 = [A.take([512], F32) for _ in range(2)]
            R_kn32 = [Res(), Res()]
            knb = [A.take([512], BF16) for _ in range(2)]
            R_knb = [Res(), Res()]
            acc = [A.take([512], F32) for _ in range(2)]
            R_acc = [Res(), Res()]
            v32 = [A.take([512], F32) for _ in range(2)]
            R_v32 = [Res(), Res()]
            vb = [A.take([512], BF16) for _ in range(2)]
            R_vb = [Res(), Res()]
            grow = A.take([2, 512], F32)
            R_grow = Res()
            t260 = A.take([260], F32)
            R_t260 = Res()
            sg = A.take([256], F32)
            R_sg = Res()
            lf4 = A.take([4], F32)
            R_lf4 = Res()
            for u_ in UP:
                dve(lambda e, u_=u_: e.memset(u_[:, 0:3], 0.0), [], [R_U[0], R_U[1], R_U[2], R_U[3]])
            cnt2 = [0]

            def norm_part(bi, tok0, ntok, segs):
                is_s = (bi == len(blocks) - 1)
                xb = bi % 2
                xTb = xT[xb]
                hTc = hT2[bi % 2]
                R_hTc = R_hT2[bi % 2]
                if l == 0:
                    src = I["xs"] if is_s else I["xp"][tok0:tok0 + ntok, :]
                    for i in range(ntok // 128):
                        tb = i % 2
                        S.dma("sp", xtm[tb], src[i * 128:(i + 1) * 128, :], writes=[R_xtm[tb]])
                        for half in range(2):
                            pb = half

                            def tr(e, tb=tb, half=half, pb=pb):
                                for c4 in range(4):
                                    c = half * 4 + c4
                                    ins = e.transpose(out=PS[pb][:, c4 * 128:(c4 + 1) * 128], in_=xtm[tb][:, c * 128:(c + 1) * 128],
                                                      identity=ident_f)
                                return ins
                            pe(tr, [R_xtm[tb], R_const], [RPS[pb]])
                            eng = act if half == 0 else dve
                            if half == 0:
                                act(lambda e, i=i, pb=pb, half=half, xTb=xTb: e.activation(
                                    out=xTb[:, half * 4:half * 4 + 4, i * 128:(i + 1) * 128],
                                    in_=PS[pb][:, :].rearrange("p (a b) -> p a b", a=4), func=AF.Copy), [RPS[pb]], [R_xT[xb]])
                            else:
                                dve(lambda e, i=i, pb=pb, half=half, xTb=xTb: e.tensor_copy(
                                    out=xTb[:, half * 4:half * 4 + 4, i * 128:(i + 1) * 128],
                                    in_=PS[pb][:, :].rearrange("p (a b) -> p a b", a=4)), [RPS[pb]], [R_xT[xb]])
                    S.dma("pool", XT[0][:, tok0:tok0 + ntok].rearrange("(c p) t -> p c t", p=128), xTb[:, :, 0:ntok],
                          reads=[R_xT[xb]], writes=[R_XT])
                else:
                    S.dma("sp", xTb[:, :, 0:ntok], XT[1][:, tok0:tok0 + ntok].rearrange("(c p) t -> p c t", p=128),
                          writes=[R_xT[xb]])
                act(lambda e, xTb=xTb, ntok=ntok: e.activation(out=sqb[:, :, 0:ntok], in_=xTb[:, :, 0:ntok], func=AF.Square),
                    [R_xT[xb]], [R_sq])

                def ssmm(e, ntok=ntok):
                    for c in range(8):
                        ins = e.matmul(out=PS[2][:, 0:ntok], lhsT=ones_b, rhs=sqb[:, c, 0:ntok], start=(c == 0), stop=(c == 7))
                    return ins
                pe(ssmm, [R_sq, R_const], [RPS[2]])
                act(lambda e, ntok=ntok: e.activation(out=rstd[:, 0:ntok], in_=PS[2][:, 0:ntok], func=AF.Ln, scale=1.0 / D, bias=EPS),
                    [RPS[2]], [R_rstd])
                act(lambda e, ntok=ntok: e.activation(out=rstd[:, 0:ntok], in_=rstd[:, 0:ntok], func=AF.Exp, scale=-0.5),
                    [R_rstd], [R_rstd])
                for c in range(8):
                    for (c0, ncol, sq_) in segs:
                        dve(lambda e, c=c, c0=c0, ncol=ncol, sq_=sq_, xTb=xTb: e.scalar_tensor_tensor(
                            out=t1[:, c, c0:c0 + ncol], in0=xTb[:, c, c0:c0 + ncol], scalar=A1[:, c, sq_:sq_ + 1],
                            in1=rstd[:, c0:c0 + ncol], op0=ALU.mult, op1=ALU.mult), [R_xT[xb], R_rstd, R_mod], [R_t1])
                        act(lambda e, c=c, c0=c0, ncol=ncol, sq_=sq_: e.activation(
                            out=hTc[:, c, c0:c0 + ncol], in_=t1[:, c, c0:c0 + ncol], func=AF.Identity,
                            bias=modT[:, c, sq_:sq_ + 1], scale=1.0), [R_t1, R_mod], [R_hTc])

            def proj_part(bi, tok0, ntok, segs):
                is_s = (bi == len(blocks) - 1)
                xb = bi % 2
                xTb = xT[xb]
                hTc = hT2[bi % 2]
                R_hTc = R_hT2[bi % 2]
                pend = []
                for j in range(16):
                    pb = 3 + (j % 2)

                    def fm(e, j=j, pb=pb, ntok=ntok):
                        for c in range(8):
                            ins = e.matmul(out=PS[pb][:, 0:ntok], lhsT=wfm[:, c, j * 128:(j + 1) * 128], rhs=hTc[:, c, 0:ntok],
                                           start=(c == 0), stop=(c == 7))
                        return ins
                    pe(fm, [R_w, R_hTc], [RPS[pb]])
                    if j < 12:
                        k2 = cnt2[0] % 2
                        cnt2[0] += 1
                        g = j // 4
                        isk = (j % 4) >= 2
                        hd = 32 if g == 1 else 64
                        bm = blk32 if g == 1 else blk64
                        act(lambda e, j=j, pb=pb, k2=k2, ntok=ntok: e.activation(out=u32[k2][:, 0:ntok], in_=PS[pb][:, 0:ntok],
                                                                                 func=AF.Identity, bias=binT[:, j:j + 1], scale=1.0),
                            [RPS[pb], R_p], [R_u32[k2]])
                        act(lambda e, k2=k2, ntok=ntok: e.activation(out=sqk[k2][:, 0:ntok], in_=u32[k2][:, 0:ntok], func=AF.Square),
                            [R_u32[k2]], [R_sqk[k2]])
                        def tail(j=j, k2=k2, g=g, isk=isk, hd=hd, bm=bm, ntok=ntok):
                            pe(lambda e, k2=k2, bm=bm, ntok=ntok: e.matmul(out=PS[5][:, 0:ntok], lhsT=bm, rhs=sqk[k2][:, 0:ntok],
                                                                           start=True, stop=True), [R_sqk[k2], R_const], [RPS[5]])
                            act(lambda e, k2=k2, hd=hd, ntok=ntok: e.activation(out=rr[k2][:, 0:ntok], in_=PS[5][:, 0:ntok], func=AF.Ln,
                                                                                scale=1.0 / hd, bias=EPS), [RPS[5]], [R_rr[k2]])
                            act(lambda e, k2=k2, ntok=ntok: e.activation(out=rr[k2][:, 0:ntok], in_=rr[k2][:, 0:ntok], func=AF.Exp, scale=-0.5),
                                [R_rr[k2]], [R_rr[k2]])
                            if isk:
                                dve(lambda e, j=j, k2=k2, ntok=ntok: e.scalar_tensor_tensor(
                                    out=kn32[k2][:, 0:ntok], in0=u32[k2][:, 0:ntok], scalar=gqk[:, j:j + 1], in1=rr[k2][:, 0:ntok],
                                    op0=ALU.mult, op1=ALU.mult), [R_u32[k2], R_rr[k2], R_p], [R_kn32[k2]])
                                okt = [O["o_fk"], O["o_dk"], O["o_bk"]][g]
                                ch = j % 2
                                S.dma("pool", okt[l][ch * 128:(ch + 1) * 128, tok0:tok0 + ntok], kn32[k2][:, 0:ntok], reads=[R_kn32[k2]])
                                dve(lambda e, k2=k2, ntok=ntok: e.tensor_copy(out=knb[k2][:, 0:ntok], in_=kn32[k2][:, 0:ntok]),
                                    [R_kn32[k2]], [R_knb[k2]])
                                S.dma("pool", KS[g][ch * 128:(ch + 1) * 128, tok0:tok0 + ntok], knb[k2][:, 0:ntok], reads=[R_knb[k2]])
                            else:
                                dve(lambda e, j=j, k2=k2, ntok=ntok: e.scalar_tensor_tensor(
                                    out=knb[k2][:, 0:ntok], in0=u32[k2][:, 0:ntok], scalar=gqk[:, j:j + 1], in1=rr[k2][:, 0:ntok],
                                    op0=ALU.mult, op1=ALU.mult), [R_u32[k2], R_rr[k2], R_p], [R_knb[k2]])
                                ch = j % 2
                                S.dma("pool", QS[g][ch * 128:(ch + 1) * 128, tok0:tok0 + ntok], knb[k2][:, 0:ntok], reads=[R_knb[k2]])
                        for t_ in pend:
                            t_()
                        pend = [tail]
                    else:
                        for t_ in pend:
                            t_()
                        pend = []
                        jj = j - 12
                        k2 = cnt2[0] % 2
                        cnt2[0] += 1
                        if not is_s:
                            Uj = UP[jj]
                            act(lambda e, j=j, pb=pb, Uj=Uj: e.activation(out=Uj[:, 3:515], in_=PS[pb][:, 0:512], func=AF.Identity,
                                                                          bias=binT[:, j:j + 1], scale=1.0), [RPS[pb], R_p], [R_U[jj]])
                            dve(lambda e, jj=jj, k2=k2, Uj=Uj: e.tensor_scalar(out=acc[k2][:, 0:512], in0=Uj[:, 0:512],
                                                                             scalar1=convw[:, jj, 0:1], scalar2=convb[:, jj:jj + 1],
                                                                             op0=ALU.mult, op1=ALU.add), [R_U[jj], R_p], [R_acc[k2]])
                            for tp in range(1, 4):
                                dve(lambda e, jj=jj, k2=k2, Uj=Uj, tp=tp: e.scalar_tensor_tensor(
                                    out=acc[k2][:, 0:512], in0=Uj[:, tp:tp + 512], scalar=convw[:, jj, tp:tp + 1], in1=acc[k2][:, 0:512],
                                    op0=ALU.mult, op1=ALU.add), [R_U[jj], R_p, R_acc[k2]], [R_acc[k2]])
                            if bi == NPB - 1:
                                S.dma("pool", O["o_conv"][l, jj, :, 0, :], Uj[:, 512:515], reads=[R_U[jj]])
                            else:
                                dve(lambda e, Uj=Uj: e.tensor_copy(out=Uj[:, 0:3], in_=Uj[:, 512:515]), [R_U[jj], R_acc[k2]], [R_U[jj]])
                        else:
                            Uj = U[jj]
                            S.dma("sp", Uj[:, :, 0:3], I["convst"][l, :, jj, :, :], writes=[R_U[jj]])
                            act(lambda e, j=j, pb=pb, Uj=Uj: e.activation(out=Uj[:, :, 3:35], in_=PS[pb][:, 0:128].rearrange("p (a b) -> p a b", a=4),
                                                                          func=AF.Identity, bias=binT[:, j:j + 1], scale=1.0),
                                [RPS[pb], R_p], [R_U[jj]])
                            a3 = acc[k2][:, 0:128].rearrange("p (a b) -> p a b", a=4)
                            dve(lambda e, jj=jj, Uj=Uj, a3=a3: e.tensor_scalar(out=a3, in0=Uj[:, :, 0:32], scalar1=convw[:, jj, 0:1],
                                                                             scalar2=convb[:, jj:jj + 1], op0=ALU.mult, op1=ALU.add),
                                [R_U[jj], R_p], [R_acc[k2]])
                            for tp in range(1, 4):
                                dve(lambda e, jj=jj, Uj=Uj, tp=tp, a3=a3: e.scalar_tensor_tensor(
                                    out=a3, in0=Uj[:, :, tp:tp + 32], scalar=convw[:, jj, tp:tp + 1], in1=a3, op0=ALU.mult, op1=ALU.add),
                                    [R_U[jj], R_p, R_acc[k2]], [R_acc[k2]])
                            S.dma("pool", O["o_conv"][l, jj, :, 1:5, :], Uj[:, :, 32:35], reads=[R_U[jj]])
                        act(lambda e, k2=k2, ntok=ntok: e.activation(out=knb[k2][:, 0:ntok], in_=acc[k2][:, 0:ntok], func=AF.Silu),
                            [R_acc[k2]], [R_knb[k2]])
                        dst = QS[3] if jj < 2 else KS[3]
                        ch = jj % 2
                        S.dma("pool", dst[ch * 128:(ch + 1) * 128, tok0:tok0 + ntok], knb[k2][:, 0:ntok], reads=[R_knb[k2]])
                for gi_ in range(2):
                    pb = 3 + gi_

                    def gm(e, gi_=gi_, pb=pb, ntok=ntok):
                        for c in range(8):
                            ins = e.matmul(out=PS[pb][0:4, 0:ntok], lhsT=wg[:, c, gi_ * 4:gi_ * 4 + 4], rhs=hTc[:, c, 0:ntok],
                                           start=(c == 0), stop=(c == 7))
                        return ins
                    pe(gm, [R_w, R_hTc], [RPS[pb]])
                act(lambda e, ntok=ntok: e.activation(out=grow[0:4, 0, 0:ntok], in_=PS[3][0:4, 0:ntok], func=AF.Identity,
                                                      bias=bgate[0:4, 0:1], scale=1.0), [RPS[3], R_p], [R_grow])
                act(lambda e, ntok=ntok: e.activation(out=grow[0:4, 1, 0:ntok], in_=PS[4][0:4, 0:ntok], func=AF.Exp,
                                                      bias=nbgate[0:4, 1:2], scale=-1.0), [RPS[4], R_p], [R_grow])
                act(lambda e, ntok=ntok: e.activation(out=grow[0:4, 1, 0:ntok], in_=grow[0:4, 1, 0:ntok], func=AF.Ln, bias=1.0, scale=1.0),
                    [R_grow], [R_grow])
                dve(lambda e, ntok=ntok: e.tensor_scalar(out=grow[0:4, 1, 0:ntok], in0=grow[0:4, 1, 0:ntok], scalar1=-1.0, scalar2=None,
                                                         op0=ALU.mult), [R_grow], [R_grow])
                S.dma("pool", GI[:, tok0:tok0 + ntok], grow[0:4, 0, 0:ntok], reads=[R_grow])
                S.dma("pool", GF[:, tok0:tok0 + ntok], grow[0:4, 1, 0:ntok], reads=[R_grow])
                for i in range(ntok // 128):
                    ts_ = tok0 + i * 128
                    for gr in range(2):
                        pb = 6 + gr
                        k2 = gr

                        def tm(e, i=i, gr=gr, pb=pb):
                            for c in range(8):
                                ins = e.matmul(out=PS[pb][:, :], lhsT=hTc[:, c, i * 128:(i + 1) * 128], rhs=wtm[:, c, gr * 512:(gr + 1) * 512],
                                               start=(c == 0), stop=(c == 7))
                            return ins
                        pe(tm, [R_w, R_hTc], [RPS[pb]])
                        dve(lambda e, gr=gr, pb=pb, k2=k2: e.tensor_tensor(out=v32[k2], in0=PS[pb][:, :], in1=btm[:, gr * 512:(gr + 1) * 512],
                                                                          op=ALU.add), [RPS[pb], R_p], [R_v32[k2]])
                        act(lambda e, k2=k2: e.activation(out=vb[k2], in_=v32[k2], func=AF.Copy), [R_v32[k2]], [R_vb[k2]])
                        if gr == 0:
                            S.dma("pool", O["o_fv"][l, ts_:ts_ + 128, :], v32[k2][:, 0:256], reads=[R_v32[k2]])
                            S.dma("pool", O["o_dv"][l, ts_:ts_ + 128, :], v32[k2][:, 256:512], reads=[R_v32[k2]])
                            S.dma("pool", VS[0][ts_:ts_ + 128, :], vb[k2][:, 0:256], reads=[R_vb[k2]])
                            S.dma("pool", VS[1][ts_:ts_ + 128, :], vb[k2][:, 256:512], reads=[R_vb[k2]])
                        else:
                            S.dma("pool", O["o_bv"][l, ts_:ts_ + 128, :], v32[k2][:, 0:256], reads=[R_v32[k2]])
                            S.dma("pool", VS[2][ts_:ts_ + 128, :], vb[k2][:, 0:256], reads=[R_vb[k2]])
                            S.dma("pool", VS[3][ts_:ts_ + 128, :], vb[k2][:, 256:512], reads=[R_vb[k2]])

                    def tm3(e, i=i):
                        for c in range(8):
                            ins = e.matmul(out=PS[6][:, 0:260], lhsT=hTc[:, c, i * 128:(i + 1) * 128], rhs=wtm[:, c, 1024:1284],
                                           start=(c == 0), stop=(c == 7))
                        return ins
                    pe(tm3, [R_w, R_hTc], [RPS[6]])
                    dve(lambda e: e.tensor_tensor(out=t260, in0=PS[6][:, 0:260], in1=btm[:, 1024:1284], op=ALU.add), [RPS[6], R_p], [R_t260])
                    act(lambda e: e.activation(out=sg, in_=t260[:, 0:256], func=AF.Sigmoid), [R_t260], [R_sg])
                    S.dma("pool", SIGO[ts_:ts_ + 128, :], sg, reads=[R_sg])
                    act(lambda e: e.activation(out=lf4, in_=t260[:, 256:260], func=AF.Exp, scale=-1.0), [R_t260], [R_lf4])
                    act(lambda e: e.activation(out=lf4, in_=lf4, func=AF.Ln, bias=1.0, scale=1.0), [R_lf4], [R_lf4])
                    dve(lambda e: e.tensor_scalar(out=lf4, in0=lf4, scalar1=-1.0, scalar2=None, op0=ALU.mult), [R_lf4], [R_lf4])
                    S.dma("pool", O["o_flf"][l, ts_:ts_ + 128, :], lf4, reads=[R_lf4])
                    S.dma("pool", LOGF[ts_:ts_ + 128, :], lf4, reads=[R_lf4])

            norm_part(0, *blocks[0])
            for bi_ in range(len(blocks)):
                if bi_ + 1 < len(blocks):
                    norm_part(bi_ + 1, *blocks[bi_ + 1])
                proj_part(bi_, *blocks[bi_])
            S.barrier()
            A.pop()
            if debug == "A":
                break
            if debug != "C":
                phase_B(l, lam_init)
            A.push()
            wo = A.take([8, 1024], BF16)
            R_wo = Res()
            S.dma("pool", wo, I["w_out"][l].rearrange("(c p) n -> p c n", p=128), writes=[R_wo])
            mt = [A.take([1024], BF16) for _ in range(2)]
            R_mt = [Res(), Res()]
            mixT = A.take([8, 512], BF16)
            R_mixT = Res()
            xc = [A.take([8, 512], F32) for _ in range(2)]
            R_xc = [Res(), Res()]
            R_XM = Res()
            for bi, (tok0, ntok, segs) in enumerate(blocks):
                xb = bi % 2
                S.dma("sp", xc[xb][:, :, 0:ntok], XT[l][:, tok0:tok0 + ntok].rearrange("(c p) t -> p c t", p=128),
                      reads=[R_XT], writes=[R_xc[xb]])
                for i in range(ntok // 128):
                    tb = i % 2
                    S.dma("sp", mt[tb], MIX[tok0 + i * 128:tok0 + (i + 1) * 128, :], reads=[R_MIX], writes=[R_mt[tb]])
                    pb = tb
                    psb = PS[pb][:, :].bitcast(BF16)

                    def trm(e, tb=tb, psb=psb):
                        for c in range(8):
                            ins = e.transpose(out=psb[:, c * 128:(c + 1) * 128], in_=mt[tb][:, c * 128:(c + 1) * 128], identity=ident_b)
                        return ins
                    pe(trm, [R_mt[tb], R_const], [RPS[pb]])
                    act(lambda e, i=i, psb=psb: e.activation(out=mixT[:, :, i * 128:(i + 1) * 128],
                                                             in_=psb.rearrange("p (a b) -> p a b", a=8), func=AF.Copy),
                        [RPS[pb]], [R_mixT])
                for dc in range(8):
                    pb = 2 + dc % 2

                    def wom(e, dc=dc, pb=pb, ntok=ntok):
                        for c in range(8):
                            ins = e.matmul(out=PS[pb][:, 0:ntok], lhsT=wo[:, c, dc * 128:(dc + 1) * 128], rhs=mixT[:, c, 0:ntok],
                                           start=(c == 0), stop=(c == 7))
                        return ins
                    pe(wom, [R_wo, R_mixT], [RPS[pb]])
                    for (c0, ncol, sq_) in segs:
                        dve(lambda e, dc=dc, pb=pb, c0=c0, ncol=ncol, sq_=sq_, xb=xb: e.scalar_tensor_tensor(
                            out=xc[xb][:, dc, c0:c0 + ncol], in0=PS[pb][:, c0:c0 + ncol], scalar=modT[:, 16 + dc, sq_:sq_ + 1],
                            in1=xc[xb][:, dc, c0:c0 + ncol], op0=ALU.mult, op1=ALU.add), [RPS[pb], R_mod, R_xc[xb]], [R_xc[xb]])
                S.dma("sp", XM[:, tok0:tok0 + ntok].rearrange("(c p) t -> p c t", p=128), xc[xb][:, :, 0:ntok],
                      reads=[R_xc[xb]], writes=[R_XM])
            S.barrier()
            A.pop()
            A.push()
            wgt = A.take([8, DFF], BF16)
            wup = A.take([8, DFF], BF16)
            wdn = A.take([22, 1024], BF16)
            R_wf = Res()
            for hh in range(2):
                S.dma("pool", wgt[:, :, hh * 1408:(hh + 1) * 1408], I["w_gate"][l][:, hh * 1408:(hh + 1) * 1408].rearrange("(c p) n -> p c n", p=128), writes=[R_wf])
                S.dma("pool", wup[:, :, hh * 1408:(hh + 1) * 1408], I["w_up"][l][:, hh * 1408:(hh + 1) * 1408].rearrange("(c p) n -> p c n", p=128), writes=[R_wf])
                S.dma("pool", wdn[:, hh * 11:(hh + 1) * 11, :], I["w_down"][l][hh * 1408:(hh + 1) * 1408, :].rearrange("(c p) n -> p c n", p=128), writes=[R_wf])
            xf = A.take([8, 512], F32)
            R_xf = Res()
            rs2 = A.take([512], F32)
            R_rs2 = Res()
            tt2 = A.take([512], F32)
            R_tt2 = Res()
            h2 = A.take([8, 512], BF16)
            R_h2 = Res()
            actT = A.take([22, 512], BF16)
            R_actT = Res()
            sq2 = actT[:, 0:8, :]
            R_sq2 = R_actT
            slu = [A.take([512], F32) for _ in range(2)]
            R_slu = [Res(), Res()]
            xo = [A.take([512], F32) for _ in range(2)]
            R_xo = [Res(), Res()]
            R_XO = Res()
            dst_all = XT[1] if l == 0 else O["YT"]
            for bi, (tok0, ntok, segs) in enumerate(blocks):
                S.dma("sp", xf[:, :, 0:ntok], XM[:, tok0:tok0 + ntok].rearrange("(c p) t -> p c t", p=128), reads=[R_XM], writes=[R_xf])
                act(lambda e, ntok=ntok: e.activation(out=sq2[:, :, 0:ntok], in_=xf[:, :, 0:ntok], func=AF.Square), [R_xf], [R_sq2])

                def ss2(e, ntok=ntok):
                    for c in range(8):
                        ins = e.matmul(out=PS[0][:, 0:ntok], lhsT=ones_b, rhs=sq2[:, c, 0:ntok], start=(c == 0), stop=(c == 7))
                    return ins
                pe(ss2, [R_sq2, R_const], [RPS[0]])
                act(lambda e, ntok=ntok: e.activation(out=rs2[:, 0:ntok], in_=PS[0][:, 0:ntok], func=AF.Ln, scale=1.0 / D, bias=EPS), [RPS[0]], [R_rs2])
                act(lambda e, ntok=ntok: e.activation(out=rs2[:, 0:ntok], in_=rs2[:, 0:ntok], func=AF.Exp, scale=-0.5), [R_rs2], [R_rs2])
                for c in range(8):
                    for (c0, ncol, sq_) in segs:
                        dve(lambda e, c=c, c0=c0, ncol=ncol, sq_=sq_: e.scalar_tensor_tensor(
                            out=tt2[:, c0:c0 + ncol], in0=xf[:, c, c0:c0 + ncol], scalar=A2[:, c, sq_:sq_ + 1], in1=rs2[:, c0:c0 + ncol],
                            op0=ALU.mult, op1=ALU.mult), [R_xf, R_rs2, R_mod], [R_tt2])
                        act(lambda e, c=c, c0=c0, ncol=ncol, sq_=sq_: e.activation(
                            out=h2[:, c, c0:c0 + ncol], in_=tt2[:, c0:c0 + ncol], func=AF.Identity, bias=modT[:, 24 + c, sq_:sq_ + 1], scale=1.0),
                            [R_tt2, R_mod], [R_h2])
                for f in range(22):
                    pa = 1 + 2 * (f % 2)
                    pbk = pa + 1

                    def gu(e, f=f, pa=pa, pbk=pbk, ntok=ntok):
                        for c in range(8):
                            e.matmul(out=PS[pa][:, 0:ntok], lhsT=wgt[:, c, f * 128:(f + 1) * 128], rhs=h2[:, c, 0:ntok], start=(c == 0), stop=(c == 7))
                        for c in range(8):
                            ins = e.matmul(out=PS[pbk][:, 0:ntok], lhsT=wup[:, c, f * 128:(f + 1) * 128], rhs=h2[:, c, 0:ntok], start=(c == 0), stop=(c == 7))
                        return ins
                    pe(gu, [R_wf, R_h2], [RPS[pa], RPS[pbk]])
                    k2 = f % 2
                    act(lambda e, pa=pa, k2=k2, ntok=ntok: e.activation(out=slu[k2][:, 0:ntok], in_=PS[pa][:, 0:ntok], func=AF.Silu), [RPS[pa]], [R_slu[k2]])
                    dve(lambda e, f=f, pbk=pbk, k2=k2, ntok=ntok: e.tensor_tensor(out=actT[:, f, 0:ntok], in0=PS[pbk][:, 0:ntok], in1=slu[k2][:, 0:ntok], op=ALU.mult),
                        [RPS[pbk], R_slu[k2]], [R_actT])
                for dc in range(8):
                    pb = 5 + dc % 2
                    k2 = dc % 2

                    def dn(e, dc=dc, pb=pb, ntok=ntok):
                        for f in range(22):
                            ins = e.matmul(out=PS[pb][:, 0:ntok], lhsT=wdn[:, f, dc * 128:(dc + 1) * 128], rhs=actT[:, f, 0:ntok], start=(f == 0), stop=(f == 21))
                        return ins
                    pe(dn, [R_wf, R_actT], [RPS[pb]])
                    for (c0, ncol, sq_) in segs:
                        dve(lambda e, dc=dc, pb=pb, c0=c0, ncol=ncol, sq_=sq_, k2=k2: e.scalar_tensor_tensor(
                            out=xo[k2][:, c0:c0 + ncol], in0=PS[pb][:, c0:c0 + ncol], scalar=modT[:, 40 + dc, sq_:sq_ + 1],
                            in1=xf[:, dc, c0:c0 + ncol], op0=ALU.mult, op1=ALU.add), [RPS[pb], R_mod, R_xf], [R_xo[k2]])
                    S.dma("sp", dst_all[dc * 128:(dc + 1) * 128, tok0:tok0 + ntok], xo[k2][:, 0:ntok], reads=[R_xo[k2]], writes=[R_XO])
            S.barrier()
            A.pop()

        S.barrier()
        S.emit()
    return nc


def host_consts():
    c = {}
    c["ident"] = np.eye(128, dtype=np.float32)
    b64 = np.zeros((128, 128), np.float32)
    b64[:64, :64] = 1
    b64[64:, 64:] = 1
    c["blk64"] = b64
    b32 = np.zeros((128, 128), np.float32)
    for i in range(4):
        b32[i * 32:(i + 1) * 32, i * 32:(i + 1) * 32] = 1
    c["blk32"] = b32
    k = np.arange(128)[:, None]
    q = np.arange(128)[None, :]
    c["tri"] = (k <= q).astype(np.float32)
    c["cmask"] = ((k // 64) <= (q // 64)).astype(np.float32)
    c["m4"] = (~((k < 64) & (q >= 64))).astype(np.float32)
    sl = np.array(alibi_slopes(), np.float64)
    o = np.arange(33)[None, None, :]
    c["alcol"] = (sl[None, :, None] * (np.arange(128)[:, None, None] - 128.0 * o)).astype(np.float32)
    s0_ = np.zeros((128, 128), np.float32)
    s0_[0, :] = 1
    c["sel0"] = s0_
    s1_ = np.zeros((128, 128), np.float32)
    s1_[127, :] = 1
    c["sel127"] = s1_
    s2_ = np.zeros((128, 128), np.float32)
    s2_[31, :] = 1
    c["sel31"] = s2_
    ow = np.arange(36)[None, None, :] - 3
    c["alw"] = (sl[None, :, None] * (np.arange(128)[:, None, None] - 128.0 * ow)).astype(np.float32)
    c["cdiff"] = (np.where(k <= q, 1.0, np.exp(-2.0 * sl[:, None, None] * (k - q)[None])) * c["cmask"][None]).astype(np.float32)
    c["emd_arg"] = (-sl[:, None, None] * np.abs(q - k)[None] + sl[:, None, None] * q[None]).astype(np.float32)
    return c


def prep_core(inp, core, T, P, LB):
    f = np.float32
    b = core // 2
    s0 = 4 * core
    m = {}
    m["xp"] = np.ascontiguousarray(inp["x_prompt"][b, :T])
    m["xs"] = np.ascontiguousarray(inp["x_sample"][s0:s0 + 4].reshape(128, D))
    cv = np.concatenate([inp["c_prompt"][b:b + 1], inp["c_sample"][s0:s0 + 4]], 0)
    m["cT"] = np.ascontiguousarray(cv.reshape(5, 8, 128).transpose(2, 1, 0))
    for k_, n_ in [("w_mod", "w_mod"), ("w_in", "w_in"), ("w_out", "w_out"), ("w_gate", "w_ffn_gate"), ("w_up", "w_ffn_up"),
                   ("w_down", "w_ffn_down")]:
        m[k_] = inp[n_]
    m["b_modT"] = np.ascontiguousarray(inp["b_mod"].reshape(DEPTH, 48, 128).transpose(0, 2, 1))
    m["g1T"] = np.ascontiguousarray(inp["norm1_g"].reshape(DEPTH, 8, 128).transpose(0, 2, 1))
    m["g2T"] = np.ascontiguousarray(inp["norm2_g"].reshape(DEPTH, 8, 128).transpose(0, 2, 1))
    b_in = inp["b_in"]
    m["binT"] = np.ascontiguousarray(np.stack([b_in[:, c0:c0 + 128] for c0 in FM_COLS], 1).transpose(0, 2, 1))
    m["bgate"] = np.ascontiguousarray(np.stack([b_in[:, C_MI:C_MI + 4], b_in[:, C_MF:C_MF + 4]], -1))
    btm = np.concatenate([b_in[:, c0:c0 + n] for (c0, n) in TM_SRC], 1)
    m["btm"] = np.ascontiguousarray(np.broadcast_to(btm[:, None, :], (DEPTH, 128, NTM)))
    gq = []
    for g_, name in enumerate(["qk_g_fox", "qk_g_diff", "qk_g_band"]):
        gg = inp[name]
        rep = 128 // gg.shape[-1]
        for qk in range(2):
            col = np.tile(gg[:, qk, :], (1, rep))
            gq += [col, col]
    m["gqk"] = np.ascontiguousarray(np.stack(gq, -1))
    m["convw"] = np.ascontiguousarray(inp["conv_w"].reshape(DEPTH, 4, 4, 128).transpose(0, 3, 2, 1))
    m["convb"] = np.ascontiguousarray(inp["conv_b"].reshape(DEPTH, 4, 128).transpose(0, 2, 1))
    stc = inp["state_conv"][:, s0:s0 + 4]
    m["convst"] = np.ascontiguousarray(stc.reshape(DEPTH, 4, 3, 4, 128).transpose(0, 4, 3, 1, 2))
    m["lamb"] = np.ascontiguousarray(np.broadcast_to(inp["diff_lambda"][:, None], (DEPTH, 128, 4, 32)))
    m["gsub"] = np.ascontiguousarray(np.broadcast_to(inp["diff_subln_g"][:, None], (DEPTH, 128, 64)))
    m["gmh"] = np.ascontiguousarray(np.broadcast_to(inp["mlstm_norm_g"][:, None], (DEPTH, 128, 64)))
    tab = inp["band_rel_bias"]
    k = np.arange(128)[:, None]
    q = np.arange(128)[None, :]
    i0 = np.clip(q - k, -128, 128) + 128
    i1 = np.clip(128 + q - k, -128, 128) + 128
    i2 = np.full((128, 128), 256)
    m["bandT"] = np.ascontiguousarray(np.stack([tab[:, :, i0], tab[:, :, i1], tab[:, :, i2]], 2))
    m["c_fk"] = np.ascontiguousarray(inp["cache_fox_k"][:, s0:s0 + 4].reshape(DEPTH, 4, P, 256))
    m["c_fv"] = np.ascontiguousarray(inp["cache_fox_v"][:, s0:s0 + 4].reshape(DEPTH, 4, P, 256))
    m["c_flf"] = np.ascontiguousarray(inp["cache_fox_logf"][:, s0:s0 + 4])
    m["c_dk"] = np.ascontiguousarray(inp["cache_diff_k"][:, s0:s0 + 4].reshape(DEPTH, 4, P, 256))
    m["c_dv"] = np.ascontiguousarray(inp["cache_diff_v"][:, s0:s0 + 4].reshape(DEPTH, 4, P, 256))
    m["c_bk"] = np.ascontiguousarray(inp["cache_band_k"][:, s0:s0 + 4].reshape(DEPTH, 4, LB, 256))
    m["c_bv"] = np.ascontiguousarray(inp["cache_band_v"][:, s0:s0 + 4].reshape(DEPTH, 4, LB, 256))
    m["s_c"] = np.ascontiguousarray(inp["state_mlstm_c"][:, s0:s0 + 4])
    m["s_n"] = np.ascontiguousarray(inp["state_mlstm_n"][:, s0:s0 + 4])
    m["s_m"] = np.ascontiguousarray(inp["state_mlstm_m"][:, s0:s0 + 4])
    m["s_mcol"] = m["s_m"][..., None]
    m["s_mb"] = np.broadcast_to(m["s_m"][:, :, None, :], (DEPTH, 4, 128, 4))
    m.update(host_consts())
    return {k_: np.ascontiguousarray(v, dtype=f) for k_, v in m.items()}


def assemble(results, T, P, LB, nb, ns):
    ncores = len(results)
    pc = [min(2 * b, ncores - 1) for b in range(nb)] if ncores >= 2 * nb else list(range(nb))
    keep = min(512, T)

    def P_(fn):
        return np.stack([fn(results[c]) for c in pc], 0)

    def S_(fn):
        return np.concatenate([fn(results[c]) for c in range(ncores)], 0)
    y_prompt = P_(lambda r: r["YT"][:, :T].T)
    y_sample = S_(lambda r: r["YT"][:, T:].T.reshape(4, 32, D))
    outs = [y_prompt, y_sample]

    def fm_p(name, lo=0):
        return np.stack([P_(lambda r: r[name][l][:, lo:T].T) for l in range(DEPTH)], 0)

    def tm_p(name, lo=0):
        return np.stack([P_(lambda r: r[name][l][lo:T]) for l in range(DEPTH)], 0)

    def fm_s(name):
        return np.stack([S_(lambda r: r[name][l][:, T:].T.reshape(4, 32, -1)) for l in range(DEPTH)], 0)

    def tm_s(name):
        return np.stack([S_(lambda r: r[name][l][T:].reshape(4, 32, -1)) for l in range(DEPTH)], 0)
    B = nb
    p_fox_k = fm_p("o_fk").reshape(DEPTH, B, T, 4, 64)
    p_fox_v = tm_p("o_fv").reshape(DEPTH, B, T, 4, 64)
    p_fox_logf = tm_p("o_flf")
    p_diff_k = fm_p("o_dk").reshape(DEPTH, B, T, 4, 2, 32)
    p_diff_v = tm_p("o_dv").reshape(DEPTH, B, T, 4, 64)
    p_band_k = fm_p("o_bk", T - keep).reshape(DEPTH, B, keep, 4, 64)
    p_band_v = tm_p("o_bv", T - keep).reshape(DEPTH, B, keep, 4, 64)
    p_mc = np.stack([P_(lambda r: r["o_mc"][l][0, :, :, 0:64]) for l in range(DEPTH)], 0)
    p_mn = np.stack([P_(lambda r: r["o_mc"][l][0, :, :, 64]) for l in range(DEPTH)], 0)
    p_mm = np.stack([P_(lambda r: r["o_mm"][l][0]) for l in range(DEPTH)], 0)
    p_conv = np.stack([P_(lambda r: r["o_conv"][l][:, :, 0, :].transpose(2, 0, 1).reshape(3, 512)) for l in range(DEPTH)], 0)
    NS = 4 * ncores
    s_fox_k = fm_s("o_fk").reshape(DEPTH, NS, 32, 4, 64)
    s_fox_v = tm_s("o_fv").reshape(DEPTH, NS, 32, 4, 64)
    s_fox_logf = tm_s("o_flf")
    s_diff_k = fm_s("o_dk").reshape(DEPTH, NS, 32, 4, 2, 32)
    s_diff_v = tm_s("o_dv").reshape(DEPTH, NS, 32, 4, 64)
    s_band_k = fm_s("o_bk").reshape(DEPTH, NS, 32, 4, 64)
    s_band_v = tm_s("o_bv").reshape(DEPTH, NS, 32, 4, 64)
    s_mc = np.stack([S_(lambda r: r["o_mc"][l][1:5, :, :, 0:64]) for l in range(DEPTH)], 0)
    s_mn = np.stack([S_(lambda r: r["o_mc"][l][1:5, :, :, 64]) for l in range(DEPTH)], 0)
    s_mm = np.stack([S_(lambda r: r["o_mm"][l][1:5]) for l in range(DEPTH)], 0)
    s_conv = np.stack([S_(lambda r: r["o_conv"][l][:, :, 1:5, :].transpose(2, 3, 0, 1).reshape(4, 3, 512)) for l in range(DEPTH)], 0)
    outs += [p_fox_k, p_fox_v, p_fox_logf, p_diff_k, p_diff_v, p_band_k, p_band_v, p_mc, p_mn, p_mm, p_conv,
             s_fox_k, s_fox_v, s_fox_logf, s_diff_k, s_diff_v, s_band_k, s_band_v, s_mc, s_mn, s_mm, s_conv]
    return tuple(np.ascontiguousarray(o, dtype=np.float32) for o in outs)


_NC_CACHE = {}


def kernel(**inputs):
    inp = {k: np.asarray(v) for k, v in inputs.items()}
    T = inp["x_prompt"].shape[1]
    P = inp["cache_fox_k"].shape[2]
    LB = inp["cache_band_k"].shape[2]
    key = (T, P, LB)
    if key not in _NC_CACHE:
        _NC_CACHE[key] = build_program(T, P, LB)
    nc = _NC_CACHE[key]
    in_maps = [prep_core(inp, c, T, P, LB) for c in range(8)]
    res = run_bass_kernel_spmd(nc, in_maps, core_ids=list(range(8)))
    return assemble(res.results, T, P, LB, nb=inp["x_prompt"].shape[0], ns=inp["x_sample"].shape[0])
```

```python
import math
import numpy as np
import concourse.bass as bass
import concourse.mybir as mybir
from concourse.bass_utils import run_bass_kernel_spmd
from contextlib import ExitStack

F32 = mybir.dt.float32
BF16 = mybir.dt.bfloat16
AF = mybir.ActivationFunctionType
ALU = mybir.AluOpType
AX = mybir.AxisListType

D = 1024
DFF = 2816
NIN = 3340
EPS = 1e-6
DEPTH = 2


class Res:
    __slots__ = ("name", "w", "r")

    def __init__(self, name=""):
        self.name = name
        self.w = {}
        self.r = {}


class Sched:
    NDS = 48
    NHW = 32

    def __init__(self, nc, stack):
        self.nc = nc
        self.names = ["pe", "act", "dve", "pool", "sp"]
        self.streams = {e: [] for e in self.names}
        self.sems = {e: stack.enter_context(nc.semaphore("s_" + e)) for e in ["pe", "act", "dve", "pool"]}
        self.cnt = {e: 0 for e in self.sems}
        self.dsems = [stack.enter_context(nc.semaphore("d%d" % i)) for i in range(self.NDS)]
        self.dval = [0] * self.NDS
        self.drr = 0
        self.drr_sw = 0
        self.seen = {e: {} for e in self.names}
        self.nops = 0

    def _waits(self, eng, reads, writes, extra=(), par=False):
        waits = {}

        def add(m):
            if m is None:
                return
            k, v = m
            if waits.get(k, 0) < v:
                waits[k] = v
        for r in reads:
            for k, v in r.w.items():
                add((k, v))
        for r in writes:
            if not par:
                for k, v in r.w.items():
                    add((k, v))
            for k, v in r.r.items():
                add((k, v))
        for m in extra:
            add(m)
        need = []
        for k, v in waits.items():
            if eng == "pe" and k == ("e", "pe"):
                continue
            if self.seen[eng].get(k, 0) >= v:
                continue
            self.seen[eng][k] = v
            need.append((k, v))
        return need

    def _commit(self, mark, reads, writes, par=False):
        k, v = mark
        for r in reads:
            if r.r.get(k, 0) < v:
                r.r[k] = v
        for r in writes:
            if par and not r.r:
                if r.w.get(k, 0) < v:
                    r.w[k] = v
            else:
                r.w = {k: v}
            r.r = {}

    def op(self, eng, fn, reads=(), writes=(), par=False):
        need = self._waits(eng, reads, writes, par=par)
        self.cnt[eng] += 1
        mark = (("e", eng), self.cnt[eng])
        self.streams[eng].append((need, fn, mark))
        self._commit(mark, reads, writes, par=par)
        self.nops += 1
        return mark

    def dma(self, q, out, in_, reads=(), writes=(), par=False, **kw):
        if q == "pool":
            j = self.NHW + self.drr_sw
            self.drr_sw = (self.drr_sw + 1) % (self.NDS - self.NHW)
        else:
            j = self.drr
            self.drr = (j + 1) % self.NHW
        extra = []
        if self.dval[j] > 0:
            extra.append((("d", j), self.dval[j]))
        need = self._waits(q, reads, writes, extra, par=par)
        self.dval[j] += 16
        mark = (("d", j), self.dval[j])
        self.streams[q].append((need, (lambda e: e.dma_start(out=out, in_=in_, **kw)), mark))
        self._commit(mark, reads, writes, par=par)
        self.nops += 1
        return mark

    def barrier(self):
        for eng in self.names:
            need = []
            for e2 in self.sems:
                if e2 == eng:
                    continue
                k = ("e", e2)
                v = self.cnt[e2]
                if v > 0 and self.seen[eng].get(k, 0) < v:
                    self.seen[eng][k] = v
                    need.append((k, v))
            for j in range(self.NDS):
                k = ("d", j)
                v = self.dval[j]
                if v > 0 and self.seen[eng].get(k, 0) < v:
                    self.seen[eng][k] = v
                    need.append((k, v))
            if need:
                self.streams[eng].append((need, None, None))

    def _h(self, k):
        return self.sems[k[1]] if k[0] == "e" else self.dsems[k[1]]

    def replay(self, name, e):
        for need, fn, mark in self.streams[name]:
            for k, v in need:
                e.wait_ge(self._h(k), v)
            if fn is None:
                continue
            ins = fn(e)
            if mark is not None:
                ins.then_inc(self._h(mark[0]), 16 if mark[0][0] == "d" else 1)

    def emit(self):
        nc = self.nc
        with nc.Block() as block:
            @block.tensor
            def _(e):
                self.replay("pe", e)

            @block.scalar
            def _(e):
                self.replay("act", e)

            @block.vector
            def _(e):
                self.replay("dve", e)

            @block.gpsimd
            def _(e):
                self.replay("pool", e)

            @block.sync
            def _(e):
                self.replay("sp", e)


class Arena:
    def __init__(self, ap, nwords):
        self.ap = ap
        self.n = nwords
        self.off = 0
        self.marks = []

    def take(self, shape, dtype):
        size = 2 if dtype == BF16 else 4
        n = 1
        for s in shape:
            n *= s
        words = (n * size + 3) // 4
        words = (words + 7) // 8 * 8
        assert self.off + words <= self.n, ("arena overflow", self.off, words, self.n)
        a = self.ap[:, self.off:self.off + words]
        self.off += words
        if dtype != F32:
            a = a.bitcast(dtype)
        a = a[:, 0:n]
        if len(shape) == 2:
            a = a.rearrange("p (a b) -> p a b", a=shape[0], b=shape[1])
        elif len(shape) == 3:
            a = a.rearrange("p (a b c) -> p a b c", a=shape[0], b=shape[1], c=shape[2])
        return a

    def push(self):
        self.marks.append(self.off)

    def pop(self):
        self.off = self.marks.pop()


def alibi_slopes():
    return [2.0 ** (-8.0 * (i + 1.0) / 4) for i in range(4)]


C_FQ, C_FK, C_FV, C_FF = 0, 256, 512, 768
C_DQ, C_DK, C_DV = 772, 1028, 1284
C_MQK, C_MV, C_MI, C_MF, C_MO = 1540, 2052, 2308, 2312, 2316
C_BQ, C_BK, C_BV = 2572, 2828, 3084
FM_COLS = [C_FQ, C_FQ + 128, C_FK, C_FK + 128, C_DQ, C_DQ + 128, C_DK, C_DK + 128,
           C_BQ, C_BQ + 128, C_BK, C_BK + 128, C_MQK, C_MQK + 128, C_MQK + 256, C_MQK + 384]
TM_SRC = [(C_FV, 256), (C_DV, 256), (C_BV, 256), (C_MV, 256), (C_MO, 256), (C_FF, 4)]
NTM = 1284


def build_program(T, P, LB, debug=False):
    TT = T + 128
    NPB = T // 512
    nc = bass.Bass("TRN2", target_bir_lowering=False)

    def din(name, shape, dt=F32):
        return nc.dram_tensor(name, list(shape), dt, kind="ExternalInput").ap()

    def dout(name, shape, dt=F32):
        return nc.dram_tensor(name, list(shape), dt, kind="ExternalOutput").ap()

    def dscr(name, shape, dt=F32):
        return nc.dram_tensor(name, list(shape), dt, kind="Internal").ap()

    I = {}
    I["xp"] = din("xp", [T, D])
    I["xs"] = din("xs", [128, D])
    I["cT"] = din("cT", [128, 8, 5])
    I["w_mod"] = din("w_mod", [DEPTH, D, 6 * D])
    I["b_modT"] = din("b_modT", [DEPTH, 128, 48])
    I["w_in"] = din("w_in", [DEPTH, D, NIN])
    I["w_out"] = din("w_out", [DEPTH, D, D])
    I["w_gate"] = din("w_gate", [DEPTH, D, DFF])
    I["w_up"] = din("w_up", [DEPTH, D, DFF])
    I["w_down"] = din("w_down", [DEPTH, DFF, D])
    I["g1T"] = din("g1T", [DEPTH, 128, 8])
    I["g2T"] = din("g2T", [DEPTH, 128, 8])
    I["binT"] = din("binT", [DEPTH, 128, 16])
    I["bgate"] = din("bgate", [DEPTH, 4, 2])
    I["btm"] = din("btm", [DEPTH, 128, NTM])
    I["gqk"] = din("gqk", [DEPTH, 128, 12])
    I["convw"] = din("convw", [DEPTH, 128, 4, 4])
    I["convb"] = din("convb", [DEPTH, 128, 4])
    I["convst"] = din("convst", [DEPTH, 128, 4, 4, 3])
    I["lamb"] = din("lamb", [DEPTH, 128, 4, 32])
    I["gsub"] = din("gsub", [DEPTH, 128, 64])
    I["gmh"] = din("gmh", [DEPTH, 128, 64])
    I["bandT"] = din("bandT", [DEPTH, 4, 3, 128, 128])
    I["c_fk"] = din("c_fk", [DEPTH, 4, P, 256])
    I["c_fv"] = din("c_fv", [DEPTH, 4, P, 256])
    I["c_flf"] = din("c_flf", [DEPTH, 4, P, 4])
    I["c_dk"] = din("c_dk", [DEPTH, 4, P, 256])
    I["c_dv"] = din("c_dv", [DEPTH, 4, P, 256])
    I["c_bk"] = din("c_bk", [DEPTH, 4, LB, 256])
    I["c_bv"] = din("c_bv", [DEPTH, 4, LB, 256])
    I["s_c"] = din("s_c", [DEPTH, 4, 4, 64, 64])
    I["s_n"] = din("s_n", [DEPTH, 4, 4, 64])
    I["s_m"] = din("s_m", [DEPTH, 4, 4])
    I["ident"] = din("ident", [128, 128])
    I["blk64"] = din("blk64", [128, 128])
    I["blk32"] = din("blk32", [128, 128])
    I["tri"] = din("tri", [128, 128])
    I["cmask"] = din("cmask", [128, 128])
    I["m4"] = din("m4", [128, 128])
    I["alcol"] = din("alcol", [128, 4, 33])
    I["emd_arg"] = din("emd_arg", [4, 128, 128])
    I["sel0"] = din("sel0", [128, 128])
    I["alw"] = din("alw", [128, 4, 36])
    I["cdiff"] = din("cdiff", [4, 128, 128])
    I["sel127"] = din("sel127", [128, 128])
    I["sel31"] = din("sel31", [128, 128])
    I["s_mcol"] = din("s_mcol", [DEPTH, 4, 4, 1])
    I["s_mb"] = din("s_mb", [DEPTH, 4, 128, 4])

    O = {}
    O["YT"] = dout("YT", [D, TT])
    O["o_fk"] = dout("o_fk", [DEPTH, 256, TT])
    O["o_fv"] = dout("o_fv", [DEPTH, TT, 256])
    O["o_flf"] = dout("o_flf", [DEPTH, TT, 4])
    O["o_dk"] = dout("o_dk", [DEPTH, 256, TT])
    O["o_dv"] = dout("o_dv", [DEPTH, TT, 256])
    O["o_bk"] = dout("o_bk", [DEPTH, 256, TT])
    O["o_bv"] = dout("o_bv", [DEPTH, TT, 256])
    O["o_mc"] = dout("o_mc", [DEPTH, 5, 4, 64, 65])
    O["o_mm"] = dout("o_mm", [DEPTH, 5, 4])
    O["o_conv"] = dout("o_conv", [DEPTH, 4, 128, 5, 3])
    if debug:
        O["dbg"] = dout("dbg", [128, 4096])

    XT = [dscr("XT0", [D, TT]), dscr("XT1", [D, TT])]
    XM = dscr("XM", [D, TT])
    QS = [dscr("QS%d" % g, [256, TT], BF16) for g in range(4)]
    KS = [dscr("KS%d" % g, [256, TT], BF16) for g in range(4)]
    VS = [dscr("VS%d" % g, [TT, 256], BF16) for g in range(4)]
    SIGO = dscr("SIGO", [TT, 256])
    GI = dscr("GI", [4, TT])
    GF = dscr("GF", [4, TT])
    LOGF = dscr("LOGF", [TT, 4])
    MIX = dscr("MIX", [TT, D], BF16)

    st = ExitStack()
    with st:
        S = Sched(nc, st)
        NW = 52000
        arena_t = st.enter_context(nc.sbuf_tensor("arena", [128, NW], F32))
        A = Arena(arena_t[:], NW)
        PS = [st.enter_context(nc.psum_tensor("ps%d" % i, [128, 512], F32)) for i in range(8)]
        RPS = [Res("ps%d" % i) for i in range(8)]

        ident_f = A.take([128], F32)
        ident_b = A.take([128], BF16)
        ones_b = A.take([128], BF16)
        blk64 = A.take([128], BF16)
        blk32 = A.take([128], BF16)
        tri_f = A.take([128], F32)
        cmask_f = A.take([128], F32)
        m4_f = A.take([128], F32)
        alcol = A.take([4, 33], F32)
        modT = A.take([48, 5], F32)
        A1 = A.take([8, 5], F32)
        A2 = A.take([8, 5], F32)
        R_const = Res("const")
        R_mod = Res("mod")
        tmpc = A.take([128], F32)
        S.dma("sp", ident_f, I["ident"], writes=[R_const])
        S.dma("sp", tri_f, I["tri"], writes=[R_const])
        S.dma("sp", cmask_f, I["cmask"], writes=[R_const])
        S.dma("sp", m4_f, I["m4"], writes=[R_const])
        S.dma("sp", alcol, I["alcol"], writes=[R_const])
        S.dma("pool", blk64, I["blk64"], writes=[R_const])
        S.dma("pool", blk32, I["blk32"], writes=[R_const])
        S.dma("pool", ident_b, I["ident"], writes=[R_const])
        S.op("dve", lambda e: e.memset(ones_b, 1.0), writes=[R_const])
        S.barrier()

        blocks = [(i * 512, 512, [(0, 512, 0)]) for i in range(NPB)]
        blocks.append((T, 128, [(32 * s, 32, 1 + s) for s in range(4)]))

        def act(fn, reads, writes, par=False):
            return S.op("act", fn, reads, writes, par=par)

        def dve(fn, reads, writes, par=False):
            return S.op("dve", fn, reads, writes, par=par)

        def pe(fn, reads, writes, par=False):
            return S.op("pe", fn, reads, writes, par=par)

        modTn = A.take([48, 5], F32)
        A1n = A.take([8, 5], F32)
        A2n = A.take([8, 5], F32)
        R_modn = Res("modn")

        def gen_M(l, modT_d, A1_d, A2_d, R_d):
            cT = A.take([8, 5], F32)
            scT = A.take([8, 5], BF16)
            sgm = A.take([8, 5], F32)
            bmod = A.take([48], F32)
            g1 = A.take([8], F32)
            g2 = A.take([8], F32)
            wm = [A.take([8, 1536], BF16) for _ in range(2)]
            R_wm = [Res(), Res()]
            R_c = Res()
            R_sc = Res()
            S.dma("sp", cT, I["cT"], writes=[R_c])
            S.dma("sp", bmod, I["b_modT"][l], writes=[R_c])
            S.dma("sp", g1, I["g1T"][l], writes=[R_c])
            S.dma("sp", g2, I["g2T"][l], writes=[R_c])
            S.op("act", lambda e: e.activation(out=sgm, in_=cT, func=AF.Sigmoid), [R_c], [R_sc])
            S.op("dve", lambda e: e.tensor_tensor(out=scT, in0=sgm, in1=cT, op=ALU.mult), [R_sc, R_c], [R_sc])
            for sl in range(4):
                b = sl % 2
                S.dma("pool", wm[b], I["w_mod"][l][:, sl * 1536:(sl + 1) * 1536].rearrange("(c p) n -> p c n", p=128),
                      writes=[R_wm[b]])
                for jj in range(12):
                    j = sl * 12 + jj

                    def mm(e, b=b, jj=jj):
                        for c in range(8):
                            ins = e.matmul(out=PS[0][:, 0:5], lhsT=wm[b][:, c, jj * 128:(jj + 1) * 128], rhs=scT[:, c, :],
                                           start=(c == 0), stop=(c == 7))
                        return ins
                    S.op("pe", mm, [R_wm[b], R_sc], [RPS[0]])
                    S.op("dve", lambda e, j=j: e.tensor_scalar(out=modT_d[:, j, :], in0=PS[0][:, 0:5], scalar1=bmod[:, j:j + 1],
                                                               scalar2=None, op0=ALU.add), [RPS[0], R_c], [R_d])
                    if jj % 2 == 1:
                        yield
            for s5 in range(5):
                S.op("dve", lambda e, s5=s5: e.scalar_tensor_tensor(out=A1_d[:, :, s5], in0=modT_d[:, 8:16, s5], scalar=1.0, in1=g1,
                                                                    op0=ALU.add, op1=ALU.mult), [R_d, R_c], [R_d])
                S.op("dve", lambda e, s5=s5: e.scalar_tensor_tensor(out=A2_d[:, :, s5], in0=modT_d[:, 32:40, s5], scalar=1.0, in1=g2,
                                                                    op0=ALU.add, op1=ALU.mult), [R_d, R_c], [R_d])
            yield

        R_MIX = Res("MIX")
        R_XT = Res("XT")

        ones_f = A.take([128], F32)
        sel0 = A.take([128], F32)
        emd = A.take([4, 128], F32)
        alw = A.take([4, 36], F32)
        cdiff = A.take([4, 128], F32)
        S.dma("sp", alw, I["alw"], writes=[R_const])
        S.dma("sp", cdiff, I["cdiff"].rearrange("h k q -> k h q"), writes=[R_const])
        S.dma("sp", sel0, I["sel0"], writes=[R_const])
        S.dma("sp", emd, I["emd_arg"].rearrange("h k q -> k h q"), writes=[R_const])
        S.op("dve", lambda e: e.memset(ones_f, 1.0), writes=[R_const])
        S.op("act", lambda e: e.activation(out=emd, in_=emd, func=AF.Exp), reads=[R_const], writes=[R_const])
        for h_ in range(4):
            S.op("dve", lambda e, h_=h_: e.tensor_tensor(out=emd[:, h_, :], in0=emd[:, h_, :], in1=cmask_f, op=ALU.mult),
                 reads=[R_const], writes=[R_const])
        S.barrier()
        NKB_P = T // 128
        NCB = P // 128
        NLB = LB // 128

        sel127 = A.take([128], F32)
        sel31 = A.take([128], F32)
        S.dma("sp", sel127, I["sel127"], writes=[R_const])
        S.dma("sp", sel31, I["sel31"], writes=[R_const])

        def phase_ML(l):
            A.push()
            NTM_ = max(T, 128)
            gi_r = A.take([NTM_], F32)
            gf_r = A.take([NTM_], F32)
            G_r = A.take([NTM_], F32)
            MM_r = A.take([NTM_], F32)
            mt_r = A.take([NTM_], F32)
            ones_r = A.take([NTM_], F32)
            dve(lambda e: e.memset(ones_r[0:4, :], 1.0), [], [R_const])
            NCM = max(T // 128, 1)
            cols = A.take([NCM, 12], F32)
            MMe = A.take([NCM, 4], F32)
            MMp = A.take([NCM, 4], F32)
            egs = A.take([NCM, 4], F32)
            eM = A.take([NCM, 4], F32)
            acol = A.take([NCM, 4], F32)
            dec = A.take([NCM, 4], F32)
            emt = A.take([NCM, 4], F32)
            m0c = A.take([1], F32)
            gmh = A.take([64], F32)
            Cn = A.take([2, 65], F32)
            Cnb = A.take([2, 65], BF16)
            Cnb2 = [Cnb, A.take([2, 65], BF16)]
            R_Cnb2 = [Res(), Res()]
            dec2 = A.take([NCM, 2], F32)
            R_osq = Res()
            R_oms = Res()
            qTm = [A.take([2, 128], BF16) for _ in range(2)]
            kTm = [A.take([2, 128], BF16) for _ in range(2)]
            vst = [A.take([256], BF16) for _ in range(2)]
            VAm = [A.take([4, 65], BF16) for _ in range(2)]
            sig = [A.take([256], F32) for _ in range(2)]
            R_in = [Res(), Res()]
            R_VAm = [Res(), Res()]
            kw = A.take([256], BF16)
            wT = [A.take([128], BF16) for _ in range(2)]
            R_wT = [Res(), Res()]
            tmpA = A.take([4, 65], F32)
            resm = A.take([4, 65], F32)
            den = A.take([4], F32)
            o4 = A.take([4, 64], F32)
            osq = A.take([4, 64], F32)
            oms = A.take([4], F32)
            o4b = [A.take([256], BF16) for _ in range(2)]
            R_o4b = [Res(), Res()]
            R_rows, R_cols, R_ex, R_Cn, R_Cnb, R_kw, R_tmp, R_res, R_o4, R_g = [Res() for _ in range(10)]
            S.dma("sp", gmh, I["gmh"][l], writes=[R_g])
            for vm in VAm:
                dve(lambda e, vm=vm: e.memset(vm[:, :, 64:65], 1.0), [], [R_VAm[0], R_VAm[1]])
            LN8 = math.log(0.125)
            mgen = gen_M(l + 1, modTn, A1n, A2n, R_modn) if l + 1 < DEPTH else iter(())
            SB_ = [2, 6]
            PB_ = [4, 7]

            def run_ml(seq, tok0, NTOK, L):
                NC_ = NTOK // L
                sel = sel127 if L == 128 else sel31
                S.dma("sp", gi_r[0:4, 0:NTOK], GI[:, tok0:tok0 + NTOK], writes=[R_rows], par=True)
                S.dma("sp", gf_r[0:4, 0:NTOK], GF[:, tok0:tok0 + NTOK], writes=[R_rows], par=True)
                if seq == 0:
                    init = 0.0
                    dve(lambda e: e.memset(MMp[:, 0, :], 0.0), [], [R_ex])
                    dve(lambda e: e.memset(Cn, 0.0), [], [R_Cn])
                    rdi = []
                else:
                    S.dma("sp", m0c[0:4, :], I["s_mcol"][l, seq - 1], writes=[R_rows])
                    S.dma("sp", MMp[:, 0, :], I["s_mb"][l, seq - 1], writes=[R_ex])
                    init = m0c[0:4, 0:1]
                    for h in range(4):
                        r0 = (h % 2) * 64
                        S.dma("sp", Cn[r0:r0 + 64, h // 2, 0:64], I["s_c"][l, seq - 1, h], writes=[R_Cn], par=True)
                        S.dma("sp", Cn[r0:r0 + 64, h // 2, 64:65], I["s_n"][l, seq - 1, h].rearrange("(k o) -> k o", o=1), writes=[R_Cn], par=True)
                dve(lambda e: e.tensor_copy(out=Cnb2[0], in_=Cn), [R_Cn], [R_Cnb2[0]])
                dve(lambda e: e.tensor_tensor_scan(out=mt_r[0:4, 0:NTOK], data0=ones_r[0:4, 0:NTOK], data1=gf_r[0:4, 0:NTOK],
                                                   initial=0.0, op0=ALU.mult, op1=ALU.add), [R_rows, R_const], [R_rows])
                dve(lambda e: e.tensor_tensor(out=G_r[0:4, 0:NTOK], in0=gi_r[0:4, 0:NTOK], in1=mt_r[0:4, 0:NTOK], op=ALU.subtract), [R_rows], [R_rows])
                dve(lambda e: e.tensor_tensor_scan(out=MM_r[0:4, 0:NTOK], data0=ones_r[0:4, 0:NTOK], data1=G_r[0:4, 0:NTOK],
                                                   initial=init, op0=ALU.mult, op1=ALU.max), [R_rows, R_const], [R_rows])
                dve(lambda e: e.tensor_tensor(out=mt_r[0:4, 0:NTOK], in0=mt_r[0:4, 0:NTOK], in1=MM_r[0:4, 0:NTOK], op=ALU.add), [R_rows], [R_rows])
                if debug == "ml0":
                    return
                S.dma("pool", O["o_mm"][l, seq].rearrange("(h o) -> h o", o=1), mt_r[0:4, NTOK - 1:NTOK], reads=[R_rows])
                if debug == "ml1":
                    return
                if L < 128:
                    dve(lambda e: e.memset(cols[:, 0:NC_, :], 0.0), [], [R_cols])
                for c in range(NC_):
                    def trc(e, c=c):
                        for i_, rr_ in enumerate([G_r, MM_r, mt_r]):
                            ins = e.matmul(out=PS[0][0:L, i_ * 4:(i_ + 1) * 4], lhsT=rr_[0:4, c * L:(c + 1) * L], rhs=ident_f[0:4, 0:4],
                                           start=True, stop=True, skip_group_check=True)
                        return ins
                    pe(trc, [R_rows, R_const], [RPS[0]])
                    act(lambda e, c=c: e.activation(out=cols[0:L, c, :], in_=PS[0][0:L, 0:12], func=AF.Copy), [RPS[0]], [R_cols])
                if debug == "ml2":
                    return
                n4 = NC_ * 4
                pe(lambda e: e.matmul(out=PS[1][:, 0:n4], lhsT=sel, rhs=cols[:, 0:NC_, 4:8], start=True, stop=True), [R_cols, R_const], [RPS[1]])
                dve(lambda e: e.tensor_copy(out=MMe[:, 0:NC_, :], in_=PS[1][:, 0:n4].rearrange("p (c h) -> p c h", h=4)), [RPS[1]], [R_ex])
                if NC_ > 1:
                    dve(lambda e: e.tensor_copy(out=MMp[:, 1:NC_, :], in_=MMe[:, 0:NC_ - 1, :]), [R_ex], [R_ex])
                Gc = cols[:, 0:NC_, 0:4]
                MMc = cols[:, 0:NC_, 4:8]
                mtc = cols[:, 0:NC_, 8:12]
                for (dst_, a_, b_, bias_) in [(egs, Gc, MMe, LN8), (eM, MMe, MMc, 0.0), (acol, MMp, MMc, 0.0), (dec, MMp, MMe, 0.0)]:
                    dve(lambda e, dst_=dst_, a_=a_, b_=b_: e.tensor_tensor(out=dst_[:, 0:NC_, :], in0=a_ if a_ is Gc else a_[:, 0:NC_, :],
                                                                           in1=b_ if b_ is MMc else b_[:, 0:NC_, :], op=ALU.subtract),
                        [R_cols, R_ex], [R_ex])
                    if bias_ != 0.0:
                        dve(lambda e, dst_=dst_, bias_=bias_: e.tensor_scalar(out=dst_[:, 0:NC_, :], in0=dst_[:, 0:NC_, :], scalar1=bias_, scalar2=None, op0=ALU.add),
                            [R_ex], [R_ex])
                    act(lambda e, dst_=dst_: e.activation(out=dst_[:, 0:NC_, :], in_=dst_[:, 0:NC_, :], func=AF.Exp), [R_ex], [R_ex])
                act(lambda e: e.activation(out=emt[:, 0:NC_, :], in_=mtc, func=AF.Exp, scale=-1.0), [R_cols], [R_ex])
                dve(lambda e: e.tensor_copy(out=dec2[0:64, 0:NC_, :], in_=dec[0:64, 0:NC_, 0:4:2]), [R_ex], [R_ex])
                dve(lambda e: e.tensor_copy(out=dec2[64:128, 0:NC_, :], in_=dec[64:128, 0:NC_, 1:4:2]), [R_ex], [R_ex])
                if debug == "ml3":
                    return
                for c in range(NC_):
                    ib = c % 2
                    t0 = tok0 + c * L
                    for hp in range(2):
                        S.dma("sp", qTm[ib][:, hp, 0:L], QS[3][hp * 128:(hp + 1) * 128, t0:t0 + L], writes=[R_in[ib]], par=True)
                        S.dma("sp", kTm[ib][:, hp, 0:L], KS[3][hp * 128:(hp + 1) * 128, t0:t0 + L], writes=[R_in[ib]], par=True)
                    S.dma("sp", vst[ib][0:L, :], VS[3][t0:t0 + L, :], writes=[R_in[ib]], par=True)
                    S.dma("sp", sig[ib][0:L, :], SIGO[t0:t0 + L, :], writes=[R_in[ib]], par=True)
                    act(lambda e, ib=ib: e.activation(out=VAm[ib][0:L, :, 0:64], in_=vst[ib][0:L, :].rearrange("p (h d) -> p h d", h=4), func=AF.Copy),
                        [R_in[ib]], [R_VAm[ib]])
                    if debug == "ml4":
                        continue
                    psb = PS[1][:, :].bitcast(BF16)

                    def trk2(e, ib=ib, psb=psb):
                        for hp in range(2):
                            ins = e.transpose(out=psb[0:L, hp * 128:(hp + 1) * 128], in_=kTm[ib][:, hp, 0:L], identity=ident_b)
                        return ins
                    pe(trk2, [R_in[ib], R_const], [RPS[1]])
                    dve(lambda e, c=c, psb=psb: e.tensor_tensor(out=kw[0:L, :].rearrange("p (h d) -> p h d", h=4),
                                                                in0=psb[0:L, 0:256].rearrange("p (h d) -> p h d", h=4),
                                                                in1=egs[0:L, c, :].unsqueeze(2).to_broadcast([L, 4, 64]), op=ALU.mult),
                        [RPS[1], R_ex], [R_kw])

                    if debug == "ml5":
                        continue

                    def qk(e, ib=ib):
                        for h in range(4):
                            r0 = (h % 2) * 64
                            ins = e.matmul(out=PS[SB_[h % 2]][0:L, h * L:(h + 1) * L], lhsT=kTm[ib][r0:r0 + 64, h // 2, 0:L], rhs=qTm[ib][r0:r0 + 64, h // 2, 0:L],
                                           start=True, stop=True, skip_group_check=True)
                        return ins
                    pe(qk, [R_in[ib]], [RPS[2], RPS[6]])

                    def p2(e, ib=ib, c=c):
                        for h in range(4):
                            r0 = (h % 2) * 64
                            ins = e.matmul(out=PS[PB_[h % 2]][0:L, h * 65:(h + 1) * 65], lhsT=qTm[ib][r0:r0 + 64, h // 2, 0:L], rhs=Cnb2[c % 2][r0:r0 + 64, h // 2, :],
                                           start=True, stop=True, skip_group_check=True)
                        return ins
                    pe(p2, [R_in[ib], R_Cnb2[c % 2]], [RPS[4], RPS[7]])
                    if debug == "ml6":
                        continue
                    for h in range(4):
                        wb_ = h % 2
                        dve(lambda e, h=h, c=c, wb_=wb_: e.scalar_tensor_tensor(out=wT[wb_][0:L, 0:L], in0=PS[SB_[h % 2]][0:L, h * L:(h + 1) * L],
                                                                                scalar=egs[0:L, c, h:h + 1], in1=tri_f[0:L, 0:L],
                                                                                op0=ALU.mult, op1=ALU.mult), [RPS[SB_[h % 2]], R_ex, R_const], [R_wT[wb_]])
                        pe(lambda e, h=h, wb_=wb_, ib=ib: e.matmul(out=PS[3][0:L, h * 65:(h + 1) * 65], lhsT=wT[wb_][0:L, 0:L], rhs=VAm[ib][0:L, h, :],
                                                                   start=True, stop=True, skip_group_check=True), [R_wT[wb_], R_VAm[ib]], [RPS[3]])
                        pe(lambda e, h=h, ib=ib: e.matmul(out=PS[5][(h % 2) * 64:(h % 2) * 64 + 64, (h // 2) * 65:(h // 2) * 65 + 65], lhsT=kw[0:L, h * 64:(h + 1) * 64],
                                                          rhs=VAm[ib][0:L, h, :], start=True, stop=True, skip_group_check=True,
                                                          tile_position=(0, (h % 2) * 64)), [R_kw, R_VAm[ib]], [RPS[5]])
                    cb_n = (c + 1) % 2
                    for hp_ in range(2):
                        dve(lambda e, hp_=hp_, c=c: e.scalar_tensor_tensor(out=Cn[:, hp_, :], in0=Cn[:, hp_, :], scalar=dec2[:, c, hp_:hp_ + 1],
                                                                         in1=PS[5][:, hp_ * 65:(hp_ + 1) * 65], op0=ALU.mult, op1=ALU.add),
                            [R_Cn, R_ex, RPS[5]], [R_Cn])
                    dve(lambda e, cb_n=cb_n: e.tensor_copy(out=Cnb2[cb_n], in_=Cn), [R_Cn], [R_Cnb2[cb_n]])
                    if seq == 0:
                        next(mgen, None)
                    if debug == "ml7":
                        continue
                    for h in range(4):
                        act(lambda e, h=h, c=c: e.activation(out=tmpA[0:L, h, :], in_=PS[PB_[h % 2]][0:L, h * 65:(h + 1) * 65], func=AF.Identity, scale=acol[0:L, c, h:h + 1]),
                            [RPS[PB_[h % 2]], R_ex], [R_tmp])
                    dve(lambda e, c=c: e.tensor_tensor(out=resm[0:L], in0=PS[3][0:L, 0:260].rearrange("p (h d) -> p h d", d=65),
                                                       in1=eM[0:L, c, :].unsqueeze(2).to_broadcast([L, 4, 65]), op=ALU.mult), [RPS[3], R_ex], [R_res])
                    dve(lambda e: e.tensor_tensor(out=resm[0:L], in0=resm[0:L], in1=tmpA[0:L], op=ALU.add), [R_res, R_tmp], [R_res])
                    act(lambda e: e.activation(out=den[0:L, :], in_=resm[0:L, :, 64], func=AF.Abs), [R_res], [R_res])
                    dve(lambda e, c=c: e.tensor_tensor(out=den[0:L, :], in0=den[0:L, :], in1=emt[0:L, c, :], op=ALU.max), [R_res, R_ex], [R_res])
                    dve(lambda e: e.reciprocal(out=den[0:L, :], in_=den[0:L, :]), [R_res], [R_res])
                    dve(lambda e: e.tensor_tensor(out=o4[0:L], in0=resm[0:L, :, 0:64], in1=den[0:L, :].unsqueeze(2).to_broadcast([L, 4, 64]), op=ALU.mult),
                        [R_res], [R_o4])
                    S.op("pool", lambda e, ib=ib: e.tensor_tensor(out=o4[0:L], in0=o4[0:L], in1=sig[ib][0:L, :].rearrange("p (h d) -> p h d", h=4), op=ALU.mult),
                         [R_o4, R_in[ib]], [R_o4])
                    S.op("pool", lambda e: e.tensor_tensor(out=osq[0:L], in0=o4[0:L], in1=o4[0:L], op=ALU.mult), [R_o4], [R_osq])
                    dve(lambda e: e.tensor_reduce(out=oms[0:L, :], in_=osq[0:L], axis=AX.X, op=ALU.add), [R_osq], [R_oms])
                    act(lambda e: e.activation(out=oms[0:L, :], in_=oms[0:L, :], func=AF.Ln, scale=1.0 / 64, bias=EPS), [R_oms], [R_oms])
                    act(lambda e: e.activation(out=oms[0:L, :], in_=oms[0:L, :], func=AF.Exp, scale=-0.5), [R_oms], [R_oms])
                    S.op("pool", lambda e: e.tensor_tensor(out=o4[0:L], in0=o4[0:L], in1=oms[0:L, :].unsqueeze(2).to_broadcast([L, 4, 64]), op=ALU.mult),
                         [R_o4, R_oms, R_osq], [R_o4])
                    ob = c % 2
                    S.op("pool", lambda e, ob=ob: e.tensor_tensor(out=o4b[ob][0:L, :].rearrange("p (h d) -> p h d", h=4), in0=o4[0:L],
                                                                 in1=gmh[0:L, :].unsqueeze(1).to_broadcast([L, 4, 64]), op=ALU.mult), [R_o4, R_g], [R_o4b[ob]])
                    S.dma("pool", MIX[t0:t0 + L, 512:768], o4b[ob][0:L, :], reads=[R_o4b[ob]], writes=[R_MIX], par=True)
                for h in range(4):
                    r0 = (h % 2) * 64
                    S.dma("pool", O["o_mc"][l, seq, h], Cn[r0:r0 + 64, h // 2, :], reads=[R_Cn])

            run_ml(0, 0, T, 128)
            for _ in mgen:
                pass
            for s_ in range(4):
                if debug == "mlp":
                    break
                run_ml(1 + s_, T + 32 * s_, 32, 32)
            S.barrier()
            A.pop()

        def phase_B(l, lam_init):
            A.push()
            emb = A.take([4, 5, 128], F32)
            R_emb = Res()
            for h_ in range(4):
                for t_, dst_ in [(0, 4), (1, 1), (2, 2)]:
                    S.dma("sp", emb[:, h_, dst_, :], I["bandT"][l, h_, t_], writes=[R_emb])
            act(lambda e: e.activation(out=emb[:, :, 1:3, :], in_=emb[:, :, 1:3, :], func=AF.Exp), [R_emb], [R_emb])
            act(lambda e: e.activation(out=emb[:, :, 4, :], in_=emb[:, :, 4, :], func=AF.Exp), [R_emb], [R_emb])
            for h_ in range(4):
                dve(lambda e, h_=h_: e.tensor_tensor(out=emb[:, h_, 0, :], in0=emb[:, h_, 4, :], in1=cmask_f, op=ALU.mult), [R_emb, R_const], [R_emb])
                dve(lambda e, h_=h_: e.tensor_tensor(out=emb[:, h_, 3, :], in0=emb[:, h_, 2, :], in1=m4_f, op=ALU.mult), [R_emb, R_const], [R_emb])
            lamt = A.take([4, 32], F32)
            lamp = A.take([2, 32], F32)
            lam2 = A.take([2], F32)
            nlam = A.take([1], F32)
            gsub = A.take([64], F32)
            R_lam = Res()
            S.dma("sp", lamt, I["lamb"][l], writes=[R_lam])
            S.dma("sp", gsub, I["gsub"][l], writes=[R_lam])
            dve(lambda e: e.tensor_tensor(out=lamp[:, 0, :], in0=lamt[:, 0, :], in1=lamt[:, 1, :], op=ALU.mult), [R_lam], [R_lam])
            dve(lambda e: e.tensor_tensor(out=lamp[:, 1, :], in0=lamt[:, 2, :], in1=lamt[:, 3, :], op=ALU.mult), [R_lam], [R_lam])
            dve(lambda e: e.tensor_reduce(out=lam2, in_=lamp, axis=AX.X, op=ALU.add), [R_lam], [R_lam])
            act(lambda e: e.activation(out=lam2, in_=lam2, func=AF.Exp), [R_lam], [R_lam])
            dve(lambda e: e.tensor_tensor(out=nlam, in0=lam2[:, 1:2], in1=lam2[:, 0:1], op=ALU.subtract), [R_lam], [R_lam])
            dve(lambda e: e.tensor_scalar(out=nlam, in0=nlam, scalar1=-lam_init, scalar2=None, op0=ALU.add), [R_lam], [R_lam])
            dve(lambda e: e.tensor_scalar(out=gsub, in0=gsub, scalar1=(1.0 - lam_init), scalar2=None, op0=ALU.mult), [R_lam], [R_lam])

            KT = A.take([2, T], BF16) if T >= P + 32 else A.take([2, P + 32], BF16)
            VST = A.take([max(NKB_P, NCB + 1), 256], BF16)
            VSTF = A.take([256], F32)
            VA = A.take([max(NKB_P, NCB + 1), 4, 65], BF16)
            QT = A.take([2, T], BF16)
            FB = A.take([max(NKB_P, 1), max(NKB_P, NCB + 1), 4], F32)
            lfc = A.take([max(NKB_P, NCB + 1), 4], F32)
            cum = A.take([max(NKB_P, NCB + 1), 4], F32)
            totb = A.take([4, max(NKB_P, NCB + 1)], F32)
            crefbc = A.take([max(NKB_P, NCB + 1), 4], F32)
            ktmA = A.take([max(NCB, 1), 256], F32)
            vstA = A.take([max(NCB, 1), 256], F32)
            R_ktmA = Res()
            R_vstA = Res()
            R_KT, R_VST, R_VSTF, R_VA, R_QT, R_FB, R_lfc, R_cum = [Res() for _ in range(8)]
            pt = [A.take([512], BF16) for _ in range(4)]
            R_pt = [Res() for _ in range(4)]
            pf = [A.take([128], F32) for _ in range(2)]
            R_pf = [Res() for _ in range(2)]
            og = [A.take([4, 256], BF16) for _ in range(2)]
            R_og = [Res(), Res()]
            rec = [A.take([4, 2], F32) for _ in range(2)]
            R_rec = [Res(), Res()]
            d0 = A.take([4, 64], F32)
            d1 = A.take([4, 64], F32)
            dsq = A.take([4, 64], F32)
            dms = A.take([4], F32)
            R_d = Res()
            dve(lambda e: e.memset(VA[:, :, :, 64:65], 1.0), [], [R_VA])
            cS = [0]
            cP = [0]
            cA = [0]
            cO = [0]
            ABANKS = [5, 6, 7, 1]
            TB = [0, 4]
            SBANKS = [2, 3, 7, 1]
            cR = [0]
            oT = [A.take([512], F32) for _ in range(2)]
            R_oT = [Res(), Res()]

            def run_seq(g, tok0, NQ, cache):
                sc = (1.0 / math.sqrt(32.0)) if g == 1 else 0.125
                nmap = 2 if g == 1 else 1
                if cache is None:
                    kbl = [(i * 128, 128) for i in range(NKB_P)]
                    nck = 0
                    qsubs = [(i * 128, 128) for i in range(NKB_P)]
                    qgroups = [list(range(4 * J, 4 * J + 4)) for J in range(NKB_P // 4)]
                else:
                    nck = (NLB if g == 2 else NCB)
                    kbl = [(i * 128, 128) for i in range(nck)] + [(nck * 128, 32)]
                    qsubs = [(0, 32)]
                    qgroups = [[0]]
                nkb = len(kbl)
                NK = kbl[-1][0] + kbl[-1][1]
                for hp in range(2):
                    S.dma("sp", QT[:, hp, 0:NQ], QS[g][hp * 128:(hp + 1) * 128, tok0:tok0 + NQ], writes=[R_QT], par=True)
                    S.dma("sp", KT[:, hp, nck * 128:nck * 128 + NQ], KS[g][hp * 128:(hp + 1) * 128, tok0:tok0 + NQ], writes=[R_KT], par=True)
                if cache is None:
                    S.dma("sp", VST[:, 0:nkb, :], VS[g][0:T, :].rearrange("(k p) c -> p k c", p=128), writes=[R_VST])
                    dve(lambda e: e.tensor_copy(out=VA[:, 0:nkb, :, 0:64], in_=VST[:, 0:nkb, :].rearrange("p k (h d) -> p k h d", h=4)),
                        [R_VST], [R_VA])
                else:
                    ck = [I["c_fk"], I["c_dk"], I["c_bk"]][g][l, cache]
                    cv = [I["c_fv"], I["c_dv"], I["c_bv"]][g][l, cache]
                    S.dma("sp", ktmA[:, 0:nck, :], ck.rearrange("(k p) c -> p k c", p=128), writes=[R_ktmA])
                    S.dma("sp", vstA[:, 0:nck, :], cv.rearrange("(k p) c -> p k c", p=128), writes=[R_vstA])
                    for kb in range(nck):
                        tb = kb % 2

                        def trk(e, tb=tb, kb=kb):
                            for hp in range(2):
                                ins = e.transpose(out=PS[tb][:, hp * 128:(hp + 1) * 128], in_=ktmA[:, kb, hp * 128:(hp + 1) * 128], identity=ident_f)
                            return ins
                        pe(trk, [R_ktmA, R_const], [RPS[tb]])
                        act(lambda e, kb=kb, tb=tb: e.activation(out=KT[:, :, kb * 128:(kb + 1) * 128],
                                                                 in_=PS[tb][:, 0:256].rearrange("p (a b) -> p a b", a=2), func=AF.Copy),
                            [RPS[tb]], [R_KT], par=True)
                    dve(lambda e: e.tensor_copy(out=VA[:, 0:nck, :, 0:64], in_=vstA[:, 0:nck, :].rearrange("p k (h d) -> p k h d", h=4)),
                        [R_vstA], [R_VA])
                    S.dma("sp", VST[0:32, 0, :], VS[g][tok0:tok0 + 32, :], writes=[R_VST])
                    act(lambda e: e.activation(out=VA[0:32, nck, :, 0:64], in_=VST[0:32, 0, :].rearrange("p (h d) -> p h d", h=4), func=AF.Copy),
                        [R_VST], [R_VA])
                if g == 0:
                    dve(lambda e: e.memset(lfc[:, 0:nkb, :], 0.0), [], [R_lfc])
                    if cache is None:
                        S.dma("sp", lfc[:, 0:nkb, :], LOGF[0:T, :].rearrange("(k p) h -> p k h", p=128), writes=[R_lfc])
                    else:
                        S.dma("sp", lfc[:, 0:nck, :], I["c_flf"][l, cache].rearrange("(k p) h -> p k h", p=128), writes=[R_lfc])
                        S.dma("sp", lfc[0:32, nck, :], LOGF[tok0:tok0 + 32, :], writes=[R_lfc])
                    n4 = nkb * 4
                    lf2 = lfc[:, 0:nkb, :].rearrange("p k h -> p (k h)")
                    pe(lambda e: e.matmul(out=PS[0][:, 0:n4], lhsT=ones_f, rhs=lf2, start=True, stop=True), [R_lfc, R_const], [RPS[0]])
                    for h_ in range(4):
                        dve(lambda e, h_=h_: e.tensor_tensor_scan(out=totb[:, h_, 0:nkb], data0=ones_f[:, 0:nkb],
                                                                  data1=PS[0][:, 0:n4].rearrange("p (k h) -> p h k", h=4)[:, h_, :],
                                                                  initial=0.0, op0=ALU.mult, op1=ALU.add), [RPS[0], R_const], [R_cum])
                    pe(lambda e: e.matmul(out=PS[1][:, 0:n4], lhsT=tri_f, rhs=lf2, start=True, stop=True), [R_lfc, R_const], [RPS[1]])
                    dve(lambda e: e.tensor_tensor(out=cum[:, 0:nkb, :], in0=PS[1][:, 0:n4].rearrange("p (k h) -> p k h", h=4),
                                                  in1=totb[:, :, 0:nkb].rearrange("p h k -> p k h"), op=ALU.add), [RPS[1], R_cum], [R_cum])
                    dve(lambda e: e.tensor_tensor(out=cum[:, 0:nkb, :], in0=cum[:, 0:nkb, :],
                                                  in1=PS[0][:, 0:n4].rearrange("p (k h) -> p k h", h=4), op=ALU.subtract), [RPS[0], R_cum], [R_cum])
                    cum2 = cum[:, 0:nkb, :].rearrange("p k h -> p (k h)")
                    pe(lambda e: e.matmul(out=PS[0][:, 0:n4], lhsT=sel0, rhs=cum2, start=True, stop=True), [R_cum, R_const], [RPS[0]])
                    dve(lambda e: e.tensor_copy(out=crefbc[:, 0:nkb, :], in_=PS[0][:, 0:n4].rearrange("p (k h) -> p k h", h=4)), [RPS[0]], [R_cum])
                    for qi_, (q0, nq) in enumerate(qsubs):
                        kq = (q0 // 128) if cache is None else nck
                        dve(lambda e, qi_=qi_, kq=kq: e.tensor_tensor(out=FB[:, qi_, 0:nkb, :],
                                                                      in0=crefbc[:, kq:kq + 1, :].to_broadcast([128, nkb, 4]),
                                                                      in1=cum[:, 0:nkb, :], op=ALU.subtract), [R_cum], [R_FB])

                def mode(kb, qs):
                    k0, nk = kbl[kb]
                    if cache is None:
                        off = qs - kb
                        if g == 0:
                            if off < 0:
                                return None
                            return ((lambda h: FB[0:nk, qs, kb, h:h + 1]), ((lambda h: tri_f) if off == 0 else None))
                        if g == 1:
                            if off < 0:
                                return None
                            if off == 0:
                                return (None, (lambda h: emd[:, h, :]))
                            return ((lambda h: alcol[0:nk, h, off:off + 1]), None)
                        if off < 0 or off > 4:
                            return None
                        ti = [0, 1, 2, 2, 3][off]
                        return (None, (lambda h: emb[:, h, ti, :]))
                    else:
                        new = (kb == nck)
                        if g == 0:
                            return ((lambda h: FB[0:nk, 0, kb, h:h + 1]), ((lambda h: tri_f[0:32, 0:32]) if new else None))
                        if g == 1:
                            if new:
                                return (None, (lambda h: emd[0:32, h, 0:32]))
                            off = nck - kb
                            return ((lambda h: alcol[0:nk, h, off:off + 1]), None)
                        if new:
                            return (None, (lambda h: emb[0:32, h, 4, 0:32]))
                        off = nck - kb
                        ti = 1 if off == 1 else 2
                        return (None, (lambda h: emb[:, h, ti, 0:32]))

                for qg in qgroups:
                    ob = cO[0] % 2
                    cO[0] += 1
                    nsub = len(qg)
                    nq = qsubs[qg[0]][1]
                    if g == 1:
                        lane_groups = [[(h_, 0), (h_, 1)] for h_ in range(4)]
                    else:
                        lane_groups = [[(0, 0), (1, 0)], [(2, 0), (3, 0)]]
                    for lanes in lane_groups:
                        lacc = [5, 6]
                        first = [True, True]
                        kneed = [kb for kb in range(nkb) if any(mode(kb, qs) is not None for qs in qg)]
                        units = []
                        for kb in kneed:
                            k0, nk = kbl[kb]
                            subs = [si for si, qs in enumerate(qg) if mode(kb, qs) is not None]
                            slo, shi = subs[0], subs[-1] + 1
                            qa = qsubs[qg[slo]][0]
                            qb_ = qsubs[qg[shi - 1]][0] + nq
                            for li, (h_, m_) in enumerate(lanes):
                                units.append(dict(kb=kb, k0=k0, nk=nk, slo=slo, shi=shi, qa=qa, qb_=qb_, m=m_, h=h_, li=li))

                        def emit_qk(u):
                            m = u["m"]
                            h = u["h"]
                            hp = h // 2
                            if g == 1:
                                r0 = (h % 2) * 64 + m * 32
                                r1 = r0 + 32
                            else:
                                r0 = (h % 2) * 64
                                r1 = r0 + 64
                            sb = SBANKS[cS[0] % 4]
                            cS[0] += 1
                            u["sb"] = sb
                            kw = {"tile_position": (r0, 0)} if r0 == 96 else {}
                            k0, nk, qa, qb_ = u["k0"], u["nk"], u["qa"], u["qb_"]
                            pe(lambda e, sb=sb, r0=r0, r1=r1, hp=hp, k0=k0, nk=nk, qa=qa, qb_=qb_, kw=kw: e.matmul(
                                out=PS[sb][0:nk, 0:qb_ - qa], lhsT=KT[r0:r1, hp, k0:k0 + nk], rhs=QT[r0:r1, hp, qa:qb_],
                                start=True, stop=True, **kw), [R_KT, R_QT], [RPS[sb]])

                        def emit_rest(u):
                            kb, k0, nk, slo, shi, m, sb = u["kb"], u["k0"], u["nk"], u["slo"], u["shi"], u["m"], u["sb"]
                            h = u["h"]
                            li = u["li"]
                            wide = (cache is None) and (g == 0 or g == 1)
                            if wide:
                                pi = cP[0] % 4
                                cP[0] += 1
                                width = (shi - slo) * nq
                                if g == 1 and h == 0:
                                    for half in range(2):
                                        lo_s = max(slo, 2 * half)
                                        hi_s = min(shi, 2 * half + 2)
                                        if hi_s <= lo_s:
                                            continue
                                        oo = (qg[0] + 2 * half - kb) + 3
                                        bcol = alw[0:nk, h, oo:oo + 1]
                                        c_lo = (lo_s - slo) * nq
                                        c_hi = (hi_s - slo) * nq
                                        act(lambda e, sb=sb, nk=nk, c_lo=c_lo, c_hi=c_hi, pi=pi, bcol=bcol: e.activation(
                                            out=pt[pi][0:nk, c_lo:c_hi], in_=PS[sb][0:nk, c_lo:c_hi], func=AF.Exp, scale=sc, bias=bcol),
                                            [RPS[sb], R_const], [R_pt[pi]], par=True)
                                else:
                                    if g == 0:
                                        bcol = FB[0:nk, qg[0], kb, h:h + 1]
                                        rdw = [RPS[sb], R_FB]
                                    else:
                                        bcol = alw[0:nk, h, (qg[0] - kb) + 3:(qg[0] - kb) + 4]
                                        rdw = [RPS[sb], R_const]
                                    act(lambda e, sb=sb, nk=nk, width=width, pi=pi, bcol=bcol: e.activation(
                                        out=pt[pi][0:nk, 0:width], in_=PS[sb][0:nk, 0:width], func=AF.Exp, scale=sc, bias=bcol),
                                        rdw, [R_pt[pi]])
                                if kb >= qg[0]:
                                    corr = tri_f if g == 0 else cdiff[:, h, :]
                                    dve(lambda e, pi=pi, nq=nq, corr=corr: e.tensor_tensor(
                                        out=pt[pi][:, 0:nq], in0=pt[pi][:, 0:nq], in1=corr, op=ALU.mult), [R_pt[pi], R_const], [R_pt[pi]])
                                ab = lacc[li]
                                st_ = first[li]
                                first[li] = False
                                pe(lambda e, ab=ab, slo=slo, nq=nq, nk=nk, pi=pi, kb=kb, h=h, st_=st_, width=width: e.matmul(
                                    out=PS[ab][0:65, slo * nq:slo * nq + width], lhsT=VA[0:nk, kb, h, :], rhs=pt[pi][0:nk, 0:width],
                                    start=st_, stop=(kb == kneed[-1])), [R_pt[pi], R_VA], [RPS[ab]])
                                return
                            for si in range(slo, shi):
                                qs = qg[si]
                                md = mode(kb, qs)
                                if md is None:
                                    continue
                                bfn, efn = md
                                c0 = (si - slo) * nq
                                pi = cP[0] % 4
                                cP[0] += 1
                                bias_kw = {"bias": bfn(h)} if bfn is not None else {}
                                rd = [RPS[sb]] + ([R_FB] if (bfn is not None and g == 0) else []) + [R_const]
                                if efn is None:
                                    act(lambda e, sb=sb, nk=nk, c0=c0, nq=nq, pi=pi, bias_kw=bias_kw: e.activation(
                                        out=pt[pi][0:nk, 0:nq], in_=PS[sb][0:nk, c0:c0 + nq], func=AF.Exp, scale=sc, **bias_kw),
                                        rd, [R_pt[pi]])
                                else:
                                    fi = pi % 2
                                    act(lambda e, sb=sb, nk=nk, c0=c0, nq=nq, fi=fi, bias_kw=bias_kw: e.activation(
                                        out=pf[fi][0:nk, 0:nq], in_=PS[sb][0:nk, c0:c0 + nq], func=AF.Exp, scale=sc, **bias_kw),
                                        rd, [R_pf[fi]])
                                    em = efn(h)
                                    dve(lambda e, nk=nk, nq=nq, fi=fi, pi=pi, em=em: e.tensor_tensor(
                                        out=pt[pi][0:nk, 0:nq], in0=pf[fi][0:nk, 0:nq], in1=em[0:nk, 0:nq] if nk < 128 else em, op=ALU.mult),
                                        [R_pf[fi], R_const, R_emb], [R_pt[pi]])
                                ab = lacc[li]
                                st_ = first[li]
                                first[li] = False
                                pe(lambda e, ab=ab, si=si, nq=nq, nk=nk, pi=pi, kb=kb, h=h, st_=st_: e.matmul(
                                    out=PS[ab][0:nq, si * 65:(si + 1) * 65], lhsT=pt[pi][0:nk, 0:nq], rhs=VA[0:nk, kb, h, :],
                                    start=st_, stop=True, skip_group_check=True), [R_pt[pi], R_VA], [RPS[ab]])

                        nl = len(lanes)
                        for j in range(0, len(units), nl):
                            if j == 0:
                                for u in units[0:nl]:
                                    emit_qk(u)
                            for u in units[j + nl:j + 2 * nl]:
                                emit_qk(u)
                            for u in units[j:j + nl]:
                                emit_rest(u)
                        if g == 1:
                            fin_list = [(lanes[0][0], [lacc[0], lacc[1]], [0, 1])]
                        else:
                            fin_list = [(lanes[li_][0], [lacc[li_]], [li_]) for li_ in range(len(lanes))]
                        for (h, accs, tix) in fin_list:
                            cR[0] += 1
                            rb = cR[0] % 2
                            wide_h = (cache is None) and (g == 0 or g == 1)
                            if wide_h:
                                for m in range(len(accs)):
                                    ti = tix[m]
                                    dve(lambda e, ti=ti, ab=accs[m]: e.tensor_copy(out=oT[ti][0:65, :], in_=PS[ab][0:65, 0:512]), [RPS[accs[m]]], [R_oT[ti]])

                                    def trO(e, ti=ti):
                                        for si in range(4):
                                            ins = e.transpose(out=PS[TB[ti]][:, si * 65:(si + 1) * 65], in_=oT[ti][0:65, si * 128:(si + 1) * 128], identity=ident_f[0:65, 0:65])
                                        return ins
                                    pe(trO, [R_oT[ti], R_const], [RPS[TB[ti]]])
                                accs = [TB[tix[m]] for m in range(len(accs))]
                            a0 = PS[accs[0]][0:nq, 0:nsub * 65].rearrange("p (s d) -> p s d", d=65)
                            dve(lambda e, rb=rb, a0=a0, nq=nq, nsub=nsub: e.reciprocal(out=rec[rb][0:nq, 0:nsub, 0:1], in_=a0[:, :, 64:65]),
                                [RPS[accs[0]]], [R_rec[rb]])
                            if g != 1:
                                dve(lambda e, rb=rb, a0=a0, nq=nq, nsub=nsub, ob=ob, h=h: e.tensor_tensor(
                                    out=og[ob][0:nq, 0:nsub, h * 64:(h + 1) * 64], in0=a0[:, :, 0:64],
                                    in1=rec[rb][0:nq, 0:nsub, 0:1].to_broadcast([nq, nsub, 64]), op=ALU.mult),
                                    [RPS[accs[0]], R_rec[rb]], [R_og[ob]])
                            else:
                                a1 = PS[accs[1]][0:nq, 0:nsub * 65].rearrange("p (s d) -> p s d", d=65)
                                dve(lambda e, rb=rb, a1=a1, nq=nq, nsub=nsub: e.reciprocal(out=rec[rb][0:nq, 0:nsub, 1:2], in_=a1[:, :, 64:65]),
                                    [RPS[accs[1]]], [R_rec[rb]])
                                dve(lambda e, rb=rb, nq=nq, nsub=nsub: e.tensor_scalar(out=rec[rb][0:nq, 0:nsub, 1:2], in0=rec[rb][0:nq, 0:nsub, 1:2],
                                                                                     scalar1=nlam[0:nq, 0:1], scalar2=None, op0=ALU.mult),
                                    [R_rec[rb], R_lam], [R_rec[rb]])
                                dve(lambda e, rb=rb, a0=a0, nq=nq, nsub=nsub: e.tensor_tensor(
                                    out=d0[0:nq, 0:nsub, :], in0=a0[:, :, 0:64], in1=rec[rb][0:nq, 0:nsub, 0:1].to_broadcast([nq, nsub, 64]), op=ALU.mult),
                                    [RPS[accs[0]], R_rec[rb]], [R_d])
                                dve(lambda e, rb=rb, a1=a1, nq=nq, nsub=nsub: e.tensor_tensor(
                                    out=d1[0:nq, 0:nsub, :], in0=a1[:, :, 0:64], in1=rec[rb][0:nq, 0:nsub, 1:2].to_broadcast([nq, nsub, 64]), op=ALU.mult),
                                    [RPS[accs[1]], R_rec[rb], R_d], [R_d])
                                dve(lambda e, nq=nq, nsub=nsub: e.tensor_tensor(out=d0[0:nq, 0:nsub, :], in0=d0[0:nq, 0:nsub, :], in1=d1[0:nq, 0:nsub, :], op=ALU.add),
                                    [R_d], [R_d])
                                dve(lambda e, nq=nq, nsub=nsub: e.tensor_tensor(out=dsq[0:nq, 0:nsub, :], in0=d0[0:nq, 0:nsub, :], in1=d0[0:nq, 0:nsub, :], op=ALU.mult),
                                    [R_d], [R_d])
                                dve(lambda e, nq=nq, nsub=nsub: e.tensor_reduce(out=dms[0:nq, 0:nsub], in_=dsq[0:nq, 0:nsub, :], axis=AX.X, op=ALU.add), [R_d], [R_d])
                                act(lambda e, nq=nq, nsub=nsub: e.activation(out=dms[0:nq, 0:nsub], in_=dms[0:nq, 0:nsub], func=AF.Ln, scale=1.0 / 64, bias=EPS), [R_d], [R_d])
                                act(lambda e, nq=nq, nsub=nsub: e.activation(out=dms[0:nq, 0:nsub], in_=dms[0:nq, 0:nsub], func=AF.Exp, scale=-0.5), [R_d], [R_d])
                                dve(lambda e, nq=nq, nsub=nsub: e.tensor_tensor(out=d0[0:nq, 0:nsub, :], in0=d0[0:nq, 0:nsub, :],
                                                                              in1=dms[0:nq, 0:nsub].unsqueeze(2).to_broadcast([nq, nsub, 64]), op=ALU.mult), [R_d], [R_d])
                                dve(lambda e, nq=nq, nsub=nsub, ob=ob, h=h: e.tensor_tensor(
                                    out=og[ob][0:nq, 0:nsub, h * 64:(h + 1) * 64], in0=d0[0:nq, 0:nsub, :],
                                    in1=gsub[0:nq, :].unsqueeze(1).to_broadcast([nq, nsub, 64]), op=ALU.mult), [R_d, R_lam], [R_og[ob]])
                    qa = tok0 + qsubs[qg[0]][0]
                    mc0 = [0, 256, 768][g]
                    if cache is None:
                        S.dma("pool", MIX[qa:qa + nsub * 128, mc0:mc0 + 256].rearrange("(s p) c -> p s c", p=128), og[ob][:, 0:nsub, :],
                              reads=[R_og[ob]], writes=[R_MIX], par=True)
                    else:
                        S.dma("pool", MIX[qa:qa + 32, mc0:mc0 + 256], og[ob][0:32, 0, :], reads=[R_og[ob]], writes=[R_MIX], par=True)

            for g in range(3):
                run_seq(g, 0, T, None)
                for s_ in range(4):
                    run_seq(g, T + 32 * s_, 32, s_)
            S.barrier()
            A.pop()
            phase_ML(l)

        for l in range(DEPTH):
            lam_init = 0.8 - 0.6 * math.exp(-0.3 * l)
            if l == 0:
                A.push()
                for _ in gen_M(0, modT, A1, A2, R_mod):
                    pass
                S.barrier()
                A.pop()
            else:
                dve(lambda e: e.tensor_copy(out=modT, in_=modTn), [R_modn], [R_mod])
                dve(lambda e: e.tensor_copy(out=A1, in_=A1n), [R_modn], [R_mod])
                dve(lambda e: e.tensor_copy(out=A2, in_=A2n), [R_modn], [R_mod])
                S.barrier()

            A.push()
            wfm = A.take([8, 2048], BF16)
            wg = A.take([8, 8], BF16)
            wtm = A.take([8, NTM], BF16)
            R_w = Res("w_in")
            w_in_l = I["w_in"][l].rearrange("(c p) n -> p c n", p=128)
            for j in range(16):
                S.dma("pool", wfm[:, :, j * 128:(j + 1) * 128], w_in_l[:, :, FM_COLS[j]:FM_COLS[j] + 128], writes=[R_w])
            S.dma("pool", wg[:, :, 0:4], w_in_l[:, :, C_MI:C_MI + 4], writes=[R_w])
            S.dma("pool", wg[:, :, 4:8], w_in_l[:, :, C_MF:C_MF + 4], writes=[R_w])
            o = 0
            for (c0, n) in TM_SRC:
                S.dma("pool", wtm[:, :, o:o + n], w_in_l[:, :, c0:c0 + n], writes=[R_w])
                o += n
            binT = A.take([16], F32)
            bgate = A.take([2], F32)
            nbgate = A.take([2], F32)
            btm = A.take([NTM], F32)
            gqk = A.take([12], F32)
            convw = A.take([4, 4], F32)
            convb = A.take([4], F32)
            R_p = Res("params")
            S.dma("sp", binT, I["binT"][l], writes=[R_p])
            S.dma("sp", bgate[0:4, :], I["bgate"][l], writes=[R_p])
            S.dma("sp", btm, I["btm"][l], writes=[R_p])
            S.dma("sp", gqk, I["gqk"][l], writes=[R_p])
            S.dma("sp", convw, I["convw"][l], writes=[R_p])
            S.dma("sp", convb, I["convb"][l], writes=[R_p])
            dve(lambda e: e.tensor_scalar(out=nbgate[0:4, :], in0=bgate[0:4, :], scalar1=-1.0, scalar2=None, op0=ALU.mult),
                [R_p], [R_p])
            xT = [A.take([8, 512], F32) for _ in range(2)]
            R_xT = [Res(), Res()]
            xtm = [A.take([1024], F32) for _ in range(2)]
            R_xtm = [Res(), Res()]
            sqb = A.take([8, 512], BF16)
            R_sq = Res()
            rstd = A.take([512], F32)
            R_rstd = Res()
            t1 = A.take([8, 512], F32)
            R_t1 = Res()
            hT2 = [A.take([8, 512], BF16) for _ in range(2)]
            R_hT2 = [Res(), Res()]
            U = [A.take([4, 35], F32) for _ in range(4)]
            UP = [A.take([515], F32) for _ in range(4)]
            R_U = [Res() for _ in range(4)]
            u32 = [A.take([512], F32) for _ in range(2)]
            R_u32 = [Res(), Res()]
            sqk = [A.take([512], BF16) for _ in range(2)]
            R_sqk = [Res(), Res()]
            rr = [A.take([512], F32) for _ in range(2)]
            R_rr = [Res(), Res()]
            kn32 = [A.take([512], F32) for _ in range(2)]
            R_kn32 = [Res(), Res()]
            knb = [A.take([512], BF16) for _ in range(2)]
            R_knb = [Res(), Res()]
            acc = [A.take([512], F32) for _ in range(2)]
            R_acc = [Res(), Res()]
            v32 = [A.take([512], F32) for _ in range(2)]
            R_v32 = [Res(), Res()]
            vb = [A.take([512], BF16) for _ in range(2)]
            R_vb = [Res(), Res()]
            grow = A.take([2, 512], F32)
            R_grow = Res()
            t260 = A.take([260], F32)
            R_t260 = Res()
            sg = A.take([256], F32)
            R_sg = Res()
            lf4 = A.take([4], F32)
            R_lf4 = Res()
            for u_ in UP:
                dve(lambda e, u_=u_: e.memset(u_[:, 0:3], 0.0), [], [R_U[0], R_U[1], R_U[2], R_U[3]])
            cnt2 = [0]

            def norm_part(bi, tok0, ntok, segs):
                is_s = (bi == len(blocks) - 1)
                xb = bi % 2
                xTb = xT[xb]
                hTc = hT2[bi % 2]
                R_hTc = R_hT2[bi % 2]
                if l == 0:
                    src = I["xs"] if is_s else I["xp"][tok0:tok0 + ntok, :]
                    for i in range(ntok // 128):
                        tb = i % 2
                        S.dma("sp", xtm[tb], src[i * 128:(i + 1) * 128, :], writes=[R_xtm[tb]])
                        for half in range(2):
                            pb = half

                            def tr(e, tb=tb, half=half, pb=pb):
                                for c4 in range(4):
                                    c = half * 4 + c4
                                    ins = e.transpose(out=PS[pb][:, c4 * 128:(c4 + 1) * 128], in_=xtm[tb][:, c * 128:(c + 1) * 128],
                                                      identity=ident_f)
                                return ins
                            pe(tr, [R_xtm[tb], R_const], [RPS[pb]])
                            eng = act if half == 0 else dve
                            if half == 0:
                                act(lambda e, i=i, pb=pb, half=half, xTb=xTb: e.activation(
                                    out=xTb[:, half * 4:half * 4 + 4, i * 128:(i + 1) * 128],
                                    in_=PS[pb][:, :].rearrange("p (a b) -> p a b", a=4), func=AF.Copy), [RPS[pb]], [R_xT[xb]])
                            else:
                                dve(lambda e, i=i, pb=pb, half=half, xTb=xTb: e.tensor_copy(
                                    out=xTb[:, half * 4:half * 4 + 4, i * 128:(i + 1) * 128],
                                    in_=PS[pb][:, :].rearrange("p (a b) -> p a b", a=4)), [RPS[pb]], [R_xT[xb]])
                    S.dma("pool", XT[0][:, tok0:tok0 + ntok].rearrange("(c p) t -> p c t", p=128), xTb[:, :, 0:ntok],
                          reads=[R_xT[xb]], writes=[R_XT])
                else:
                    S.dma("sp", xTb[:, :, 0:ntok], XT[1][:, tok0:tok0 + ntok].rearrange("(c p) t -> p c t", p=128),
                          writes=[R_xT[xb]])
                act(lambda e, xTb=xTb, ntok=ntok: e.activation(out=sqb[:, :, 0:ntok], in_=xTb[:, :, 0:ntok], func=AF.Square),
                    [R_xT[xb]], [R_sq])

                def ssmm(e, ntok=ntok):
                    for c in range(8):
                        ins = e.matmul(out=PS[2][:, 0:ntok], lhsT=ones_b, rhs=sqb[:, c, 0:ntok], start=(c == 0), stop=(c == 7))
                    return ins
                pe(ssmm, [R_sq, R_const], [RPS[2]])
                act(lambda e, ntok=ntok: e.activation(out=rstd[:, 0:ntok], in_=PS[2][:, 0:ntok], func=AF.Ln, scale=1.0 / D, bias=EPS),
                    [RPS[2]], [R_rstd])
                act(lambda e, ntok=ntok: e.activation(out=rstd[:, 0:ntok], in_=rstd[:, 0:ntok], func=AF.Exp, scale=-0.5),
                    [R_rstd], [R_rstd])
                for c in range(8):
                    for (c0, ncol, sq_) in segs:
                        dve(lambda e, c=c, c0=c0, ncol=ncol, sq_=sq_, xTb=xTb: e.scalar_tensor_tensor(
                            out=t1[:, c, c0:c0 + ncol], in0=xTb[:, c, c0:c0 + ncol], scalar=A1[:, c, sq_:sq_ + 1],
                            in1=rstd[:, c0:c0 + ncol], op0=ALU.mult, op1=ALU.mult), [R_xT[xb], R_rstd, R_mod], [R_t1])
                        act(lambda e, c=c, c0=c0, ncol=ncol, sq_=sq_: e.activation(
                            out=hTc[:, c, c0:c0 + ncol], in_=t1[:, c, c0:c0 + ncol], func=AF.Identity,
                            bias=modT[:, c, sq_:sq_ + 1], scale=1.0), [R_t1, R_mod], [R_hTc])

            def proj_part(bi, tok0, ntok, segs):
                is_s = (bi == len(blocks) - 1)
                xb = bi % 2
                xTb = xT[xb]
                hTc = hT2[bi % 2]
                R_hTc = R_hT2[bi % 2]
                pend = []
                for j in range(16):
                    pb = 3 + (j % 2)

                    def fm(e, j=j, pb=pb, ntok=ntok):
                        for c in range(8):
                            ins = e.matmul(out=PS[pb][:, 0:ntok], lhsT=wfm[:, c, j * 128:(j + 1) * 128], rhs=hTc[:, c, 0:ntok],
                                           start=(c == 0), stop=(c == 7))
                        return ins
                    pe(fm, [R_w, R_hTc], [RPS[pb]])
                    if j < 12:
                        k2 = cnt2[0] % 2
                        cnt2[0] += 1
                        g = j // 4
                        isk = (j % 4) >= 2
                        hd = 32 if g == 1 else 64
                        bm = blk32 if g == 1 else blk64
                        act(lambda e, j=j, pb=pb, k2=k2, ntok=ntok: e.activation(out=u32[k2][:, 0:ntok], in_=PS[pb][:, 0:ntok],
                                                                                 func=AF.Identity, bias=binT[:, j:j + 1], scale=1.0),
                            [RPS[pb], R_p], [R_u32[k2]])
                        act(lambda e, k2=k2, ntok=ntok: e.activation(out=sqk[k2][:, 0:ntok], in_=u32[k2][:, 0:ntok], func=AF.Square),
                            [R_u32[k2]], [R_sqk[k2]])
                        def tail(j=j, k2=k2, g=g, isk=isk, hd=hd, bm=bm, ntok=ntok):
                            pe(lambda e, k2=k2, bm=bm, ntok=ntok: e.matmul(out=PS[5][:, 0:ntok], lhsT=bm, rhs=sqk[k2][:, 0:ntok],
                                                                           start=True, stop=True), [R_sqk[k2], R_const], [RPS[5]])
                            act(lambda e, k2=k2, hd=hd, ntok=ntok: e.activation(out=rr[k2][:, 0:ntok], in_=PS[5][:, 0:ntok], func=AF.Ln,
                                                                                scale=1.0 / hd, bias=EPS), [RPS[5]], [R_rr[k2]])
                            act(lambda e, k2=k2, ntok=ntok: e.activation(out=rr[k2][:, 0:ntok], in_=rr[k2][:, 0:ntok], func=AF.Exp, scale=-0.5),
                                [R_rr[k2]], [R_rr[k2]])
                            if isk:
                                dve(lambda e, j=j, k2=k2, ntok=ntok: e.scalar_tensor_tensor(
                                    out=kn32[k2][:, 0:ntok], in0=u32[k2][:, 0:ntok], scalar=gqk[:, j:j + 1], in1=rr[k2][:, 0:ntok],
                                    op0=ALU.mult, op1=ALU.mult), [R_u32[k2], R_rr[k2], R_p], [R_kn32[k2]])
                                okt = [O["o_fk"], O["o_dk"], O["o_bk"]][g]
                                ch = j % 2
                                S.dma("pool", okt[l][ch * 128:(ch + 1) * 128, tok0:tok0 + ntok], kn32[k2][:, 0:ntok], reads=[R_kn32[k2]])
                                dve(lambda e, k2=k2, ntok=ntok: e.tensor_copy(out=knb[k2][:, 0:ntok], in_=kn32[k2][:, 0:ntok]),
                                    [R_kn32[k2]], [R_knb[k2]])
                                S.dma("pool", KS[g][ch * 128:(ch + 1) * 128, tok0:tok0 + ntok], knb[k2][:, 0:ntok], reads=[R_knb[k2]])
                            else:
                                dve(lambda e, j=j, k2=k2, ntok=ntok: e.scalar_tensor_tensor(
                                    out=knb[k2][:, 0:ntok], in0=u32[k2][:, 0:ntok], scalar=gqk[:, j:j + 1], in1=rr[k2][:, 0:ntok],
                                    op0=ALU.mult, op1=ALU.mult), [R_u32[k2], R_rr[k2], R_p], [R_knb[k2]])
                                ch = j % 2
                                S.dma("pool", QS[g][ch * 128:(ch + 1) * 128, tok0:tok0 + ntok], knb[k2][:, 0:ntok], reads=[R_knb[k2]])
                        for t_ in pend:
                            t_()
                        pend = [tail]
                    else:
                        for t_ in pend:
                            t_()
                        pend = []
                        jj = j - 12
                        k2 = cnt2[0] % 2
                        cnt2[0] += 1
                        if not is_s:
                            Uj = UP[jj]
                            act(lambda e, j=j, pb=pb, Uj=Uj: e.activation(out=Uj[:, 3:515], in_=PS[pb][:, 0:512], func=AF.Identity,
                                                                          bias=binT[:, j:j + 1], scale=1.0), [RPS[pb], R_p], [R_U[jj]])
                            dve(lambda e, jj=jj, k2=k2, Uj=Uj: e.tensor_scalar(out=acc[k2][:, 0:512], in0=Uj[:, 0:512],
                                                                             scalar1=convw[:, jj, 0:1], scalar2=convb[:, jj:jj + 1],
                                                                             op0=ALU.mult, op1=ALU.add), [R_U[jj], R_p], [R_acc[k2]])
                            for tp in range(1, 4):
                                dve(lambda e, jj=jj, k2=k2, Uj=Uj, tp=tp: e.scalar_tensor_tensor(
                                    out=acc[k2][:, 0:512], in0=Uj[:, tp:tp + 512], scalar=convw[:, jj, tp:tp + 1], in1=acc[k2][:, 0:512],
                                    op0=ALU.mult, op1=ALU.add), [R_U[jj], R_p, R_acc[k2]], [R_acc[k2]])
                            if bi == NPB - 1:
                                S.dma("pool", O["o_conv"][l, jj, :, 0, :], Uj[:, 512:515], reads=[R_U[jj]])
                            else:
                                dve(lambda e, Uj=Uj: e.tensor_copy(out=Uj[:, 0:3], in_=Uj[:, 512:515]), [R_U[jj], R_acc[k2]], [R_U[jj]])
                        else:
                            Uj = U[jj]
                            S.dma("sp", Uj[:, :, 0:3], I["convst"][l, :, jj, :, :], writes=[R_U[jj]])
                            act(lambda e, j=j, pb=pb, Uj=Uj: e.activation(out=Uj[:, :, 3:35], in_=PS[pb][:, 0:128].rearrange("p (a b) -> p a b", a=4),
                                                                          func=AF.Identity, bias=binT[:, j:j + 1], scale=1.0),
                                [RPS[pb], R_p], [R_U[jj]])
                            a3 = acc[k2][:, 0:128].rearrange("p (a b) -> p a b", a=4)
                            dve(lambda e, jj=jj, Uj=Uj, a3=a3: e.tensor_scalar(out=a3, in0=Uj[:, :, 0:32], scalar1=convw[:, jj, 0:1],
                                                                             scalar2=convb[:, jj:jj + 1], op0=ALU.mult, op1=ALU.add),
                                [R_U[jj], R_p], [R_acc[k2]])
                            for tp in range(1, 4):
                                dve(lambda e, jj=jj, Uj=Uj, tp=tp, a3=a3: e.scalar_tensor_tensor(
                                    out=a3, in0=Uj[:, :, tp:tp + 32], scalar=convw[:, jj, tp:tp + 1], in1=a3, op0=ALU.mult, op1=ALU.add),
                                    [R_U[jj], R_p, R_acc[k2]], [R_acc[k2]])
                            S.dma("pool", O["o_conv"][l, jj, :, 1:5, :], Uj[:, :, 32:35], reads=[R_U[jj]])
                        act(lambda e, k2=k2, ntok=ntok: e.activation(out=knb[k2][:, 0:ntok], in_=acc[k2][:, 0:ntok], func=AF.Silu),
                            [R_acc[k2]], [R_knb[k2]])
                        dst = QS[3] if jj < 2 else KS[3]
                        ch = jj % 2
                        S.dma("pool", dst[ch * 128:(ch + 1) * 128, tok0:tok0 + ntok], knb[k2][:, 0:ntok], reads=[R_knb[k2]])
                for gi_ in range(2):
                    pb = 3 + gi_

                    def gm(e, gi_=gi_, pb=pb, ntok=ntok):
                        for c in range(8):
                            ins = e.matmul(out=PS[pb][0:4, 0:ntok], lhsT=wg[:, c, gi_ * 4:gi_ * 4 + 4], rhs=hTc[:, c, 0:ntok],
                                           start=(c == 0), stop=(c == 7))
                        return ins
                    pe(gm, [R_w, R_hTc], [RPS[pb]])
                act(lambda e, ntok=ntok: e.activation(out=grow[0:4, 0, 0:ntok], in_=PS[3][0:4, 0:ntok], func=AF.Identity,
                                                      bias=bgate[0:4, 0:1], scale=1.0), [RPS[3], R_p], [R_grow])
                act(lambda e, ntok=ntok: e.activation(out=grow[0:4, 1, 0:ntok], in_=PS[4][0:4, 0:ntok], func=AF.Exp,
                                                      bias=nbgate[0:4, 1:2], scale=-1.0), [RPS[4], R_p], [R_grow])
                act(lambda e, ntok=ntok: e.activation(out=grow[0:4, 1, 0:ntok], in_=grow[0:4, 1, 0:ntok], func=AF.Ln, bias=1.0, scale=1.0),
                    [R_grow], [R_grow])
                dve(lambda e, ntok=ntok: e.tensor_scalar(out=grow[0:4, 1, 0:ntok], in0=grow[0:4, 1, 0:ntok], scalar1=-1.0, scalar2=None,
                                                         op0=ALU.mult), [R_grow], [R_grow])
                S.dma("pool", GI[:, tok0:tok0 + ntok], grow[0:4, 0, 0:ntok], reads=[R_grow])
                S.dma("pool", GF[:, tok0:tok0 + ntok], grow[0:4, 1, 0:ntok], reads=[R_grow])
                for i in range(ntok // 128):
                    ts_ = tok0 + i * 128
                    for gr in range(2):
                        pb = 6 + gr
                        k2 = gr

                        def tm(e, i=i, gr=gr, pb=pb):
                            for c in range(8):
                                ins = e.matmul(out=PS[pb][:, :], lhsT=hTc[:, c, i * 128:(i + 1) * 128], rhs=wtm[:, c, gr * 512:(gr + 1) * 512],
                                               start=(c == 0), stop=(c == 7))
                            return ins
                        pe(tm, [R_w, R_hTc], [RPS[pb]])
                        dve(lambda e, gr=gr, pb=pb, k2=k2: e.tensor_tensor(out=v32[k2], in0=PS[pb][:, :], in1=btm[:, gr * 512:(gr + 1) * 512],
                                                                          op=ALU.add), [RPS[pb], R_p], [R_v32[k2]])
                        act(lambda e, k2=k2: e.activation(out=vb[k2], in_=v32[k2], func=AF.Copy), [R_v32[k2]], [R_vb[k2]])
                        if gr == 0:
                            S.dma("pool", O["o_fv"][l, ts_:ts_ + 128, :], v32[k2][:, 0:256], reads=[R_v32[k2]])
                            S.dma("pool", O["o_dv"][l, ts_:ts_ + 128, :], v32[k2][:, 256:512], reads=[R_v32[k2]])
                            S.dma("pool", VS[0][ts_:ts_ + 128, :], vb[k2][:, 0:256], reads=[R_vb[k2]])
                            S.dma("pool", VS[1][ts_:ts_ + 128, :], vb[k2][:, 256:512], reads=[R_vb[k2]])
                        else:
                            S.dma("pool", O["o_bv"][l, ts_:ts_ + 128, :], v32[k2][:, 0:256], reads=[R_v32[k2]])
                            S.dma("pool", VS[2][ts_:ts_ + 128, :], vb[k2][:, 0:256], reads=[R_vb[k2]])
                            S.dma("pool", VS[3][ts_:ts_ + 128, :], vb[k2][:, 256:512], reads=[R_vb[k2]])

                    def tm3(e, i=i):
                        for c in range(8):
                            ins = e.matmul(out=PS[6][:, 0:260], lhsT=hTc[:, c, i * 128:(i + 1) * 128], rhs=wtm[:, c, 1024:1284],
                                           start=(c == 0), stop=(c == 7))
                        return ins
                    pe(tm3, [R_w, R_hTc], [RPS[6]])
                    dve(lambda e: e.tensor_tensor(out=t260, in0=PS[6][:, 0:260], in1=btm[:, 1024:1284], op=ALU.add), [RPS[6], R_p], [R_t260])
                    act(lambda e: e.activation(out=sg, in_=t260[:, 0:256], func=AF.Sigmoid), [R_t260], [R_sg])
                    S.dma("pool", SIGO[ts_:ts_ + 128, :], sg, reads=[R_sg])
                    act(lambda e: e.activation(out=lf4, in_=t260[:, 256:260], func=AF.Exp, scale=-1.0), [R_t260], [R_lf4])
                    act(lambda e: e.activation(out=lf4, in_=lf4, func=AF.Ln, bias=1.0, scale=1.0), [R_lf4], [R_lf4])
                    dve(lambda e: e.tensor_scalar(out=lf4, in0=lf4, scalar1=-1.0, scalar2=None, op0=ALU.mult), [R_lf4], [R_lf4])
                    S.dma("pool", O["o_flf"][l, ts_:ts_ + 128, :], lf4, reads=[R_lf4])
                    S.dma("pool", LOGF[ts_:ts_ + 128, :], lf4, reads=[R_lf4])

            norm_part(0, *blocks[0])
            for bi_ in range(len(blocks)):
                if bi_ + 1 < len(blocks):
                    norm_part(bi_ + 1, *blocks[bi_ + 1])
                proj_part(bi_, *blocks[bi_])
            S.barrier()
            A.pop()
            if debug == "A":
                break
            if debug != "C":
                phase_B(l, lam_init)
            A.push()
            wo = A.take([8, 1024], BF16)
            R_wo = Res()
            S.dma("pool", wo, I["w_out"][l].rearrange("(c p) n -> p c n", p=128), writes=[R_wo])
            mt = [A.take([1024], BF16) for _ in range(2)]
            R_mt = [Res(), Res()]
            mixT = A.take([8, 512], BF16)
            R_mixT = Res()
            xc = [A.take([8, 512], F32) for _ in range(2)]
            R_xc = [Res(), Res()]
            R_XM = Res()
            for bi, (tok0, ntok, segs) in enumerate(blocks):
                xb = bi % 2
                S.dma("sp", xc[xb][:, :, 0:ntok], XT[l][:, tok0:tok0 + ntok].rearrange("(c p) t -> p c t", p=128),
                      reads=[R_XT], writes=[R_xc[xb]])
                for i in range(ntok // 128):
                    tb = i % 2
                    S.dma("sp", mt[tb], MIX[tok0 + i * 128:tok0 + (i + 1) * 128, :], reads=[R_MIX], writes=[R_mt[tb]])
                    pb = tb
                    psb = PS[pb][:, :].bitcast(BF16)

                    def trm(e, tb=tb, psb=psb):
                        for c in range(8):
                            ins = e.transpose(out=psb[:, c * 128:(c + 1) * 128], in_=mt[tb][:, c * 128:(c + 1) * 128], identity=ident_b)
                        return ins
                    pe(trm, [R_mt[tb], R_const], [RPS[pb]])
                    act(lambda e, i=i, psb=psb: e.activation(out=mixT[:, :, i * 128:(i + 1) * 128],
                                                             in_=psb.rearrange("p (a b) -> p a b", a=8), func=AF.Copy),
                        [RPS[pb]], [R_mixT])
                for dc in range(8):
                    pb = 2 + dc % 2

                    def wom(e, dc=dc, pb=pb, ntok=ntok):
                        for c in range(8):
                            ins = e.matmul(out=PS[pb][:, 0:ntok], lhsT=wo[:, c, dc * 128:(dc + 1) * 128], rhs=mixT[:, c, 0:ntok],
                                           start=(c == 0), stop=(c == 7))
                        return ins
                    pe(wom, [R_wo, R_mixT], [RPS[pb]])
                    for (c0, ncol, sq_) in segs:
                        dve(lambda e, dc=dc, pb=pb, c0=c0, ncol=ncol, sq_=sq_, xb=xb: e.scalar_tensor_tensor(
                            out=xc[xb][:, dc, c0:c0 + ncol], in0=PS[pb][:, c0:c0 + ncol], scalar=modT[:, 16 + dc, sq_:sq_ + 1],
                            in1=xc[xb][:, dc, c0:c0 + ncol], op0=ALU.mult, op1=ALU.add), [RPS[pb], R_mod, R_xc[xb]], [R_xc[xb]])
                S.dma("sp", XM[:, tok0:tok0 + ntok].rearrange("(c p) t -> p c t", p=128), xc[xb][:, :, 0:ntok],
                      reads=[R_xc[xb]], writes=[R_XM])
            S.barrier()
            A.pop()
            A.push()
            wgt = A.take([8, DFF], BF16)
            wup = A.take([8, DFF], BF16)
            wdn = A.take([22, 1024], BF16)
            R_wf = Res()
            for hh in range(2):
                S.dma("pool", wgt[:, :, hh * 1408:(hh + 1) * 1408], I["w_gate"][l][:, hh * 1408:(hh + 1) * 1408].rearrange("(c p) n -> p c n", p=128), writes=[R_wf])
                S.dma("pool", wup[:, :, hh * 1408:(hh + 1) * 1408], I["w_up"][l][:, hh * 1408:(hh + 1) * 1408].rearrange("(c p) n -> p c n", p=128), writes=[R_wf])
                S.dma("pool", wdn[:, hh * 11:(hh + 1) * 11, :], I["w_down"][l][hh * 1408:(hh + 1) * 1408, :].rearrange("(c p) n -> p c n", p=128), writes=[R_wf])
            xf = A.take([8, 512], F32)
            R_xf = Res()
            rs2 = A.take([512], F32)
            R_rs2 = Res()
            tt2 = A.take([512], F32)
            R_tt2 = Res()
            h2 = A.take([8, 512], BF16)
            R_h2 = Res()
            actT = A.take([22, 512], BF16)
            R_actT = Res()
            sq2 = actT[:, 0:8, :]
            R_sq2 = R_actT
            slu = [A.take([512], F32) for _ in range(2)]
            R_slu = [Res(), Res()]
            xo = [A.take([512], F32) for _ in range(2)]
            R_xo = [Res(), Res()]
            R_XO = Res()
            dst_all = XT[1] if l == 0 else O["YT"]
            for bi, (tok0, ntok, segs) in enumerate(blocks):
                S.dma("sp", xf[:, :, 0:ntok], XM[:, tok0:tok0 + ntok].rearrange("(c p) t -> p c t", p=128), reads=[R_XM], writes=[R_xf])
                act(lambda e, ntok=ntok: e.activation(out=sq2[:, :, 0:ntok], in_=xf[:, :, 0:ntok], func=AF.Square), [R_xf], [R_sq2])

                def ss2(e, ntok=ntok):
                    for c in range(8):
                        ins = e.matmul(out=PS[0][:, 0:ntok], lhsT=ones_b, rhs=sq2[:, c, 0:ntok], start=(c == 0), stop=(c == 7))
                    return ins
                pe(ss2, [R_sq2, R_const], [RPS[0]])
                act(lambda e, ntok=ntok: e.activation(out=rs2[:, 0:ntok], in_=PS[0][:, 0:ntok], func=AF.Ln, scale=1.0 / D, bias=EPS), [RPS[0]], [R_rs2])
                act(lambda e, ntok=ntok: e.activation(out=rs2[:, 0:ntok], in_=rs2[:, 0:ntok], func=AF.Exp, scale=-0.5), [R_rs2], [R_rs2])
                for c in range(8):
                    for (c0, ncol, sq_) in segs:
                        dve(lambda e, c=c, c0=c0, ncol=ncol, sq_=sq_: e.scalar_tensor_tensor(
                            out=tt2[:, c0:c0 + ncol], in0=xf[:, c, c0:c0 + ncol], scalar=A2[:, c, sq_:sq_ + 1], in1=rs2[:, c0:c0 + ncol],
                            op0=ALU.mult, op1=ALU.mult), [R_xf, R_rs2, R_mod], [R_tt2])
                        act(lambda e, c=c, c0=c0, ncol=ncol, sq_=sq_: e.activation(
                            out=h2[:, c, c0:c0 + ncol], in_=tt2[:, c0:c0 + ncol], func=AF.Identity, bias=modT[:, 24 + c, sq_:sq_ + 1], scale=1.0),
                            [R_tt2, R_mod], [R_h2])
                for f in range(22):
                    pa = 1 + 2 * (f % 2)
                    pbk = pa + 1

                    def gu(e, f=f, pa=pa, pbk=pbk, ntok=ntok):
                        for c in range(8):
                            e.matmul(out=PS[pa][:, 0:ntok], lhsT=wgt[:, c, f * 128:(f + 1) * 128], rhs=h2[:, c, 0:ntok], start=(c == 0), stop=(c == 7))
                        for c in range(8):
                            ins = e.matmul(out=PS[pbk][:, 0:ntok], lhsT=wup[:, c, f * 128:(f + 1) * 128], rhs=h2[:, c, 0:ntok], start=(c == 0), stop=(c == 7))
                        return ins
                    pe(gu, [R_wf, R_h2], [RPS[pa], RPS[pbk]])
                    k2 = f % 2
                    act(lambda e, pa=pa, k2=k2, ntok=ntok: e.activation(out=slu[k2][:, 0:ntok], in_=PS[pa][:, 0:ntok], func=AF.Silu), [RPS[pa]], [R_slu[k2]])
                    dve(lambda e, f=f, pbk=pbk, k2=k2, ntok=ntok: e.tensor_tensor(out=actT[:, f, 0:ntok], in0=PS[pbk][:, 0:ntok], in1=slu[k2][:, 0:ntok], op=ALU.mult),
                        [RPS[pbk], R_slu[k2]], [R_actT])
                for dc in range(8):
                    pb = 5 + dc % 2
                    k2 = dc % 2

                    def dn(e, dc=dc, pb=pb, ntok=ntok):
                        for f in range(22):
                            ins = e.matmul(out=PS[pb][:, 0:ntok], lhsT=wdn[:, f, dc * 128:(dc + 1) * 128], rhs=actT[:, f, 0:ntok], start=(f == 0), stop=(f == 21))
                        return ins
                    pe(dn, [R_wf, R_actT], [RPS[pb]])
                    for (c0, ncol, sq_) in segs:
                        dve(lambda e, dc=dc, pb=pb, c0=c0, ncol=ncol, sq_=sq_, k2=k2: e.scalar_tensor_tensor(
                            out=xo[k2][:, c0:c0 + ncol], in0=PS[pb][:, c0:c0 + ncol], scalar=modT[:, 40 + dc, sq_:sq_ + 1],
                            in1=xf[:, dc, c0:c0 + ncol], op0=ALU.mult, op1=ALU.add), [RPS[pb], R_mod, R_xf], [R_xo[k2]])
                    S.dma("sp", dst_all[dc * 128:(dc + 1) * 128, tok0:tok0 + ntok], xo[k2][:, 0:ntok], reads=[R_xo[k2]], writes=[R_XO])
            S.barrier()
            A.pop()

        S.barrier()
        S.emit()
    return nc


def host_consts():
    c = {}
    c["ident"] = np.eye(128, dtype=np.float32)
    b64 = np.zeros((128, 128), np.float32)
    b64[:64, :64] = 1
    b64[64:, 64:] = 1
    c["blk64"] = b64
    b32 = np.zeros((128, 128), np.float32)
    for i in range(4):
        b32[i * 32:(i + 1) * 32, i * 32:(i + 1) * 32] = 1
    c["blk32"] = b32
    k = np.arange(128)[:, None]
    q = np.arange(128)[None, :]
    c["tri"] = (k <= q).astype(np.float32)
    c["cmask"] = ((k // 64) <= (q // 64)).astype(np.float32)
    c["m4"] = (~((k < 64) & (q >= 64))).astype(np.float32)
    sl = np.array(alibi_slopes(), np.float64)
    o = np.arange(33)[None, None, :]
    c["alcol"] = (sl[None, :, None] * (np.arange(128)[:, None, None] - 128.0 * o)).astype(np.float32)
    s0_ = np.zeros((128, 128), np.float32)
    s0_[0, :] = 1
    c["sel0"] = s0_
    s1_ = np.zeros((128, 128), np.float32)
    s1_[127, :] = 1
    c["sel127"] = s1_
    s2_ = np.zeros((128, 128), np.float32)
    s2_[31, :] = 1
    c["sel31"] = s2_
    ow = np.arange(36)[None, None, :] - 3
    c["alw"] = (sl[None, :, None] * (np.arange(128)[:, None, None] - 128.0 * ow)).astype(np.float32)
    c["cdiff"] = (np.where(k <= q, 1.0, np.exp(-2.0 * sl[:, None, None] * (k - q)[None])) * c["cmask"][None]).astype(np.float32)
    c["emd_arg"] = (-sl[:, None, None] * np.abs(q - k)[None] + sl[:, None, None] * q[None]).astype(np.float32)
    return c


def prep_core(inp, core, T, P, LB):
    f = np.float32
    b = core // 2
    s0 = 4 * core
    m = {}
    m["xp"] = np.ascontiguousarray(inp["x_prompt"][b, :T])
    m["xs"] = np.ascontiguousarray(inp["x_sample"][s0:s0 + 4].reshape(128, D))
    cv = np.concatenate([inp["c_prompt"][b:b + 1], inp["c_sample"][s0:s0 + 4]], 0)
    m["cT"] = np.ascontiguousarray(cv.reshape(5, 8, 128).transpose(2, 1, 0))
    for k_, n_ in [("w_mod", "w_mod"), ("w_in", "w_in"), ("w_out", "w_out"), ("w_gate", "w_ffn_gate"), ("w_up", "w_ffn_up"),
                   ("w_down", "w_ffn_down")]:
        m[k_] = inp[n_]
    m["b_modT"] = np.ascontiguousarray(inp["b_mod"].reshape(DEPTH, 48, 128).transpose(0, 2, 1))
    m["g1T"] = np.ascontiguousarray(inp["norm1_g"].reshape(DEPTH, 8, 128).transpose(0, 2, 1))
    m["g2T"] = np.ascontiguousarray(inp["norm2_g"].reshape(DEPTH, 8, 128).transpose(0, 2, 1))
    b_in = inp["b_in"]
    m["binT"] = np.ascontiguousarray(np.stack([b_in[:, c0:c0 + 128] for c0 in FM_COLS], 1).transpose(0, 2, 1))
    m["bgate"] = np.ascontiguousarray(np.stack([b_in[:, C_MI:C_MI + 4], b_in[:, C_MF:C_MF + 4]], -1))
    btm = np.concatenate([b_in[:, c0:c0 + n] for (c0, n) in TM_SRC], 1)
    m["btm"] = np.ascontiguousarray(np.broadcast_to(btm[:, None, :], (DEPTH, 128, NTM)))
    gq = []
    for g_, name in enumerate(["qk_g_fox", "qk_g_diff", "qk_g_band"]):
        gg = inp[name]
        rep = 128 // gg.shape[-1]
        for qk in range(2):
            col = np.tile(gg[:, qk, :], (1, rep))
            gq += [col, col]
    m["gqk"] = np.ascontiguousarray(np.stack(gq, -1))
    m["convw"] = np.ascontiguousarray(inp["conv_w"].reshape(DEPTH, 4, 4, 128).transpose(0, 3, 2, 1))
    m["convb"] = np.ascontiguousarray(inp["conv_b"].reshape(DEPTH, 4, 128).transpose(0, 2, 1))
    stc = inp["state_conv"][:, s0:s0 + 4]
    m["convst"] = np.ascontiguousarray(stc.reshape(DEPTH, 4, 3, 4, 128).transpose(0, 4, 3, 1, 2))
    m["lamb"] = np.ascontiguousarray(np.broadcast_to(inp["diff_lambda"][:, None], (DEPTH, 128, 4, 32)))
    m["gsub"] = np.ascontiguousarray(np.broadcast_to(inp["diff_subln_g"][:, None], (DEPTH, 128, 64)))
    m["gmh"] = np.ascontiguousarray(np.broadcast_to(inp["mlstm_norm_g"][:, None], (DEPTH, 128, 64)))
    tab = inp["band_rel_bias"]
    k = np.arange(128)[:, None]
    q = np.arange(128)[None, :]
    i0 = np.clip(q - k, -128, 128) + 128
    i1 = np.clip(128 + q - k, -128, 128) + 128
    i2 = np.full((128, 128), 256)
    m["bandT"] = np.ascontiguousarray(np.stack([tab[:, :, i0], tab[:, :, i1], tab[:, :, i2]], 2))
    m["c_fk"] = np.ascontiguousarray(inp["cache_fox_k"][:, s0:s0 + 4].reshape(DEPTH, 4, P, 256))
    m["c_fv"] = np.ascontiguousarray(inp["cache_fox_v"][:, s0:s0 + 4].reshape(DEPTH, 4, P, 256))
    m["c_flf"] = np.ascontiguousarray(inp["cache_fox_logf"][:, s0:s0 + 4])
    m["c_dk"] = np.ascontiguousarray(inp["cache_diff_k"][:, s0:s0 + 4].reshape(DEPTH, 4, P, 256))
    m["c_dv"] = np.ascontiguousarray(inp["cache_diff_v"][:, s0:s0 + 4].reshape(DEPTH, 4, P, 256))
    m["c_bk"] = np.ascontiguousarray(inp["cache_band_k"][:, s0:s0 + 4].reshape(DEPTH, 4, LB, 256))
    m["c_bv"] = np.ascontiguousarray(inp["cache_band_v"][:, s0:s0 + 4].reshape(DEPTH, 4, LB, 256))
    m["s_c"] = np.ascontiguousarray(inp["state_mlstm_c"][:, s0:s0 + 4])
    m["s_n"] = np.ascontiguousarray(inp["state_mlstm_n"][:, s0:s0 + 4])
    m["s_m"] = np.ascontiguousarray(inp["state_mlstm_m"][:, s0:s0 + 4])
    m["s_mcol"] = m["s_m"][..., None]
    m["s_mb"] = np.broadcast_to(m["s_m"][:, :, None, :], (DEPTH, 4, 128, 4))
    m.update(host_consts())
    return {k_: np.ascontiguousarray(v, dtype=f) for k_, v in m.items()}


def assemble(results, T, P, LB, nb, ns):
    ncores = len(results)
    pc = [min(2 * b, ncores - 1) for b in range(nb)] if ncores >= 2 * nb else list(range(nb))
    keep = min(512, T)

    def P_(fn):
        return np.stack([fn(results[c]) for c in pc], 0)

    def S_(fn):
        return np.concatenate([fn(results[c]) for c in range(ncores)], 0)
    y_prompt = P_(lambda r: r["YT"][:, :T].T)
    y_sample = S_(lambda r: r["YT"][:, T:].T.reshape(4, 32, D))
    outs = [y_prompt, y_sample]

    def fm_p(name, lo=0):
        return np.stack([P_(lambda r: r[name][l][:, lo:T].T) for l in range(DEPTH)], 0)

    def tm_p(name, lo=0):
        return np.stack([P_(lambda r: r[name][l][lo:T]) for l in range(DEPTH)], 0)

    def fm_s(name):
        return np.stack([S_(lambda r: r[name][l][:, T:].T.reshape(4, 32, -1)) for l in range(DEPTH)], 0)

    def tm_s(name):
        return np.stack([S_(lambda r: r[name][l][T:].reshape(4, 32, -1)) for l in range(DEPTH)], 0)
    B = nb
    p_fox_k = fm_p("o_fk").reshape(DEPTH, B, T, 4, 64)
    p_fox_v = tm_p("o_fv").reshape(DEPTH, B, T, 4, 64)
    p_fox_logf = tm_p("o_flf")
    p_diff_k = fm_p("o_dk").reshape(DEPTH, B, T, 4, 2, 32)
    p_diff_v = tm_p("o_dv").reshape(DEPTH, B, T, 4, 64)
    p_band_k = fm_p("o_bk", T - keep).reshape(DEPTH, B, keep, 4, 64)
    p_band_v = tm_p("o_bv", T - keep).reshape(DEPTH, B, keep, 4, 64)
    p_mc = np.stack([P_(lambda r: r["o_mc"][l][0, :, :, 0:64]) for l in range(DEPTH)], 0)
    p_mn = np.stack([P_(lambda r: r["o_mc"][l][0, :, :, 64]) for l in range(DEPTH)], 0)
    p_mm = np.stack([P_(lambda r: r["o_mm"][l][0]) for l in range(DEPTH)], 0)
    p_conv = np.stack([P_(lambda r: r["o_conv"][l][:, :, 0, :].transpose(2, 0, 1).reshape(3, 512)) for l in range(DEPTH)], 0)
    NS = 4 * ncores
    s_fox_k = fm_s("o_fk").reshape(DEPTH, NS, 32, 4, 64)
    s_fox_v = tm_s("o_fv").reshape(DEPTH, NS, 32, 4, 64)
    s_fox_logf = tm_s("o_flf")
    s_diff_k = fm_s("o_dk").reshape(DEPTH, NS, 32, 4, 2, 32)
    s_diff_v = tm_s("o_dv").reshape(DEPTH, NS, 32, 4, 64)
    s_band_k = fm_s("o_bk").reshape(DEPTH, NS, 32, 4, 64)
    s_band_v = tm_s("o_bv").reshape(DEPTH, NS, 32, 4, 64)
    s_mc = np.stack([S_(lambda r: r["o_mc"][l][1:5, :, :, 0:64]) for l in range(DEPTH)], 0)
    s_mn = np.stack([S_(lambda r: r["o_mc"][l][1:5, :, :, 64]) for l in range(DEPTH)], 0)
    s_mm = np.stack([S_(lambda r: r["o_mm"][l][1:5]) for l in range(DEPTH)], 0)
    s_conv = np.stack([S_(lambda r: r["o_conv"][l][:, :, 1:5, :].transpose(2, 3, 0, 1).reshape(4, 3, 512)) for l in range(DEPTH)], 0)
    outs += [p_fox_k, p_fox_v, p_fox_logf, p_diff_k, p_diff_v, p_band_k, p_band_v, p_mc, p_mn, p_mm, p_conv,
             s_fox_k, s_fox_v, s_fox_logf, s_diff_k, s_diff_v, s_band_k, s_band_v, s_mc, s_mn, s_mm, s_conv]
    return tuple(np.ascontiguousarray(o, dtype=np.float32) for o in outs)


_NC_CACHE = {}


def kernel(**inputs):
    inp = {k: np.asarray(v) for k, v in inputs.items()}
    T = inp["x_prompt"].shape[1]
    P = inp["cache_fox_k"].shape[2]
    LB = inp["cache_band_k"].shape[2]
    key = (T, P, LB)
    if key not in _NC_CACHE:
        _NC_CACHE[key] = build_program(T, P, LB)
    nc = _NC_CACHE[key]
    in_maps = [prep_core(inp, c, T, P, LB) for c in range(8)]
    res = run_bass_kernel_spmd(nc, in_maps, core_ids=list(range(8)))
    return assemble(res.results, T, P, LB, nb=inp["x_prompt"].shape[0], ns=inp["x_sample"].shape[0])
```

```python
import math
import numpy as np
import concourse.bass as bass
import concourse.mybir as mybir
from concourse.bass_utils import run_bass_kernel_spmd
from contextlib import ExitStack

F32 = mybir.dt.float32
BF16 = mybir.dt.bfloat16
AF = mybir.ActivationFunctionType
ALU = mybir.AluOpType
AX = mybir.AxisListType

D = 1024
DFF = 2816
NIN = 3340
EPS = 1e-6
DEPTH = 2


class Res:
    __slots__ = ("name", "w", "r")

    def __init__(self, name=""):
        self.name = name
        self.w = {}
        self.r = {}


class Sched:
    NDS = 48
    NHW = 32

    def __init__(self, nc, stack):
        self.nc = nc
        self.names = ["pe", "act", "dve", "pool", "sp"]
        self.streams = {e: [] for e in self.names}
        self.sems = {e: stack.enter_context(nc.semaphore("s_" + e)) for e in ["pe", "act", "dve", "pool"]}
        self.cnt = {e: 0 for e in self.sems}
        self.dsems = [stack.enter_context(nc.semaphore("d%d" % i)) for i in range(self.NDS)]
        self.dval = [0] * self.NDS
        self.drr = 0
        self.drr_sw = 0
        self.seen = {e: {} for e in self.names}
        self.know = {}
        self.EKEYS = [("e", "pe"), ("e", "act"), ("e", "dve"), ("e", "pool")]
        self.nops = 0

    def _waits(self, eng, reads, writes, extra=(), par=False):
        waits = {}

        def add(m):
            if m is None:
                return
            k, v = m
            if waits.get(k, 0) < v:
                waits[k] = v
        for r in reads:
            for k, v in r.w.items():
                add((k, v))
        for r in writes:
            if not par:
                for k, v in r.w.items():
                    add((k, v))
            for k, v in r.r.items():
                add((k, v))
        for m in extra:
            add(m)
        need = []
        seen = self.seen[eng]
        for k, v in sorted(waits.items(), key=lambda kv: -kv[1]):
            if eng == "pe" and k == ("e", "pe"):
                continue
            if seen.get(k, 0) >= v:
                continue
            seen[k] = v
            need.append((k, v))
            kn = self.know.get((k, v))
            if kn is not None:
                for k2, v2 in zip(self.EKEYS, kn):
                    if seen.get(k2, 0) < v2:
                        seen[k2] = v2
        return need

    def _snap(self, eng, mark):
        seen = self.seen[eng]
        self.know[mark] = tuple(seen.get(k2, 0) for k2 in self.EKEYS)

    def _commit(self, mark, reads, writes, par=False):
        k, v = mark
        for r in reads:
            if r.r.get(k, 0) < v:
                r.r[k] = v
        for r in writes:
            if par and not r.r:
                if r.w.get(k, 0) < v:
                    r.w[k] = v
            else:
                r.w = {k: v}
            r.r = {}

    def op(self, eng, fn, reads=(), writes=(), par=False):
        need = self._waits(eng, reads, writes, par=par)
        self.cnt[eng] += 1
        mark = (("e", eng), self.cnt[eng])
        self._snap(eng, mark)
        self.streams[eng].append((need, fn, mark))
        self._commit(mark, reads, writes, par=par)
        self.nops += 1
        return mark

    def dma(self, q, out, in_, reads=(), writes=(), par=False, **kw):
        if q == "pool":
            j = self.NHW + self.drr_sw
            self.drr_sw = (self.drr_sw + 1) % (self.NDS - self.NHW)
        else:
            j = self.drr
            self.drr = (j + 1) % self.NHW
        extra = []
        if self.dval[j] > 0:
            extra.append((("d", j), self.dval[j]))
        need = self._waits(q, reads, writes, extra, par=par)
        self.dval[j] += 16
        mark = (("d", j), self.dval[j])
        self._snap(q, mark)
        self.streams[q].append((need, (lambda e: e.dma_start(out=out, in_=in_, **kw)), mark))
        self._commit(mark, reads, writes, par=par)
        self.nops += 1
        return mark

    def barrier(self):
        for eng in self.names:
            need = []
            for e2 in self.sems:
                if e2 == eng:
                    continue
                k = ("e", e2)
                v = self.cnt[e2]
                if v > 0 and self.seen[eng].get(k, 0) < v:
                    self.seen[eng][k] = v
                    need.append((k, v))
            for j in range(self.NDS):
                k = ("d", j)
                v = self.dval[j]
                if v > 0 and self.seen[eng].get(k, 0) < v:
                    self.seen[eng][k] = v
                    need.append((k, v))
            if need:
                self.streams[eng].append((need, None, None))

    def _h(self, k):
        return self.sems[k[1]] if k[0] == "e" else self.dsems[k[1]]

    def replay(self, name, e):
        for need, fn, mark in self.streams[name]:
            for k, v in need:
                e.wait_ge(self._h(k), v)
            if fn is None:
                continue
            ins = fn(e)
            if mark is not None:
                ins.then_inc(self._h(mark[0]), 16 if mark[0][0] == "d" else 1)

    def emit(self):
        nc = self.nc
        with nc.Block() as block:
            @block.tensor
            def _(e):
                self.replay("pe", e)

            @block.scalar
            def _(e):
                self.replay("act", e)

            @block.vector
            def _(e):
                self.replay("dve", e)

            @block.gpsimd
            def _(e):
                self.replay("pool", e)

            @block.sync
            def _(e):
                self.replay("sp", e)


class Arena:
    def __init__(self, ap, nwords):
        self.ap = ap
        self.n = nwords
        self.off = 0
        self.marks = []

    def take(self, shape, dtype):
        size = 2 if dtype == BF16 else 4
        n = 1
        for s in shape:
            n *= s
        words = (n * size + 3) // 4
        words = (words + 7) // 8 * 8
        assert self.off + words <= self.n, ("arena overflow", self.off, words, self.n)
        a = self.ap[:, self.off:self.off + words]
        self.off += words
        if dtype != F32:
            a = a.bitcast(dtype)
        a = a[:, 0:n]
        if len(shape) == 2:
            a = a.rearrange("p (a b) -> p a b", a=shape[0], b=shape[1])
        elif len(shape) == 3:
            a = a.rearrange("p (a b c) -> p a b c", a=shape[0], b=shape[1], c=shape[2])
        return a

    def push(self):
        self.marks.append(self.off)

    def pop(self):
        self.off = self.marks.pop()


def alibi_slopes():
    return [2.0 ** (-8.0 * (i + 1.0) / 4) for i in range(4)]


C_FQ, C_FK, C_FV, C_FF = 0, 256, 512, 768
C_DQ, C_DK, C_DV = 772, 1028, 1284
C_MQK, C_MV, C_MI, C_MF, C_MO = 1540, 2052, 2308, 2312, 2316
C_BQ, C_BK, C_BV = 2572, 2828, 3084
FM_COLS = [C_FQ, C_FQ + 128, C_FK, C_FK + 128, C_DQ, C_DQ + 128, C_DK, C_DK + 128,
           C_BQ, C_BQ + 128, C_BK, C_BK + 128, C_MQK, C_MQK + 128, C_MQK + 256, C_MQK + 384]
TM_SRC = [(C_FV, 256), (C_DV, 256), (C_BV, 256), (C_MV, 256), (C_MO, 256), (C_FF, 4)]
NTM = 1284


def build_program(T, P, LB, debug=False):
    TT = T + 128
    NPB = T // 512
    nc = bass.Bass("TRN2", target_bir_lowering=False)

    def din(name, shape, dt=F32):
        return nc.dram_tensor(name, list(shape), dt, kind="ExternalInput").ap()

    def dout(name, shape, dt=F32):
        return nc.dram_tensor(name, list(shape), dt, kind="ExternalOutput").ap()

    def dscr(name, shape, dt=F32):
        return nc.dram_tensor(name, list(shape), dt, kind="Internal").ap()

    I = {}
    I["xp"] = din("xp", [T, D])
    I["xs"] = din("xs", [128, D])
    I["cT"] = din("cT", [128, 8, 5])
    I["w_mod"] = din("w_mod", [DEPTH, D, 6 * D])
    I["b_modT"] = din("b_modT", [DEPTH, 128, 48])
    I["w_in"] = din("w_in", [DEPTH, D, NIN])
    I["w_out"] = din("w_out", [DEPTH, D, D])
    I["w_gate"] = din("w_gate", [DEPTH, D, DFF])
    I["w_up"] = din("w_up", [DEPTH, D, DFF])
    I["w_down"] = din("w_down", [DEPTH, DFF, D])
    I["g1T"] = din("g1T", [DEPTH, 128, 8])
    I["g2T"] = din("g2T", [DEPTH, 128, 8])
    I["binT"] = din("binT", [DEPTH, 128, 16])
    I["bgate"] = din("bgate", [DEPTH, 4, 2])
    I["btm"] = din("btm", [DEPTH, 128, NTM])
    I["gqk"] = din("gqk", [DEPTH, 128, 12])
    I["convw"] = din("convw", [DEPTH, 128, 4, 4])
    I["convb"] = din("convb", [DEPTH, 128, 4])
    I["convst"] = din("convst", [DEPTH, 128, 4, 4, 3])
    I["lamb"] = din("lamb", [DEPTH, 128, 4, 32])
    I["gsub"] = din("gsub", [DEPTH, 128, 64])
    I["gmh"] = din("gmh", [DEPTH, 128, 64])
    I["bandT"] = din("bandT", [DEPTH, 4, 3, 128, 128])
    I["c_fk"] = din("c_fk", [DEPTH, 4, P, 256])
    I["c_fv"] = din("c_fv", [DEPTH, 4, P, 256])
    I["c_flf"] = din("c_flf", [DEPTH, 4, P, 4])
    I["c_dk"] = din("c_dk", [DEPTH, 4, P, 256])
    I["c_dv"] = din("c_dv", [DEPTH, 4, P, 256])
    I["c_bk"] = din("c_bk", [DEPTH, 4, LB, 256])
    I["c_bv"] = din("c_bv", [DEPTH, 4, LB, 256])
    I["s_c"] = din("s_c", [DEPTH, 4, 4, 64, 64])
    I["s_n"] = din("s_n", [DEPTH, 4, 4, 64])
    I["s_m"] = din("s_m", [DEPTH, 4, 4])
    I["ident"] = din("ident", [128, 128])
    I["blk64"] = din("blk64", [128, 128])
    I["blk32"] = din("blk32", [128, 128])
    I["tri"] = din("tri", [128, 128])
    I["cmask"] = din("cmask", [128, 128])
    I["m4"] = din("m4", [128, 128])
    I["alcol"] = din("alcol", [128, 4, 33])
    I["emd_arg"] = din("emd_arg", [4, 128, 128])
    I["sel0"] = din("sel0", [128, 128])
    I["alw"] = din("alw", [128, 4, 36])
    I["cdiff"] = din("cdiff", [4, 128, 128])
    I["sel127"] = din("sel127", [128, 128])
    I["sel31"] = din("sel31", [128, 128])
    I["s_mcol"] = din("s_mcol", [DEPTH, 4, 4, 1])
    I["s_mb"] = din("s_mb", [DEPTH, 4, 128, 4])

    O = {}
    O["YT"] = dout("YT", [D, TT])
    O["o_fk"] = dout("o_fk", [DEPTH, 256, TT])
    O["o_fv"] = dout("o_fv", [DEPTH, TT, 256])
    O["o_flf"] = dout("o_flf", [DEPTH, TT, 4])
    O["o_dk"] = dout("o_dk", [DEPTH, 256, TT])
    O["o_dv"] = dout("o_dv", [DEPTH, TT, 256])
    O["o_bk"] = dout("o_bk", [DEPTH, 256, TT])
    O["o_bv"] = dout("o_bv", [DEPTH, TT, 256])
    O["o_mc"] = dout("o_mc", [DEPTH, 5, 4, 64, 65])
    O["o_mm"] = dout("o_mm", [DEPTH, 5, 4])
    O["o_conv"] = dout("o_conv", [DEPTH, 4, 128, 5, 3])
    if debug:
        O["dbg"] = dout("dbg", [128, 4096])

    XT = [dscr("XT0", [D, TT]), dscr("XT1", [D, TT])]
    XM = dscr("XM", [D, TT])
    QS = [dscr("QS%d" % g, [256, TT], BF16) for g in range(4)]
    KS = [dscr("KS%d" % g, [256, TT], BF16) for g in range(4)]
    VS = [dscr("VS%d" % g, [TT, 256], BF16) for g in range(4)]
    SIGO = dscr("SIGO", [TT, 256])
    GI = dscr("GI", [4, TT])
    GF = dscr("GF", [4, TT])
    LOGF = dscr("LOGF", [TT, 4])
    MIX = dscr("MIX", [TT, D], BF16)

    st = ExitStack()
    with st:
        S = Sched(nc, st)
        NW = 52000
        arena_t = st.enter_context(nc.sbuf_tensor("arena", [128, NW], F32))
        A = Arena(arena_t[:], NW)
        PS = [st.enter_context(nc.psum_tensor("ps%d" % i, [128, 512], F32)) for i in range(8)]
        RPS = [Res("ps%d" % i) for i in range(8)]

        ident_f = A.take([128], F32)
        ident_b = A.take([128], BF16)
        ones_b = A.take([128], BF16)
        blk64 = A.take([128], BF16)
        blk32 = A.take([128], BF16)
        tri_f = A.take([128], F32)
        cmask_f = A.take([128], F32)
        m4_f = A.take([128], F32)
        alcol = A.take([4, 33], F32)
        modT = A.take([48, 5], F32)
        A1 = A.take([8, 5], F32)
        A2 = A.take([8, 5], F32)
        R_const = Res("const")
        R_mod = Res("mod")
        tmpc = A.take([128], F32)
        S.dma("sp", ident_f, I["ident"], writes=[R_const])
        S.dma("sp", tri_f, I["tri"], writes=[R_const])
        S.dma("sp", cmask_f, I["cmask"], writes=[R_const])
        S.dma("sp", m4_f, I["m4"], writes=[R_const])
        S.dma("sp", alcol, I["alcol"], writes=[R_const])
        S.dma("pool", blk64, I["blk64"], writes=[R_const])
        S.dma("pool", blk32, I["blk32"], writes=[R_const])
        S.dma("pool", ident_b, I["ident"], writes=[R_const])
        S.op("dve", lambda e: e.memset(ones_b, 1.0), writes=[R_const])
        S.barrier()

        blocks = [(i * 512, 512, [(0, 512, 0)]) for i in range(NPB)]
        blocks.append((T, 128, [(32 * s, 32, 1 + s) for s in range(4)]))

        def act(fn, reads, writes, par=False):
            return S.op("act", fn, reads, writes, par=par)

        def dve(fn, reads, writes, par=False):
            return S.op("dve", fn, reads, writes, par=par)

        def pe(fn, reads, writes, par=False):
            return S.op("pe", fn, reads, writes, par=par)

        modTn = A.take([48, 5], F32)
        A1n = A.take([8, 5], F32)
        A2n = A.take([8, 5], F32)
        R_modn = Res("modn")

        def gen_M(l, modT_d, A1_d, A2_d, R_d):
            cT = A.take([8, 5], F32)
            scT = A.take([8, 5], BF16)
            sgm = A.take([8, 5], F32)
            bmod = A.take([48], F32)
            g1 = A.take([8], F32)
            g2 = A.take([8], F32)
            wm = [A.take([8, 1536], BF16) for _ in range(2)]
            R_wm = [Res(), Res()]
            R_c = Res()
            R_sc = Res()
            S.dma("sp", cT, I["cT"], writes=[R_c])
            S.dma("sp", bmod, I["b_modT"][l], writes=[R_c])
            S.dma("sp", g1, I["g1T"][l], writes=[R_c])
            S.dma("sp", g2, I["g2T"][l], writes=[R_c])
            S.op("act", lambda e: e.activation(out=sgm, in_=cT, func=AF.Sigmoid), [R_c], [R_sc])
            S.op("dve", lambda e: e.tensor_tensor(out=scT, in0=sgm, in1=cT, op=ALU.mult), [R_sc, R_c], [R_sc])
            for sl in range(4):
                b = sl % 2
                S.dma("pool", wm[b], I["w_mod"][l][:, sl * 1536:(sl + 1) * 1536].rearrange("(c p) n -> p c n", p=128),
                      writes=[R_wm[b]])
                for jj in range(12):
                    j = sl * 12 + jj

                    def mm(e, b=b, jj=jj):
                        for c in range(8):
                            ins = e.matmul(out=PS[0][:, 0:5], lhsT=wm[b][:, c, jj * 128:(jj + 1) * 128], rhs=scT[:, c, :],
                                           start=(c == 0), stop=(c == 7))
                        return ins
                    S.op("pe", mm, [R_wm[b], R_sc], [RPS[0]])
                    S.op("dve", lambda e, j=j: e.tensor_scalar(out=modT_d[:, j, :], in0=PS[0][:, 0:5], scalar1=bmod[:, j:j + 1],
                                                               scalar2=None, op0=ALU.add), [RPS[0], R_c], [R_d])
                    if jj % 2 == 1:
                        yield
            for s5 in range(5):
                S.op("dve", lambda e, s5=s5: e.scalar_tensor_tensor(out=A1_d[:, :, s5], in0=modT_d[:, 8:16, s5], scalar=1.0, in1=g1,
                                                                    op0=ALU.add, op1=ALU.mult), [R_d, R_c], [R_d])
                S.op("dve", lambda e, s5=s5: e.scalar_tensor_tensor(out=A2_d[:, :, s5], in0=modT_d[:, 32:40, s5], scalar=1.0, in1=g2,
                                                                    op0=ALU.add, op1=ALU.mult), [R_d, R_c], [R_d])
            yield

        R_MIX = Res("MIX")
        R_XT = Res("XT")

        ones_f = A.take([128], F32)
        sel0 = A.take([128], F32)
        emd = A.take([4, 128], F32)
        alw = A.take([4, 36], F32)
        cdiff = A.take([4, 128], F32)
        S.dma("sp", alw, I["alw"], writes=[R_const])
        S.dma("sp", cdiff, I["cdiff"].rearrange("h k q -> k h q"), writes=[R_const])
        S.dma("sp", sel0, I["sel0"], writes=[R_const])
        S.dma("sp", emd, I["emd_arg"].rearrange("h k q -> k h q"), writes=[R_const])
        S.op("dve", lambda e: e.memset(ones_f, 1.0), writes=[R_const])
        S.op("act", lambda e: e.activation(out=emd, in_=emd, func=AF.Exp), reads=[R_const], writes=[R_const])
        for h_ in range(4):
            S.op("dve", lambda e, h_=h_: e.tensor_tensor(out=emd[:, h_, :], in0=emd[:, h_, :], in1=cmask_f, op=ALU.mult),
                 reads=[R_const], writes=[R_const])
        S.barrier()
        NKB_P = T // 128
        NCB = P // 128
        NLB = LB // 128

        sel127 = A.take([128], F32)
        sel31 = A.take([128], F32)
        S.dma("sp", sel127, I["sel127"], writes=[R_const])
        S.dma("sp", sel31, I["sel31"], writes=[R_const])

        def phase_ML(l):
            A.push()
            NTM_ = max(T, 128)
            gi_r = A.take([NTM_], F32)
            gf_r = A.take([NTM_], F32)
            G_r = A.take([NTM_], F32)
            MM_r = A.take([NTM_], F32)
            mt_r = A.take([NTM_], F32)
            ones_r = A.take([NTM_], F32)
            dve(lambda e: e.memset(ones_r[0:4, :], 1.0), [], [R_const])
            NCM = max(T // 128, 1)
            cols = A.take([NCM, 12], F32)
            MMe = A.take([NCM, 4], F32)
            MMp = A.take([NCM, 4], F32)
            egs = A.take([NCM, 4], F32)
            eM = A.take([NCM, 4], F32)
            acol = A.take([NCM, 4], F32)
            dec = A.take([NCM, 4], F32)
            emt = A.take([NCM, 4], F32)
            m0c = A.take([1], F32)
            gmh = A.take([64], F32)
            Cn = A.take([2, 65], F32)
            Cnb = A.take([2, 65], BF16)
            Cnb2 = [Cnb, A.take([2, 65], BF16)]
            R_Cnb2 = [Res(), Res()]
            dec2 = A.take([NCM, 2], F32)
            R_osq = Res()
            R_oms = Res()
            qTm = [A.take([2, 128], BF16) for _ in range(2)]
            kTm = [A.take([2, 128], BF16) for _ in range(2)]
            vst = [A.take([256], BF16) for _ in range(2)]
            VAm = [A.take([4, 65], BF16) for _ in range(2)]
            sig = [A.take([256], F32) for _ in range(2)]
            R_in = [Res(), Res()]
            R_VAm = [Res(), Res()]
            kw = A.take([256], BF16)
            wT = [A.take([128], BF16) for _ in range(2)]
            R_wT = [Res(), Res()]
            tmpA = A.take([4, 65], F32)
            resm = A.take([4, 65], F32)
            den = A.take([4], F32)
            o4 = A.take([4, 64], F32)
            osq = A.take([4, 64], F32)
            oms = A.take([4], F32)
            o4b = [A.take([256], BF16) for _ in range(2)]
            R_o4b = [Res(), Res()]
            R_rows, R_cols, R_ex, R_Cn, R_Cnb, R_kw, R_tmp, R_res, R_o4, R_g = [Res() for _ in range(10)]
            S.dma("sp", gmh, I["gmh"][l], writes=[R_g])
            for vm in VAm:
                dve(lambda e, vm=vm: e.memset(vm[:, :, 64:65], 1.0), [], [R_VAm[0], R_VAm[1]])
            LN8 = math.log(0.125)
            mgen = gen_M(l + 1, modTn, A1n, A2n, R_modn) if l + 1 < DEPTH else iter(())
            SB_ = [2, 6]
            PB_ = [4, 7]

            def run_ml(seq, tok0, NTOK, L):
                NC_ = NTOK // L
                sel = sel127 if L == 128 else sel31
                S.dma("sp", gi_r[0:4, 0:NTOK], GI[:, tok0:tok0 + NTOK], writes=[R_rows], par=True)
                S.dma("sp", gf_r[0:4, 0:NTOK], GF[:, tok0:tok0 + NTOK], writes=[R_rows], par=True)
                if seq == 0:
                    init = 0.0
                    dve(lambda e: e.memset(MMp[:, 0, :], 0.0), [], [R_ex])
                    dve(lambda e: e.memset(Cn, 0.0), [], [R_Cn])
                    rdi = []
                else:
                    S.dma("sp", m0c[0:4, :], I["s_mcol"][l, seq - 1], writes=[R_rows])
                    S.dma("sp", MMp[:, 0, :], I["s_mb"][l, seq - 1], writes=[R_ex])
                    init = m0c[0:4, 0:1]
                    for h in range(4):
                        r0 = (h % 2) * 64
                        S.dma("sp", Cn[r0:r0 + 64, h // 2, 0:64], I["s_c"][l, seq - 1, h], writes=[R_Cn], par=True)
                        S.dma("sp", Cn[r0:r0 + 64, h // 2, 64:65], I["s_n"][l, seq - 1, h].rearrange("(k o) -> k o", o=1), writes=[R_Cn], par=True)
                dve(lambda e: e.tensor_copy(out=Cnb2[0], in_=Cn), [R_Cn], [R_Cnb2[0]])
                dve(lambda e: e.tensor_tensor_scan(out=mt_r[0:4, 0:NTOK], data0=ones_r[0:4, 0:NTOK], data1=gf_r[0:4, 0:NTOK],
                                                   initial=0.0, op0=ALU.mult, op1=ALU.add), [R_rows, R_const], [R_rows])
                dve(lambda e: e.tensor_tensor(out=G_r[0:4, 0:NTOK], in0=gi_r[0:4, 0:NTOK], in1=mt_r[0:4, 0:NTOK], op=ALU.subtract), [R_rows], [R_rows])
                dve(lambda e: e.tensor_tensor_scan(out=MM_r[0:4, 0:NTOK], data0=ones_r[0:4, 0:NTOK], data1=G_r[0:4, 0:NTOK],
                                                   initial=init, op0=ALU.mult, op1=ALU.max), [R_rows, R_const], [R_rows])
                dve(lambda e: e.tensor_tensor(out=mt_r[0:4, 0:NTOK], in0=mt_r[0:4, 0:NTOK], in1=MM_r[0:4, 0:NTOK], op=ALU.add), [R_rows], [R_rows])
                if debug == "ml0":
                    return
                S.dma("pool", O["o_mm"][l, seq].rearrange("(h o) -> h o", o=1), mt_r[0:4, NTOK - 1:NTOK], reads=[R_rows])
                if debug == "ml1":
                    return
                if L < 128:
                    dve(lambda e: e.memset(cols[:, 0:NC_, :], 0.0), [], [R_cols])
                for c in range(NC_):
                    def trc(e, c=c):
                        for i_, rr_ in enumerate([G_r, MM_r, mt_r]):
                            ins = e.matmul(out=PS[0][0:L, i_ * 4:(i_ + 1) * 4], lhsT=rr_[0:4, c * L:(c + 1) * L], rhs=ident_f[0:4, 0:4],
                                           start=True, stop=True, skip_group_check=True)
                        return ins
                    pe(trc, [R_rows, R_const], [RPS[0]])
                    act(lambda e, c=c: e.activation(out=cols[0:L, c, :], in_=PS[0][0:L, 0:12], func=AF.Copy), [RPS[0]], [R_cols])
                if debug == "ml2":
                    return
                n4 = NC_ * 4
                pe(lambda e: e.matmul(out=PS[1][:, 0:n4], lhsT=sel, rhs=cols[:, 0:NC_, 4:8], start=True, stop=True), [R_cols, R_const], [RPS[1]])
                dve(lambda e: e.tensor_copy(out=MMe[:, 0:NC_, :], in_=PS[1][:, 0:n4].rearrange("p (c h) -> p c h", h=4)), [RPS[1]], [R_ex])
                if NC_ > 1:
                    dve(lambda e: e.tensor_copy(out=MMp[:, 1:NC_, :], in_=MMe[:, 0:NC_ - 1, :]), [R_ex], [R_ex])
                Gc = cols[:, 0:NC_, 0:4]
                MMc = cols[:, 0:NC_, 4:8]
                mtc = cols[:, 0:NC_, 8:12]
                for (dst_, a_, b_, bias_) in [(egs, Gc, MMe, LN8), (eM, MMe, MMc, 0.0), (acol, MMp, MMc, 0.0), (dec, MMp, MMe, 0.0)]:
                    dve(lambda e, dst_=dst_, a_=a_, b_=b_: e.tensor_tensor(out=dst_[:, 0:NC_, :], in0=a_ if a_ is Gc else a_[:, 0:NC_, :],
                                                                           in1=b_ if b_ is MMc else b_[:, 0:NC_, :], op=ALU.subtract),
                        [R_cols, R_ex], [R_ex])
                    if bias_ != 0.0:
                        dve(lambda e, dst_=dst_, bias_=bias_: e.tensor_scalar(out=dst_[:, 0:NC_, :], in0=dst_[:, 0:NC_, :], scalar1=bias_, scalar2=None, op0=ALU.add),
                            [R_ex], [R_ex])
                    act(lambda e, dst_=dst_: e.activation(out=dst_[:, 0:NC_, :], in_=dst_[:, 0:NC_, :], func=AF.Exp), [R_ex], [R_ex])
                act(lambda e: e.activation(out=emt[:, 0:NC_, :], in_=mtc, func=AF.Exp, scale=-1.0), [R_cols], [R_ex])
                dve(lambda e: e.tensor_copy(out=dec2[0:64, 0:NC_, :], in_=dec[0:64, 0:NC_, 0:4:2]), [R_ex], [R_ex])
                dve(lambda e: e.tensor_copy(out=dec2[64:128, 0:NC_, :], in_=dec[64:128, 0:NC_, 1:4:2]), [R_ex], [R_ex])
                if debug == "ml3":
                    return
                for c in range(NC_):
                    ib = c % 2
                    t0 = tok0 + c * L
                    for hp in range(2):
                        S.dma("sp", qTm[ib][:, hp, 0:L], QS[3][hp * 128:(hp + 1) * 128, t0:t0 + L], writes=[R_in[ib]], par=True)
                        S.dma("sp", kTm[ib][:, hp, 0:L], KS[3][hp * 128:(hp + 1) * 128, t0:t0 + L], writes=[R_in[ib]], par=True)
                    S.dma("sp", vst[ib][0:L, :], VS[3][t0:t0 + L, :], writes=[R_in[ib]], par=True)
                    S.dma("sp", sig[ib][0:L, :], SIGO[t0:t0 + L, :], writes=[R_in[ib]], par=True)
                    act(lambda e, ib=ib: e.activation(out=VAm[ib][0:L, :, 0:64], in_=vst[ib][0:L, :].rearrange("p (h d) -> p h d", h=4), func=AF.Copy),
                        [R_in[ib]], [R_VAm[ib]])
                    if debug == "ml4":
                        continue
                    psb = PS[1][:, :].bitcast(BF16)

                    def trk2(e, ib=ib, psb=psb):
                        for hp in range(2):
                            ins = e.transpose(out=psb[0:L, hp * 128:(hp + 1) * 128], in_=kTm[ib][:, hp, 0:L], identity=ident_b)
                        return ins
                    pe(trk2, [R_in[ib], R_const], [RPS[1]])
                    dve(lambda e, c=c, psb=psb: e.tensor_tensor(out=kw[0:L, :].rearrange("p (h d) -> p h d", h=4),
                                                                in0=psb[0:L, 0:256].rearrange("p (h d) -> p h d", h=4),
                                                                in1=egs[0:L, c, :].unsqueeze(2).to_broadcast([L, 4, 64]), op=ALU.mult),
                        [RPS[1], R_ex], [R_kw])

                    if debug == "ml5":
                        continue

                    def qk(e, ib=ib):
                        for h in range(4):
                            r0 = (h % 2) * 64
                            ins = e.matmul(out=PS[SB_[h % 2]][0:L, h * L:(h + 1) * L], lhsT=kTm[ib][r0:r0 + 64, h // 2, 0:L], rhs=qTm[ib][r0:r0 + 64, h // 2, 0:L],
                                           start=True, stop=True, skip_group_check=True)
                        return ins
                    pe(qk, [R_in[ib]], [RPS[2], RPS[6]])

                    def p2(e, ib=ib, c=c):
                        for h in range(4):
                            r0 = (h % 2) * 64
                            ins = e.matmul(out=PS[PB_[h % 2]][0:L, h * 65:(h + 1) * 65], lhsT=qTm[ib][r0:r0 + 64, h // 2, 0:L], rhs=Cnb2[c % 2][r0:r0 + 64, h // 2, :],
                                           start=True, stop=True, skip_group_check=True)
                        return ins
                    pe(p2, [R_in[ib], R_Cnb2[c % 2]], [RPS[4], RPS[7]])
                    if debug == "ml6":
                        continue
                    for h in range(4):
                        wb_ = h % 2
                        dve(lambda e, h=h, c=c, wb_=wb_: e.scalar_tensor_tensor(out=wT[wb_][0:L, 0:L], in0=PS[SB_[h % 2]][0:L, h * L:(h + 1) * L],
                                                                                scalar=egs[0:L, c, h:h + 1], in1=tri_f[0:L, 0:L],
                                                                                op0=ALU.mult, op1=ALU.mult), [RPS[SB_[h % 2]], R_ex, R_const], [R_wT[wb_]])
                        pe(lambda e, h=h, wb_=wb_, ib=ib: e.matmul(out=PS[3][0:L, h * 65:(h + 1) * 65], lhsT=wT[wb_][0:L, 0:L], rhs=VAm[ib][0:L, h, :],
                                                                   start=True, stop=True, skip_group_check=True), [R_wT[wb_], R_VAm[ib]], [RPS[3]])
                        pe(lambda e, h=h, ib=ib: e.matmul(out=PS[5][(h % 2) * 64:(h % 2) * 64 + 64, (h // 2) * 65:(h // 2) * 65 + 65], lhsT=kw[0:L, h * 64:(h + 1) * 64],
                                                          rhs=VAm[ib][0:L, h, :], start=True, stop=True, skip_group_check=True,
                                                          tile_position=(0, (h % 2) * 64)), [R_kw, R_VAm[ib]], [RPS[5]])
                    cb_n = (c + 1) % 2
                    for hp_ in range(2):
                        dve(lambda e, hp_=hp_, c=c: e.scalar_tensor_tensor(out=Cn[:, hp_, :], in0=Cn[:, hp_, :], scalar=dec2[:, c, hp_:hp_ + 1],
                                                                         in1=PS[5][:, hp_ * 65:(hp_ + 1) * 65], op0=ALU.mult, op1=ALU.add),
                            [R_Cn, R_ex, RPS[5]], [R_Cn])
                    dve(lambda e, cb_n=cb_n: e.tensor_copy(out=Cnb2[cb_n], in_=Cn), [R_Cn], [R_Cnb2[cb_n]])
                    if seq == 0:
                        next(mgen, None)
                    if debug == "ml7":
                        continue
                    for h in range(4):
                        act(lambda e, h=h, c=c: e.activation(out=tmpA[0:L, h, :], in_=PS[PB_[h % 2]][0:L, h * 65:(h + 1) * 65], func=AF.Identity, scale=acol[0:L, c, h:h + 1]),
                            [RPS[PB_[h % 2]], R_ex], [R_tmp])
                    dve(lambda e, c=c: e.tensor_tensor(out=resm[0:L], in0=PS[3][0:L, 0:260].rearrange("p (h d) -> p h d", d=65),
                                                       in1=eM[0:L, c, :].unsqueeze(2).to_broadcast([L, 4, 65]), op=ALU.mult), [RPS[3], R_ex], [R_res])
                    dve(lambda e: e.tensor_tensor(out=resm[0:L], in0=resm[0:L], in1=tmpA[0:L], op=ALU.add), [R_res, R_tmp], [R_res])
                    act(lambda e: e.activation(out=den[0:L, :], in_=resm[0:L, :, 64], func=AF.Abs), [R_res], [R_res])
                    dve(lambda e, c=c: e.tensor_tensor(out=den[0:L, :], in0=den[0:L, :], in1=emt[0:L, c, :], op=ALU.max), [R_res, R_ex], [R_res])
                    dve(lambda e: e.reciprocal(out=den[0:L, :], in_=den[0:L, :]), [R_res], [R_res])
                    dve(lambda e: e.tensor_tensor(out=o4[0:L], in0=resm[0:L, :, 0:64], in1=den[0:L, :].unsqueeze(2).to_broadcast([L, 4, 64]), op=ALU.mult),
                        [R_res], [R_o4])
                    S.op("pool", lambda e, ib=ib: e.tensor_tensor(out=o4[0:L], in0=o4[0:L], in1=sig[ib][0:L, :].rearrange("p (h d) -> p h d", h=4), op=ALU.mult),
                         [R_o4, R_in[ib]], [R_o4])
                    S.op("pool", lambda e: e.tensor_tensor(out=osq[0:L], in0=o4[0:L], in1=o4[0:L], op=ALU.mult), [R_o4], [R_osq])
                    dve(lambda e: e.tensor_reduce(out=oms[0:L, :], in_=osq[0:L], axis=AX.X, op=ALU.add), [R_osq], [R_oms])
                    act(lambda e: e.activation(out=oms[0:L, :], in_=oms[0:L, :], func=AF.Ln, scale=1.0 / 64, bias=EPS), [R_oms], [R_oms])
                    act(lambda e: e.activation(out=oms[0:L, :], in_=oms[0:L, :], func=AF.Exp, scale=-0.5), [R_oms], [R_oms])
                    S.op("pool", lambda e: e.tensor_tensor(out=o4[0:L], in0=o4[0:L], in1=oms[0:L, :].unsqueeze(2).to_broadcast([L, 4, 64]), op=ALU.mult),
                         [R_o4, R_oms, R_osq], [R_o4])
                    ob = c % 2
                    S.op("pool", lambda e, ob=ob: e.tensor_tensor(out=o4b[ob][0:L, :].rearrange("p (h d) -> p h d", h=4), in0=o4[0:L],
                                                                 in1=gmh[0:L, :].unsqueeze(1).to_broadcast([L, 4, 64]), op=ALU.mult), [R_o4, R_g], [R_o4b[ob]])
                    S.dma("pool", MIX[t0:t0 + L, 512:768], o4b[ob][0:L, :], reads=[R_o4b[ob]], writes=[R_MIX], par=True)
                for h in range(4):
                    r0 = (h % 2) * 64
                    S.dma("pool", O["o_mc"][l, seq, h], Cn[r0:r0 + 64, h // 2, :], reads=[R_Cn])

            run_ml(0, 0, T, 128)
            for _ in mgen:
                pass
            for s_ in range(4):
                if debug == "mlp":
                    break
                run_ml(1 + s_, T + 32 * s_, 32, 32)
            S.barrier()
            A.pop()

        def phase_B(l, lam_init):
            A.push()
            emb = A.take([4, 5, 128], F32)
            R_emb = Res()
            for h_ in range(4):
                for t_, dst_ in [(0, 4), (1, 1), (2, 2)]:
                    S.dma("sp", emb[:, h_, dst_, :], I["bandT"][l, h_, t_], writes=[R_emb])
            act(lambda e: e.activation(out=emb[:, :, 1:3, :], in_=emb[:, :, 1:3, :], func=AF.Exp), [R_emb], [R_emb])
            act(lambda e: e.activation(out=emb[:, :, 4, :], in_=emb[:, :, 4, :], func=AF.Exp), [R_emb], [R_emb])
            for h_ in range(4):
                dve(lambda e, h_=h_: e.tensor_tensor(out=emb[:, h_, 0, :], in0=emb[:, h_, 4, :], in1=cmask_f, op=ALU.mult), [R_emb, R_const], [R_emb])
                dve(lambda e, h_=h_: e.tensor_tensor(out=emb[:, h_, 3, :], in0=emb[:, h_, 2, :], in1=m4_f, op=ALU.mult), [R_emb, R_const], [R_emb])
            lamt = A.take([4, 32], F32)
            lamp = A.take([2, 32], F32)
            lam2 = A.take([2], F32)
            nlam = A.take([1], F32)
            gsub = A.take([64], F32)
            R_lam = Res()
            S.dma("sp", lamt, I["lamb"][l], writes=[R_lam])
            S.dma("sp", gsub, I["gsub"][l], writes=[R_lam])
            dve(lambda e: e.tensor_tensor(out=lamp[:, 0, :], in0=lamt[:, 0, :], in1=lamt[:, 1, :], op=ALU.mult), [R_lam], [R_lam])
            dve(lambda e: e.tensor_tensor(out=lamp[:, 1, :], in0=lamt[:, 2, :], in1=lamt[:, 3, :], op=ALU.mult), [R_lam], [R_lam])
            dve(lambda e: e.tensor_reduce(out=lam2, in_=lamp, axis=AX.X, op=ALU.add), [R_lam], [R_lam])
            act(lambda e: e.activation(out=lam2, in_=lam2, func=AF.Exp), [R_lam], [R_lam])
            dve(lambda e: e.tensor_tensor(out=nlam, in0=lam2[:, 1:2], in1=lam2[:, 0:1], op=ALU.subtract), [R_lam], [R_lam])
            dve(lambda e: e.tensor_scalar(out=nlam, in0=nlam, scalar1=-lam_init, scalar2=None, op0=ALU.add), [R_lam], [R_lam])
            dve(lambda e: e.tensor_scalar(out=gsub, in0=gsub, scalar1=(1.0 - lam_init), scalar2=None, op0=ALU.mult), [R_lam], [R_lam])

            KT = A.take([2, T], BF16) if T >= P + 32 else A.take([2, P + 32], BF16)
            VST = A.take([max(NKB_P, NCB + 1), 256], BF16)
            VSTF = A.take([256], F32)
            VA = A.take([max(NKB_P, NCB + 1), 4, 65], BF16)
            QT = A.take([2, T], BF16)
            FB = A.take([max(NKB_P, 1), max(NKB_P, NCB + 1), 4], F32)
            lfc = A.take([max(NKB_P, NCB + 1), 4], F32)
            cum = A.take([max(NKB_P, NCB + 1), 4], F32)
            totb = A.take([4, max(NKB_P, NCB + 1)], F32)
            crefbc = A.take([max(NKB_P, NCB + 1), 4], F32)
            ktmA = A.take([max(NCB, 1), 256], F32)
            vstA = A.take([max(NCB, 1), 256], F32)
            R_ktmA = Res()
            R_vstA = Res()
            R_KT, R_VST, R_VSTF, R_VA, R_QT, R_FB, R_lfc, R_cum = [Res() for _ in range(8)]
            pt = [A.take([512], BF16) for _ in range(4)]
            R_pt = [Res() for _ in range(4)]
            pf = [A.take([128], F32) for _ in range(2)]
            R_pf = [Res() for _ in range(2)]
            og = [A.take([4, 256], BF16) for _ in range(2)]
            R_og = [Res(), Res()]
            rec = [A.take([4, 2], F32) for _ in range(2)]
            R_rec = [Res(), Res()]
            d0 = A.take([4, 64], F32)
            d1 = A.take([4, 64], F32)
            dsq = A.take([4, 64], F32)
            dms = A.take([4], F32)
            R_d = Res()
            dve(lambda e: e.memset(VA[:, :, :, 64:65], 1.0), [], [R_VA])
            cS = [0]
            cP = [0]
            cA = [0]
            cO = [0]
            ABANKS = [5, 6, 7, 1]
            TB = [0, 4]
            SBANKS = [2, 3, 7, 1]
            cR = [0]
            oT = [A.take([512], F32) for _ in range(2)]
            R_oT = [Res(), Res()]

            def run_seq(g, tok0, NQ, cache):
                sc = (1.0 / math.sqrt(32.0)) if g == 1 else 0.125
                nmap = 2 if g == 1 else 1
                if cache is None:
                    kbl = [(i * 128, 128) for i in range(NKB_P)]
                    nck = 0
                    qsubs = [(i * 128, 128) for i in range(NKB_P)]
                    qgroups = [list(range(4 * J, 4 * J + 4)) for J in range(NKB_P // 4)]
                else:
                    nck = (NLB if g == 2 else NCB)
                    kbl = [(i * 128, 128) for i in range(nck)] + [(nck * 128, 32)]
                    qsubs = [(0, 32)]
                    qgroups = [[0]]
                nkb = len(kbl)
                NK = kbl[-1][0] + kbl[-1][1]
                for hp in range(2):
                    S.dma("sp", QT[:, hp, 0:NQ], QS[g][hp * 128:(hp + 1) * 128, tok0:tok0 + NQ], writes=[R_QT], par=True)
                    S.dma("sp", KT[:, hp, nck * 128:nck * 128 + NQ], KS[g][hp * 128:(hp + 1) * 128, tok0:tok0 + NQ], writes=[R_KT], par=True)
                if cache is None:
                    S.dma("sp", VST[:, 0:nkb, :], VS[g][0:T, :].rearrange("(k p) c -> p k c", p=128), writes=[R_VST])
                    act(lambda e: e.activation(out=VA[:, 0:nkb, :, 0:64], in_=VST[:, 0:nkb, :].rearrange("p k (h d) -> p k h d", h=4),
                                               func=AF.Copy), [R_VST], [R_VA])
                else:
                    ck = [I["c_fk"], I["c_dk"], I["c_bk"]][g][l, cache]
                    cv = [I["c_fv"], I["c_dv"], I["c_bv"]][g][l, cache]
                    S.dma("sp", ktmA[:, 0:nck, :], ck.rearrange("(k p) c -> p k c", p=128), writes=[R_ktmA])
                    S.dma("sp", vstA[:, 0:nck, :], cv.rearrange("(k p) c -> p k c", p=128), writes=[R_vstA])
                    for kb in range(nck):
                        tb = kb % 2

                        def trk(e, tb=tb, kb=kb):
                            for hp in range(2):
                                ins = e.transpose(out=PS[tb][:, hp * 128:(hp + 1) * 128], in_=ktmA[:, kb, hp * 128:(hp + 1) * 128], identity=ident_f)
                            return ins
                        pe(trk, [R_ktmA, R_const], [RPS[tb]])
                        act(lambda e, kb=kb, tb=tb: e.activation(out=KT[:, :, kb * 128:(kb + 1) * 128],
                                                                 in_=PS[tb][:, 0:256].rearrange("p (a b) -> p a b", a=2), func=AF.Copy),
                            [RPS[tb]], [R_KT], par=True)
                    dve(lambda e: e.tensor_copy(out=VA[:, 0:nck, :, 0:64], in_=vstA[:, 0:nck, :].rearrange("p k (h d) -> p k h d", h=4)),
                        [R_vstA], [R_VA])
                    S.dma("sp", VST[0:32, 0, :], VS[g][tok0:tok0 + 32, :], writes=[R_VST])
                    act(lambda e: e.activation(out=VA[0:32, nck, :, 0:64], in_=VST[0:32, 0, :].rearrange("p (h d) -> p h d", h=4), func=AF.Copy),
                        [R_VST], [R_VA])
                if g == 0:
                    dve(lambda e: e.memset(lfc[:, 0:nkb, :], 0.0), [], [R_lfc])
                    if cache is None:
                        S.dma("sp", lfc[:, 0:nkb, :], LOGF[0:T, :].rearrange("(k p) h -> p k h", p=128), writes=[R_lfc])
                    else:
                        S.dma("sp", lfc[:, 0:nck, :], I["c_flf"][l, cache].rearrange("(k p) h -> p k h", p=128), writes=[R_lfc])
                        S.dma("sp", lfc[0:32, nck, :], LOGF[tok0:tok0 + 32, :], writes=[R_lfc])
                    n4 = nkb * 4
                    lf2 = lfc[:, 0:nkb, :].rearrange("p k h -> p (k h)")
                    pe(lambda e: e.matmul(out=PS[0][:, 0:n4], lhsT=ones_f, rhs=lf2, start=True, stop=True), [R_lfc, R_const], [RPS[0]])
                    for h_ in range(4):
                        dve(lambda e, h_=h_: e.tensor_tensor_scan(out=totb[:, h_, 0:nkb], data0=ones_f[:, 0:nkb],
                                                                  data1=PS[0][:, 0:n4].rearrange("p (k h) -> p h k", h=4)[:, h_, :],
                                                                  initial=0.0, op0=ALU.mult, op1=ALU.add), [RPS[0], R_const], [R_cum])
                    pe(lambda e: e.matmul(out=PS[1][:, 0:n4], lhsT=tri_f, rhs=lf2, start=True, stop=True), [R_lfc, R_const], [RPS[1]])
                    dve(lambda e: e.tensor_tensor(out=cum[:, 0:nkb, :], in0=PS[1][:, 0:n4].rearrange("p (k h) -> p k h", h=4),
                                                  in1=totb[:, :, 0:nkb].rearrange("p h k -> p k h"), op=ALU.add), [RPS[1], R_cum], [R_cum])
                    dve(lambda e: e.tensor_tensor(out=cum[:, 0:nkb, :], in0=cum[:, 0:nkb, :],
                                                  in1=PS[0][:, 0:n4].rearrange("p (k h) -> p k h", h=4), op=ALU.subtract), [RPS[0], R_cum], [R_cum])
                    cum2 = cum[:, 0:nkb, :].rearrange("p k h -> p (k h)")
                    pe(lambda e: e.matmul(out=PS[0][:, 0:n4], lhsT=sel0, rhs=cum2, start=True, stop=True), [R_cum, R_const], [RPS[0]])
                    dve(lambda e: e.tensor_copy(out=crefbc[:, 0:nkb, :], in_=PS[0][:, 0:n4].rearrange("p (k h) -> p k h", h=4)), [RPS[0]], [R_cum])
                    for qi_, (q0, nq) in enumerate(qsubs):
                        kq = (q0 // 128) if cache is None else nck
                        dve(lambda e, qi_=qi_, kq=kq: e.tensor_tensor(out=FB[:, qi_, 0:nkb, :],
                                                                      in0=crefbc[:, kq:kq + 1, :].to_broadcast([128, nkb, 4]),
                                                                      in1=cum[:, 0:nkb, :], op=ALU.subtract), [R_cum], [R_FB])

                def mode(kb, qs):
                    k0, nk = kbl[kb]
                    if cache is None:
                        off = qs - kb
                        if g == 0:
                            if off < 0:
                                return None
                            return ((lambda h: FB[0:nk, qs, kb, h:h + 1]), ((lambda h: tri_f) if off == 0 else None))
                        if g == 1:
                            if off < 0:
                                return None
                            if off == 0:
                                return (None, (lambda h: emd[:, h, :]))
                            return ((lambda h: alcol[0:nk, h, off:off + 1]), None)
                        if off < 0 or off > 4:
                            return None
                        ti = [0, 1, 2, 2, 3][off]
                        return (None, (lambda h: emb[:, h, ti, :]))
                    else:
                        new = (kb == nck)
                        if g == 0:
                            return ((lambda h: FB[0:nk, 0, kb, h:h + 1]), ((lambda h: tri_f[0:32, 0:32]) if new else None))
                        if g == 1:
                            if new:
                                return (None, (lambda h: emd[0:32, h, 0:32]))
                            off = nck - kb
                            return ((lambda h: alcol[0:nk, h, off:off + 1]), None)
                        if new:
                            return (None, (lambda h: emb[0:32, h, 4, 0:32]))
                        off = nck - kb
                        ti = 1 if off == 1 else 2
                        return (None, (lambda h: emb[:, h, ti, 0:32]))

                for qg in qgroups:
                    ob = cO[0] % 2
                    cO[0] += 1
                    nsub = len(qg)
                    nq = qsubs[qg[0]][1]
                    if g == 1:
                        lane_groups = [[(h_, 0), (h_, 1)] for h_ in range(4)]
                    else:
                        lane_groups = [[(0, 0), (1, 0)], [(2, 0), (3, 0)]]
                    for lanes in lane_groups:
                        lacc = [5, 6]
                        first = [True, True]
                        kneed = [kb for kb in range(nkb) if any(mode(kb, qs) is not None for qs in qg)]
                        units = []
                        for kb in kneed:
                            k0, nk = kbl[kb]
                            subs = [si for si, qs in enumerate(qg) if mode(kb, qs) is not None]
                            slo, shi = subs[0], subs[-1] + 1
                            qa = qsubs[qg[slo]][0]
                            qb_ = qsubs[qg[shi - 1]][0] + nq
                            for li, (h_, m_) in enumerate(lanes):
                                units.append(dict(kb=kb, k0=k0, nk=nk, slo=slo, shi=shi, qa=qa, qb_=qb_, m=m_, h=h_, li=li))

                        def emit_qk(u):
                            m = u["m"]
                            h = u["h"]
                            hp = h // 2
                            if g == 1:
                                r0 = (h % 2) * 64 + m * 32
                                r1 = r0 + 32
                            else:
                                r0 = (h % 2) * 64
                                r1 = r0 + 64
                            sb = SBANKS[cS[0] % 4]
                            cS[0] += 1
                            u["sb"] = sb
                            kw = {"tile_position": (r0, 0)} if r0 == 96 else {}
                            k0, nk, qa, qb_ = u["k0"], u["nk"], u["qa"], u["qb_"]
                            pe(lambda e, sb=sb, r0=r0, r1=r1, hp=hp, k0=k0, nk=nk, qa=qa, qb_=qb_, kw=kw: e.matmul(
                                out=PS[sb][0:nk, 0:qb_ - qa], lhsT=KT[r0:r1, hp, k0:k0 + nk], rhs=QT[r0:r1, hp, qa:qb_],
                                start=True, stop=True, **kw), [R_KT, R_QT], [RPS[sb]])

                        def emit_rest(u):
                            kb, k0, nk, slo, shi, m, sb = u["kb"], u["k0"], u["nk"], u["slo"], u["shi"], u["m"], u["sb"]
                            h = u["h"]
                            li = u["li"]
                            wide = (cache is None) and (g == 0 or g == 1)
                            if wide:
                                pi = cP[0] % 4
                                cP[0] += 1
                                width = (shi - slo) * nq
                                if g == 1 and h == 0:
                                    for half in range(2):
                                        lo_s = max(slo, 2 * half)
                                        hi_s = min(shi, 2 * half + 2)
                                        if hi_s <= lo_s:
                                            continue
                                        oo = (qg[0] + 2 * half - kb) + 3
                                        bcol = alw[0:nk, h, oo:oo + 1]
                                        c_lo = (lo_s - slo) * nq
                                        c_hi = (hi_s - slo) * nq
                                        act(lambda e, sb=sb, nk=nk, c_lo=c_lo, c_hi=c_hi, pi=pi, bcol=bcol: e.activation(
                                            out=pt[pi][0:nk, c_lo:c_hi], in_=PS[sb][0:nk, c_lo:c_hi], func=AF.Exp, scale=sc, bias=bcol),
                                            [RPS[sb], R_const], [R_pt[pi]], par=True)
                                else:
                                    if g == 0:
                                        bcol = FB[0:nk, qg[0], kb, h:h + 1]
                                        rdw = [RPS[sb], R_FB]
                                    else:
                                        bcol = alw[0:nk, h, (qg[0] - kb) + 3:(qg[0] - kb) + 4]
                                        rdw = [RPS[sb], R_const]
                                    act(lambda e, sb=sb, nk=nk, width=width, pi=pi, bcol=bcol: e.activation(
                                        out=pt[pi][0:nk, 0:width], in_=PS[sb][0:nk, 0:width], func=AF.Exp, scale=sc, bias=bcol),
                                        rdw, [R_pt[pi]])
                                if kb >= qg[0]:
                                    corr = tri_f if g == 0 else cdiff[:, h, :]
                                    dve(lambda e, pi=pi, nq=nq, corr=corr: e.tensor_tensor(
                                        out=pt[pi][:, 0:nq], in0=pt[pi][:, 0:nq], in1=corr, op=ALU.mult), [R_pt[pi], R_const], [R_pt[pi]])
                                ab = lacc[li]
                                st_ = first[li]
                                first[li] = False
                                pe(lambda e, ab=ab, slo=slo, nq=nq, nk=nk, pi=pi, kb=kb, h=h, st_=st_, width=width: e.matmul(
                                    out=PS[ab][0:65, slo * nq:slo * nq + width], lhsT=VA[0:nk, kb, h, :], rhs=pt[pi][0:nk, 0:width],
                                    start=st_, stop=(kb == kneed[-1])), [R_pt[pi], R_VA], [RPS[ab]])
                                return
                            for si in range(slo, shi):
                                qs = qg[si]
                                md = mode(kb, qs)
                                if md is None:
                                    continue
                                bfn, efn = md
                                c0 = (si - slo) * nq
                                pi = cP[0] % 4
                                cP[0] += 1
                                bias_kw = {"bias": bfn(h)} if bfn is not None else {}
                                rd = [RPS[sb]] + ([R_FB] if (bfn is not None and g == 0) else []) + [R_const]
                                if efn is None:
                                    act(lambda e, sb=sb, nk=nk, c0=c0, nq=nq, pi=pi, bias_kw=bias_kw: e.activation(
                                        out=pt[pi][0:nk, 0:nq], in_=PS[sb][0:nk, c0:c0 + nq], func=AF.Exp, scale=sc, **bias_kw),
                                        rd, [R_pt[pi]])
                                else:
                                    fi = pi % 2
                                    act(lambda e, sb=sb, nk=nk, c0=c0, nq=nq, fi=fi, bias_kw=bias_kw: e.activation(
                                        out=pf[fi][0:nk, 0:nq], in_=PS[sb][0:nk, c0:c0 + nq], func=AF.Exp, scale=sc, **bias_kw),
                                        rd, [R_pf[fi]])
                                    em = efn(h)
                                    dve(lambda e, nk=nk, nq=nq, fi=fi, pi=pi, em=em: e.tensor_tensor(
                                        out=pt[pi][0:nk, 0:nq], in0=pf[fi][0:nk, 0:nq], in1=em[0:nk, 0:nq] if nk < 128 else em, op=ALU.mult),
                                        [R_pf[fi], R_const, R_emb], [R_pt[pi]])
                                ab = lacc[li]
                                st_ = first[li]
                                first[li] = False
                                pe(lambda e, ab=ab, si=si, nq=nq, nk=nk, pi=pi, kb=kb, h=h, st_=st_: e.matmul(
                                    out=PS[ab][0:nq, si * 65:(si + 1) * 65], lhsT=pt[pi][0:nk, 0:nq], rhs=VA[0:nk, kb, h, :],
                                    start=st_, stop=True, skip_group_check=True), [R_pt[pi], R_VA], [RPS[ab]])

                        nl = len(lanes)
                        for j in range(0, len(units), nl):
                            if j == 0:
                                for u in units[0:nl]:
                                    emit_qk(u)
                            for u in units[j + nl:j + 2 * nl]:
                                emit_qk(u)
                            for u in units[j:j + nl]:
                                emit_rest(u)
                        if g == 1:
                            fin_list = [(lanes[0][0], [lacc[0], lacc[1]], [0, 1])]
                        else:
                            fin_list = [(lanes[li_][0], [lacc[li_]], [li_]) for li_ in range(len(lanes))]
                        for (h, accs, tix) in fin_list:
                            cR[0] += 1
                            rb = cR[0] % 2
                            wide_h = (cache is None) and (g == 0 or g == 1)
                            if wide_h:
                                for m in range(len(accs)):
                                    ti = tix[m]
                                    act(lambda e, ti=ti, ab=accs[m]: e.activation(out=oT[ti][0:65, :], in_=PS[ab][0:65, 0:512], func=AF.Copy), [RPS[accs[m]]], [R_oT[ti]])

                                    def trO(e, ti=ti):
                                        for si in range(4):
                                            ins = e.transpose(out=PS[TB[ti]][:, si * 65:(si + 1) * 65], in_=oT[ti][0:65, si * 128:(si + 1) * 128], identity=ident_f[0:65, 0:65])
                                        return ins
                                    pe(trO, [R_oT[ti], R_const], [RPS[TB[ti]]])
                                accs = [TB[tix[m]] for m in range(len(accs))]
                            a0 = PS[accs[0]][0:nq, 0:nsub * 65].rearrange("p (s d) -> p s d", d=65)
                            dve(lambda e, rb=rb, a0=a0, nq=nq, nsub=nsub: e.reciprocal(out=rec[rb][0:nq, 0:nsub, 0:1], in_=a0[:, :, 64:65]),
                                [RPS[accs[0]]], [R_rec[rb]])
                            if g != 1:
                                dve(lambda e, rb=rb, a0=a0, nq=nq, nsub=nsub, ob=ob, h=h: e.tensor_tensor(
                                    out=og[ob][0:nq, 0:nsub, h * 64:(h + 1) * 64], in0=a0[:, :, 0:64],
                                    in1=rec[rb][0:nq, 0:nsub, 0:1].to_broadcast([nq, nsub, 64]), op=ALU.mult),
                                    [RPS[accs[0]], R_rec[rb]], [R_og[ob]])
                            else:
                                a1 = PS[accs[1]][0:nq, 0:nsub * 65].rearrange("p (s d) -> p s d", d=65)
                                dve(lambda e, rb=rb, a1=a1, nq=nq, nsub=nsub: e.reciprocal(out=rec[rb][0:nq, 0:nsub, 1:2], in_=a1[:, :, 64:65]),
                                    [RPS[accs[1]]], [R_rec[rb]])
                                dve(lambda e, rb=rb, nq=nq, nsub=nsub: e.tensor_scalar(out=rec[rb][0:nq, 0:nsub, 1:2], in0=rec[rb][0:nq, 0:nsub, 1:2],
                                                                                     scalar1=nlam[0:nq, 0:1], scalar2=None, op0=ALU.mult),
                                    [R_rec[rb], R_lam], [R_rec[rb]])
                                dve(lambda e, rb=rb, a0=a0, nq=nq, nsub=nsub: e.tensor_tensor(
                                    out=d0[0:nq, 0:nsub, :], in0=a0[:, :, 0:64], in1=rec[rb][0:nq, 0:nsub, 0:1].to_broadcast([nq, nsub, 64]), op=ALU.mult),
                                    [RPS[accs[0]], R_rec[rb]], [R_d])
                                dve(lambda e, rb=rb, a1=a1, nq=nq, nsub=nsub: e.tensor_tensor(
                                    out=d1[0:nq, 0:nsub, :], in0=a1[:, :, 0:64], in1=rec[rb][0:nq, 0:nsub, 1:2].to_broadcast([nq, nsub, 64]), op=ALU.mult),
                                    [RPS[accs[1]], R_rec[rb], R_d], [R_d])
                                dve(lambda e, nq=nq, nsub=nsub: e.tensor_tensor(out=d0[0:nq, 0:nsub, :], in0=d0[0:nq, 0:nsub, :], in1=d1[0:nq, 0:nsub, :], op=ALU.add),
                                    [R_d], [R_d])
                                dve(lambda e, nq=nq, nsub=nsub: e.tensor_tensor(out=dsq[0:nq, 0:nsub, :], in0=d0[0:nq, 0:nsub, :], in1=d0[0:nq, 0:nsub, :], op=ALU.mult),
                                    [R_d], [R_d])
                                dve(lambda e, nq=nq, nsub=nsub: e.tensor_reduce(out=dms[0:nq, 0:nsub], in_=dsq[0:nq, 0:nsub, :], axis=AX.X, op=ALU.add), [R_d], [R_d])
                                act(lambda e, nq=nq, nsub=nsub: e.activation(out=dms[0:nq, 0:nsub], in_=dms[0:nq, 0:nsub], func=AF.Ln, scale=1.0 / 64, bias=EPS), [R_d], [R_d])
                                act(lambda e, nq=nq, nsub=nsub: e.activation(out=dms[0:nq, 0:nsub], in_=dms[0:nq, 0:nsub], func=AF.Exp, scale=-0.5), [R_d], [R_d])
                                dve(lambda e, nq=nq, nsub=nsub: e.tensor_tensor(out=d0[0:nq, 0:nsub, :], in0=d0[0:nq, 0:nsub, :],
                                                                              in1=dms[0:nq, 0:nsub].unsqueeze(2).to_broadcast([nq, nsub, 64]), op=ALU.mult), [R_d], [R_d])
                                dve(lambda e, nq=nq, nsub=nsub, ob=ob, h=h: e.tensor_tensor(
                                    out=og[ob][0:nq, 0:nsub, h * 64:(h + 1) * 64], in0=d0[0:nq, 0:nsub, :],
                                    in1=gsub[0:nq, :].unsqueeze(1).to_broadcast([nq, nsub, 64]), op=ALU.mult), [R_d, R_lam], [R_og[ob]])
                    qa = tok0 + qsubs[qg[0]][0]
                    mc0 = [0, 256, 768][g]
                    if cache is None:
                        S.dma("pool", MIX[qa:qa + nsub * 128, mc0:mc0 + 256].rearrange("(s p) c -> p s c", p=128), og[ob][:, 0:nsub, :],
                              reads=[R_og[ob]], writes=[R_MIX], par=True)
                    else:
                        S.dma("pool", MIX[qa:qa + 32, mc0:mc0 + 256], og[ob][0:32, 0, :], reads=[R_og[ob]], writes=[R_MIX], par=True)

            for g in range(3):
                run_seq(g, 0, T, None)
                for s_ in range(4):
                    run_seq(g, T + 32 * s_, 32, s_)
            S.barrier()
            A.pop()
            phase_ML(l)

        for l in range(DEPTH):
            lam_init = 0.8 - 0.6 * math.exp(-0.3 * l)
            if l == 0:
                A.push()
                for _ in gen_M(0, modT, A1, A2, R_mod):
                    pass
                S.barrier()
                A.pop()
            else:
                dve(lambda e: e.tensor_copy(out=modT, in_=modTn), [R_modn], [R_mod])
                dve(lambda e: e.tensor_copy(out=A1, in_=A1n), [R_modn], [R_mod])
                dve(lambda e: e.tensor_copy(out=A2, in_=A2n), [R_modn], [R_mod])
                S.barrier()

            A.push()
            wfm = A.take([8, 2048], BF16)
            wg = A.take([8, 8], BF16)
            wtm = A.take([8, NTM], BF16)
            R_w = Res("w_in")
            w_in_l = I["w_in"][l].rearrange("(c p) n -> p c n", p=128)
            for j in range(16):
                S.dma("pool", wfm[:, :, j * 128:(j + 1) * 128], w_in_l[:, :, FM_COLS[j]:FM_COLS[j] + 128], writes=[R_w])
            S.dma("pool", wg[:, :, 0:4], w_in_l[:, :, C_MI:C_MI + 4], writes=[R_w])
            S.dma("pool", wg[:, :, 4:8], w_in_l[:, :, C_MF:C_MF + 4], writes=[R_w])
            o = 0
            for (c0, n) in TM_SRC:
                S.dma("pool", wtm[:, :, o:o + n], w_in_l[:, :, c0:c0 + n], writes=[R_w])
                o += n
            binT = A.take([16], F32)
            bgate = A.take([2], F32)
            nbgate = A.take([2], F32)
            btm = A.take([NTM], F32)
            gqk = A.take([12], F32)
            convw = A.take([4, 4], F32)
            convb = A.take([4], F32)
            R_p = Res("params")
            S.dma("sp", binT, I["binT"][l], writes=[R_p])
            S.dma("sp", bgate[0:4, :], I["bgate"][l], writes=[R_p])
            S.dma("sp", btm, I["btm"][l], writes=[R_p])
            S.dma("sp", gqk, I["gqk"][l], writes=[R_p])
            S.dma("sp", convw, I["convw"][l], writes=[R_p])
            S.dma("sp", convb, I["convb"][l], writes=[R_p])
            dve(lambda e: e.tensor_scalar(out=nbgate[0:4, :], in0=bgate[0:4, :], scalar1=-1.0, scalar2=None, op0=ALU.mult),
                [R_p], [R_p])
            xT = [A.take([8, 512], F32) for _ in range(2)]
            R_xT = [Res(), Res()]
            xtm = [A.take([1024], F32) for _ in range(2)]
            R_xtm = [Res(), Res()]
            sqb = A.take([8, 512], BF16)
            R_sq = Res()
            rstd = A.take([512], F32)
            R_rstd = Res()
            t1 = A.take([8, 512], F32)
            R_t1 = Res()
            hT2 = [A.take([8, 512], BF16) for _ in range(2)]
            R_hT2 = [Res(), Res()]
            U = [A.take([4, 35], F32) for _ in range(4)]
            UP = [A.take([515], F32) for _ in range(4)]
            R_U = [Res() for _ in range(4)]
            u32 = [A.take([512], F32) for _ in range(2)]
            R_u32 = [Res(), Res()]
            sqk = [A.take([512], BF16) for _ in range(2)]
            R_sqk = [Res(), Res()]
            rr = [A.take([512], F32) for _ in range(2)]
            R_rr = [Res(), Res()]
            kn32 = [A.take([512], F32) for _ in range(2)]
            R_kn32 = [Res(), Res()]
            knb = [A.take([512], BF16) for _ in range(2)]
            R_knb = [Res(), Res()]
            acc = [A.take([512], F32) for _ in range(2)]
            R_acc = [Res(), Res()]
            v32 = [A.take([512], F32) for _ in range(2)]
            R_v32 = [Res(), Res()]
            vb = [A.take([512], BF16) for _ in range(2)]
            R_vb = [Res(), Res()]
            grow = A.take([2, 512], F32)
            R_grow = Res()
            t260 = A.take([260], F32)
            R_t260 = Res()
            sg = A.take([256], F32)
            R_sg = Res()
            lf4 = A.take([4], F32)
            R_lf4 = Res()
            for u_ in UP:
                dve(lambda e, u_=u_: e.memset(u_[:, 0:3], 0.0), [], [R_U[0], R_U[1], R_U[2], R_U[3]])
            cnt2 = [0]

            def norm_part(bi, tok0, ntok, segs):
                is_s = (bi == len(blocks) - 1)
                xb = bi % 2
                xTb = xT[xb]
                hTc = hT2[bi % 2]
                R_hTc = R_hT2[bi % 2]
                if l == 0:
                    src = I["xs"] if is_s else I["xp"][tok0:tok0 + ntok, :]
                    for i in range(ntok // 128):
                        tb = i % 2
                        S.dma("sp", xtm[tb], src[i * 128:(i + 1) * 128, :], writes=[R_xtm[tb]])
                        for half in range(2):
                            pb = half

                            def tr(e, tb=tb, half=half, pb=pb):
                                for c4 in range(4):
                                    c = half * 4 + c4
                                    ins = e.transpose(out=PS[pb][:, c4 * 128:(c4 + 1) * 128], in_=xtm[tb][:, c * 128:(c + 1) * 128],
                                                      identity=ident_f)
                                return ins
                            pe(tr, [R_xtm[tb], R_const], [RPS[pb]])
                            eng = act if half == 0 else dve
                            if half == 0:
                                act(lambda e, i=i, pb=pb, half=half, xTb=xTb: e.activation(
                                    out=xTb[:, half * 4:half * 4 + 4, i * 128:(i + 1) * 128],
                                    in_=PS[pb][:, :].rearrange("p (a b) -> p a b", a=4), func=AF.Copy), [RPS[pb]], [R_xT[xb]])
                            else:
                                dve(lambda e, i=i, pb=pb, half=half, xTb=xTb: e.tensor_copy(
                                    out=xTb[:, half * 4:half * 4 + 4, i * 128:(i + 1) * 128],
                                    in_=PS[pb][:, :].rearrange("p (a b) -> p a b", a=4)), [RPS[pb]], [R_xT[xb]])
                    S.dma("pool", XT[0][:, tok0:tok0 + ntok].rearrange("(c p) t -> p c t", p=128), xTb[:, :, 0:ntok],
                          reads=[R_xT[xb]], writes=[R_XT])
                else:
                    S.dma("sp", xTb[:, :, 0:ntok], XT[1][:, tok0:tok0 + ntok].rearrange("(c p) t -> p c t", p=128),
                          writes=[R_xT[xb]])
                act(lambda e, xTb=xTb, ntok=ntok: e.activation(out=sqb[:, :, 0:ntok], in_=xTb[:, :, 0:ntok], func=AF.Square),
                    [R_xT[xb]], [R_sq])

                def ssmm(e, ntok=ntok):
                    for c in range(8):
                        ins = e.matmul(out=PS[2][:, 0:ntok], lhsT=ones_b, rhs=sqb[:, c, 0:ntok], start=(c == 0), stop=(c == 7))
                    return ins
                pe(ssmm, [R_sq, R_const], [RPS[2]])
                act(lambda e, ntok=ntok: e.activation(out=rstd[:, 0:ntok], in_=PS[2][:, 0:ntok], func=AF.Ln, scale=1.0 / D, bias=EPS),
                    [RPS[2]], [R_rstd])
                act(lambda e, ntok=ntok: e.activation(out=rstd[:, 0:ntok], in_=rstd[:, 0:ntok], func=AF.Exp, scale=-0.5),
                    [R_rstd], [R_rstd])
                for c in range(8):
                    for (c0, ncol, sq_) in segs:
                        dve(lambda e, c=c, c0=c0, ncol=ncol, sq_=sq_, xTb=xTb: e.scalar_tensor_tensor(
                            out=t1[:, c, c0:c0 + ncol], in0=xTb[:, c, c0:c0 + ncol], scalar=A1[:, c, sq_:sq_ + 1],
                            in1=rstd[:, c0:c0 + ncol], op0=ALU.mult, op1=ALU.mult), [R_xT[xb], R_rstd, R_mod], [R_t1])
                        act(lambda e, c=c, c0=c0, ncol=ncol, sq_=sq_: e.activation(
                            out=hTc[:, c, c0:c0 + ncol], in_=t1[:, c, c0:c0 + ncol], func=AF.Identity,
                            bias=modT[:, c, sq_:sq_ + 1], scale=1.0), [R_t1, R_mod], [R_hTc])

            def proj_part(bi, tok0, ntok, segs):
                is_s = (bi == len(blocks) - 1)
                xb = bi % 2
                xTb = xT[xb]
                hTc = hT2[bi % 2]
                R_hTc = R_hT2[bi % 2]
                pend = []
                for j in range(16):
                    pb = 3 + (j % 2)

                    def fm(e, j=j, pb=pb, ntok=ntok):
                        for c in range(8):
                            ins = e.matmul(out=PS[pb][:, 0:ntok], lhsT=wfm[:, c, j * 128:(j + 1) * 128], rhs=hTc[:, c, 0:ntok],
                                           start=(c == 0), stop=(c == 7))
                        return ins
                    pe(fm, [R_w, R_hTc], [RPS[pb]])
                    if j < 12:
                        k2 = cnt2[0] % 2
                        cnt2[0] += 1
                        g = j // 4
                        isk = (j % 4) >= 2
                        hd = 32 if g == 1 else 64
                        bm = blk32 if g == 1 else blk64
                        act(lambda e, j=j, pb=pb, k2=k2, ntok=ntok: e.activation(out=u32[k2][:, 0:ntok], in_=PS[pb][:, 0:ntok],
                                                                                 func=AF.Identity, bias=binT[:, j:j + 1], scale=1.0),
                            [RPS[pb], R_p], [R_u32[k2]])
                        act(lambda e, k2=k2, ntok=ntok: e.activation(out=sqk[k2][:, 0:ntok], in_=u32[k2][:, 0:ntok], func=AF.Square),
                            [R_u32[k2]], [R_sqk[k2]])
                        def tail(j=j, k2=k2, g=g, isk=isk, hd=hd, bm=bm, ntok=ntok):
                            pe(lambda e, k2=k2, bm=bm, ntok=ntok: e.matmul(out=PS[5][:, 0:ntok], lhsT=bm, rhs=sqk[k2][:, 0:ntok],
                                                                           start=True, stop=True), [R_sqk[k2], R_const], [RPS[5]])
                            act(lambda e, k2=k2, hd=hd, ntok=ntok: e.activation(out=rr[k2][:, 0:ntok], in_=PS[5][:, 0:ntok], func=AF.Ln,
                                                                                scale=1.0 / hd, bias=EPS), [RPS[5]], [R_rr[k2]])
                            act(lambda e, k2=k2, ntok=ntok: e.activation(out=rr[k2][:, 0:ntok], in_=rr[k2][:, 0:ntok], func=AF.Exp, scale=-0.5),
                                [R_rr[k2]], [R_rr[k2]])
                            if isk:
                                dve(lambda e, j=j, k2=k2, ntok=ntok: e.scalar_tensor_tensor(
                                    out=kn32[k2][:, 0:ntok], in0=u32[k2][:, 0:ntok], scalar=gqk[:, j:j + 1], in1=rr[k2][:, 0:ntok],
                                    op0=ALU.mult, op1=ALU.mult), [R_u32[k2], R_rr[k2], R_p], [R_kn32[k2]])
                                okt = [O["o_fk"], O["o_dk"], O["o_bk"]][g]
                                ch = j % 2
                                S.dma("pool", okt[l][ch * 128:(ch + 1) * 128, tok0:tok0 + ntok], kn32[k2][:, 0:ntok], reads=[R_kn32[k2]])
                                dve(lambda e, k2=k2, ntok=ntok: e.tensor_copy(out=knb[k2][:, 0:ntok], in_=kn32[k2][:, 0:ntok]),
                                    [R_kn32[k2]], [R_knb[k2]])
                                S.dma("pool", KS[g][ch * 128:(ch + 1) * 128, tok0:tok0 + ntok], knb[k2][:, 0:ntok], reads=[R_knb[k2]])
                            else:
                                dve(lambda e, j=j, k2=k2, ntok=ntok: e.scalar_tensor_tensor(
                                    out=knb[k2][:, 0:ntok], in0=u32[k2][:, 0:ntok], scalar=gqk[:, j:j + 1], in1=rr[k2][:, 0:ntok],
                                    op0=ALU.mult, op1=ALU.mult), [R_u32[k2], R_rr[k2], R_p], [R_knb[k2]])
                                ch = j % 2
                                S.dma("pool", QS[g][ch * 128:(ch + 1) * 128, tok0:tok0 + ntok], knb[k2][:, 0:ntok], reads=[R_knb[k2]])
                        for t_ in pend:
                            t_()
                        pend = [tail]
                    else:
                        for t_ in pend:
                            t_()
                        pend = []
                        jj = j - 12
                        k2 = cnt2[0] % 2
                        cnt2[0] += 1
                        if not is_s:
                            Uj = UP[jj]
                            act(lambda e, j=j, pb=pb, Uj=Uj: e.activation(out=Uj[:, 3:515], in_=PS[pb][:, 0:512], func=AF.Identity,
                                                                          bias=binT[:, j:j + 1], scale=1.0), [RPS[pb], R_p], [R_U[jj]])
                            dve(lambda e, jj=jj, k2=k2, Uj=Uj: e.tensor_scalar(out=acc[k2][:, 0:512], in0=Uj[:, 0:512],
                                                                             scalar1=convw[:, jj, 0:1], scalar2=convb[:, jj:jj + 1],
                                                                             op0=ALU.mult, op1=ALU.add), [R_U[jj], R_p], [R_acc[k2]])
                            for tp in range(1, 4):
                                dve(lambda e, jj=jj, k2=k2, Uj=Uj, tp=tp: e.scalar_tensor_tensor(
                                    out=acc[k2][:, 0:512], in0=Uj[:, tp:tp + 512], scalar=convw[:, jj, tp:tp + 1], in1=acc[k2][:, 0:512],
                                    op0=ALU.mult, op1=ALU.add), [R_U[jj], R_p, R_acc[k2]], [R_acc[k2]])
                            if bi == NPB - 1:
                                S.dma("pool", O["o_conv"][l, jj, :, 0, :], Uj[:, 512:515], reads=[R_U[jj]])
                            else:
                                dve(lambda e, Uj=Uj: e.tensor_copy(out=Uj[:, 0:3], in_=Uj[:, 512:515]), [R_U[jj], R_acc[k2]], [R_U[jj]])
                        else:
                            Uj = U[jj]
                            S.dma("sp", Uj[:, :, 0:3], I["convst"][l, :, jj, :, :], writes=[R_U[jj]])
                            act(lambda e, j=j, pb=pb, Uj=Uj: e.activation(out=Uj[:, :, 3:35], in_=PS[pb][:, 0:128].rearrange("p (a b) -> p a b", a=4),
                                                                          func=AF.Identity, bias=binT[:, j:j + 1], scale=1.0),
                                [RPS[pb], R_p], [R_U[jj]])
                            a3 = acc[k2][:, 0:128].rearrange("p (a b) -> p a b", a=4)
                            dve(lambda e, jj=jj, Uj=Uj, a3=a3: e.tensor_scalar(out=a3, in0=Uj[:, :, 0:32], scalar1=convw[:, jj, 0:1],
                                                                             scalar2=convb[:, jj:jj + 1], op0=ALU.mult, op1=ALU.add),
                                [R_U[jj], R_p], [R_acc[k2]])
                            for tp in range(1, 4):
                                dve(lambda e, jj=jj, Uj=Uj, tp=tp, a3=a3: e.scalar_tensor_tensor(
                                    out=a3, in0=Uj[:, :, tp:tp + 32], scalar=convw[:, jj, tp:tp + 1], in1=a3, op0=ALU.mult, op1=ALU.add),
                                    [R_U[jj], R_p, R_acc[k2]], [R_acc[k2]])
                            S.dma("pool", O["o_conv"][l, jj, :, 1:5, :], Uj[:, :, 32:35], reads=[R_U[jj]])
                        act(lambda e, k2=k2, ntok=ntok: e.activation(out=knb[k2][:, 0:ntok], in_=acc[k2][:, 0:ntok], func=AF.Silu),
                            [R_acc[k2]], [R_knb[k2]])
                        dst = QS[3] if jj < 2 else KS[3]
                        ch = jj % 2
                        S.dma("pool", dst[ch * 128:(ch + 1) * 128, tok0:tok0 + ntok], knb[k2][:, 0:ntok], reads=[R_knb[k2]])
                for gi_ in range(2):
                    pb = 3 + gi_

                    def gm(e, gi_=gi_, pb=pb, ntok=ntok):
                        for c in range(8):
                            ins = e.matmul(out=PS[pb][0:4, 0:ntok], lhsT=wg[:, c, gi_ * 4:gi_ * 4 + 4], rhs=hTc[:, c, 0:ntok],
                                           start=(c == 0), stop=(c == 7))
                        return ins
                    pe(gm, [R_w, R_hTc], [RPS[pb]])
                act(lambda e, ntok=ntok: e.activation(out=grow[0:4, 0, 0:ntok], in_=PS[3][0:4, 0:ntok], func=AF.Identity,
                                                      bias=bgate[0:4, 0:1], scale=1.0), [RPS[3], R_p], [R_grow])
                act(lambda e, ntok=ntok: e.activation(out=grow[0:4, 1, 0:ntok], in_=PS[4][0:4, 0:ntok], func=AF.Exp,
                                                      bias=nbgate[0:4, 1:2], scale=-1.0), [RPS[4], R_p], [R_grow])
                act(lambda e, ntok=ntok: e.activation(out=grow[0:4, 1, 0:ntok], in_=grow[0:4, 1, 0:ntok], func=AF.Ln, bias=1.0, scale=1.0),
                    [R_grow], [R_grow])
                dve(lambda e, ntok=ntok: e.tensor_scalar(out=grow[0:4, 1, 0:ntok], in0=grow[0:4, 1, 0:ntok], scalar1=-1.0, scalar2=None,
                                                         op0=ALU.mult), [R_grow], [R_grow])
                S.dma("pool", GI[:, tok0:tok0 + ntok], grow[0:4, 0, 0:ntok], reads=[R_grow])
                S.dma("pool", GF[:, tok0:tok0 + ntok], grow[0:4, 1, 0:ntok], reads=[R_grow])
                for i in range(ntok // 128):
                    ts_ = tok0 + i * 128
                    for gr in range(2):
                        pb = 6 + gr
                        k2 = gr

                        def tm(e, i=i, gr=gr, pb=pb):
                            for c in range(8):
                                ins = e.matmul(out=PS[pb][:, :], lhsT=hTc[:, c, i * 128:(i + 1) * 128], rhs=wtm[:, c, gr * 512:(gr + 1) * 512],
                                               start=(c == 0), stop=(c == 7))
                            return ins
                        pe(tm, [R_w, R_hTc], [RPS[pb]])
                        dve(lambda e, gr=gr, pb=pb, k2=k2: e.tensor_tensor(out=v32[k2], in0=PS[pb][:, :], in1=btm[:, gr * 512:(gr + 1) * 512],
                                                                          op=ALU.add), [RPS[pb], R_p], [R_v32[k2]])
                        act(lambda e, k2=k2: e.activation(out=vb[k2], in_=v32[k2], func=AF.Copy), [R_v32[k2]], [R_vb[k2]])
                        if gr == 0:
                            S.dma("pool", O["o_fv"][l, ts_:ts_ + 128, :], v32[k2][:, 0:256], reads=[R_v32[k2]])
                            S.dma("pool", O["o_dv"][l, ts_:ts_ + 128, :], v32[k2][:, 256:512], reads=[R_v32[k2]])
                            S.dma("pool", VS[0][ts_:ts_ + 128, :], vb[k2][:, 0:256], reads=[R_vb[k2]])
                            S.dma("pool", VS[1][ts_:ts_ + 128, :], vb[k2][:, 256:512], reads=[R_vb[k2]])
                        else:
                            S.dma("pool", O["o_bv"][l, ts_:ts_ + 128, :], v32[k2][:, 0:256], reads=[R_v32[k2]])
                            S.dma("pool", VS[2][ts_:ts_ + 128, :], vb[k2][:, 0:256], reads=[R_vb[k2]])
                            S.dma("pool", VS[3][ts_:ts_ + 128, :], vb[k2][:, 256:512], reads=[R_vb[k2]])

                    def tm3(e, i=i):
                        for c in range(8):
                            ins = e.matmul(out=PS[6][:, 0:260], lhsT=hTc[:, c, i * 128:(i + 1) * 128], rhs=wtm[:, c, 1024:1284],
                                           start=(c == 0), stop=(c == 7))
                        return ins
                    pe(tm3, [R_w, R_hTc], [RPS[6]])
                    dve(lambda e: e.tensor_tensor(out=t260, in0=PS[6][:, 0:260], in1=btm[:, 1024:1284], op=ALU.add), [RPS[6], R_p], [R_t260])
                    act(lambda e: e.activation(out=sg, in_=t260[:, 0:256], func=AF.Sigmoid), [R_t260], [R_sg])
                    S.dma("pool", SIGO[ts_:ts_ + 128, :], sg, reads=[R_sg])
                    act(lambda e: e.activation(out=lf4, in_=t260[:, 256:260], func=AF.Exp, scale=-1.0), [R_t260], [R_lf4])
                    act(lambda e: e.activation(out=lf4, in_=lf4, func=AF.Ln, bias=1.0, scale=1.0), [R_lf4], [R_lf4])
                    dve(lambda e: e.tensor_scalar(out=lf4, in0=lf4, scalar1=-1.0, scalar2=None, op0=ALU.mult), [R_lf4], [R_lf4])
                    S.dma("pool", O["o_flf"][l, ts_:ts_ + 128, :], lf4, reads=[R_lf4])
                    S.dma("pool", LOGF[ts_:ts_ + 128, :], lf4, reads=[R_lf4])

            norm_part(0, *blocks[0])
            for bi_ in range(len(blocks)):
                if bi_ + 1 < len(blocks):
                    norm_part(bi_ + 1, *blocks[bi_ + 1])
                proj_part(bi_, *blocks[bi_])
            S.barrier()
            A.pop()
            if debug == "A":
                break
            if debug != "C":
                phase_B(l, lam_init)
            A.push()
            wo = A.take([8, 1024], BF16)
            R_wo = Res()
            S.dma("pool", wo, I["w_out"][l].rearrange("(c p) n -> p c n", p=128), writes=[R_wo])
            mt = [A.take([1024], BF16) for _ in range(2)]
            R_mt = [Res(), Res()]
            mixT = A.take([8, 512], BF16)
            R_mixT = Res()
            xc = [A.take([8, 512], F32) for _ in range(2)]
            R_xc = [Res(), Res()]
            R_XM = Res()
            for bi, (tok0, ntok, segs) in enumerate(blocks):
                xb = bi % 2
                S.dma("sp", xc[xb][:, :, 0:ntok], XT[l][:, tok0:tok0 + ntok].rearrange("(c p) t -> p c t", p=128),
                      reads=[R_XT], writes=[R_xc[xb]])
                for i in range(ntok // 128):
                    tb = i % 2
                    S.dma("sp", mt[tb], MIX[tok0 + i * 128:tok0 + (i + 1) * 128, :], reads=[R_MIX], writes=[R_mt[tb]])
                    pb = tb
                    psb = PS[pb][:, :].bitcast(BF16)

                    def trm(e, tb=tb, psb=psb):
                        for c in range(8):
                            ins = e.transpose(out=psb[:, c * 128:(c + 1) * 128], in_=mt[tb][:, c * 128:(c + 1) * 128], identity=ident_b)
                        return ins
                    pe(trm, [R_mt[tb], R_const], [RPS[pb]])
                    act(lambda e, i=i, psb=psb: e.activation(out=mixT[:, :, i * 128:(i + 1) * 128],
                                                             in_=psb.rearrange("p (a b) -> p a b", a=8), func=AF.Copy),
                        [RPS[pb]], [R_mixT])
                for dc in range(8):
                    pb = 2 + dc % 2

                    def wom(e, dc=dc, pb=pb, ntok=ntok):
                        for c in range(8):
                            ins = e.matmul(out=PS[pb][:, 0:ntok], lhsT=wo[:, c, dc * 128:(dc + 1) * 128], rhs=mixT[:, c, 0:ntok],
                                           start=(c == 0), stop=(c == 7))
                        return ins
                    pe(wom, [R_wo, R_mixT], [RPS[pb]])
                    for (c0, ncol, sq_) in segs:
                        dve(lambda e, dc=dc, pb=pb, c0=c0, ncol=ncol, sq_=sq_, xb=xb: e.scalar_tensor_tensor(
                            out=xc[xb][:, dc, c0:c0 + ncol], in0=PS[pb][:, c0:c0 + ncol], scalar=modT[:, 16 + dc, sq_:sq_ + 1],
                            in1=xc[xb][:, dc, c0:c0 + ncol], op0=ALU.mult, op1=ALU.add), [RPS[pb], R_mod, R_xc[xb]], [R_xc[xb]])
                S.dma("sp", XM[:, tok0:tok0 + ntok].rearrange("(c p) t -> p c t", p=128), xc[xb][:, :, 0:ntok],
                      reads=[R_xc[xb]], writes=[R_XM])
            S.barrier()
            A.pop()
            A.push()
            wgt = A.take([8, DFF], BF16)
            wup = A.take([8, DFF], BF16)
            wdn = A.take([22, 1024], BF16)
            R_wf = Res()
            for hh in range(2):
                S.dma("pool", wgt[:, :, hh * 1408:(hh + 1) * 1408], I["w_gate"][l][:, hh * 1408:(hh + 1) * 1408].rearrange("(c p) n -> p c n", p=128), writes=[R_wf])
                S.dma("pool", wup[:, :, hh * 1408:(hh + 1) * 1408], I["w_up"][l][:, hh * 1408:(hh + 1) * 1408].rearrange("(c p) n -> p c n", p=128), writes=[R_wf])
                S.dma("pool", wdn[:, hh * 11:(hh + 1) * 11, :], I["w_down"][l][hh * 1408:(hh + 1) * 1408, :].rearrange("(c p) n -> p c n", p=128), writes=[R_wf])
            xf = A.take([8, 512], F32)
            R_xf = Res()
            rs2 = A.take([512], F32)
            R_rs2 = Res()
            tt2 = A.take([512], F32)
            R_tt2 = Res()
            h2 = A.take([8, 512], BF16)
            R_h2 = Res()
            actT = A.take([22, 512], BF16)
            R_actT = Res()
            sq2 = actT[:, 0:8, :]
            R_sq2 = R_actT
            slu = [A.take([512], F32) for _ in range(2)]
            R_slu = [Res(), Res()]
            xo = [A.take([512], F32) for _ in range(2)]
            R_xo = [Res(), Res()]
            R_XO = Res()
            dst_all = XT[1] if l == 0 else O["YT"]
            for bi, (tok0, ntok, segs) in enumerate(blocks):
                S.dma("sp", xf[:, :, 0:ntok], XM[:, tok0:tok0 + ntok].rearrange("(c p) t -> p c t", p=128), reads=[R_XM], writes=[R_xf])
                act(lambda e, ntok=ntok: e.activation(out=sq2[:, :, 0:ntok], in_=xf[:, :, 0:ntok], func=AF.Square), [R_xf], [R_sq2])

                def ss2(e, ntok=ntok):
                    for c in range(8):
                        ins = e.matmul(out=PS[0][:, 0:ntok], lhsT=ones_b, rhs=sq2[:, c, 0:ntok], start=(c == 0), stop=(c == 7))
                    return ins
                pe(ss2, [R_sq2, R_const], [RPS[0]])
                act(lambda e, ntok=ntok: e.activation(out=rs2[:, 0:ntok], in_=PS[0][:, 0:ntok], func=AF.Ln, scale=1.0 / D, bias=EPS), [RPS[0]], [R_rs2])
                act(lambda e, ntok=ntok: e.activation(out=rs2[:, 0:ntok], in_=rs2[:, 0:ntok], func=AF.Exp, scale=-0.5), [R_rs2], [R_rs2])
                for c in range(8):
                    for (c0, ncol, sq_) in segs:
                        dve(lambda e, c=c, c0=c0, ncol=ncol, sq_=sq_: e.scalar_tensor_tensor(
                            out=tt2[:, c0:c0 + ncol], in0=xf[:, c, c0:c0 + ncol], scalar=A2[:, c, sq_:sq_ + 1], in1=rs2[:, c0:c0 + ncol],
                            op0=ALU.mult, op1=ALU.mult), [R_xf, R_rs2, R_mod], [R_tt2])
                        act(lambda e, c=c, c0=c0, ncol=ncol, sq_=sq_: e.activation(
                            out=h2[:, c, c0:c0 + ncol], in_=tt2[:, c0:c0 + ncol], func=AF.Identity, bias=modT[:, 24 + c, sq_:sq_ + 1], scale=1.0),
                            [R_tt2, R_mod], [R_h2])
                for f in range(22):
                    pa = 1 + 2 * (f % 2)
                    pbk = pa + 1

                    def gu(e, f=f, pa=pa, pbk=pbk, ntok=ntok):
                        for c in range(8):
                            e.matmul(out=PS[pa][:, 0:ntok], lhsT=wgt[:, c, f * 128:(f + 1) * 128], rhs=h2[:, c, 0:ntok], start=(c == 0), stop=(c == 7))
                        for c in range(8):
                            ins = e.matmul(out=PS[pbk][:, 0:ntok], lhsT=wup[:, c, f * 128:(f + 1) * 128], rhs=h2[:, c, 0:ntok], start=(c == 0), stop=(c == 7))
                        return ins
                    pe(gu, [R_wf, R_h2], [RPS[pa], RPS[pbk]])
                    k2 = f % 2
                    act(lambda e, pa=pa, k2=k2, ntok=ntok: e.activation(out=slu[k2][:, 0:ntok], in_=PS[pa][:, 0:ntok], func=AF.Silu), [RPS[pa]], [R_slu[k2]])
                    dve(lambda e, f=f, pbk=pbk, k2=k2, ntok=ntok: e.tensor_tensor(out=actT[:, f, 0:ntok], in0=PS[pbk][:, 0:ntok], in1=slu[k2][:, 0:ntok], op=ALU.mult),
                        [RPS[pbk], R_slu[k2]], [R_actT])
                for dc in range(8):
                    pb = 5 + dc % 2
                    k2 = dc % 2

                    def dn(e, dc=dc, pb=pb, ntok=ntok):
                        for f in range(22):
                            ins = e.matmul(out=PS[pb][:, 0:ntok], lhsT=wdn[:, f, dc * 128:(dc + 1) * 128], rhs=actT[:, f, 0:ntok], start=(f == 0), stop=(f == 21))
                        return ins
                    pe(dn, [R_wf, R_actT], [RPS[pb]])
                    for (c0, ncol, sq_) in segs:
                        dve(lambda e, dc=dc, pb=pb, c0=c0, ncol=ncol, sq_=sq_, k2=k2: e.scalar_tensor_tensor(
                            out=xo[k2][:, c0:c0 + ncol], in0=PS[pb][:, c0:c0 + ncol], scalar=modT[:, 40 + dc, sq_:sq_ + 1],
                            in1=xf[:, dc, c0:c0 + ncol], op0=ALU.mult, op1=ALU.add), [RPS[pb], R_mod, R_xf], [R_xo[k2]])
                    S.dma("sp", dst_all[dc * 128:(dc + 1) * 128, tok0:tok0 + ntok], xo[k2][:, 0:ntok], reads=[R_xo[k2]], writes=[R_XO])
            S.barrier()
            A.pop()

        S.barrier()
        S.emit()
    return nc


def host_consts():
    c = {}
    c["ident"] = np.eye(128, dtype=np.float32)
    b64 = np.zeros((128, 128), np.float32)
    b64[:64, :64] = 1
    b64[64:, 64:] = 1
    c["blk64"] = b64
    b32 = np.zeros((128, 128), np.float32)
    for i in range(4):
        b32[i * 32:(i + 1) * 32, i * 32:(i + 1) * 32] = 1
    c["blk32"] = b32
    k = np.arange(128)[:, None]
    q = np.arange(128)[None, :]
    c["tri"] = (k <= q).astype(np.float32)
    c["cmask"] = ((k // 64) <= (q // 64)).astype(np.float32)
    c["m4"] = (~((k < 64) & (q >= 64))).astype(np.float32)
    sl = np.array(alibi_slopes(), np.float64)
    o = np.arange(33)[None, None, :]
    c["alcol"] = (sl[None, :, None] * (np.arange(128)[:, None, None] - 128.0 * o)).astype(np.float32)
    s0_ = np.zeros((128, 128), np.float32)
    s0_[0, :] = 1
    c["sel0"] = s0_
    s1_ = np.zeros((128, 128), np.float32)
    s1_[127, :] = 1
    c["sel127"] = s1_
    s2_ = np.zeros((128, 128), np.float32)
    s2_[31, :] = 1
    c["sel31"] = s2_
    ow = np.arange(36)[None, None, :] - 3
    c["alw"] = (sl[None, :, None] * (np.arange(128)[:, None, None] - 128.0 * ow)).astype(np.float32)
    c["cdiff"] = (np.where(k <= q, 1.0, np.exp(-2.0 * sl[:, None, None] * (k - q)[None])) * c["cmask"][None]).astype(np.float32)
    c["emd_arg"] = (-sl[:, None, None] * np.abs(q - k)[None] + sl[:, None, None] * q[None]).astype(np.float32)
    return c


def prep_core(inp, core, T, P, LB):
    f = np.float32
    b = core // 2
    s0 = 4 * core
    m = {}
    m["xp"] = np.ascontiguousarray(inp["x_prompt"][b, :T])
    m["xs"] = np.ascontiguousarray(inp["x_sample"][s0:s0 + 4].reshape(128, D))
    cv = np.concatenate([inp["c_prompt"][b:b + 1], inp["c_sample"][s0:s0 + 4]], 0)
    m["cT"] = np.ascontiguousarray(cv.reshape(5, 8, 128).transpose(2, 1, 0))
    for k_, n_ in [("w_mod", "w_mod"), ("w_in", "w_in"), ("w_out", "w_out"), ("w_gate", "w_ffn_gate"), ("w_up", "w_ffn_up"),
                   ("w_down", "w_ffn_down")]:
        m[k_] = inp[n_]
    m["b_modT"] = np.ascontiguousarray(inp["b_mod"].reshape(DEPTH, 48, 128).transpose(0, 2, 1))
    m["g1T"] = np.ascontiguousarray(inp["norm1_g"].reshape(DEPTH, 8, 128).transpose(0, 2, 1))
    m["g2T"] = np.ascontiguousarray(inp["norm2_g"].reshape(DEPTH, 8, 128).transpose(0, 2, 1))
    b_in = inp["b_in"]
    m["binT"] = np.ascontiguousarray(np.stack([b_in[:, c0:c0 + 128] for c0 in FM_COLS], 1).transpose(0, 2, 1))
    m["bgate"] = np.ascontiguousarray(np.stack([b_in[:, C_MI:C_MI + 4], b_in[:, C_MF:C_MF + 4]], -1))
    btm = np.concatenate([b_in[:, c0:c0 + n] for (c0, n) in TM_SRC], 1)
    m["btm"] = np.ascontiguousarray(np.broadcast_to(btm[:, None, :], (DEPTH, 128, NTM)))
    gq = []
    for g_, name in enumerate(["qk_g_fox", "qk_g_diff", "qk_g_band"]):
        gg = inp[name]
        rep = 128 // gg.shape[-1]
        for qk in range(2):
            col = np.tile(gg[:, qk, :], (1, rep))
            gq += [col, col]
    m["gqk"] = np.ascontiguousarray(np.stack(gq, -1))
    m["convw"] = np.ascontiguousarray(inp["conv_w"].reshape(DEPTH, 4, 4, 128).transpose(0, 3, 2, 1))
    m["convb"] = np.ascontiguousarray(inp["conv_b"].reshape(DEPTH, 4, 128).transpose(0, 2, 1))
    stc = inp["state_conv"][:, s0:s0 + 4]
    m["convst"] = np.ascontiguousarray(stc.reshape(DEPTH, 4, 3, 4, 128).transpose(0, 4, 3, 1, 2))
    m["lamb"] = np.ascontiguousarray(np.broadcast_to(inp["diff_lambda"][:, None], (DEPTH, 128, 4, 32)))
    m["gsub"] = np.ascontiguousarray(np.broadcast_to(inp["diff_subln_g"][:, None], (DEPTH, 128, 64)))
    m["gmh"] = np.ascontiguousarray(np.broadcast_to(inp["mlstm_norm_g"][:, None], (DEPTH, 128, 64)))
    tab = inp["band_rel_bias"]
    k = np.arange(128)[:, None]
    q = np.arange(128)[None, :]
    i0 = np.clip(q - k, -128, 128) + 128
    i1 = np.clip(128 + q - k, -128, 128) + 128
    i2 = np.full((128, 128), 256)
    m["bandT"] = np.ascontiguousarray(np.stack([tab[:, :, i0], tab[:, :, i1], tab[:, :, i2]], 2))
    m["c_fk"] = np.ascontiguousarray(inp["cache_fox_k"][:, s0:s0 + 4].reshape(DEPTH, 4, P, 256))
    m["c_fv"] = np.ascontiguousarray(inp["cache_fox_v"][:, s0:s0 + 4].reshape(DEPTH, 4, P, 256))
    m["c_flf"] = np.ascontiguousarray(inp["cache_fox_logf"][:, s0:s0 + 4])
    m["c_dk"] = np.ascontiguousarray(inp["cache_diff_k"][:, s0:s0 + 4].reshape(DEPTH, 4, P, 256))
    m["c_dv"] = np.ascontiguousarray(inp["cache_diff_v"][:, s0:s0 + 4].reshape(DEPTH, 4, P, 256))
    m["c_bk"] = np.ascontiguousarray(inp["cache_band_k"][:, s0:s0 + 4].reshape(DEPTH, 4, LB, 256))
    m["c_bv"] = np.ascontiguousarray(inp["cache_band_v"][:, s0:s0 + 4].reshape(DEPTH, 4, LB, 256))
    m["s_c"] = np.ascontiguousarray(inp["state_mlstm_c"][:, s0:s0 + 4])
    m["s_n"] = np.ascontiguousarray(inp["state_mlstm_n"][:, s0:s0 + 4])
    m["s_m"] = np.ascontiguousarray(inp["state_mlstm_m"][:, s0:s0 + 4])
    m["s_mcol"] = m["s_m"][..., None]
    m["s_mb"] = np.broadcast_to(m["s_m"][:, :, None, :], (DEPTH, 4, 128, 4))
    m.update(host_consts())
    return {k_: np.ascontiguousarray(v, dtype=f) for k_, v in m.items()}


def assemble(results, T, P, LB, nb, ns):
    ncores = len(results)
    pc = [min(2 * b, ncores - 1) for b in range(nb)] if ncores >= 2 * nb else list(range(nb))
    keep = min(512, T)

    def P_(fn):
        return np.stack([fn(results[c]) for c in pc], 0)

    def S_(fn):
        return np.concatenate([fn(results[c]) for c in range(ncores)], 0)
    y_prompt = P_(lambda r: r["YT"][:, :T].T)
    y_sample = S_(lambda r: r["YT"][:, T:].T.reshape(4, 32, D))
    outs = [y_prompt, y_sample]

    def fm_p(name, lo=0):
        return np.stack([P_(lambda r: r[name][l][:, lo:T].T) for l in range(DEPTH)], 0)

    def tm_p(name, lo=0):
        return np.stack([P_(lambda r: r[name][l][lo:T]) for l in range(DEPTH)], 0)

    def fm_s(name):
        return np.stack([S_(lambda r: r[name][l][:, T:].T.reshape(4, 32, -1)) for l in range(DEPTH)], 0)

    def tm_s(name):
        return np.stack([S_(lambda r: r[name][l][T:].reshape(4, 32, -1)) for l in range(DEPTH)], 0)
    B = nb
    p_fox_k = fm_p("o_fk").reshape(DEPTH, B, T, 4, 64)
    p_fox_v = tm_p("o_fv").reshape(DEPTH, B, T, 4, 64)
    p_fox_logf = tm_p("o_flf")
    p_diff_k = fm_p("o_dk").reshape(DEPTH, B, T, 4, 2, 32)
    p_diff_v = tm_p("o_dv").reshape(DEPTH, B, T, 4, 64)
    p_band_k = fm_p("o_bk", T - keep).reshape(DEPTH, B, keep, 4, 64)
    p_band_v = tm_p("o_bv", T - keep).reshape(DEPTH, B, keep, 4, 64)
    p_mc = np.stack([P_(lambda r: r["o_mc"][l][0, :, :, 0:64]) for l in range(DEPTH)], 0)
    p_mn = np.stack([P_(lambda r: r["o_mc"][l][0, :, :, 64]) for l in range(DEPTH)], 0)
    p_mm = np.stack([P_(lambda r: r["o_mm"][l][0]) for l in range(DEPTH)], 0)
    p_conv = np.stack([P_(lambda r: r["o_conv"][l][:, :, 0, :].transpose(2, 0, 1).reshape(3, 512)) for l in range(DEPTH)], 0)
    NS = 4 * ncores
    s_fox_k = fm_s("o_fk").reshape(DEPTH, NS, 32, 4, 64)
    s_fox_v = tm_s("o_fv").reshape(DEPTH, NS, 32, 4, 64)
    s_fox_logf = tm_s("o_flf")
    s_diff_k = fm_s("o_dk").reshape(DEPTH, NS, 32, 4, 2, 32)
    s_diff_v = tm_s("o_dv").reshape(DEPTH, NS, 32, 4, 64)
    s_band_k = fm_s("o_bk").reshape(DEPTH, NS, 32, 4, 64)
    s_band_v = tm_s("o_bv").reshape(DEPTH, NS, 32, 4, 64)
    s_mc = np.stack([S_(lambda r: r["o_mc"][l][1:5, :, :, 0:64]) for l in range(DEPTH)], 0)
    s_mn = np.stack([S_(lambda r: r["o_mc"][l][1:5, :, :, 64]) for l in range(DEPTH)], 0)
    s_mm = np.stack([S_(lambda r: r["o_mm"][l][1:5]) for l in range(DEPTH)], 0)
    s_conv = np.stack([S_(lambda r: r["o_conv"][l][:, :, 1:5, :].transpose(2, 3, 0, 1).reshape(4, 3, 512)) for l in range(DEPTH)], 0)
    outs += [p_fox_k, p_fox_v, p_fox_logf, p_diff_k, p_diff_v, p_band_k, p_band_v, p_mc, p_mn, p_mm, p_conv,
             s_fox_k, s_fox_v, s_fox_logf, s_diff_k, s_diff_v, s_band_k, s_band_v, s_mc, s_mn, s_mm, s_conv]
    return tuple(np.ascontiguousarray(o, dtype=np.float32) for o in outs)


_NC_CACHE = {}


def kernel(**inputs):
    inp = {k: np.asarray(v) for k, v in inputs.items()}
    T = inp["x_prompt"].shape[1]
    P = inp["cache_fox_k"].shape[2]
    LB = inp["cache_band_k"].shape[2]
    key = (T, P, LB)
    if key not in _NC_CACHE:
        _NC_CACHE[key] = build_program(T, P, LB)
    nc = _NC_CACHE[key]
    in_maps = [prep_core(inp, c, T, P, LB) for c in range(8)]
    res = run_bass_kernel_spmd(nc, in_maps, core_ids=list(range(8)))
    return assemble(res.results, T, P, LB, nb=inp["x_prompt"].shape[0], ns=inp["x_sample"].shape[0])
```

```python
import math
import numpy as np
import concourse.bass as bass
import concourse.mybir as mybir
from concourse.bass_utils import run_bass_kernel_spmd
from contextlib import ExitStack

F32 = mybir.dt.float32
BF16 = mybir.dt.bfloat16
AF = mybir.ActivationFunctionType
ALU = mybir.AluOpType
AX = mybir.AxisListType

D = 1024
DFF = 2816
NIN = 3340
EPS = 1e-6
DEPTH = 2


class Res:
    __slots__ = ("name", "w", "r")

    def __init__(self, name=""):
        self.name = name
        self.w = {}
        self.r = {}


class Sched:
    NDS = 48
    NHW = 32

    def __init__(self, nc, stack):
        self.nc = nc
        self.names = ["pe", "act", "dve", "pool", "sp"]
        self.streams = {e: [] for e in self.names}
        self.sems = {e: stack.enter_context(nc.semaphore("s_" + e)) for e in ["pe", "act", "dve", "pool"]}
        self.cnt = {e: 0 for e in self.sems}
        self.dsems = [stack.enter_context(nc.semaphore("d%d" % i)) for i in range(self.NDS)]
        self.dval = [0] * self.NDS
        self.drr = 0
        self.drr_sw = 0
        self.seen = {e: {} for e in self.names}
        self.know = {}
        self.EKEYS = [("e", "pe"), ("e", "act"), ("e", "dve"), ("e", "pool")]
        self.nops = 0

    def _waits(self, eng, reads, writes, extra=(), par=False):
        waits = {}

        def add(m):
            if m is None:
                return
            k, v = m
            if waits.get(k, 0) < v:
                waits[k] = v
        for r in reads:
            for k, v in r.w.items():
                add((k, v))
        for r in writes:
            if not par:
                for k, v in r.w.items():
                    add((k, v))
            for k, v in r.r.items():
                add((k, v))
        for m in extra:
            add(m)
        need = []
        seen = self.seen[eng]
        for k, v in sorted(waits.items(), key=lambda kv: -kv[1]):
            if eng == "pe" and k == ("e", "pe"):
                continue
            if seen.get(k, 0) >= v:
                continue
            seen[k] = v
            need.append((k, v))
            kn = self.know.get((k, v))
            if kn is not None:
                for k2, v2 in zip(self.EKEYS, kn):
                    if seen.get(k2, 0) < v2:
                        seen[k2] = v2
        return need

    def _snap(self, eng, mark):
        seen = self.seen[eng]
        self.know[mark] = tuple(seen.get(k2, 0) for k2 in self.EKEYS)

    def _commit(self, mark, reads, writes, par=False):
        k, v = mark
        for r in reads:
            if r.r.get(k, 0) < v:
                r.r[k] = v
        for r in writes:
            if par and not r.r:
                if r.w.get(k, 0) < v:
                    r.w[k] = v
            else:
                r.w = {k: v}
            r.r = {}

    def op(self, eng, fn, reads=(), writes=(), par=False):
        need = self._waits(eng, reads, writes, par=par)
        self.cnt[eng] += 1
        mark = (("e", eng), self.cnt[eng])
        self._snap(eng, mark)
        self.streams[eng].append((need, fn, mark))
        self._commit(mark, reads, writes, par=par)
        self.nops += 1
        return mark

    def dma(self, q, out, in_, reads=(), writes=(), par=False, **kw):
        if q == "pool":
            j = self.NHW + self.drr_sw
            self.drr_sw = (self.drr_sw + 1) % (self.NDS - self.NHW)
        else:
            j = self.drr
            self.drr = (j + 1) % self.NHW
        extra = []
        if self.dval[j] > 0:
            extra.append((("d", j), self.dval[j]))
        need = self._waits(q, reads, writes, extra, par=par)
        self.dval[j] += 16
        mark = (("d", j), self.dval[j])
        self._snap(q, mark)
        self.streams[q].append((need, (lambda e: e.dma_start(out=out, in_=in_, **kw)), mark))
        self._commit(mark, reads, writes, par=par)
        self.nops += 1
        return mark

    def barrier(self):
        for eng in self.names:
            need = []
            for e2 in self.sems:
                if e2 == eng:
                    continue
                k = ("e", e2)
                v = self.cnt[e2]
                if v > 0 and self.seen[eng].get(k, 0) < v:
                    self.seen[eng][k] = v
                    need.append((k, v))
            for j in range(self.NDS):
                k = ("d", j)
                v = self.dval[j]
                if v > 0 and self.seen[eng].get(k, 0) < v:
                    self.seen[eng][k] = v
                    need.append((k, v))
            if need:
                self.streams[eng].append((need, None, None))

    def _h(self, k):
        return self.sems[k[1]] if k[0] == "e" else self.dsems[k[1]]

    def replay(self, name, e):
        for need, fn, mark in self.streams[name]:
            for k, v in need:
                e.wait_ge(self._h(k), v)
            if fn is None:
                continue
            ins = fn(e)
            if mark is not None:
                ins.then_inc(self._h(mark[0]), 16 if mark[0][0] == "d" else 1)

    def emit(self):
        nc = self.nc
        with nc.Block() as block:
            @block.tensor
            def _(e):
                self.replay("pe", e)

            @block.scalar
            def _(e):
                self.replay("act", e)

            @block.vector
            def _(e):
                self.replay("dve", e)

            @block.gpsimd
            def _(e):
                self.replay("pool", e)

            @block.sync
            def _(e):
                self.replay("sp", e)


class Arena:
    def __init__(self, ap, nwords):
        self.ap = ap
        self.n = nwords
        self.off = 0
        self.marks = []

    def take(self, shape, dtype):
        size = 2 if dtype == BF16 else 4
        n = 1
        for s in shape:
            n *= s
        words = (n * size + 3) // 4
        words = (words + 7) // 8 * 8
        assert self.off + words <= self.n, ("arena overflow", self.off, words, self.n)
        a = self.ap[:, self.off:self.off + words]
        self.off += words
        if dtype != F32:
            a = a.bitcast(dtype)
        a = a[:, 0:n]
        if len(shape) == 2:
            a = a.rearrange("p (a b) -> p a b", a=shape[0], b=shape[1])
        elif len(shape) == 3:
            a = a.rearrange("p (a b c) -> p a b c", a=shape[0], b=shape[1], c=shape[2])
        return a

    def push(self):
        self.marks.append(self.off)

    def pop(self):
        self.off = self.marks.pop()


def alibi_slopes():
    return [2.0 ** (-8.0 * (i + 1.0) / 4) for i in range(4)]


C_FQ, C_FK, C_FV, C_FF = 0, 256, 512, 768
C_DQ, C_DK, C_DV = 772, 1028, 1284
C_MQK, C_MV, C_MI, C_MF, C_MO = 1540, 2052, 2308, 2312, 2316
C_BQ, C_BK, C_BV = 2572, 2828, 3084
FM_COLS = [C_FQ, C_FQ + 128, C_FK, C_FK + 128, C_DQ, C_DQ + 128, C_DK, C_DK + 128,
           C_BQ, C_BQ + 128, C_BK, C_BK + 128, C_MQK, C_MQK + 128, C_MQK + 256, C_MQK + 384]
TM_SRC = [(C_FV, 256), (C_DV, 256), (C_BV, 256), (C_MV, 256), (C_MO, 256), (C_FF, 4)]
NTM = 1284


def build_program(T, P, LB, debug=False):
    TT = T + 128
    NPB = T // 512
    nc = bass.Bass("TRN2", target_bir_lowering=False)

    def din(name, shape, dt=F32):
        return nc.dram_tensor(name, list(shape), dt, kind="ExternalInput").ap()

    def dout(name, shape, dt=F32):
        return nc.dram_tensor(name, list(shape), dt, kind="ExternalOutput").ap()

    def dscr(name, shape, dt=F32):
        return nc.dram_tensor(name, list(shape), dt, kind="Internal").ap()

    I = {}
    I["xp"] = din("xp", [T, D])
    I["xs"] = din("xs", [128, D])
    I["cT"] = din("cT", [128, 8, 5])
    I["w_mod"] = din("w_mod", [DEPTH, D, 6 * D])
    I["b_modT"] = din("b_modT", [DEPTH, 128, 48])
    I["w_in"] = din("w_in", [DEPTH, D, NIN])
    I["w_out"] = din("w_out", [DEPTH, D, D])
    I["w_gate"] = din("w_gate", [DEPTH, D, DFF])
    I["w_up"] = din("w_up", [DEPTH, D, DFF])
    I["w_down"] = din("w_down", [DEPTH, DFF, D])
    I["g1T"] = din("g1T", [DEPTH, 128, 8])
    I["g2T"] = din("g2T", [DEPTH, 128, 8])
    I["binT"] = din("binT", [DEPTH, 128, 16])
    I["bgate"] = din("bgate", [DEPTH, 4, 2])
    I["btm"] = din("btm", [DEPTH, 128, NTM])
    I["gqk"] = din("gqk", [DEPTH, 128, 12])
    I["convw"] = din("convw", [DEPTH, 128, 4, 4])
    I["convb"] = din("convb", [DEPTH, 128, 4])
    I["convst"] = din("convst", [DEPTH, 128, 4, 4, 3])
    I["lamb"] = din("lamb", [DEPTH, 128, 4, 32])
    I["gsub"] = din("gsub", [DEPTH, 128, 64])
    I["gmh"] = din("gmh", [DEPTH, 128, 64])
    I["bandT"] = din("bandT", [DEPTH, 4, 3, 128, 128])
    I["c_fk"] = din("c_fk", [DEPTH, 4, P, 256])
    I["c_fv"] = din("c_fv", [DEPTH, 4, P, 256])
    I["c_flf"] = din("c_flf", [DEPTH, 4, P, 4])
    I["c_dk"] = din("c_dk", [DEPTH, 4, P, 256])
    I["c_dv"] = din("c_dv", [DEPTH, 4, P, 256])
    I["c_bk"] = din("c_bk", [DEPTH, 4, LB, 256])
    I["c_bv"] = din("c_bv", [DEPTH, 4, LB, 256])
    I["s_c"] = din("s_c", [DEPTH, 4, 4, 64, 64])
    I["s_n"] = din("s_n", [DEPTH, 4, 4, 64])
    I["s_m"] = din("s_m", [DEPTH, 4, 4])
    I["ident"] = din("ident", [128, 128])
    I["blk64"] = din("blk64", [128, 128])
    I["blk32"] = din("blk32", [128, 128])
    I["tri"] = din("tri", [128, 128])
    I["cmask"] = din("cmask", [128, 128])
    I["m4"] = din("m4", [128, 128])
    I["alcol"] = din("alcol", [128, 4, 33])
    I["emd_arg"] = din("emd_arg", [4, 128, 128])
    I["sel0"] = din("sel0", [128, 128])
    I["alw"] = din("alw", [128, 4, 36])
    I["cdiff"] = din("cdiff", [4, 128, 128])
    I["sel127"] = din("sel127", [128, 128])
    I["sel31"] = din("sel31", [128, 128])
    I["s_mcol"] = din("s_mcol", [DEPTH, 4, 4, 1])
    I["s_mb"] = din("s_mb", [DEPTH, 4, 128, 4])

    O = {}
    O["YT"] = dout("YT", [D, TT])
    O["o_fk"] = dout("o_fk", [DEPTH, 256, TT])
    O["o_fv"] = dout("o_fv", [DEPTH, TT, 256])
    O["o_flf"] = dout("o_flf", [DEPTH, TT, 4])
    O["o_dk"] = dout("o_dk", [DEPTH, 256, TT])
    O["o_dv"] = dout("o_dv", [DEPTH, TT, 256])
    O["o_bk"] = dout("o_bk", [DEPTH, 256, TT])
    O["o_bv"] = dout("o_bv", [DEPTH, TT, 256])
    O["o_mc"] = dout("o_mc", [DEPTH, 5, 4, 64, 65])
    O["o_mm"] = dout("o_mm", [DEPTH, 5, 4])
    O["o_conv"] = dout("o_conv", [DEPTH, 4, 128, 5, 3])
    if debug:
        O["dbg"] = dout("dbg", [128, 4096])

    XT = [dscr("XT0", [D, TT]), dscr("XT1", [D, TT])]
    XM = dscr("XM", [D, TT])
    QS = [dscr("QS%d" % g, [256, TT], BF16) for g in range(4)]
    KS = [dscr("KS%d" % g, [256, TT], BF16) for g in range(4)]
    VS = [dscr("VS%d" % g, [TT, 256], BF16) for g in range(4)]
    SIGO = dscr("SIGO", [TT, 256])
    GI = dscr("GI", [4, TT])
    GF = dscr("GF", [4, TT])
    LOGF = dscr("LOGF", [TT, 4])
    MIX = dscr("MIX", [TT, D], BF16)

    st = ExitStack()
    with st:
        S = Sched(nc, st)
        NW = 52000
        arena_t = st.enter_context(nc.sbuf_tensor("arena", [128, NW], F32))
        A = Arena(arena_t[:], NW)
        PS = [st.enter_context(nc.psum_tensor("ps%d" % i, [128, 512], F32)) for i in range(8)]
        RPS = [Res("ps%d" % i) for i in range(8)]

        ident_f = A.take([128], F32)
        ident_b = A.take([128], BF16)
        ones_b = A.take([128], BF16)
        blk64 = A.take([128], BF16)
        blk32 = A.take([128], BF16)
        tri_f = A.take([128], F32)
        cmask_f = A.take([128], F32)
        m4_f = A.take([128], F32)
        alcol = A.take([4, 33], F32)
        modT = A.take([48, 5], F32)
        A1 = A.take([8, 5], F32)
        A2 = A.take([8, 5], F32)
        R_const = Res("const")
        R_mod = Res("mod")
        tmpc = A.take([128], F32)
        S.dma("sp", ident_f, I["ident"], writes=[R_const])
        S.dma("sp", tri_f, I["tri"], writes=[R_const])
        S.dma("sp", cmask_f, I["cmask"], writes=[R_const])
        S.dma("sp", m4_f, I["m4"], writes=[R_const])
        S.dma("sp", alcol, I["alcol"], writes=[R_const])
        S.dma("pool", blk64, I["blk64"], writes=[R_const])
        S.dma("pool", blk32, I["blk32"], writes=[R_const])
        S.dma("pool", ident_b, I["ident"], writes=[R_const])
        S.op("dve", lambda e: e.memset(ones_b, 1.0), writes=[R_const])
        S.barrier()

        blocks = [(i * 512, 512, [(0, 512, 0)]) for i in range(NPB)]
        blocks.append((T, 128, [(32 * s, 32, 1 + s) for s in range(4)]))

        def act(fn, reads, writes, par=False):
            return S.op("act", fn, reads, writes, par=par)

        def dve(fn, reads, writes, par=False):
            return S.op("dve", fn, reads, writes, par=par)

        def pe(fn, reads, writes, par=False):
            return S.op("pe", fn, reads, writes, par=par)

        modTn = A.take([48, 5], F32)
        A1n = A.take([8, 5], F32)
        A2n = A.take([8, 5], F32)
        R_modn = Res("modn")

        def gen_M(l, modT_d, A1_d, A2_d, R_d):
            cT = A.take([8, 5], F32)
            scT = A.take([8, 5], BF16)
            sgm = A.take([8, 5], F32)
            bmod = A.take([48], F32)
            g1 = A.take([8], F32)
            g2 = A.take([8], F32)
            wm = [A.take([8, 1536], BF16) for _ in range(2)]
            R_wm = [Res(), Res()]
            R_c = Res()
            R_sc = Res()
            S.dma("sp", cT, I["cT"], writes=[R_c])
            S.dma("sp", bmod, I["b_modT"][l], writes=[R_c])
            S.dma("sp", g1, I["g1T"][l], writes=[R_c])
            S.dma("sp", g2, I["g2T"][l], writes=[R_c])
            S.op("act", lambda e: e.activation(out=sgm, in_=cT, func=AF.Sigmoid), [R_c], [R_sc])
            S.op("dve", lambda e: e.tensor_tensor(out=scT, in0=sgm, in1=cT, op=ALU.mult), [R_sc, R_c], [R_sc])
            for sl in range(4):
                b = sl % 2
                S.dma("pool", wm[b], I["w_mod"][l][:, sl * 1536:(sl + 1) * 1536].rearrange("(c p) n -> p c n", p=128),
                      writes=[R_wm[b]])
                for jj in range(12):
                    j = sl * 12 + jj

                    def mm(e, b=b, jj=jj):
                        for c in range(8):
                            ins = e.matmul(out=PS[0][:, 0:5], lhsT=wm[b][:, c, jj * 128:(jj + 1) * 128], rhs=scT[:, c, :],
                                           start=(c == 0), stop=(c == 7))
                        return ins
                    S.op("pe", mm, [R_wm[b], R_sc], [RPS[0]])
                    S.op("dve", lambda e, j=j: e.tensor_scalar(out=modT_d[:, j, :], in0=PS[0][:, 0:5], scalar1=bmod[:, j:j + 1],
                                                               scalar2=None, op0=ALU.add), [RPS[0], R_c], [R_d])
                    if jj % 2 == 1:
                        yield
            for s5 in range(5):
                S.op("dve", lambda e, s5=s5: e.scalar_tensor_tensor(out=A1_d[:, :, s5], in0=modT_d[:, 8:16, s5], scalar=1.0, in1=g1,
                                                                    op0=ALU.add, op1=ALU.mult), [R_d, R_c], [R_d])
                S.op("dve", lambda e, s5=s5: e.scalar_tensor_tensor(out=A2_d[:, :, s5], in0=modT_d[:, 32:40, s5], scalar=1.0, in1=g2,
                                                                    op0=ALU.add, op1=ALU.mult), [R_d, R_c], [R_d])
            yield

        R_MIX = Res("MIX")
        R_XT = Res("XT")

        ones_f = A.take([128], F32)
        sel0 = A.take([128], F32)
        emd = A.take([4, 128], F32)
        alw = A.take([4, 36], F32)
        cdiff = A.take([4, 128], F32)
        S.dma("sp", alw, I["alw"], writes=[R_const])
        S.dma("sp", cdiff, I["cdiff"].rearrange("h k q -> k h q"), writes=[R_const])
        S.dma("sp", sel0, I["sel0"], writes=[R_const])
        S.dma("sp", emd, I["emd_arg"].rearrange("h k q -> k h q"), writes=[R_const])
        S.op("dve", lambda e: e.memset(ones_f, 1.0), writes=[R_const])
        S.op("act", lambda e: e.activation(out=emd, in_=emd, func=AF.Exp), reads=[R_const], writes=[R_const])
        for h_ in range(4):
            S.op("dve", lambda e, h_=h_: e.tensor_tensor(out=emd[:, h_, :], in0=emd[:, h_, :], in1=cmask_f, op=ALU.mult),
                 reads=[R_const], writes=[R_const])
        S.barrier()
        NKB_P = T // 128
        NCB = P // 128
        NLB = LB // 128

        sel127 = A.take([128], F32)
        sel31 = A.take([128], F32)
        S.dma("sp", sel127, I["sel127"], writes=[R_const])
        S.dma("sp", sel31, I["sel31"], writes=[R_const])

        def phase_ML(l):
            A.push()
            NTM_ = max(T, 128)
            gi_r = A.take([NTM_], F32)
            gf_r = A.take([NTM_], F32)
            G_r = A.take([NTM_], F32)
            MM_r = A.take([NTM_], F32)
            mt_r = A.take([NTM_], F32)
            ones_r = A.take([NTM_], F32)
            dve(lambda e: e.memset(ones_r[0:4, :], 1.0), [], [R_const])
            NCM = max(T // 128, 1)
            cols = A.take([NCM, 12], F32)
            MMe = A.take([NCM, 4], F32)
            MMp = A.take([NCM, 4], F32)
            egs = A.take([NCM, 4], F32)
            eM = A.take([NCM, 4], F32)
            acol = A.take([NCM, 4], F32)
            dec = A.take([NCM, 4], F32)
            emt = A.take([NCM, 4], F32)
            m0c = A.take([1], F32)
            gmh = A.take([64], F32)
            Cn = A.take([2, 65], F32)
            Cnb = A.take([2, 65], BF16)
            Cnb2 = [Cnb, A.take([2, 65], BF16)]
            R_Cnb2 = [Res(), Res()]
            dec2 = A.take([NCM, 2], F32)
            R_osq = Res()
            R_oms = Res()
            qTm = [A.take([2, 128], BF16) for _ in range(2)]
            kTm = [A.take([2, 128], BF16) for _ in range(2)]
            vst = [A.take([256], BF16) for _ in range(2)]
            VAm = [A.take([4, 65], BF16) for _ in range(2)]
            sig = [A.take([256], F32) for _ in range(2)]
            R_in = [Res(), Res()]
            R_VAm = [Res(), Res()]
            kw = A.take([256], BF16)
            wT = [A.take([128], BF16) for _ in range(2)]
            R_wT = [Res(), Res()]
            tmpA = A.take([4, 65], F32)
            resm = A.take([4, 65], F32)
            den = A.take([4], F32)
            o4 = A.take([4, 64], F32)
            osq = A.take([4, 64], F32)
            oms = A.take([4], F32)
            o4b = [A.take([256], BF16) for _ in range(2)]
            R_o4b = [Res(), Res()]
            R_rows, R_cols, R_ex, R_Cn, R_Cnb, R_kw, R_tmp, R_res, R_o4, R_g = [Res() for _ in range(10)]
            S.dma("sp", gmh, I["gmh"][l], writes=[R_g])
            for vm in VAm:
                dve(lambda e, vm=vm: e.memset(vm[:, :, 64:65], 1.0), [], [R_VAm[0], R_VAm[1]])
            LN8 = math.log(0.125)
            mgen = gen_M(l + 1, modTn, A1n, A2n, R_modn) if l + 1 < DEPTH else iter(())
            SB_ = [2, 6]
            PB_ = [4, 7]

            def run_ml(seq, tok0, NTOK, L):
                NC_ = NTOK // L
                sel = sel127 if L == 128 else sel31
                S.dma("sp", gi_r[0:4, 0:NTOK], GI[:, tok0:tok0 + NTOK], writes=[R_rows], par=True)
                S.dma("sp", gf_r[0:4, 0:NTOK], GF[:, tok0:tok0 + NTOK], writes=[R_rows], par=True)
                if seq == 0:
                    init = 0.0
                    dve(lambda e: e.memset(MMp[:, 0, :], 0.0), [], [R_ex])
                    dve(lambda e: e.memset(Cn, 0.0), [], [R_Cn])
                    rdi = []
                else:
                    S.dma("sp", m0c[0:4, :], I["s_mcol"][l, seq - 1], writes=[R_rows])
                    S.dma("sp", MMp[:, 0, :], I["s_mb"][l, seq - 1], writes=[R_ex])
                    init = m0c[0:4, 0:1]
                    for h in range(4):
                        r0 = (h % 2) * 64
                        S.dma("sp", Cn[r0:r0 + 64, h // 2, 0:64], I["s_c"][l, seq - 1, h], writes=[R_Cn], par=True)
                        S.dma("sp", Cn[r0:r0 + 64, h // 2, 64:65], I["s_n"][l, seq - 1, h].rearrange("(k o) -> k o", o=1), writes=[R_Cn], par=True)
                dve(lambda e: e.tensor_copy(out=Cnb2[0], in_=Cn), [R_Cn], [R_Cnb2[0]])
                dve(lambda e: e.tensor_tensor_scan(out=mt_r[0:4, 0:NTOK], data0=ones_r[0:4, 0:NTOK], data1=gf_r[0:4, 0:NTOK],
                                                   initial=0.0, op0=ALU.mult, op1=ALU.add), [R_rows, R_const], [R_rows])
                dve(lambda e: e.tensor_tensor(out=G_r[0:4, 0:NTOK], in0=gi_r[0:4, 0:NTOK], in1=mt_r[0:4, 0:NTOK], op=ALU.subtract), [R_rows], [R_rows])
                dve(lambda e: e.tensor_tensor_scan(out=MM_r[0:4, 0:NTOK], data0=ones_r[0:4, 0:NTOK], data1=G_r[0:4, 0:NTOK],
                                                   initial=init, op0=ALU.mult, op1=ALU.max), [R_rows, R_const], [R_rows])
                dve(lambda e: e.tensor_tensor(out=mt_r[0:4, 0:NTOK], in0=mt_r[0:4, 0:NTOK], in1=MM_r[0:4, 0:NTOK], op=ALU.add), [R_rows], [R_rows])
                if debug == "ml0":
                    return
                S.dma("pool", O["o_mm"][l, seq].rearrange("(h o) -> h o", o=1), mt_r[0:4, NTOK - 1:NTOK], reads=[R_rows])
                if debug == "ml1":
                    return
                if L < 128:
                    dve(lambda e: e.memset(cols[:, 0:NC_, :], 0.0), [], [R_cols])
                for c in range(NC_):
                    def trc(e, c=c):
                        for i_, rr_ in enumerate([G_r, MM_r, mt_r]):
                            ins = e.matmul(out=PS[0][0:L, i_ * 4:(i_ + 1) * 4], lhsT=rr_[0:4, c * L:(c + 1) * L], rhs=ident_f[0:4, 0:4],
                                           start=True, stop=True, skip_group_check=True)
                        return ins
                    pe(trc, [R_rows, R_const], [RPS[0]])
                    act(lambda e, c=c: e.activation(out=cols[0:L, c, :], in_=PS[0][0:L, 0:12], func=AF.Copy), [RPS[0]], [R_cols])
                if debug == "ml2":
                    return
                n4 = NC_ * 4
                pe(lambda e: e.matmul(out=PS[1][:, 0:n4], lhsT=sel, rhs=cols[:, 0:NC_, 4:8], start=True, stop=True), [R_cols, R_const], [RPS[1]])
                dve(lambda e: e.tensor_copy(out=MMe[:, 0:NC_, :], in_=PS[1][:, 0:n4].rearrange("p (c h) -> p c h", h=4)), [RPS[1]], [R_ex])
                if NC_ > 1:
                    dve(lambda e: e.tensor_copy(out=MMp[:, 1:NC_, :], in_=MMe[:, 0:NC_ - 1, :]), [R_ex], [R_ex])
                Gc = cols[:, 0:NC_, 0:4]
                MMc = cols[:, 0:NC_, 4:8]
                mtc = cols[:, 0:NC_, 8:12]
                for (dst_, a_, b_, bias_) in [(egs, Gc, MMe, LN8), (eM, MMe, MMc, 0.0), (acol, MMp, MMc, 0.0), (dec, MMp, MMe, 0.0)]:
                    dve(lambda e, dst_=dst_, a_=a_, b_=b_: e.tensor_tensor(out=dst_[:, 0:NC_, :], in0=a_ if a_ is Gc else a_[:, 0:NC_, :],
                                                                           in1=b_ if b_ is MMc else b_[:, 0:NC_, :], op=ALU.subtract),
                        [R_cols, R_ex], [R_ex])
                    if bias_ != 0.0:
                        dve(lambda e, dst_=dst_, bias_=bias_: e.tensor_scalar(out=dst_[:, 0:NC_, :], in0=dst_[:, 0:NC_, :], scalar1=bias_, scalar2=None, op0=ALU.add),
                            [R_ex], [R_ex])
                    act(lambda e, dst_=dst_: e.activation(out=dst_[:, 0:NC_, :], in_=dst_[:, 0:NC_, :], func=AF.Exp), [R_ex], [R_ex])
                act(lambda e: e.activation(out=emt[:, 0:NC_, :], in_=mtc, func=AF.Exp, scale=-1.0), [R_cols], [R_ex])
                dve(lambda e: e.tensor_copy(out=dec2[0:64, 0:NC_, :], in_=dec[0:64, 0:NC_, 0:4:2]), [R_ex], [R_ex])
                dve(lambda e: e.tensor_copy(out=dec2[64:128, 0:NC_, :], in_=dec[64:128, 0:NC_, 1:4:2]), [R_ex], [R_ex])
                if debug == "ml3":
                    return
                for c in range(NC_):
                    ib = c % 2
                    t0 = tok0 + c * L
                    for hp in range(2):
                        S.dma("sp", qTm[ib][:, hp, 0:L], QS[3][hp * 128:(hp + 1) * 128, t0:t0 + L], writes=[R_in[ib]], par=True)
                        S.dma("sp", kTm[ib][:, hp, 0:L], KS[3][hp * 128:(hp + 1) * 128, t0:t0 + L], writes=[R_in[ib]], par=True)
                    S.dma("sp", vst[ib][0:L, :], VS[3][t0:t0 + L, :], writes=[R_in[ib]], par=True)
                    S.dma("sp", sig[ib][0:L, :], SIGO[t0:t0 + L, :], writes=[R_in[ib]], par=True)
                    act(lambda e, ib=ib: e.activation(out=VAm[ib][0:L, :, 0:64], in_=vst[ib][0:L, :].rearrange("p (h d) -> p h d", h=4), func=AF.Copy),
                        [R_in[ib]], [R_VAm[ib]])
                    if debug == "ml4":
                        continue
                    psb = PS[1][:, :].bitcast(BF16)

                    def trk2(e, ib=ib, psb=psb):
                        for hp in range(2):
                            ins = e.transpose(out=psb[0:L, hp * 128:(hp + 1) * 128], in_=kTm[ib][:, hp, 0:L], identity=ident_b)
                        return ins
                    pe(trk2, [R_in[ib], R_const], [RPS[1]])
                    dve(lambda e, c=c, psb=psb: e.tensor_tensor(out=kw[0:L, :].rearrange("p (h d) -> p h d", h=4),
                                                                in0=psb[0:L, 0:256].rearrange("p (h d) -> p h d", h=4),
                                                                in1=egs[0:L, c, :].unsqueeze(2).to_broadcast([L, 4, 64]), op=ALU.mult),
                        [RPS[1], R_ex], [R_kw])

                    if debug == "ml5":
                        continue

                    def qk(e, ib=ib):
                        for h in range(4):
                            r0 = (h % 2) * 64
                            ins = e.matmul(out=PS[SB_[h % 2]][0:L, h * L:(h + 1) * L], lhsT=kTm[ib][r0:r0 + 64, h // 2, 0:L], rhs=qTm[ib][r0:r0 + 64, h // 2, 0:L],
                                           start=True, stop=True, skip_group_check=True)
                        return ins
                    pe(qk, [R_in[ib]], [RPS[2], RPS[6]])

                    def p2(e, ib=ib, c=c):
                        for h in range(4):
                            r0 = (h % 2) * 64
                            ins = e.matmul(out=PS[PB_[h % 2]][0:L, h * 65:(h + 1) * 65], lhsT=qTm[ib][r0:r0 + 64, h // 2, 0:L], rhs=Cnb2[c % 2][r0:r0 + 64, h // 2, :],
                                           start=True, stop=True, skip_group_check=True)
                        return ins
                    pe(p2, [R_in[ib], R_Cnb2[c % 2]], [RPS[4], RPS[7]])
                    if debug == "ml6":
                        continue
                    for h in range(4):
                        wb_ = h % 2
                        dve(lambda e, h=h, c=c, wb_=wb_: e.scalar_tensor_tensor(out=wT[wb_][0:L, 0:L], in0=PS[SB_[h % 2]][0:L, h * L:(h + 1) * L],
                                                                                scalar=egs[0:L, c, h:h + 1], in1=tri_f[0:L, 0:L],
                                                                                op0=ALU.mult, op1=ALU.mult), [RPS[SB_[h % 2]], R_ex, R_const], [R_wT[wb_]])
                        pe(lambda e, h=h, wb_=wb_, ib=ib: e.matmul(out=PS[3][0:L, h * 65:(h + 1) * 65], lhsT=wT[wb_][0:L, 0:L], rhs=VAm[ib][0:L, h, :],
                                                                   start=True, stop=True, skip_group_check=True), [R_wT[wb_], R_VAm[ib]], [RPS[3]])
                        pe(lambda e, h=h, ib=ib: e.matmul(out=PS[5][(h % 2) * 64:(h % 2) * 64 + 64, (h // 2) * 65:(h // 2) * 65 + 65], lhsT=kw[0:L, h * 64:(h + 1) * 64],
                                                          rhs=VAm[ib][0:L, h, :], start=True, stop=True, skip_group_check=True,
                                                          tile_position=(0, (h % 2) * 64)), [R_kw, R_VAm[ib]], [RPS[5]])
                    cb_n = (c + 1) % 2
                    for hp_ in range(2):
                        dve(lambda e, hp_=hp_, c=c: e.scalar_tensor_tensor(out=Cn[:, hp_, :], in0=Cn[:, hp_, :], scalar=dec2[:, c, hp_:hp_ + 1],
                                                                         in1=PS[5][:, hp_ * 65:(hp_ + 1) * 65], op0=ALU.mult, op1=ALU.add),
                            [R_Cn, R_ex, RPS[5]], [R_Cn])
                    dve(lambda e, cb_n=cb_n: e.tensor_copy(out=Cnb2[cb_n], in_=Cn), [R_Cn], [R_Cnb2[cb_n]])
                    if seq == 0:
                        next(mgen, None)
                    if debug == "ml7":
                        continue
                    for h in range(4):
                        act(lambda e, h=h, c=c: e.activation(out=tmpA[0:L, h, :], in_=PS[PB_[h % 2]][0:L, h * 65:(h + 1) * 65], func=AF.Identity, scale=acol[0:L, c, h:h + 1]),
                            [RPS[PB_[h % 2]], R_ex], [R_tmp])
                    dve(lambda e, c=c: e.tensor_tensor(out=resm[0:L], in0=PS[3][0:L, 0:260].rearrange("p (h d) -> p h d", d=65),
                                                       in1=eM[0:L, c, :].unsqueeze(2).to_broadcast([L, 4, 65]), op=ALU.mult), [RPS[3], R_ex], [R_res])
                    dve(lambda e: e.tensor_tensor(out=resm[0:L], in0=resm[0:L], in1=tmpA[0:L], op=ALU.add), [R_res, R_tmp], [R_res])
                    act(lambda e: e.activation(out=den[0:L, :], in_=resm[0:L, :, 64], func=AF.Abs), [R_res], [R_res])
                    dve(lambda e, c=c: e.tensor_tensor(out=den[0:L, :], in0=den[0:L, :], in1=emt[0:L, c, :], op=ALU.max), [R_res, R_ex], [R_res])
                    dve(lambda e: e.reciprocal(out=den[0:L, :], in_=den[0:L, :]), [R_res], [R_res])
                    dve(lambda e: e.tensor_tensor(out=o4[0:L], in0=resm[0:L, :, 0:64], in1=den[0:L, :].unsqueeze(2).to_broadcast([L, 4, 64]), op=ALU.mult),
                        [R_res], [R_o4])
                    S.op("pool", lambda e, ib=ib: e.tensor_tensor(out=o4[0:L], in0=o4[0:L], in1=sig[ib][0:L, :].rearrange("p (h d) -> p h d", h=4), op=ALU.mult),
                         [R_o4, R_in[ib]], [R_o4])
                    S.op("pool", lambda e: e.tensor_tensor(out=osq[0:L], in0=o4[0:L], in1=o4[0:L], op=ALU.mult), [R_o4], [R_osq])
                    dve(lambda e: e.tensor_reduce(out=oms[0:L, :], in_=osq[0:L], axis=AX.X, op=ALU.add), [R_osq], [R_oms])
                    act(lambda e: e.activation(out=oms[0:L, :], in_=oms[0:L, :], func=AF.Ln, scale=1.0 / 64, bias=EPS), [R_oms], [R_oms])
                    act(lambda e: e.activation(out=oms[0:L, :], in_=oms[0:L, :], func=AF.Exp, scale=-0.5), [R_oms], [R_oms])
                    S.op("pool", lambda e: e.tensor_tensor(out=o4[0:L], in0=o4[0:L], in1=oms[0:L, :].unsqueeze(2).to_broadcast([L, 4, 64]), op=ALU.mult),
                         [R_o4, R_oms, R_osq], [R_o4])
                    ob = c % 2
                    S.op("pool", lambda e, ob=ob: e.tensor_tensor(out=o4b[ob][0:L, :].rearrange("p (h d) -> p h d", h=4), in0=o4[0:L],
                                                                 in1=gmh[0:L, :].unsqueeze(1).to_broadcast([L, 4, 64]), op=ALU.mult), [R_o4, R_g], [R_o4b[ob]])
                    S.dma("pool", MIX[t0:t0 + L, 512:768], o4b[ob][0:L, :], reads=[R_o4b[ob]], writes=[R_MIX], par=True)
                for h in range(4):
                    r0 = (h % 2) * 64
                    S.dma("pool", O["o_mc"][l, seq, h], Cn[r0:r0 + 64, h // 2, :], reads=[R_Cn])

            run_ml(0, 0, T, 128)
            for _ in mgen:
                pass
            for s_ in range(4):
                if debug == "mlp":
                    break
                run_ml(1 + s_, T + 32 * s_, 32, 32)
            S.barrier()
            A.pop()

        def phase_B(l, lam_init):
            A.push()
            emb = A.take([4, 5, 128], F32)
            R_emb = Res()
            for h_ in range(4):
                for t_, dst_ in [(0, 4), (1, 1), (2, 2)]:
                    S.dma("sp", emb[:, h_, dst_, :], I["bandT"][l, h_, t_], writes=[R_emb])
            act(lambda e: e.activation(out=emb[:, :, 1:3, :], in_=emb[:, :, 1:3, :], func=AF.Exp), [R_emb], [R_emb])
            act(lambda e: e.activation(out=emb[:, :, 4, :], in_=emb[:, :, 4, :], func=AF.Exp), [R_emb], [R_emb])
            for h_ in range(4):
                dve(lambda e, h_=h_: e.tensor_tensor(out=emb[:, h_, 0, :], in0=emb[:, h_, 4, :], in1=cmask_f, op=ALU.mult), [R_emb, R_const], [R_emb])
                dve(lambda e, h_=h_: e.tensor_tensor(out=emb[:, h_, 3, :], in0=emb[:, h_, 2, :], in1=m4_f, op=ALU.mult), [R_emb, R_const], [R_emb])
            lamt = A.take([4, 32], F32)
            lamp = A.take([2, 32], F32)
            lam2 = A.take([2], F32)
            nlam = A.take([1], F32)
            gsub = A.take([64], F32)
            R_lam = Res()
            S.dma("sp", lamt, I["lamb"][l], writes=[R_lam])
            S.dma("sp", gsub, I["gsub"][l], writes=[R_lam])
            dve(lambda e: e.tensor_tensor(out=lamp[:, 0, :], in0=lamt[:, 0, :], in1=lamt[:, 1, :], op=ALU.mult), [R_lam], [R_lam])
            dve(lambda e: e.tensor_tensor(out=lamp[:, 1, :], in0=lamt[:, 2, :], in1=lamt[:, 3, :], op=ALU.mult), [R_lam], [R_lam])
            dve(lambda e: e.tensor_reduce(out=lam2, in_=lamp, axis=AX.X, op=ALU.add), [R_lam], [R_lam])
            act(lambda e: e.activation(out=lam2, in_=lam2, func=AF.Exp), [R_lam], [R_lam])
            dve(lambda e: e.tensor_tensor(out=nlam, in0=lam2[:, 1:2], in1=lam2[:, 0:1], op=ALU.subtract), [R_lam], [R_lam])
            dve(lambda e: e.tensor_scalar(out=nlam, in0=nlam, scalar1=-lam_init, scalar2=None, op0=ALU.add), [R_lam], [R_lam])
            dve(lambda e: e.tensor_scalar(out=gsub, in0=gsub, scalar1=(1.0 - lam_init), scalar2=None, op0=ALU.mult), [R_lam], [R_lam])

            KT = A.take([2, T], BF16) if T >= P + 32 else A.take([2, P + 32], BF16)
            VST = A.take([max(NKB_P, NCB + 1), 256], BF16)
            VSTF = A.take([256], F32)
            VA = A.take([max(NKB_P, NCB + 1), 4, 65], BF16)
            QT = A.take([2, T], BF16)
            FB = A.take([max(NKB_P, 1), max(NKB_P, NCB + 1), 4], F32)
            lfc = A.take([max(NKB_P, NCB + 1), 4], F32)
            cum = A.take([max(NKB_P, NCB + 1), 4], F32)
            totb = A.take([4, max(NKB_P, NCB + 1)], F32)
            crefbc = A.take([max(NKB_P, NCB + 1), 4], F32)
            ktmA = A.take([max(NCB, 1), 256], F32)
            vstA = A.take([max(NCB, 1), 256], F32)
            R_ktmA = Res()
            R_vstA = Res()
            R_KT, R_VST, R_VSTF, R_VA, R_QT, R_FB, R_lfc, R_cum = [Res() for _ in range(8)]
            pt = [A.take([512], BF16) for _ in range(4)]
            R_pt = [Res() for _ in range(4)]
            pf = [A.take([128], F32) for _ in range(2)]
            R_pf = [Res() for _ in range(2)]
            og = [A.take([4, 256], BF16) for _ in range(2)]
            R_og = [Res(), Res()]
            rec = [A.take([4, 2], F32) for _ in range(2)]
            R_rec = [Res(), Res()]
            d0 = A.take([4, 64], F32)
            d1 = A.take([4, 64], F32)
            dsq = A.take([4, 64], F32)
            dms = A.take([4], F32)
            R_d = Res()
            dve(lambda e: e.memset(VA[:, :, :, 64:65], 1.0), [], [R_VA])
            cS = [0]
            cP = [0]
            cA = [0]
            cO = [0]
            ABANKS = [5, 6, 7, 1]
            TB = [0, 4]
            SBANKS = [2, 3, 7, 1]
            cR = [0]
            oT = [A.take([512], F32) for _ in range(2)]
            R_oT = [Res(), Res()]

            def run_seq(g, tok0, NQ, cache):
                sc = (1.0 / math.sqrt(32.0)) if g == 1 else 0.125
                nmap = 2 if g == 1 else 1
                if cache is None:
                    kbl = [(i * 128, 128) for i in range(NKB_P)]
                    nck = 0
                    qsubs = [(i * 128, 128) for i in range(NKB_P)]
                    qgroups = [list(range(4 * J, 4 * J + 4)) for J in range(NKB_P // 4)]
                else:
                    nck = (NLB if g == 2 else NCB)
                    kbl = [(i * 128, 128) for i in range(nck)] + [(nck * 128, 32)]
                    qsubs = [(0, 32)]
                    qgroups = [[0]]
                nkb = len(kbl)
                NK = kbl[-1][0] + kbl[-1][1]
                for hp in range(2):
                    S.dma("sp", QT[:, hp, 0:NQ], QS[g][hp * 128:(hp + 1) * 128, tok0:tok0 + NQ], writes=[R_QT], par=True)
                    S.dma("sp", KT[:, hp, nck * 128:nck * 128 + NQ], KS[g][hp * 128:(hp + 1) * 128, tok0:tok0 + NQ], writes=[R_KT], par=True)
                if cache is None:
                    S.dma("sp", VST[:, 0:nkb, :], VS[g][0:T, :].rearrange("(k p) c -> p k c", p=128), writes=[R_VST])
                    act(lambda e: e.activation(out=VA[:, 0:nkb, :, 0:64], in_=VST[:, 0:nkb, :].rearrange("p k (h d) -> p k h d", h=4),
                                               func=AF.Copy), [R_VST], [R_VA])
                else:
                    ck = [I["c_fk"], I["c_dk"], I["c_bk"]][g][l, cache]
                    cv = [I["c_fv"], I["c_dv"], I["c_bv"]][g][l, cache]
                    S.dma("sp", ktmA[:, 0:nck, :], ck.rearrange("(k p) c -> p k c", p=128), writes=[R_ktmA])
                    S.dma("sp", vstA[:, 0:nck, :], cv.rearrange("(k p) c -> p k c", p=128), writes=[R_vstA])
                    for kb in range(nck):
                        tb = kb % 2

                        def trk(e, tb=tb, kb=kb):
                            for hp in range(2):
                                ins = e.transpose(out=PS[tb][:, hp * 128:(hp + 1) * 128], in_=ktmA[:, kb, hp * 128:(hp + 1) * 128], identity=ident_f)
                            return ins
                        pe(trk, [R_ktmA, R_const], [RPS[tb]])
                        act(lambda e, kb=kb, tb=tb: e.activation(out=KT[:, :, kb * 128:(kb + 1) * 128],
                                                                 in_=PS[tb][:, 0:256].rearrange("p (a b) -> p a b", a=2), func=AF.Copy),
                            [RPS[tb]], [R_KT], par=True)
                    dve(lambda e: e.tensor_copy(out=VA[:, 0:nck, :, 0:64], in_=vstA[:, 0:nck, :].rearrange("p k (h d) -> p k h d", h=4)),
                        [R_vstA], [R_VA])
                    S.dma("sp", VST[0:32, 0, :], VS[g][tok0:tok0 + 32, :], writes=[R_VST])
                    act(lambda e: e.activation(out=VA[0:32, nck, :, 0:64], in_=VST[0:32, 0, :].rearrange("p (h d) -> p h d", h=4), func=AF.Copy),
                        [R_VST], [R_VA])
                if g == 0:
                    dve(lambda e: e.memset(lfc[:, 0:nkb, :], 0.0), [], [R_lfc])
                    if cache is None:
                        S.dma("sp", lfc[:, 0:nkb, :], LOGF[0:T, :].rearrange("(k p) h -> p k h", p=128), writes=[R_lfc])
                    else:
                        S.dma("sp", lfc[:, 0:nck, :], I["c_flf"][l, cache].rearrange("(k p) h -> p k h", p=128), writes=[R_lfc])
                        S.dma("sp", lfc[0:32, nck, :], LOGF[tok0:tok0 + 32, :], writes=[R_lfc])
                    n4 = nkb * 4
                    lf2 = lfc[:, 0:nkb, :].rearrange("p k h -> p (k h)")
                    pe(lambda e: e.matmul(out=PS[0][:, 0:n4], lhsT=ones_f, rhs=lf2, start=True, stop=True), [R_lfc, R_const], [RPS[0]])
                    for h_ in range(4):
                        dve(lambda e, h_=h_: e.tensor_tensor_scan(out=totb[:, h_, 0:nkb], data0=ones_f[:, 0:nkb],
                                                                  data1=PS[0][:, 0:n4].rearrange("p (k h) -> p h k", h=4)[:, h_, :],
                                                                  initial=0.0, op0=ALU.mult, op1=ALU.add), [RPS[0], R_const], [R_cum])
                    pe(lambda e: e.matmul(out=PS[1][:, 0:n4], lhsT=tri_f, rhs=lf2, start=True, stop=True), [R_lfc, R_const], [RPS[1]])
                    dve(lambda e: e.tensor_tensor(out=cum[:, 0:nkb, :], in0=PS[1][:, 0:n4].rearrange("p (k h) -> p k h", h=4),
                                                  in1=totb[:, :, 0:nkb].rearrange("p h k -> p k h"), op=ALU.add), [RPS[1], R_cum], [R_cum])
                    dve(lambda e: e.tensor_tensor(out=cum[:, 0:nkb, :], in0=cum[:, 0:nkb, :],
                                                  in1=PS[0][:, 0:n4].rearrange("p (k h) -> p k h", h=4), op=ALU.subtract), [RPS[0], R_cum], [R_cum])
                    cum2 = cum[:, 0:nkb, :].rearrange("p k h -> p (k h)")
                    pe(lambda e: e.matmul(out=PS[0][:, 0:n4], lhsT=sel0, rhs=cum2, start=True, stop=True), [R_cum, R_const], [RPS[0]])
                    dve(lambda e: e.tensor_copy(out=crefbc[:, 0:nkb, :], in_=PS[0][:, 0:n4].rearrange("p (k h) -> p k h", h=4)), [RPS[0]], [R_cum])
                    for qi_, (q0, nq) in enumerate(qsubs):
                        kq = (q0 // 128) if cache is None else nck
                        dve(lambda e, qi_=qi_, kq=kq: e.tensor_tensor(out=FB[:, qi_, 0:nkb, :],
                                                                      in0=crefbc[:, kq:kq + 1, :].to_broadcast([128, nkb, 4]),
                                                                      in1=cum[:, 0:nkb, :], op=ALU.subtract), [R_cum], [R_FB])

                def mode(kb, qs):
                    k0, nk = kbl[kb]
                    if cache is None:
                        off = qs - kb
                        if g == 0:
                            if off < 0:
                                return None
                            return ((lambda h: FB[0:nk, qs, kb, h:h + 1]), ((lambda h: tri_f) if off == 0 else None))
                        if g == 1:
                            if off < 0:
                                return None
                            if off == 0:
                                return (None, (lambda h: emd[:, h, :]))
                            return ((lambda h: alcol[0:nk, h, off:off + 1]), None)
                        if off < 0 or off > 4:
                            return None
                        ti = [0, 1, 2, 2, 3][off]
                        return (None, (lambda h: emb[:, h, ti, :]))
                    else:
                        new = (kb == nck)
                        if g == 0:
                            return ((lambda h: FB[0:nk, 0, kb, h:h + 1]), ((lambda h: tri_f[0:32, 0:32]) if new else None))
                        if g == 1:
                            if new:
                                return (None, (lambda h: emd[0:32, h, 0:32]))
                            off = nck - kb
                            return ((lambda h: alcol[0:nk, h, off:off + 1]), None)
                        if new:
                            return (None, (lambda h: emb[0:32, h, 4, 0:32]))
                        off = nck - kb
                        ti = 1 if off == 1 else 2
                        return (None, (lambda h: emb[:, h, ti, 0:32]))

                for qg in qgroups:
                    ob = cO[0] % 2
                    cO[0] += 1
                    nsub = len(qg)
                    nq = qsubs[qg[0]][1]
                    if g == 1:
                        lane_groups = [[(h_, 0), (h_, 1)] for h_ in range(4)]
                    else:
                        lane_groups = [[(0, 0), (1, 0)], [(2, 0), (3, 0)]]
                    for lanes in lane_groups:
                        lacc = [5, 6]
                        first = [True, True]
                        kneed = [kb for kb in range(nkb) if any(mode(kb, qs) is not None for qs in qg)]
                        units = []
                        for kb in kneed:
                            k0, nk = kbl[kb]
                            subs = [si for si, qs in enumerate(qg) if mode(kb, qs) is not None]
                            slo, shi = subs[0], subs[-1] + 1
                            qa = qsubs[qg[slo]][0]
                            qb_ = qsubs[qg[shi - 1]][0] + nq
                            for li, (h_, m_) in enumerate(lanes):
                                units.append(dict(kb=kb, k0=k0, nk=nk, slo=slo, shi=shi, qa=qa, qb_=qb_, m=m_, h=h_, li=li))

                        def emit_qk(u):
                            m = u["m"]
                            h = u["h"]
                            hp = h // 2
                            if g == 1:
                                r0 = (h % 2) * 64 + m * 32
                                r1 = r0 + 32
                            else:
                                r0 = (h % 2) * 64
                                r1 = r0 + 64
                            sb = SBANKS[cS[0] % 4]
                            cS[0] += 1
                            u["sb"] = sb
                            kw = {"tile_position": (r0, 0)} if r0 == 96 else {}
                            k0, nk, qa, qb_ = u["k0"], u["nk"], u["qa"], u["qb_"]
                            pe(lambda e, sb=sb, r0=r0, r1=r1, hp=hp, k0=k0, nk=nk, qa=qa, qb_=qb_, kw=kw: e.matmul(
                                out=PS[sb][0:nk, 0:qb_ - qa], lhsT=KT[r0:r1, hp, k0:k0 + nk], rhs=QT[r0:r1, hp, qa:qb_],
                                start=True, stop=True, **kw), [R_KT, R_QT], [RPS[sb]])

                        def emit_rest(u):
                            kb, k0, nk, slo, shi, m, sb = u["kb"], u["k0"], u["nk"], u["slo"], u["shi"], u["m"], u["sb"]
                            h = u["h"]
                            li = u["li"]
                            wide = (cache is None) and (g == 0 or g == 1)
                            if wide:
                                pi = cP[0] % 4
                                cP[0] += 1
                                width = (shi - slo) * nq
                                if g == 1 and h == 0:
                                    for half in range(2):
                                        lo_s = max(slo, 2 * half)
                                        hi_s = min(shi, 2 * half + 2)
                                        if hi_s <= lo_s:
                                            continue
                                        oo = (qg[0] + 2 * half - kb) + 3
                                        bcol = alw[0:nk, h, oo:oo + 1]
                                        c_lo = (lo_s - slo) * nq
                                        c_hi = (hi_s - slo) * nq
                                        act(lambda e, sb=sb, nk=nk, c_lo=c_lo, c_hi=c_hi, pi=pi, bcol=bcol: e.activation(
                                            out=pt[pi][0:nk, c_lo:c_hi], in_=PS[sb][0:nk, c_lo:c_hi], func=AF.Exp, scale=sc, bias=bcol),
                                            [RPS[sb], R_const], [R_pt[pi]], par=True)
                                else:
                                    if g == 0:
                                        bcol = FB[0:nk, qg[0], kb, h:h + 1]
                                        rdw = [RPS[sb], R_FB]
                                    else:
                                        bcol = alw[0:nk, h, (qg[0] - kb) + 3:(qg[0] - kb) + 4]
                                        rdw = [RPS[sb], R_const]
                                    act(lambda e, sb=sb, nk=nk, width=width, pi=pi, bcol=bcol: e.activation(
                                        out=pt[pi][0:nk, 0:width], in_=PS[sb][0:nk, 0:width], func=AF.Exp, scale=sc, bias=bcol),
                                        rdw, [R_pt[pi]])
                                if kb >= qg[0]:
                                    corr = tri_f if g == 0 else cdiff[:, h, :]
                                    dve(lambda e, pi=pi, nq=nq, corr=corr: e.tensor_tensor(
                                        out=pt[pi][:, 0:nq], in0=pt[pi][:, 0:nq], in1=corr, op=ALU.mult), [R_pt[pi], R_const], [R_pt[pi]])
                                ab = lacc[li]
                                st_ = first[li]
                                first[li] = False
                                lastk = (kb == kneed[-1])
                                pe(lambda e, ab=ab, slo=slo, nq=nq, nk=nk, pi=pi, kb=kb, h=h, st_=st_, width=width, lastk=lastk: e.matmul(
                                    out=PS[ab][0:65, slo * nq:slo * nq + width], lhsT=VA[0:nk, kb, h, :], rhs=pt[pi][0:nk, 0:width],
                                    start=st_, stop=lastk), [R_pt[pi], R_VA], [RPS[ab]])
                                return
                            for si in range(slo, shi):
                                qs = qg[si]
                                md = mode(kb, qs)
                                if md is None:
                                    continue
                                bfn, efn = md
                                c0 = (si - slo) * nq
                                pi = cP[0] % 4
                                cP[0] += 1
                                bias_kw = {"bias": bfn(h)} if bfn is not None else {}
                                rd = [RPS[sb]] + ([R_FB] if (bfn is not None and g == 0) else []) + [R_const]
                                if efn is None:
                                    act(lambda e, sb=sb, nk=nk, c0=c0, nq=nq, pi=pi, bias_kw=bias_kw: e.activation(
                                        out=pt[pi][0:nk, 0:nq], in_=PS[sb][0:nk, c0:c0 + nq], func=AF.Exp, scale=sc, **bias_kw),
                                        rd, [R_pt[pi]])
                                else:
                                    fi = pi % 2
                                    act(lambda e, sb=sb, nk=nk, c0=c0, nq=nq, fi=fi, bias_kw=bias_kw: e.activation(
                                        out=pf[fi][0:nk, 0:nq], in_=PS[sb][0:nk, c0:c0 + nq], func=AF.Exp, scale=sc, **bias_kw),
                                        rd, [R_pf[fi]])
                                    em = efn(h)
                                    dve(lambda e, nk=nk, nq=nq, fi=fi, pi=pi, em=em: e.tensor_tensor(
                                        out=pt[pi][0:nk, 0:nq], in0=pf[fi][0:nk, 0:nq], in1=em[0:nk, 0:nq] if nk < 128 else em, op=ALU.mult),
                                        [R_pf[fi], R_const, R_emb], [R_pt[pi]])
                                ab = lacc[li]
                                st_ = first[li]
                                first[li] = False
                                pe(lambda e, ab=ab, si=si, nq=nq, nk=nk, pi=pi, kb=kb, h=h, st_=st_: e.matmul(
                                    out=PS[ab][0:nq, si * 65:(si + 1) * 65], lhsT=pt[pi][0:nk, 0:nq], rhs=VA[0:nk, kb, h, :],
                                    start=st_, stop=True, skip_group_check=True), [R_pt[pi], R_VA], [RPS[ab]])

                        nl = len(lanes)
                        for j in range(0, len(units), nl):
                            if j == 0:
                                for u in units[0:nl]:
                                    emit_qk(u)
                            for u in units[j + nl:j + 2 * nl]:
                                emit_qk(u)
                            for u in units[j:j + nl]:
                                emit_rest(u)
                        if g == 1:
                            fin_list = [(lanes[0][0], [lacc[0], lacc[1]], [0, 1])]
                        else:
                            fin_list = [(lanes[li_][0], [lacc[li_]], [li_]) for li_ in range(len(lanes))]
                        for (h, accs, tix) in fin_list:
                            cR[0] += 1
                            rb = cR[0] % 2
                            wide_h = (cache is None) and (g == 0 or g == 1)
                            if wide_h:
                                for m in range(len(accs)):
                                    ti = tix[m]
                                    act(lambda e, ti=ti, ab=accs[m]: e.activation(out=oT[ti][0:65, :], in_=PS[ab][0:65, 0:512], func=AF.Copy), [RPS[accs[m]]], [R_oT[ti]])

                                    def trO(e, ti=ti):
                                        for si in range(4):
                                            ins = e.transpose(out=PS[TB[ti]][:, si * 65:(si + 1) * 65], in_=oT[ti][0:65, si * 128:(si + 1) * 128], identity=ident_f[0:65, 0:65])
                                        return ins
                                    pe(trO, [R_oT[ti], R_const], [RPS[TB[ti]]])
                                accs = [TB[tix[m]] for m in range(len(accs))]
                            a0 = PS[accs[0]][0:nq, 0:nsub * 65].rearrange("p (s d) -> p s d", d=65)
                            dve(lambda e, rb=rb, a0=a0, nq=nq, nsub=nsub: e.reciprocal(out=rec[rb][0:nq, 0:nsub, 0:1], in_=a0[:, :, 64:65]),
                                [RPS[accs[0]]], [R_rec[rb]])
                            if g != 1:
                                dve(lambda e, rb=rb, a0=a0, nq=nq, nsub=nsub, ob=ob, h=h: e.tensor_tensor(
                                    out=og[ob][0:nq, 0:nsub, h * 64:(h + 1) * 64], in0=a0[:, :, 0:64],
                                    in1=rec[rb][0:nq, 0:nsub, 0:1].to_broadcast([nq, nsub, 64]), op=ALU.mult),
                                    [RPS[accs[0]], R_rec[rb]], [R_og[ob]])
                            else:
                                a1 = PS[accs[1]][0:nq, 0:nsub * 65].rearrange("p (s d) -> p s d", d=65)
                                dve(lambda e, rb=rb, a1=a1, nq=nq, nsub=nsub: e.reciprocal(out=rec[rb][0:nq, 0:nsub, 1:2], in_=a1[:, :, 64:65]),
                                    [RPS[accs[1]]], [R_rec[rb]])
                                dve(lambda e, rb=rb, nq=nq, nsub=nsub: e.tensor_scalar(out=rec[rb][0:nq, 0:nsub, 1:2], in0=rec[rb][0:nq, 0:nsub, 1:2],
                                                                                     scalar1=nlam[0:nq, 0:1], scalar2=None, op0=ALU.mult),
                                    [R_rec[rb], R_lam], [R_rec[rb]])
                                dve(lambda e, rb=rb, a0=a0, nq=nq, nsub=nsub: e.tensor_tensor(
                                    out=d0[0:nq, 0:nsub, :], in0=a0[:, :, 0:64], in1=rec[rb][0:nq, 0:nsub, 0:1].to_broadcast([nq, nsub, 64]), op=ALU.mult),
                                    [RPS[accs[0]], R_rec[rb]], [R_d])
                                dve(lambda e, rb=rb, a1=a1, nq=nq, nsub=nsub: e.tensor_tensor(
                                    out=d1[0:nq, 0:nsub, :], in0=a1[:, :, 0:64], in1=rec[rb][0:nq, 0:nsub, 1:2].to_broadcast([nq, nsub, 64]), op=ALU.mult),
                                    [RPS[accs[1]], R_rec[rb], R_d], [R_d])
                                dve(lambda e, nq=nq, nsub=nsub: e.tensor_tensor(out=d0[0:nq, 0:nsub, :], in0=d0[0:nq, 0:nsub, :], in1=d1[0:nq, 0:nsub, :], op=ALU.add),
                                    [R_d], [R_d])
                                dve(lambda e, nq=nq, nsub=nsub: e.tensor_tensor(out=dsq[0:nq, 0:nsub, :], in0=d0[0:nq, 0:nsub, :], in1=d0[0:nq, 0:nsub, :], op=ALU.mult),
                                    [R_d], [R_d])
                                dve(lambda e, nq=nq, nsub=nsub: e.tensor_reduce(out=dms[0:nq, 0:nsub], in_=dsq[0:nq, 0:nsub, :], axis=AX.X, op=ALU.add), [R_d], [R_d])
                                act(lambda e, nq=nq, nsub=nsub: e.activation(out=dms[0:nq, 0:nsub], in_=dms[0:nq, 0:nsub], func=AF.Ln, scale=1.0 / 64, bias=EPS), [R_d], [R_d])
                                act(lambda e, nq=nq, nsub=nsub: e.activation(out=dms[0:nq, 0:nsub], in_=dms[0:nq, 0:nsub], func=AF.Exp, scale=-0.5), [R_d], [R_d])
                                dve(lambda e, nq=nq, nsub=nsub: e.tensor_tensor(out=d0[0:nq, 0:nsub, :], in0=d0[0:nq, 0:nsub, :],
                                                                              in1=dms[0:nq, 0:nsub].unsqueeze(2).to_broadcast([nq, nsub, 64]), op=ALU.mult), [R_d], [R_d])
                                dve(lambda e, nq=nq, nsub=nsub, ob=ob, h=h: e.tensor_tensor(
                                    out=og[ob][0:nq, 0:nsub, h * 64:(h + 1) * 64], in0=d0[0:nq, 0:nsub, :],
                                    in1=gsub[0:nq, :].unsqueeze(1).to_broadcast([nq, nsub, 64]), op=ALU.mult), [R_d, R_lam], [R_og[ob]])
                    qa = tok0 + qsubs[qg[0]][0]
                    mc0 = [0, 256, 768][g]
                    if cache is None:
                        S.dma("pool", MIX[qa:qa + nsub * 128, mc0:mc0 + 256].rearrange("(s p) c -> p s c", p=128), og[ob][:, 0:nsub, :],
                              reads=[R_og[ob]], writes=[R_MIX], par=True)
                    else:
                        S.dma("pool", MIX[qa:qa + 32, mc0:mc0 + 256], og[ob][0:32, 0, :], reads=[R_og[ob]], writes=[R_MIX], par=True)

            for g in range(3):
                run_seq(g, 0, T, None)
                for s_ in range(4):
                    run_seq(g, T + 32 * s_, 32, s_)
            S.barrier()
            A.pop()
            phase_ML(l)

        for l in range(DEPTH):
            lam_init = 0.8 - 0.6 * math.exp(-0.3 * l)
            if l == 0:
                A.push()
                for _ in gen_M(0, modT, A1, A2, R_mod):
                    pass
                S.barrier()
                A.pop()
            else:
                dve(lambda e: e.tensor_copy(out=modT, in_=modTn), [R_modn], [R_mod])
                dve(lambda e: e.tensor_copy(out=A1, in_=A1n), [R_modn], [R_mod])
                dve(lambda e: e.tensor_copy(out=A2, in_=A2n), [R_modn], [R_mod])
                S.barrier()

            A.push()
            wfm = A.take([8, 2048], BF16)
            wg = A.take([8, 8], BF16)
            wtm = A.take([8, NTM], BF16)
            R_w = Res("w_in")
            w_in_l = I["w_in"][l].rearrange("(c p) n -> p c n", p=128)
            for j in range(16):
                S.dma("pool", wfm[:, :, j * 128:(j + 1) * 128], w_in_l[:, :, FM_COLS[j]:FM_COLS[j] + 128], writes=[R_w])
            S.dma("pool", wg[:, :, 0:4], w_in_l[:, :, C_MI:C_MI + 4], writes=[R_w])
            S.dma("pool", wg[:, :, 4:8], w_in_l[:, :, C_MF:C_MF + 4], writes=[R_w])
            o = 0
            for (c0, n) in TM_SRC:
                S.dma("pool", wtm[:, :, o:o + n], w_in_l[:, :, c0:c0 + n], writes=[R_w])
                o += n
            binT = A.take([16], F32)
            bgate = A.take([2], F32)
            nbgate = A.take([2], F32)
            btm = A.take([NTM], F32)
            gqk = A.take([12], F32)
            convw = A.take([4, 4], F32)
            convb = A.take([4], F32)
            R_p = Res("params")
            S.dma("sp", binT, I["binT"][l], writes=[R_p])
            S.dma("sp", bgate[0:4, :], I["bgate"][l], writes=[R_p])
            S.dma("sp", btm, I["btm"][l], writes=[R_p])
            S.dma("sp", gqk, I["gqk"][l], writes=[R_p])
            S.dma("sp", convw, I["convw"][l], writes=[R_p])
            S.dma("sp", convb, I["convb"][l], writes=[R_p])
            dve(lambda e: e.tensor_scalar(out=nbgate[0:4, :], in0=bgate[0:4, :], scalar1=-1.0, scalar2=None, op0=ALU.mult),
                [R_p], [R_p])
            xT = [A.take([8, 512], F32) for _ in range(2)]
            R_xT = [Res(), Res()]
            xtm = [A.take([1024], F32) for _ in range(2)]
            R_xtm = [Res(), Res()]
            sqb = A.take([8, 512], BF16)
            R_sq = Res()
            rstd = A.take([512], F32)
            R_rstd = Res()
            t1 = A.take([8, 512], F32)
            R_t1 = Res()
            hT2 = [A.take([8, 512], BF16) for _ in range(2)]
            R_hT2 = [Res(), Res()]
            U = [A.take([4, 35], F32) for _ in range(4)]
            UP = [A.take([515], F32) for _ in range(4)]
            R_U = [Res() for _ in range(4)]
            u32 = [A.take([512], F32) for _ in range(2)]
            R_u32 = [Res(), Res()]
            sqk = [A.take([512], BF16) for _ in range(2)]
            R_sqk = [Res(), Res()]
            rr = [A.take([512], F32) for _ in range(2)]
            R_rr = [Res(), Res()]
            kn32 = [A.take([512], F32) for _ in range(2)]
            R_kn32 = [Res(), Res()]
            knb = [A.take([512], BF16) for _ in range(2)]
            R_knb = [Res(), Res()]
            acc = [A.take([512], F32) for _ in range(2)]
            R_acc = [Res(), Res()]
            v32 = [A.take([512], F32) for _ in range(2)]
            R_v32 = [Res(), Res()]
            vb = [A.take([512], BF16) for _ in range(2)]
            R_vb = [Res(), Res()]
            grow = A.take([2, 512], F32)
            R_grow = Res()
            t260 = A.take([260], F32)
            R_t260 = Res()
            sg = A.take([256], F32)
            R_sg = Res()
            lf4 = A.take([4], F32)
            R_lf4 = Res()
            for u_ in UP:
                dve(lambda e, u_=u_: e.memset(u_[:, 0:3], 0.0), [], [R_U[0], R_U[1], R_U[2], R_U[3]])
            cnt2 = [0]

            def norm_part(bi, tok0, ntok, segs):
                is_s = (bi == len(blocks) - 1)
                xb = bi % 2
                xTb = xT[xb]
                hTc = hT2[bi % 2]
                R_hTc = R_hT2[bi % 2]
                if l == 0:
                    src = I["xs"] if is_s else I["xp"][tok0:tok0 + ntok, :]
                    for i in range(ntok // 128):
                        tb = i % 2
                        S.dma("sp", xtm[tb], src[i * 128:(i + 1) * 128, :], writes=[R_xtm[tb]])
                        for half in range(2):
                            pb = half

                            def tr(e, tb=tb, half=half, pb=pb):
                                for c4 in range(4):
                                    c = half * 4 + c4
                                    ins = e.transpose(out=PS[pb][:, c4 * 128:(c4 + 1) * 128], in_=xtm[tb][:, c * 128:(c + 1) * 128],
                                                      identity=ident_f)
                                return ins
                            pe(tr, [R_xtm[tb], R_const], [RPS[pb]])
                            eng = act if half == 0 else dve
                            if half == 0:
                                act(lambda e, i=i, pb=pb, half=half, xTb=xTb: e.activation(
                                    out=xTb[:, half * 4:half * 4 + 4, i * 128:(i + 1) * 128],
                                    in_=PS[pb][:, :].rearrange("p (a b) -> p a b", a=4), func=AF.Copy), [RPS[pb]], [R_xT[xb]])
                            else:
                                dve(lambda e, i=i, pb=pb, half=half, xTb=xTb: e.tensor_copy(
                                    out=xTb[:, half * 4:half * 4 + 4, i * 128:(i + 1) * 128],
                                    in_=PS[pb][:, :].rearrange("p (a b) -> p a b", a=4)), [RPS[pb]], [R_xT[xb]])
                    S.dma("pool", XT[0][:, tok0:tok0 + ntok].rearrange("(c p) t -> p c t", p=128), xTb[:, :, 0:ntok],
                          reads=[R_xT[xb]], writes=[R_XT])
                else:
                    S.dma("sp", xTb[:, :, 0:ntok], XT[1][:, tok0:tok0 + ntok].rearrange("(c p) t -> p c t", p=128),
                          writes=[R_xT[xb]])
                act(lambda e, xTb=xTb, ntok=ntok: e.activation(out=sqb[:, :, 0:ntok], in_=xTb[:, :, 0:ntok], func=AF.Square),
                    [R_xT[xb]], [R_sq])

                def ssmm(e, ntok=ntok):
                    for c in range(8):
                        ins = e.matmul(out=PS[2][:, 0:ntok], lhsT=ones_b, rhs=sqb[:, c, 0:ntok], start=(c == 0), stop=(c == 7))
                    return ins
                pe(ssmm, [R_sq, R_const], [RPS[2]])
                act(lambda e, ntok=ntok: e.activation(out=rstd[:, 0:ntok], in_=PS[2][:, 0:ntok], func=AF.Ln, scale=1.0 / D, bias=EPS),
                    [RPS[2]], [R_rstd])
                act(lambda e, ntok=ntok: e.activation(out=rstd[:, 0:ntok], in_=rstd[:, 0:ntok], func=AF.Exp, scale=-0.5),
                    [R_rstd], [R_rstd])
                for c in range(8):
                    for (c0, ncol, sq_) in segs:
                        dve(lambda e, c=c, c0=c0, ncol=ncol, sq_=sq_, xTb=xTb: e.scalar_tensor_tensor(
                            out=t1[:, c, c0:c0 + ncol], in0=xTb[:, c, c0:c0 + ncol], scalar=A1[:, c, sq_:sq_ + 1],
                            in1=rstd[:, c0:c0 + ncol], op0=ALU.mult, op1=ALU.mult), [R_xT[xb], R_rstd, R_mod], [R_t1])
                        act(lambda e, c=c, c0=c0, ncol=ncol, sq_=sq_: e.activation(
                            out=hTc[:, c, c0:c0 + ncol], in_=t1[:, c, c0:c0 + ncol], func=AF.Identity,
                            bias=modT[:, c, sq_:sq_ + 1], scale=1.0), [R_t1, R_mod], [R_hTc])

            def proj_part(bi, tok0, ntok, segs):
                is_s = (bi == len(blocks) - 1)
                xb = bi % 2
                xTb = xT[xb]
                hTc = hT2[bi % 2]
                R_hTc = R_hT2[bi % 2]
                pend = []
                for j in range(16):
                    pb = 3 + (j % 2)

                    def fm(e, j=j, pb=pb, ntok=ntok):
                        for c in range(8):
                            ins = e.matmul(out=PS[pb][:, 0:ntok], lhsT=wfm[:, c, j * 128:(j + 1) * 128], rhs=hTc[:, c, 0:ntok],
                                           start=(c == 0), stop=(c == 7))
                        return ins
                    pe(fm, [R_w, R_hTc], [RPS[pb]])
                    if j < 12:
                        k2 = cnt2[0] % 2
                        cnt2[0] += 1
                        g = j // 4
                        isk = (j % 4) >= 2
                        hd = 32 if g == 1 else 64
                        bm = blk32 if g == 1 else blk64
                        act(lambda e, j=j, pb=pb, k2=k2, ntok=ntok: e.activation(out=u32[k2][:, 0:ntok], in_=PS[pb][:, 0:ntok],
                                                                                 func=AF.Identity, bias=binT[:, j:j + 1], scale=1.0),
                            [RPS[pb], R_p], [R_u32[k2]])
                        act(lambda e, k2=k2, ntok=ntok: e.activation(out=sqk[k2][:, 0:ntok], in_=u32[k2][:, 0:ntok], func=AF.Square),
                            [R_u32[k2]], [R_sqk[k2]])
                        def tail(j=j, k2=k2, g=g, isk=isk, hd=hd, bm=bm, ntok=ntok):
                            pe(lambda e, k2=k2, bm=bm, ntok=ntok: e.matmul(out=PS[5][:, 0:ntok], lhsT=bm, rhs=sqk[k2][:, 0:ntok],
                                                                           start=True, stop=True), [R_sqk[k2], R_const], [RPS[5]])
                            act(lambda e, k2=k2, hd=hd, ntok=ntok: e.activation(out=rr[k2][:, 0:ntok], in_=PS[5][:, 0:ntok], func=AF.Ln,
                                                                                scale=1.0 / hd, bias=EPS), [RPS[5]], [R_rr[k2]])
                            act(lambda e, k2=k2, ntok=ntok: e.activation(out=rr[k2][:, 0:ntok], in_=rr[k2][:, 0:ntok], func=AF.Exp, scale=-0.5),
                                [R_rr[k2]], [R_rr[k2]])
                            if isk:
                                dve(lambda e, j=j, k2=k2, ntok=ntok: e.scalar_tensor_tensor(
                                    out=kn32[k2][:, 0:ntok], in0=u32[k2][:, 0:ntok], scalar=gqk[:, j:j + 1], in1=rr[k2][:, 0:ntok],
                                    op0=ALU.mult, op1=ALU.mult), [R_u32[k2], R_rr[k2], R_p], [R_kn32[k2]])
                                okt = [O["o_fk"], O["o_dk"], O["o_bk"]][g]
                                ch = j % 2
                                S.dma("pool", okt[l][ch * 128:(ch + 1) * 128, tok0:tok0 + ntok], kn32[k2][:, 0:ntok], reads=[R_kn32[k2]])
                                dve(lambda e, k2=k2, ntok=ntok: e.tensor_copy(out=knb[k2][:, 0:ntok], in_=kn32[k2][:, 0:ntok]),
                                    [R_kn32[k2]], [R_knb[k2]])
                                S.dma("pool", KS[g][ch * 128:(ch + 1) * 128, tok0:tok0 + ntok], knb[k2][:, 0:ntok], reads=[R_knb[k2]])
                            else:
                                dve(lambda e, j=j, k2=k2, ntok=ntok: e.scalar_tensor_tensor(
                                    out=knb[k2][:, 0:ntok], in0=u32[k2][:, 0:ntok], scalar=gqk[:, j:j + 1], in1=rr[k2][:, 0:ntok],
                                    op0=ALU.mult, op1=ALU.mult), [R_u32[k2], R_rr[k2], R_p], [R_knb[k2]])
                                ch = j % 2
                                S.dma("pool", QS[g][ch * 128:(ch + 1) * 128, tok0:tok0 + ntok], knb[k2][:, 0:ntok], reads=[R_knb[k2]])
                        for t_ in pend:
                            t_()
                        pend = [tail]
                    else:
                        for t_ in pend:
                            t_()
                        pend = []
                        jj = j - 12
                        k2 = cnt2[0] % 2
                        cnt2[0] += 1
                        if not is_s:
                            Uj = UP[jj]
                            act(lambda e, j=j, pb=pb, Uj=Uj: e.activation(out=Uj[:, 3:515], in_=PS[pb][:, 0:512], func=AF.Identity,
                                                                          bias=binT[:, j:j + 1], scale=1.0), [RPS[pb], R_p], [R_U[jj]])
                            dve(lambda e, jj=jj, k2=k2, Uj=Uj: e.tensor_scalar(out=acc[k2][:, 0:512], in0=Uj[:, 0:512],
                                                                             scalar1=convw[:, jj, 0:1], scalar2=convb[:, jj:jj + 1],
                                                                             op0=ALU.mult, op1=ALU.add), [R_U[jj], R_p], [R_acc[k2]])
                            for tp in range(1, 4):
                                dve(lambda e, jj=jj, k2=k2, Uj=Uj, tp=tp: e.scalar_tensor_tensor(
                                    out=acc[k2][:, 0:512], in0=Uj[:, tp:tp + 512], scalar=convw[:, jj, tp:tp + 1], in1=acc[k2][:, 0:512],
                                    op0=ALU.mult, op1=ALU.add), [R_U[jj], R_p, R_acc[k2]], [R_acc[k2]])
                            if bi == NPB - 1:
                                S.dma("pool", O["o_conv"][l, jj, :, 0, :], Uj[:, 512:515], reads=[R_U[jj]])
                            else:
                                dve(lambda e, Uj=Uj: e.tensor_copy(out=Uj[:, 0:3], in_=Uj[:, 512:515]), [R_U[jj], R_acc[k2]], [R_U[jj]])
                        else:
                            Uj = U[jj]
                            S.dma("sp", Uj[:, :, 0:3], I["convst"][l, :, jj, :, :], writes=[R_U[jj]])
                            act(lambda e, j=j, pb=pb, Uj=Uj: e.activation(out=Uj[:, :, 3:35], in_=PS[pb][:, 0:128].rearrange("p (a b) -> p a b", a=4),
                                                                          func=AF.Identity, bias=binT[:, j:j + 1], scale=1.0),
                                [RPS[pb], R_p], [R_U[jj]])
                            a3 = acc[k2][:, 0:128].rearrange("p (a b) -> p a b", a=4)
                            dve(lambda e, jj=jj, Uj=Uj, a3=a3: e.tensor_scalar(out=a3, in0=Uj[:, :, 0:32], scalar1=convw[:, jj, 0:1],
                                                                             scalar2=convb[:, jj:jj + 1], op0=ALU.mult, op1=ALU.add),
                                [R_U[jj], R_p], [R_acc[k2]])
                            for tp in range(1, 4):
                                dve(lambda e, jj=jj, Uj=Uj, tp=tp, a3=a3: e.scalar_tensor_tensor(
                                    out=a3, in0=Uj[:, :, tp:tp + 32], scalar=convw[:, jj, tp:tp + 1], in1=a3, op0=ALU.mult, op1=ALU.add),
                                    [R_U[jj], R_p, R_acc[k2]], [R_acc[k2]])
                            S.dma("pool", O["o_conv"][l, jj, :, 1:5, :], Uj[:, :, 32:35], reads=[R_U[jj]])
                        act(lambda e, k2=k2, ntok=ntok: e.activation(out=knb[k2][:, 0:ntok], in_=acc[k2][:, 0:ntok], func=AF.Silu),
                            [R_acc[k2]], [R_knb[k2]])
                        dst = QS[3] if jj < 2 else KS[3]
                        ch = jj % 2
                        S.dma("pool", dst[ch * 128:(ch + 1) * 128, tok0:tok0 + ntok], knb[k2][:, 0:ntok], reads=[R_knb[k2]])
                for gi_ in range(2):
                    pb = 3 + gi_

                    def gm(e, gi_=gi_, pb=pb, ntok=ntok):
                        for c in range(8):
                            ins = e.matmul(out=PS[pb][0:4, 0:ntok], lhsT=wg[:, c, gi_ * 4:gi_ * 4 + 4], rhs=hTc[:, c, 0:ntok],
                                           start=(c == 0), stop=(c == 7))
                        return ins
                    pe(gm, [R_w, R_hTc], [RPS[pb]])
                act(lambda e, ntok=ntok: e.activation(out=grow[0:4, 0, 0:ntok], in_=PS[3][0:4, 0:ntok], func=AF.Identity,
                                                      bias=bgate[0:4, 0:1], scale=1.0), [RPS[3], R_p], [R_grow])
                act(lambda e, ntok=ntok: e.activation(out=grow[0:4, 1, 0:ntok], in_=PS[4][0:4, 0:ntok], func=AF.Exp,
                                                      bias=nbgate[0:4, 1:2], scale=-1.0), [RPS[4], R_p], [R_grow])
                act(lambda e, ntok=ntok: e.activation(out=grow[0:4, 1, 0:ntok], in_=grow[0:4, 1, 0:ntok], func=AF.Ln, bias=1.0, scale=1.0),
                    [R_grow], [R_grow])
                dve(lambda e, ntok=ntok: e.tensor_scalar(out=grow[0:4, 1, 0:ntok], in0=grow[0:4, 1, 0:ntok], scalar1=-1.0, scalar2=None,
                                                         op0=ALU.mult), [R_grow], [R_grow])
                S.dma("pool", GI[:, tok0:tok0 + ntok], grow[0:4, 0, 0:ntok], reads=[R_grow])
                S.dma("pool", GF[:, tok0:tok0 + ntok], grow[0:4, 1, 0:ntok], reads=[R_grow])
                for i in range(ntok // 128):
                    ts_ = tok0 + i * 128
                    for gr in range(2):
                        pb = 6 + gr
                        k2 = gr

                        def tm(e, i=i, gr=gr, pb=pb):
                            for c in range(8):
                                ins = e.matmul(out=PS[pb][:, :], lhsT=hTc[:, c, i * 128:(i + 1) * 128], rhs=wtm[:, c, gr * 512:(gr + 1) * 512],
                                               start=(c == 0), stop=(c == 7))
                            return ins
                        pe(tm, [R_w, R_hTc], [RPS[pb]])
                        dve(lambda e, gr=gr, pb=pb, k2=k2: e.tensor_tensor(out=v32[k2], in0=PS[pb][:, :], in1=btm[:, gr * 512:(gr + 1) * 512],
                                                                          op=ALU.add), [RPS[pb], R_p], [R_v32[k2]])
                        act(lambda e, k2=k2: e.activation(out=vb[k2], in_=v32[k2], func=AF.Copy), [R_v32[k2]], [R_vb[k2]])
                        if gr == 0:
                            S.dma("pool", O["o_fv"][l, ts_:ts_ + 128, :], v32[k2][:, 0:256], reads=[R_v32[k2]])
                            S.dma("pool", O["o_dv"][l, ts_:ts_ + 128, :], v32[k2][:, 256:512], reads=[R_v32[k2]])
                            S.dma("pool", VS[0][ts_:ts_ + 128, :], vb[k2][:, 0:256], reads=[R_vb[k2]])
                            S.dma("pool", VS[1][ts_:ts_ + 128, :], vb[k2][:, 256:512], reads=[R_vb[k2]])
                        else:
                            S.dma("pool", O["o_bv"][l, ts_:ts_ + 128, :], v32[k2][:, 0:256], reads=[R_v32[k2]])
                            S.dma("pool", VS[2][ts_:ts_ + 128, :], vb[k2][:, 0:256], reads=[R_vb[k2]])
                            S.dma("pool", VS[3][ts_:ts_ + 128, :], vb[k2][:, 256:512], reads=[R_vb[k2]])

                    def tm3(e, i=i):
                        for c in range(8):
                            ins = e.matmul(out=PS[6][:, 0:260], lhsT=hTc[:, c, i * 128:(i + 1) * 128], rhs=wtm[:, c, 1024:1284],
                                           start=(c == 0), stop=(c == 7))
                        return ins
                    pe(tm3, [R_w, R_hTc], [RPS[6]])
                    dve(lambda e: e.tensor_tensor(out=t260, in0=PS[6][:, 0:260], in1=btm[:, 1024:1284], op=ALU.add), [RPS[6], R_p], [R_t260])
                    act(lambda e: e.activation(out=sg, in_=t260[:, 0:256], func=AF.Sigmoid), [R_t260], [R_sg])
                    S.dma("pool", SIGO[ts_:ts_ + 128, :], sg, reads=[R_sg])
                    act(lambda e: e.activation(out=lf4, in_=t260[:, 256:260], func=AF.Exp, scale=-1.0), [R_t260], [R_lf4])
                    act(lambda e: e.activation(out=lf4, in_=lf4, func=AF.Ln, bias=1.0, scale=1.0), [R_lf4], [R_lf4])
                    dve(lambda e: e.tensor_scalar(out=lf4, in0=lf4, scalar1=-1.0, scalar2=None, op0=ALU.mult), [R_lf4], [R_lf4])
                    S.dma("pool", O["o_flf"][l, ts_:ts_ + 128, :], lf4, reads=[R_lf4])
                    S.dma("pool", LOGF[ts_:ts_ + 128, :], lf4, reads=[R_lf4])

            norm_part(0, *blocks[0])
            for bi_ in range(len(blocks)):
                if bi_ + 1 < len(blocks):
                    norm_part(bi_ + 1, *blocks[bi_ + 1])
                proj_part(bi_, *blocks[bi_])
            S.barrier()
            A.pop()
            if debug == "A":
                break
            if debug != "C":
                phase_B(l, lam_init)
            A.push()
            wo = A.take([8, 1024], BF16)
            R_wo = Res()
            S.dma("pool", wo, I["w_out"][l].rearrange("(c p) n -> p c n", p=128), writes=[R_wo])
            mt = [A.take([1024], BF16) for _ in range(2)]
            R_mt = [Res(), Res()]
            mixT = A.take([8, 512], BF16)
            R_mixT = Res()
            xc = [A.take([8, 512], F32) for _ in range(2)]
            R_xc = [Res(), Res()]
            R_XM = Res()
            for bi, (tok0, ntok, segs) in enumerate(blocks):
                xb = bi % 2
                S.dma("sp", xc[xb][:, :, 0:ntok], XT[l][:, tok0:tok0 + ntok].rearrange("(c p) t -> p c t", p=128),
                      reads=[R_XT], writes=[R_xc[xb]])
                for i in range(ntok // 128):
                    tb = i % 2
                    S.dma("sp", mt[tb], MIX[tok0 + i * 128:tok0 + (i + 1) * 128, :], reads=[R_MIX], writes=[R_mt[tb]])
                    pb = tb
                    psb = PS[pb][:, :].bitcast(BF16)

                    def trm(e, tb=tb, psb=psb):
                        for c in range(8):
                            ins = e.transpose(out=psb[:, c * 128:(c + 1) * 128], in_=mt[tb][:, c * 128:(c + 1) * 128], identity=ident_b)
                        return ins
                    pe(trm, [R_mt[tb], R_const], [RPS[pb]])
                    act(lambda e, i=i, psb=psb: e.activation(out=mixT[:, :, i * 128:(i + 1) * 128],
                                                             in_=psb.rearrange("p (a b) -> p a b", a=8), func=AF.Copy),
                        [RPS[pb]], [R_mixT])
                for dc in range(8):
                    pb = 2 + dc % 2

                    def wom(e, dc=dc, pb=pb, ntok=ntok):
                        for c in range(8):
                            ins = e.matmul(out=PS[pb][:, 0:ntok], lhsT=wo[:, c, dc * 128:(dc + 1) * 128], rhs=mixT[:, c, 0:ntok],
                                           start=(c == 0), stop=(c == 7))
                        return ins
                    pe(wom, [R_wo, R_mixT], [RPS[pb]])
                    for (c0, ncol, sq_) in segs:
                        dve(lambda e, dc=dc, pb=pb, c0=c0, ncol=ncol, sq_=sq_, xb=xb: e.scalar_tensor_tensor(
                            out=xc[xb][:, dc, c0:c0 + ncol], in0=PS[pb][:, c0:c0 + ncol], scalar=modT[:, 16 + dc, sq_:sq_ + 1],
                            in1=xc[xb][:, dc, c0:c0 + ncol], op0=ALU.mult, op1=ALU.add), [RPS[pb], R_mod, R_xc[xb]], [R_xc[xb]])
                S.dma("sp", XM[:, tok0:tok0 + ntok].rearrange("(c p) t -> p c t", p=128), xc[xb][:, :, 0:ntok],
                      reads=[R_xc[xb]], writes=[R_XM])
            S.barrier()
            A.pop()
            A.push()
            wgt = A.take([8, DFF], BF16)
            wup = A.take([8, DFF], BF16)
            wdn = A.take([22, 1024], BF16)
            R_wf = Res()
            for hh in range(2):
                S.dma("pool", wgt[:, :, hh * 1408:(hh + 1) * 1408], I["w_gate"][l][:, hh * 1408:(hh + 1) * 1408].rearrange("(c p) n -> p c n", p=128), writes=[R_wf])
                S.dma("pool", wup[:, :, hh * 1408:(hh + 1) * 1408], I["w_up"][l][:, hh * 1408:(hh + 1) * 1408].rearrange("(c p) n -> p c n", p=128), writes=[R_wf])
                S.dma("pool", wdn[:, hh * 11:(hh + 1) * 11, :], I["w_down"][l][hh * 1408:(hh + 1) * 1408, :].rearrange("(c p) n -> p c n", p=128), writes=[R_wf])
            xf = A.take([8, 512], F32)
            R_xf = Res()
            rs2 = A.take([512], F32)
            R_rs2 = Res()
            tt2 = A.take([512], F32)
            R_tt2 = Res()
            h2 = A.take([8, 512], BF16)
            R_h2 = Res()
            actT = A.take([22, 512], BF16)
            R_actT = Res()
            sq2 = actT[:, 0:8, :]
            R_sq2 = R_actT
            slu = [A.take([512], F32) for _ in range(2)]
            R_slu = [Res(), Res()]
            xo = [A.take([512], F32) for _ in range(2)]
            R_xo = [Res(), Res()]
            R_XO = Res()
            dst_all = XT[1] if l == 0 else O["YT"]
            for bi, (tok0, ntok, segs) in enumerate(blocks):
                S.dma("sp", xf[:, :, 0:ntok], XM[:, tok0:tok0 + ntok].rearrange("(c p) t -> p c t", p=128), reads=[R_XM], writes=[R_xf])
                act(lambda e, ntok=ntok: e.activation(out=sq2[:, :, 0:ntok], in_=xf[:, :, 0:ntok], func=AF.Square), [R_xf], [R_sq2])

                def ss2(e, ntok=ntok):
                    for c in range(8):
                        ins = e.matmul(out=PS[0][:, 0:ntok], lhsT=ones_b, rhs=sq2[:, c, 0:ntok], start=(c == 0), stop=(c == 7))
                    return ins
                pe(ss2, [R_sq2, R_const], [RPS[0]])
                act(lambda e, ntok=ntok: e.activation(out=rs2[:, 0:ntok], in_=PS[0][:, 0:ntok], func=AF.Ln, scale=1.0 / D, bias=EPS), [RPS[0]], [R_rs2])
                act(lambda e, ntok=ntok: e.activation(out=rs2[:, 0:ntok], in_=rs2[:, 0:ntok], func=AF.Exp, scale=-0.5), [R_rs2], [R_rs2])
                for c in range(8):
                    for (c0, ncol, sq_) in segs:
                        dve(lambda e, c=c, c0=c0, ncol=ncol, sq_=sq_: e.scalar_tensor_tensor(
                            out=tt2[:, c0:c0 + ncol], in0=xf[:, c, c0:c0 + ncol], scalar=A2[:, c, sq_:sq_ + 1], in1=rs2[:, c0:c0 + ncol],
                            op0=ALU.mult, op1=ALU.mult), [R_xf, R_rs2, R_mod], [R_tt2])
                        act(lambda e, c=c, c0=c0, ncol=ncol, sq_=sq_: e.activation(
                            out=h2[:, c, c0:c0 + ncol], in_=tt2[:, c0:c0 + ncol], func=AF.Identity, bias=modT[:, 24 + c, sq_:sq_ + 1], scale=1.0),
                            [R_tt2, R_mod], [R_h2])
                for f in range(22):
                    pa = 1 + 2 * (f % 2)
                    pbk = pa + 1

                    def gu(e, f=f, pa=pa, pbk=pbk, ntok=ntok):
                        for c in range(8):
                            e.matmul(out=PS[pa][:, 0:ntok], lhsT=wgt[:, c, f * 128:(f + 1) * 128], rhs=h2[:, c, 0:ntok], start=(c == 0), stop=(c == 7))
                        for c in range(8):
                            ins = e.matmul(out=PS[pbk][:, 0:ntok], lhsT=wup[:, c, f * 128:(f + 1) * 128], rhs=h2[:, c, 0:ntok], start=(c == 0), stop=(c == 7))
                        return ins
                    pe(gu, [R_wf, R_h2], [RPS[pa], RPS[pbk]])
                    k2 = f % 2
                    act(lambda e, pa=pa, k2=k2, ntok=ntok: e.activation(out=slu[k2][:, 0:ntok], in_=PS[pa][:, 0:ntok], func=AF.Silu), [RPS[pa]], [R_slu[k2]])
                    dve(lambda e, f=f, pbk=pbk, k2=k2, ntok=ntok: e.tensor_tensor(out=actT[:, f, 0:ntok], in0=PS[pbk][:, 0:ntok], in1=slu[k2][:, 0:ntok], op=ALU.mult),
                        [RPS[pbk], R_slu[k2]], [R_actT])
                for dc in range(8):
                    pb = 5 + dc % 2
                    k2 = dc % 2

                    def dn(e, dc=dc, pb=pb, ntok=ntok):
                        for f in range(22):
                            ins = e.matmul(out=PS[pb][:, 0:ntok], lhsT=wdn[:, f, dc * 128:(dc + 1) * 128], rhs=actT[:, f, 0:ntok], start=(f == 0), stop=(f == 21))
                        return ins
                    pe(dn, [R_wf, R_actT], [RPS[pb]])
                    for (c0, ncol, sq_) in segs:
                        dve(lambda e, dc=dc, pb=pb, c0=c0, ncol=ncol, sq_=sq_, k2=k2: e.scalar_tensor_tensor(
                            out=xo[k2][:, c0:c0 + ncol], in0=PS[pb][:, c0:c0 + ncol], scalar=modT[:, 40 + dc, sq_:sq_ + 1],
                            in1=xf[:, dc, c0:c0 + ncol], op0=ALU.mult, op1=ALU.add), [RPS[pb], R_mod, R_xf], [R_xo[k2]])
                    S.dma("sp", dst_all[dc * 128:(dc + 1) * 128, tok0:tok0 + ntok], xo[k2][:, 0:ntok], reads=[R_xo[k2]], writes=[R_XO])
            S.barrier()
            A.pop()

        S.barrier()
        S.emit()
    return nc


def host_consts():
    c = {}
    c["ident"] = np.eye(128, dtype=np.float32)
    b64 = np.zeros((128, 128), np.float32)
    b64[:64, :64] = 1
    b64[64:, 64:] = 1
    c["blk64"] = b64
    b32 = np.zeros((128, 128), np.float32)
    for i in range(4):
        b32[i * 32:(i + 1) * 32, i * 32:(i + 1) * 32] = 1
    c["blk32"] = b32
    k = np.arange(128)[:, None]
    q = np.arange(128)[None, :]
    c["tri"] = (k <= q).astype(np.float32)
    c["cmask"] = ((k // 64) <= (q // 64)).astype(np.float32)
    c["m4"] = (~((k < 64) & (q >= 64))).astype(np.float32)
    sl = np.array(alibi_slopes(), np.float64)
    o = np.arange(33)[None, None, :]
    c["alcol"] = (sl[None, :, None] * (np.arange(128)[:, None, None] - 128.0 * o)).astype(np.float32)
    s0_ = np.zeros((128, 128), np.float32)
    s0_[0, :] = 1
    c["sel0"] = s0_
    s1_ = np.zeros((128, 128), np.float32)
    s1_[127, :] = 1
    c["sel127"] = s1_
    s2_ = np.zeros((128, 128), np.float32)
    s2_[31, :] = 1
    c["sel31"] = s2_
    ow = np.arange(36)[None, None, :] - 3
    c["alw"] = (sl[None, :, None] * (np.arange(128)[:, None, None] - 128.0 * ow)).astype(np.float32)
    c["cdiff"] = (np.where(k <= q, 1.0, np.exp(-2.0 * sl[:, None, None] * (k - q)[None])) * c["cmask"][None]).astype(np.float32)
    c["emd_arg"] = (-sl[:, None, None] * np.abs(q - k)[None] + sl[:, None, None] * q[None]).astype(np.float32)
    return c


def prep_core(inp, core, T, P, LB):
    f = np.float32
    b = core // 2
    s0 = 4 * core
    m = {}
    m["xp"] = np.ascontiguousarray(inp["x_prompt"][b, :T])
    m["xs"] = np.ascontiguousarray(inp["x_sample"][s0:s0 + 4].reshape(128, D))
    cv = np.concatenate([inp["c_prompt"][b:b + 1], inp["c_sample"][s0:s0 + 4]], 0)
    m["cT"] = np.ascontiguousarray(cv.reshape(5, 8, 128).transpose(2, 1, 0))
    for k_, n_ in [("w_mod", "w_mod"), ("w_in", "w_in"), ("w_out", "w_out"), ("w_gate", "w_ffn_gate"), ("w_up", "w_ffn_up"),
                   ("w_down", "w_ffn_down")]:
        m[k_] = inp[n_]
    m["b_modT"] = np.ascontiguousarray(inp["b_mod"].reshape(DEPTH, 48, 128).transpose(0, 2, 1))
    m["g1T"] = np.ascontiguousarray(inp["norm1_g"].reshape(DEPTH, 8, 128).transpose(0, 2, 1))
    m["g2T"] = np.ascontiguousarray(inp["norm2_g"].reshape(DEPTH, 8, 128).transpose(0, 2, 1))
    b_in = inp["b_in"]
    m["binT"] = np.ascontiguousarray(np.stack([b_in[:, c0:c0 + 128] for c0 in FM_COLS], 1).transpose(0, 2, 1))
    m["bgate"] = np.ascontiguousarray(np.stack([b_in[:, C_MI:C_MI + 4], b_in[:, C_MF:C_MF + 4]], -1))
    btm = np.concatenate([b_in[:, c0:c0 + n] for (c0, n) in TM_SRC], 1)
    m["btm"] = np.ascontiguousarray(np.broadcast_to(btm[:, None, :], (DEPTH, 128, NTM)))
    gq = []
    for g_, name in enumerate(["qk_g_fox", "qk_g_diff", "qk_g_band"]):
        gg = inp[name]
        rep = 128 // gg.shape[-1]
        for qk in range(2):
            col = np.tile(gg[:, qk, :], (1, rep))
            gq += [col, col]
    m["gqk"] = np.ascontiguousarray(np.stack(gq, -1))
    m["convw"] = np.ascontiguousarray(inp["conv_w"].reshape(DEPTH, 4, 4, 128).transpose(0, 3, 2, 1))
    m["convb"] = np.ascontiguousarray(inp["conv_b"].reshape(DEPTH, 4, 128).transpose(0, 2, 1))
    stc = inp["state_conv"][:, s0:s0 + 4]
    m["convst"] = np.ascontiguousarray(stc.reshape(DEPTH, 4, 3, 4, 128).transpose(0, 4, 3, 1, 2))
    m["lamb"] = np.ascontiguousarray(np.broadcast_to(inp["diff_lambda"][:, None], (DEPTH, 128, 4, 32)))
    m["gsub"] = np.ascontiguousarray(np.broadcast_to(inp["diff_subln_g"][:, None], (DEPTH, 128, 64)))
    m["gmh"] = np.ascontiguousarray(np.broadcast_to(inp["mlstm_norm_g"][:, None], (DEPTH, 128, 64)))
    tab = inp["band_rel_bias"]
    k = np.arange(128)[:, None]
    q = np.arange(128)[None, :]
    i0 = np.clip(q - k, -128, 128) + 128
    i1 = np.clip(128 + q - k, -128, 128) + 128
    i2 = np.full((128, 128), 256)
    m["bandT"] = np.ascontiguousarray(np.stack([tab[:, :, i0], tab[:, :, i1], tab[:, :, i2]], 2))
    m["c_fk"] = np.ascontiguousarray(inp["cache_fox_k"][:, s0:s0 + 4].reshape(DEPTH, 4, P, 256))
    m["c_fv"] = np.ascontiguousarray(inp["cache_fox_v"][:, s0:s0 + 4].reshape(DEPTH, 4, P, 256))
    m["c_flf"] = np.ascontiguousarray(inp["cache_fox_logf"][:, s0:s0 + 4])
    m["c_dk"] = np.ascontiguousarray(inp["cache_diff_k"][:, s0:s0 + 4].reshape(DEPTH, 4, P, 256))
    m["c_dv"] = np.ascontiguousarray(inp["cache_diff_v"][:, s0:s0 + 4].reshape(DEPTH, 4, P, 256))
    m["c_bk"] = np.ascontiguousarray(inp["cache_band_k"][:, s0:s0 + 4].reshape(DEPTH, 4, LB, 256))
    m["c_bv"] = np.ascontiguousarray(inp["cache_band_v"][:, s0:s0 + 4].reshape(DEPTH, 4, LB, 256))
    m["s_c"] = np.ascontiguousarray(inp["state_mlstm_c"][:, s0:s0 + 4])
    m["s_n"] = np.ascontiguousarray(inp["state_mlstm_n"][:, s0:s0 + 4])
    m["s_m"] = np.ascontiguousarray(inp["state_mlstm_m"][:, s0:s0 + 4])
    m["s_mcol"] = m["s_m"][..., None]
    m["s_mb"] = np.broadcast_to(m["s_m"][:, :, None, :], (DEPTH, 4, 128, 4))
    m.update(host_consts())
    return {k_: np.ascontiguousarray(v, dtype=f) for k_, v in m.items()}


def assemble(results, T, P, LB, nb, ns):
    ncores = len(results)
    pc = [min(2 * b, ncores - 1) for b in range(nb)] if ncores >= 2 * nb else list(range(nb))
    keep = min(512, T)

    def P_(fn):
        return np.stack([fn(results[c]) for c in pc], 0)

    def S_(fn):
        return np.concatenate([fn(results[c]) for c in range(ncores)], 0)
    y_prompt = P_(lambda r: r["YT"][:, :T].T)
    y_sample = S_(lambda r: r["YT"][:, T:].T.reshape(4, 32, D))
    outs = [y_prompt, y_sample]

    def fm_p(name, lo=0):
        return np.stack([P_(lambda r: r[name][l][:, lo:T].T) for l in range(DEPTH)], 0)

    def tm_p(name, lo=0):
        return np.stack([P_(lambda r: r[name][l][lo:T]) for l in range(DEPTH)], 0)

    def fm_s(name):
        return np.stack([S_(lambda r: r[name][l][:, T:].T.reshape(4, 32, -1)) for l in range(DEPTH)], 0)

    def tm_s(name):
        return np.stack([S_(lambda r: r[name][l][T:].reshape(4, 32, -1)) for l in range(DEPTH)], 0)
    B = nb
    p_fox_k = fm_p("o_fk").reshape(DEPTH, B, T, 4, 64)
    p_fox_v = tm_p("o_fv").reshape(DEPTH, B, T, 4, 64)
    p_fox_logf = tm_p("o_flf")
    p_diff_k = fm_p("o_dk").reshape(DEPTH, B, T, 4, 2, 32)
    p_diff_v = tm_p("o_dv").reshape(DEPTH, B, T, 4, 64)
    p_band_k = fm_p("o_bk", T - keep).reshape(DEPTH, B, keep, 4, 64)
    p_band_v = tm_p("o_bv", T - keep).reshape(DEPTH, B, keep, 4, 64)
    p_mc = np.stack([P_(lambda r: r["o_mc"][l][0, :, :, 0:64]) for l in range(DEPTH)], 0)
    p_mn = np.stack([P_(lambda r: r["o_mc"][l][0, :, :, 64]) for l in range(DEPTH)], 0)
    p_mm = np.stack([P_(lambda r: r["o_mm"][l][0]) for l in range(DEPTH)], 0)
    p_conv = np.stack([P_(lambda r: r["o_conv"][l][:, :, 0, :].transpose(2, 0, 1).reshape(3, 512)) for l in range(DEPTH)], 0)
    NS = 4 * ncores
    s_fox_k = fm_s("o_fk").reshape(DEPTH, NS, 32, 4, 64)
    s_fox_v = tm_s("o_fv").reshape(DEPTH, NS, 32, 4, 64)
    s_fox_logf = tm_s("o_flf")
    s_diff_k = fm_s("o_dk").reshape(DEPTH, NS, 32, 4, 2, 32)
    s_diff_v = tm_s("o_dv").reshape(DEPTH, NS, 32, 4, 64)
    s_band_k = fm_s("o_bk").reshape(DEPTH, NS, 32, 4, 64)
    s_band_v = tm_s("o_bv").reshape(DEPTH, NS, 32, 4, 64)
    s_mc = np.stack([S_(lambda r: r["o_mc"][l][1:5, :, :, 0:64]) for l in range(DEPTH)], 0)
    s_mn = np.stack([S_(lambda r: r["o_mc"][l][1:5, :, :, 64]) for l in range(DEPTH)], 0)
    s_mm = np.stack([S_(lambda r: r["o_mm"][l][1:5]) for l in range(DEPTH)], 0)
    s_conv = np.stack([S_(lambda r: r["o_conv"][l][:, :, 1:5, :].transpose(2, 3, 0, 1).reshape(4, 3, 512)) for l in range(DEPTH)], 0)
    outs += [p_fox_k, p_fox_v, p_fox_logf, p_diff_k, p_diff_v, p_band_k, p_band_v, p_mc, p_mn, p_mm, p_conv,
             s_fox_k, s_fox_v, s_fox_logf, s_diff_k, s_diff_v, s_band_k, s_band_v, s_mc, s_mn, s_mm, s_conv]
    return tuple(np.ascontiguousarray(o, dtype=np.float32) for o in outs)


_NC_CACHE = {}


def kernel(**inputs):
    inp = {k: np.asarray(v) for k, v in inputs.items()}
    T = inp["x_prompt"].shape[1]
    P = inp["cache_fox_k"].shape[2]
    LB = inp["cache_band_k"].shape[2]
    key = (T, P, LB)
    if key not in _NC_CACHE:
        _NC_CACHE[key] = build_program(T, P, LB)
    nc = _NC_CACHE[key]
    in_maps = [prep_core(inp, c, T, P, LB) for c in range(8)]
    res = run_bass_kernel_spmd(nc, in_maps, core_ids=list(range(8)))
    return assemble(res.results, T, P, LB, nb=inp["x_prompt"].shape[0], ns=inp["x_sample"].shape[0])
```
